# Optimizing a Trainium2 kernel written in Bass

```python
import math
import jax, jax.numpy as jnp
from jax import lax
import numpy as np

D_MODEL = 1024
BATCH = 2
SEQ = 8192
DEPTH = 4

N_MIXERS = 2
N_GDN_LAYERS = (DEPTH + N_MIXERS - 1) // N_MIXERS
N_GLA_LAYERS = DEPTH // N_MIXERS
CHUNK = 64
CONV_K = 4

GDN_HEADS = 8
GDN_DK = 128
GDN_DV = 256
GDN_KEY = GDN_HEADS * GDN_DK
GDN_VAL = GDN_HEADS * GDN_DV
GDN_QKV = 2 * GDN_KEY + GDN_VAL
GDN_IN = GDN_QKV + GDN_VAL + 2 * GDN_HEADS

GLA_HEADS = 4
GLA_DK = 128
GLA_DV = 256
GLA_KEY = GLA_HEADS * GLA_DK
GLA_VAL = GLA_HEADS * GLA_DV
GLA_RANK = 16
GLA_GATE_NORM = 16.0
GLA_IN = 2 * GLA_KEY + 2 * GLA_VAL + GLA_RANK

DEEP_ALPHA = (2.0 * DEPTH) ** 0.25
DEEP_BETA = (8.0 * DEPTH) ** -0.25
LN_EPS = 1e-5
RMS_EPS = 1e-6
L2_EPS = 1e-6

kernel_name = "hybrid_gdn_gla_deepnorm"


def layer_norm(x, g, b):
    xf = x.astype(jnp.float32)
    mu = jnp.mean(xf, -1, keepdims=True)
    xc = xf - mu
    var = jnp.mean(xc * xc, -1, keepdims=True)
    y = xc * lax.rsqrt(var + LN_EPS) * g.astype(jnp.float32) + b.astype(jnp.float32)
    return y.astype(x.dtype)


def gated_rmsnorm(o, w, gate):
    o = o * lax.rsqrt(jnp.mean(o * o, -1, keepdims=True) + RMS_EPS)
    return o * w.astype(jnp.float32) * jax.nn.silu(gate.astype(jnp.float32))


def l2norm(t):
    return t * lax.rsqrt(jnp.sum(t * t, -1, keepdims=True) + L2_EPS)


def causal_conv_silu(x, w):
    y = lax.conv_general_dilated(
        x, w[:, None, :].astype(x.dtype), window_strides=(1,), padding=[(CONV_K - 1, 0)],
        dimension_numbers=('NWC', 'WIO', 'NWC'), feature_group_count=x.shape[-1])
    return jax.nn.silu(y)


def to_chunks(t):
    B, T, H, d = t.shape
    return t.reshape(B, T // CHUNK, CHUNK, H, d).transpose(0, 3, 1, 2, 4)


def from_scan_chunks(o):
    N, B, H, C, d = o.shape
    return o.transpose(1, 0, 3, 2, 4).reshape(B, N * C, H, d)


def gated_delta_rule(q, k, v, g, beta):
    B, T, H, dk = q.shape
    dv = v.shape[-1]
    q = to_chunks(q) * (dk ** -0.5)
    k = to_chunks(k)
    v = to_chunks(v)
    g = to_chunks(g[..., None])[..., 0]
    beta = to_chunks(beta[..., None])
    gc = jnp.cumsum(g, axis=-1)
    causal = jnp.tril(jnp.ones((CHUNK, CHUNK), bool))
    strict = jnp.tril(jnp.ones((CHUNK, CHUNK), bool), -1)
    diff = gc[..., :, None] - gc[..., None, :]
    decay = jnp.where(causal, jnp.exp(jnp.where(causal, diff, 0.0)), 0.0)
    kb = k * beta
    A = jnp.where(strict, jnp.einsum('bhncd,bhnsd->bhncs', kb, k) * decay, 0.0)
    eye = jnp.eye(CHUNK, dtype=A.dtype)
    rhs = jnp.concatenate([v * beta, kb * jnp.exp(gc)[..., None]], axis=-1)
    uw = lax.linalg.triangular_solve(A + eye, rhs, left_side=True, lower=True, unit_diagonal=True)
    u, w = uw[..., :dv], uw[..., dv:]
    qk = jnp.einsum('bhncd,bhnsd->bhncs', q, k) * decay
    q_dec = q * jnp.exp(gc)[..., None]
    k_dec = k * jnp.exp(gc[..., -1:] - gc)[..., None]
    g_last = jnp.exp(gc[..., -1])

    def step(S, inp):
        qk_n, qd_n, w_n, u_n, kd_n, gl_n = inp
        v_new = u_n - jnp.einsum('bhcd,bhde->bhce', w_n, S)
        o = jnp.einsum('bhcd,bhde->bhce', qd_n, S) + jnp.einsum('bhcs,bhse->bhce', qk_n, v_new)
        S = S * gl_n[..., None, None] + jnp.einsum('bhcd,bhce->bhde', kd_n, v_new)
        return S, o

    xs = tuple(jnp.moveaxis(a, 2, 0) for a in (qk, q_dec, w, u, k_dec, g_last))
    S0 = jnp.zeros((B, H, dk, dv), jnp.float32)
    _, o = lax.scan(step, S0, xs)
    return from_scan_chunks(o)


def gla_chunked(q, k, v, gk):
    B, T, H, dk = q.shape
    dv = v.shape[-1]
    q = to_chunks(q) * (dk ** -0.5)
    k = to_chunks(k)
    v = to_chunks(v)
    b = jnp.cumsum(to_chunks(gk), axis=3)
    causal = jnp.tril(jnp.ones((CHUNK, CHUNK), bool))[:, :, None]

    def step(S, inp):
        q_n, k_n, v_n, b_n = inp
        diff = b_n[:, :, :, None, :] - b_n[:, :, None, :, :]
        decay = jnp.exp(jnp.where(causal, diff, -jnp.inf))
        A = jnp.einsum('bhcsd,bhsd->bhcs', q_n[:, :, :, None, :] * decay, k_n)
        o = jnp.einsum('bhcd,bhde->bhce', q_n * jnp.exp(b_n), S) + jnp.einsum('bhcs,bhse->bhce', A, v_n)
        b_last = b_n[:, :, -1]
        S = S * jnp.exp(b_last)[..., None] + jnp.einsum(
            'bhcd,bhce->bhde', k_n * jnp.exp(b_last[:, :, None, :] - b_n), v_n)
        return S, o

    xs = tuple(jnp.moveaxis(a, 2, 0) for a in (q, k, v, b))
    S0 = jnp.zeros((B, H, dk, dv), jnp.float32)
    _, o = lax.scan(step, S0, xs)
    return from_scan_chunks(o)


def gdn_mixer(x, w_in, conv_w, a_log, dt_bias, norm_w, w_out):
    B, T, _ = x.shape
    f32 = jnp.float32
    p = x @ w_in
    qkv = causal_conv_silu(p[..., :GDN_QKV], conv_w)
    rest = p[..., GDN_QKV:]
    gate = rest[..., :GDN_VAL].reshape(B, T, GDN_HEADS, GDN_DV)
    b_logit = rest[..., GDN_VAL:GDN_VAL + GDN_HEADS].astype(f32)
    a_logit = rest[..., GDN_VAL + GDN_HEADS:].astype(f32)
    q = l2norm(qkv[..., :GDN_KEY].reshape(B, T, GDN_HEADS, GDN_DK).astype(f32))
    k = l2norm(qkv[..., GDN_KEY:2 * GDN_KEY].reshape(B, T, GDN_HEADS, GDN_DK).astype(f32))
    v = qkv[..., 2 * GDN_KEY:].reshape(B, T, GDN_HEADS, GDN_DV).astype(f32)
    beta = jax.nn.sigmoid(b_logit)
    g = -jnp.exp(a_log.astype(f32)) * jax.nn.softplus(a_logit + dt_bias.astype(f32))
    o = gated_delta_rule(q, k, v, g, beta)
    o = gated_rmsnorm(o, norm_w, gate)
    return o.reshape(B, T, GDN_VAL).astype(x.dtype) @ w_out


def gla_mixer(x, w_in, w_gk_up, b_gk, norm_w, w_out):
    B, T, _ = x.shape
    f32 = jnp.float32
    p = x @ w_in
    q = p[..., :GLA_KEY].reshape(B, T, GLA_HEADS, GLA_DK).astype(f32)
    k = p[..., GLA_KEY:2 * GLA_KEY].reshape(B, T, GLA_HEADS, GLA_DK).astype(f32)
    v = p[..., 2 * GLA_KEY:2 * GLA_KEY + GLA_VAL].reshape(B, T, GLA_HEADS, GLA_DV).astype(f32)
    gate = p[..., 2 * GLA_KEY + GLA_VAL:2 * GLA_KEY + 2 * GLA_VAL].reshape(B, T, GLA_HEADS, GLA_DV)
    gk_low = p[..., 2 * GLA_KEY + 2 * GLA_VAL:]
    gk_logit = (gk_low @ w_gk_up).astype(f32) + b_gk.astype(f32)
    gk = (jax.nn.log_sigmoid(gk_logit) / GLA_GATE_NORM).reshape(B, T, GLA_HEADS, GLA_DK)
    o = gla_chunked(q, k, v, gk)
    o = gated_rmsnorm(o, norm_w, gate)
    return o.reshape(B, T, GLA_VAL).astype(x.dtype) @ w_out


def setup_inputs(seed: int = 0) -> dict:
    key = jax.random.key(seed)
    ks = jax.random.split(key, 16)
    f32 = jnp.float32

    def nrm(k, shape, s):
        return jax.random.normal(k, shape, f32) * s

    NA, NB, D = N_GDN_LAYERS, N_GLA_LAYERS, D_MODEL
    x = nrm(ks[0], (BATCH, SEQ, D), 1.0)
    gdn_col = jnp.concatenate([jnp.ones((2 * GDN_KEY,), f32), jnp.full((GDN_VAL,), DEEP_BETA, f32),
                               jnp.ones((GDN_VAL + 2 * GDN_HEADS,), f32)])
    gdn_w_in = nrm(ks[1], (NA, D, GDN_IN), D ** -0.5) * gdn_col
    gdn_conv_w = nrm(ks[2], (NA, CONV_K, GDN_QKV), CONV_K ** -0.5)
    gdn_a_log = jnp.log(jax.random.uniform(ks[3], (NA, GDN_HEADS), f32, 1.0, 16.0))
    dt = jnp.exp(jax.random.uniform(ks[4], (NA, GDN_HEADS), f32, math.log(1e-3), math.log(1e-1)))
    gdn_dt_bias = dt + jnp.log(-jnp.expm1(-dt))
    gdn_norm_w = 1.0 + nrm(ks[5], (NA, GDN_DV), 0.02)
    gdn_w_out = nrm(ks[6], (NA, GDN_VAL, D), (GDN_VAL ** -0.5) * DEEP_BETA)

    gla_col = jnp.concatenate([jnp.ones((2 * GLA_KEY,), f32), jnp.full((GLA_VAL,), DEEP_BETA, f32),
                               jnp.ones((GLA_VAL + GLA_RANK,), f32)])
    gla_w_in = nrm(ks[7], (NB, D, GLA_IN), D ** -0.5) * gla_col
    gla_w_gk_up = nrm(ks[8], (NB, GLA_RANK, GLA_KEY), GLA_RANK ** -0.5)
    gla_b_gk = nrm(ks[9], (NB, GLA_KEY), 0.1)
    gla_norm_w = 1.0 + nrm(ks[10], (NB, GLA_DV), 0.02)
    gla_w_out = nrm(ks[11], (NB, GLA_VAL, D), (GLA_VAL ** -0.5) * DEEP_BETA)

    ln_g = 1.0 + nrm(ks[12], (DEPTH, D), 0.02)
    ln_b = nrm(ks[13], (DEPTH, D), 0.02)
    return {"x": x, "gdn_w_in": gdn_w_in, "gdn_conv_w": gdn_conv_w, "gdn_a_log": gdn_a_log,
            "gdn_dt_bias": gdn_dt_bias, "gdn_norm_w": gdn_norm_w, "gdn_w_out": gdn_w_out,
            "gla_w_in": gla_w_in, "gla_w_gk_up": gla_w_gk_up, "gla_b_gk": gla_b_gk,
            "gla_norm_w": gla_norm_w, "gla_w_out": gla_w_out, "ln_g": ln_g, "ln_b": ln_b}


def reference(x, gdn_w_in, gdn_conv_w, gdn_a_log, gdn_dt_bias, gdn_norm_w, gdn_w_out,
              gla_w_in, gla_w_gk_up, gla_b_gk, gla_norm_w, gla_w_out, ln_g, ln_b):
    for i in range(DEPTH):
        j = i // N_MIXERS
        if i % N_MIXERS == 0:
            y = gdn_mixer(x, gdn_w_in[j], gdn_conv_w[j], gdn_a_log[j], gdn_dt_bias[j],
                          gdn_norm_w[j], gdn_w_out[j])
        else:
            y = gla_mixer(x, gla_w_in[j], gla_w_gk_up[j], gla_b_gk[j], gla_norm_w[j], gla_w_out[j])
        x = layer_norm(DEEP_ALPHA * x + y, ln_g[i], ln_b[i])
    return x
```

```python
import re as _re
import numpy as np
from contextlib import ExitStack
import concourse.bass as bass
import concourse.mybir as mybir
from concourse.bass_utils import run_bass_kernel_spmd

F32 = mybir.dt.float32
BF16 = mybir.dt.bfloat16
AF = mybir.ActivationFunctionType
ALU = mybir.AluOpType

D_MODEL = 1024
SEQ = 8192
BATCH = 2
DEPTH = 4
GDN_IN = 6160
GLA_IN = 3088
DEEP_ALPHA = (2.0 * DEPTH) ** 0.25
LN_EPS = 1e-5
RMS_EPS = 1e-6
L2_EPS = 1e-6
SCALE = 128.0 ** -0.5
GEN = 8000


class Tk:
    __slots__ = ("key", "gen", "val", "sem")

    def __init__(self, key, gen, val, sem):
        self.key, self.gen, self.val, self.sem = key, gen, val, sem


class T:
    __slots__ = ("ap", "w", "r", "name", "excl")

    def __init__(self, ap, name=""):
        self.excl = False
        self.ap = ap
        self.w = None
        self.r = {}
        self.name = name

    def __getitem__(self, idx):
        return self.ap[idx]


class Ctx:
    def __init__(self, nc, stack):
        self.nc = nc
        self.stack = stack
        self.sem_stack = stack
        self.eng = {"pe": nc.tensor, "act": nc.scalar, "dve": nc.vector, "pool": nc.gpsimd, "sp": nc.sync}
        self.seq = {k: 0 for k in self.eng}
        self.sems = {k: [] for k in self.eng}
        self.known = {k: {} for k in self.eng}
        self.dma_sems = {}
        self.dma_cnt = {}
        self.nsem = 0
        self.ninst = 0
        self._rec = None

    def new_sem(self, name):
        self.nsem += 1
        return self.sem_stack.enter_context(self.nc.semaphore(f"{name}_{self.nsem}"))

    def sb(self, name, shape, dt):
        self.nsem += 1
        name = f"{name}_u{self.nsem}"
        return T(self.stack.enter_context(self.nc.sbuf_tensor(name, list(shape), dt)), name)

    def ps(self, name, shape, dt):
        t = T(self.stack.enter_context(self.nc.psum_tensor(name, list(shape), dt)), name)
        t.excl = True
        return t

    def _wait(self, e, tk):
        if tk is None:
            return
        if tk.key == e and e == "pe":
            return
        kn = self.known[e].get(tk.key)
        if kn is not None and kn >= (tk.gen, tk.val):
            return
        self.eng[e].wait_ge(tk.sem, tk.val)
        self.known[e][tk.key] = (tk.gen, tk.val)
        self.ninst += 1

    def _deps(self, e, reads, writes, defer=False):
        need = []
        for t in reads:
            if t.w is not None:
                need.append(t.w)
        for t in writes:
            if t.w is not None:
                need.append(t.w)
            need.extend(t.r.values())
        best = {}
        for tk in need:
            if tk.key == e and e == "pe":
                continue
            kn = self.known[e].get(tk.key)
            if kn is not None and kn >= (tk.gen, tk.val):
                continue
            b = best.get(tk.key)
            if b is None or (b.gen, b.val) < (tk.gen, tk.val):
                best[tk.key] = tk
        toks = list(best.values())
        last = None
        if defer and toks and EMBED:
            last = toks.pop()
        for tk in toks:
            self._wait(e, tk)
        return last

    def _mark(self, tk, reads, writes):
        for t in reads:
            t.r[tk.key] = tk
        for t in writes:
            t.w = tk
            t.r = {}

    def mark(self):
        if self._rec is not None:
            self._rec.append(("mark", None))

    @staticmethod
    def split_at_mark(lst):
        for k, it in enumerate(lst):
            if it[0] == "mark":
                return lst[:k], lst[k + 1:]
        return lst, []

    def rec(self, fn, *args):
        assert self._rec is None
        self._rec = []
        fn(*args)
        out, self._rec = self._rec, None
        return out

    def play(self, lists):
        if SCHED:
            return self.play_sched(lists)
        return self.play_prop(lists)

    @staticmethod
    def _dur(kind, e, writes):
        if kind == "dma":
            return 0.06, 2.5
        if kind == "collective":
            return 0.3, 25.0
        if e == "pe":
            return 0.22, 0.35
        n = 128
        for t in writes:
            try:
                sh = t.ap.shape
                m = 1
                for d in sh[1:]:
                    m *= int(d)
                n = max(n, min(m, 512))
            except Exception:
                pass
        d = 0.1 + n / 960.0
        return d, d + 0.25

    def play_sched(self, lists):
        ops = []
        for l in lists:
            ops += [it for it in l if it[0] != "mark"]
        n = len(ops)
        if n == 0:
            return
        lw, rd = {}, {}
        preds = [None] * n
        meta = [None] * n
        for k, (kind, a) in enumerate(ops):
            if kind == "op":
                e, fn, reads, writes = a[0]
            elif kind == "dma":
                e, reads, writes = a[1]["e"], a[1]["reads"], a[1]["writes"]
            else:
                e, reads, writes = "pool", a[1]["reads"], a[1]["writes"]
            ex = [t for t in reads if t.excl]
            if ex:
                reads = [t for t in reads if not t.excl]
                writes = list(writes) + ex
            ps = set()
            for t in reads:
                if id(t) in lw:
                    ps.add(lw[id(t)])
            for t in writes:
                if id(t) in lw:
                    ps.add(lw[id(t)])
                ps.update(rd.get(id(t), ()))
            ps.discard(k)
            preds[k] = ps
            for t in reads:
                rd.setdefault(id(t), []).append(k)
            for t in writes:
                lw[id(t)] = k
                rd[id(t)] = []
            meta[k] = (e,) + self._dur(kind, e, writes)
        succs = [[] for _ in range(n)]
        npred = [len(p) for p in preds]
        for k in range(n):
            for p in preds[k]:
                succs[p].append(k)
        fin = [0.0] * n
        clock = {}
        ready = [k for k in range(n) if npred[k] == 0]
        rt = {k: 0.0 for k in ready}
        order = []
        while ready:
            best, bs = None, None
            for k in ready:
                st = max(rt[k], clock.get(meta[k][0], 0.0))
                key = (st, k)
                if bs is None or key < bs:
                    best, bs = k, key
            ready.remove(best)
            e, busy, lat = meta[best]
            st = bs[0]
            clock[e] = st + busy
            fin[best] = st + lat
            order.append(best)
            for q in succs[best]:
                npred[q] -= 1
                rt[q] = max(rt.get(q, 0.0), fin[best])
                if npred[q] == 0:
                    ready.append(q)
        assert len(order) == n
        for k in order:
            kind, a = ops[k]
            getattr(self, kind)(*a[0], **a[1])

    def play_prop(self, lists):
        lists = [l for l in lists if l]
        pos = [0] * len(lists)
        total = sum(len(l) for l in lists)
        for _ in range(total):
            k = min((i for i in range(len(lists)) if pos[i] < len(lists[i])),
                    key=lambda i: (pos[i] + 0.5) / len(lists[i]))
            kind, a = lists[k][pos[k]]
            pos[k] += 1
            if kind == "mark":
                continue
            getattr(self, kind)(*a[0], **a[1])

    def op(self, e, fn, reads=(), writes=()):
        if self._rec is not None:
            self._rec.append(("op", ((e, fn, list(reads), list(writes)), {})))
            return None
        ex = [t for t in reads if t.excl]
        if ex:
            reads = [t for t in reads if not t.excl]
            writes = list(writes) + ex
        last = self._deps(e, reads, writes, defer=True)
        ins = fn(self.eng[e])
        if last is not None:
            ins._wait_ge(last.sem, last.val)
            self.known[e][last.key] = (last.gen, last.val)
        n = self.seq[e]
        gen, val = n // GEN, n % GEN + 1
        if gen >= len(self.sems[e]):
            self.sems[e].append(self.new_sem(f"s_{e}"))
        sem = self.sems[e][gen]
        ins.then_inc(sem, 1)
        self.seq[e] = n + 1
        self.ninst += 1
        tk = Tk(e, gen, val, sem)
        self._mark(tk, reads, writes)
        return tk

    def dma(self, out_ap, in_ap, reads=(), writes=(), key=None, e="sp", slow=False):
        if self._rec is not None:
            self._rec.append(("dma", ((out_ap, in_ap), dict(reads=list(reads), writes=list(writes), key=key, e=e,
                                                              slow=slow))))
            return None
        key = _re.sub(r"_u\d+", "", str(key))
        if key not in self.dma_sems:
            self.dma_sems[key] = self.new_sem("d")
            self.dma_cnt[key] = 0
        self._deps(e, reads, writes)
        sem = self.dma_sems[key]
        if slow:
            ins = self.eng[e].dma_start(out=out_ap, in_=in_ap, allow_slow_non_contiguous=True)
        else:
            ins = self.eng[e].dma_start(out=out_ap, in_=in_ap)
        self.dma_cnt[key] += 16
        ins.then_inc(sem, 16)
        self.ninst += 1
        tk = Tk(("dma", key), 0, self.dma_cnt[key], sem)
        self._mark(tk, reads, writes)
        return tk

    def collective(self, kind, in_ap, out_ap, groups, reads=(), writes=()):
        if self._rec is not None:
            self._rec.append(("collective", ((kind, in_ap, out_ap, groups), dict(reads=list(reads),
                                                                                 writes=list(writes)))))
            return None
        e = "pool"
        self._deps(e, reads, writes)
        if not hasattr(self, "cc_sem"):
            self.cc_sem = self.new_sem("cc")
            self.cc_cnt = 0
        sem = self.cc_sem
        ins = self.eng[e].collective_compute(kind, ALU.bypass, replica_groups=groups, ins=[in_ap], outs=[out_ap])
        ins.then_inc(sem, 1)
        self.cc_cnt += 1
        self.ninst += 1
        tk = Tk(("cc", 0), 0, self.cc_cnt, sem)
        self._mark(tk, reads, writes)
        return tk

    def barrier(self):
        toks = []
        for e2, n in self.seq.items():
            if n > 0:
                m = n - 1
                toks.append(Tk(e2, m // GEN, m % GEN + 1, self.sems[e2][m // GEN]))
        for key, cnt in self.dma_cnt.items():
            if cnt > 0:
                toks.append(Tk(("dma", key), 0, cnt, self.dma_sems[key]))
        for e in self.eng:
            for tk in toks:
                if tk.key != e:
                    self._wait(e, tk)

    def finish(self, tks):
        for tk in tks:
            if tk is not None:
                self._wait("sp", tk)
        self.barrier()


C_ID, C_M1, C_M2, C_ONE = 0, 128, 256, 384
C_NMR = 512
C_NMC = 512 + 7 * 128
C_EPSR = 512 + 14 * 128
C_EPSL = C_EPSR + 1
C_TOT = C_EPSR + 8


def make_consts():
    j = np.arange(128)
    c = np.zeros((128, C_TOT), np.float32)
    c[:, C_ID:C_ID + 128] = np.eye(128)
    c[:, C_M1:C_M1 + 128] = (j[:, None] <= j[None, :])
    c[:, C_M2:C_M2 + 128] = (j[:, None] > j[None, :])
    c[:, C_ONE:C_ONE + 128] = 1.0
    c[:, C_EPSR] = RMS_EPS
    c[:, C_EPSL] = LN_EPS
    for l in range(7):
        b = 1 << l
        row = ((j[:, None] // (2 * b)) == (j[None, :] // (2 * b))) & ((j[:, None] // b) > (j[None, :] // b))
        c[:, C_NMR + l * 128:C_NMR + (l + 1) * 128] = -row.astype(np.float32)
        c[:, C_NMC + l * 128:C_NMC + (l + 1) * 128] = -row.T.astype(np.float32)
    return c


V_LNG, V_LNB, V_NW = 0, 1024, 2048
V_A = 2304
V_DT = 2312
V_CW = 2320
V_BGK = 2448
VTOT = 2960


class StopEmit(Exception):
    pass


import os as _os
_KSTOP = float(_os.environ.get("KSTOP", "99"))


STOPPED = [False]
SCHED = int(_os.environ.get("KSCHED", "1"))
EMBED = int(_os.environ.get("KEMBED", "1"))


def ck(k):
    if _KSTOP <= k:
        STOPPED[0] = True
        return True
    return False


class Prog:
    def __init__(self, layers, ntiles, ranks=1):
        self.layers = layers
        self.nt = ntiles
        self.ranks = ranks
        self.build()

    def nh(self, kind):
        return (8 if kind == "gdn" else 4) // self.ranks

    def build(self):
        from contextlib import ExitStack
        nc = bass.Bass("TRN2", target_bir_lowering=False)
        self.nc = nc
        nt = self.nt
        L = len(self.layers)
        ntok = nt * 128
        dr = {}
        dr["x"] = nc.dram_tensor("x", [ntok, D_MODEL], F32, kind="ExternalInput").ap()
        if "gdn" in self.layers:
            dr["xhalo"] = nc.dram_tensor("xhalo", [L, 3, D_MODEL], F32, kind="ExternalInput").ap()
        dr["consts"] = nc.dram_tensor("consts", [128, C_TOT], F32, kind="ExternalInput").ap()
        dr["vecs"] = nc.dram_tensor("vecs", [L, 128, VTOT], F32, kind="ExternalInput").ap()
        for li, kind in enumerate(self.layers):
            nh = self.nh(kind)
            cin = nh * 768 + (2 * nh if kind == "gdn" else 16)
            val = 2048 if kind == "gdn" else 1024
            dr[f"win{li}"] = nc.dram_tensor(f"win{li}", [D_MODEL, cin], F32, kind="ExternalInput").ap()
            dr[f"wout{li}"] = nc.dram_tensor(f"wout{li}", [val, D_MODEL], F32, kind="ExternalInput").ap()
            dr[f"sin{li}"] = nc.dram_tensor(f"sin{li}", [128, nh * 256], F32, kind="ExternalInput").ap()
            dr[f"sout{li}"] = nc.dram_tensor(f"sout{li}", [128, nh * 256], F32, kind="ExternalOutput").ap()
            if kind == "gla":
                dr[f"wgk{li}"] = nc.dram_tensor(f"wgk{li}", [16, nh * 128], F32, kind="ExternalInput").ap()
            self.drt = getattr(self, "drt", {})
            self.tpc = getattr(self, "tpc", {})
            tpc = max(1, (1 << 20) // (128 * nh * 256 * 2))
            self.tpc[li] = tpc
            for q in range((nt + tpc - 1) // tpc):
                tq = min(tpc, nt - q * tpc)
                self.drt[f"ont{li}_{q}"] = nc.dram_tensor(f"ont{li}_{q}", [tq * 128, nh * 256], BF16)
                if self.ranks > 1:
                    self.drt[f"onta{li}_{q}"] = nc.dram_tensor(f"onta{li}_{q}", [self.ranks * tq * 128, nh * 256], BF16)
        dr["y"] = nc.dram_tensor("y", [ntok, D_MODEL], F32, kind="ExternalOutput").ap()
        for i in range(min(2, L - 1)):
            dr[f"act{i}"] = nc.dram_tensor(f"act{i}", [ntok, D_MODEL], F32).ap()
        self.dr = dr
        self.dt = {}
        with ExitStack() as stack:
            cx = Ctx(nc, stack)
            self.cx = cx
            self.emit(stack)
        return nc

    def dtile(self, name, i):
        k = (name, i)
        if k not in self.dt:
            self.dt[k] = T(None, f"{name}{i}")
        return self.dt[k]

    def emit(self, stack):
        from contextlib import ExitStack
        cx, nc, dr = self.cx, self.nc, self.dr
        L = len(self.layers)
        cst = cx.sb("cst", [128, C_TOT], F32)
        cx.dma(cst[:, :], dr["consts"][:, :], writes=[cst], key="cst")
        idb = cx.sb("idb", [128, 128], BF16)
        oneb = cx.sb("oneb", [128, 128], BF16)
        cx.op("dve", lambda e: e.tensor_copy(out=idb[:, :], in_=cst[:, C_ID:C_ID + 128]), [cst], [idb])
        cx.op("dve", lambda e: e.tensor_copy(out=oneb[:, :], in_=cst[:, C_ONE:C_ONE + 128]), [cst], [oneb])
        m1b = cx.sb("m1b", [128, 128], BF16)
        m2b = cx.sb("m2b", [128, 128], BF16)
        cx.op("dve", lambda e: e.tensor_copy(out=m1b[:, :], in_=cst[:, C_M1:C_M1 + 128]), [cst], [m1b])
        cx.op("dve", lambda e: e.tensor_copy(out=m2b[:, :], in_=cst[:, C_M2:C_M2 + 128]), [cst], [m2b])
        self.m1b, self.m2b = m1b, m2b
        self.cst, self.idb, self.oneb = cst, idb, oneb
        self.pb = [cx.ps(f"pb{i}", [128, 512], F32) for i in range(7)]
        pbb0 = cx.ps("pbb0", [128, 1024], BF16)
        self.pbb = [pbb0, pbb0]
        self.wbuf = {}
        self.pref = []
        out_tks = []
        try:
            self.emit_layers(stack, out_tks)
        except StopEmit:
            cx.sem_stack = stack
            cx.stack = stack
        cx.finish(out_tks)

    def emit_layers(self, stack, out_tks):
        cx, nc, dr = self.cx, self.nc, self.dr
        L = len(self.layers)
        for li, kind in enumerate(self.layers):
            src = dr["x"] if li == 0 else dr[f"act{(li - 1) % 2}"]
            srcn = "x" if li == 0 else f"act{(li - 1) % 2}"
            dst = dr["y"] if li == L - 1 else dr[f"act{li % 2}"]
            dstn = "y" if li == L - 1 else f"act{li % 2}"
            with ExitStack() as ls:
                cx.stack = ls
                vec = cx.sb(f"vec{li}", [128, VTOT], F32)
                cx.dma(vec[:, :], dr["vecs"][li], writes=[vec], key="vec")
                self.vec = vec
                with ExitStack() as pa:
                    cx.stack = pa
                    self.alloc_w("win", li)
                    self.weight_ops("win", li)
                    self.pref = []
                    if kind == "gla":
                        tks = self.gla_phase_a(li, src, srcn)
                    else:
                        tks = self.gdn_phase_a(li, src, srcn)
                    out_tks += tks
                    cx.play([self.take_pref(1)])
                    cx.barrier()
                with ExitStack() as pbk:
                    cx.stack = pbk
                    if STOPPED[0]:
                        continue
                    self.alloc_w("wout", li)
                    self.weight_ops("wout", li)
                    self.pref = []
                    tks = self.phase_b(li, kind, src, srcn, dst, dstn, last=(li == L - 1))
                    out_tks += tks
                    cx.play([self.take_pref(1)])
                    cx.barrier()
            cx.stack = stack

    def alloc_w(self, which, li):
        cx = self.cx
        kind = self.layers[li]
        nh = self.nh(kind)
        if which == "win":
            shp = [128, 8, nh * 768 + (2 * nh if kind == "gdn" else 16)]
        else:
            shp = [128, (2048 if kind == "gdn" else 1024) // 128, 1024]
        self.wbuf[f"{which}_{kind}"] = cx.sb(f"{which}_{kind}", shp, BF16)
        self.wstg = [cx.sb(f"wstg{i}", [128, 1024], F32) for i in range(2)]

    def weight_ops(self, which, li):
        kind = self.layers[li]
        nh = self.nh(kind)
        if which == "win":
            rows, cols = 1024, nh * 768 + (2 * nh if kind == "gdn" else 16)
        else:
            rows, cols = (2048 if kind == "gdn" else 1024), 1024
        self.load_weight(self.dr[f"{which}{li}"], self.wbuf[f"{which}_{kind}"], rows, cols, self.wstg, pool_only=False)

    def take_pref(self, nleft):
        n = (len(self.pref) + nleft - 1) // max(1, nleft)
        n += n % 2
        out, self.pref = self.pref[:n], self.pref[n:]
        return out

    def load_weight(self, wdram, wsb, rows, cols, stg, pool_only=False):
        cx = self.cx
        kcn = rows // 128
        cb = 1024
        n = 0
        for kc in range(kcn):
            for c0 in range(0, cols, cb):
                c1 = min(cols, c0 + cb)
                s = stg[n % 2]
                cx.dma(s[:, 0:c1 - c0], wdram[kc * 128:(kc + 1) * 128, c0:c1], writes=[s], key=f"wstg{n % 2}",
                       e="sp")
                eng = "pool" if (n % 2 == 0 or pool_only) else "dve"
                cx.op(eng, lambda e, s=s, kc=kc, c0=c0, c1=c1: e.tensor_copy(out=wsb[:, kc, c0:c1],
                                                                             in_=s[:, 0:c1 - c0]), [s], [wsb])
                n += 1

    def load_x_tile(self, src, srcn, i, xt):
        cx = self.cx
        cx.dma(xt[:, :], src[i * 128:(i + 1) * 128, :], reads=[self.dtile(srcn, i)], writes=[xt],
               key=xt.name)

    def make_xT(self, xt, xb, xT):
        cx = self.cx
        pbb = self.pbb[0]
        cx.op("pool", lambda e: e.tensor_copy(out=xb[:, :], in_=xt[:, :]), [xt], [xb])
        for kc in range(8):
            cx.op("pe", lambda e, kc=kc: e.transpose(out=pbb[:, kc * 128:(kc + 1) * 128],
                                                     in_=xb[:, kc * 128:(kc + 1) * 128], identity=self.idb[:, :]),
                  [xb, self.idb], [pbb])
        cx.op("act", lambda e: e.activation(out=xT[:, :, :], in_=pbb[:, :].rearrange("p (k t) -> p k t", k=8),
                                            func=AF.Copy), [pbb], [xT])

    def split(self, src_t, src_ap, hi, lo, hi_ap=None, lo_ap=None):
        cx = self.cx
        hi_ap = hi[:, :] if hi_ap is None else hi_ap
        lo_ap = lo[:, :] if lo_ap is None else lo_ap
        cx.op("dve", lambda e: e.tensor_copy(out=hi_ap, in_=src_ap), [src_t], [hi])
        cx.op("dve", lambda e: e.tensor_tensor(out=lo_ap, in0=src_ap, in1=hi_ap, op=ALU.subtract), [src_t, hi], [lo])

    def silu(self, src_t, src_ap, dst_t, dst_ap, tmp_t, tmp_ap):
        cx = self.cx
        one = self.cst[:, C_ONE:C_ONE + 1]
        cx.op("act", lambda e: e.activation(out=tmp_ap, in_=src_ap, func=AF.Exp, scale=-1.0), [src_t], [tmp_t])
        cx.op("act", lambda e: e.activation(out=tmp_ap, in_=tmp_ap, func=AF.Ln, bias=one), [tmp_t, self.cst], [tmp_t])
        cx.op("act", lambda e: e.activation(out=tmp_ap, in_=tmp_ap, func=AF.Exp, scale=-1.0), [tmp_t], [tmp_t])
        cx.op("dve", lambda e: e.tensor_tensor(out=dst_ap, in0=src_ap, in1=tmp_ap, op=ALU.mult), [src_t, tmp_t], [dst_t])

    def mm_acc(self, ps_t, ps_ap, pairs, reads):
        cx = self.cx
        n = len(pairs)
        for k, (l, r) in enumerate(pairs):
            cx.op("pe", lambda e, l=l, r=r, k=k: e.matmul(ps_ap, lhsT=l, rhs=r, start=(k == 0), stop=(k == n - 1)),
                  reads, [ps_t])

    def store_onT(self, o_n, nkc, i):
        cx = self.cx
        onT = self.onT
        for g0 in range(0, nkc, 8):
            n = min(8, nkc - g0)
            pbb = self.pbb[1]
            for k in range(n):
                kc = g0 + k
                cx.op("pe", lambda e, kc=kc, k=k: e.transpose(out=pbb[:, k * 128:(k + 1) * 128],
                                                              in_=o_n[:, kc * 128:(kc + 1) * 128],
                                                              identity=self.idb[:, :]), [o_n, self.idb], [pbb])
            cx.op("act", lambda e, g0=g0, n=n: e.activation(out=onT[:, g0 * 128:(g0 + n) * 128],
                                                           in_=pbb[:, 0:n * 128], func=AF.Copy), [pbb], [onT])
        li = self.li
        tpc = self.tpc[li]
        q, t = i // tpc, i % tpc
        tk = cx.dma(self.drt[f"ont{li}_{q}"].ap()[t * 128:(t + 1) * 128, :], onT[:, 0:nkc * 128], reads=[onT],
                    writes=[self.dtile("ont", i)], key="ontst")
        if self.ranks > 1 and (t == tpc - 1 or i == self.nt - 1):
            R = self.ranks
            groups = [list(range(g * R, (g + 1) * R)) for g in range(8 // R)]
            tk = cx.collective("AllGather", self.drt[f"ont{li}_{q}"].ap().opt(), self.drt[f"onta{li}_{q}"].ap().opt(),
                               groups, reads=[self.dtile("ont", k) for k in range(q * tpc, i + 1)],
                               writes=[self.dtile("onta", q)])
        return tk

    def rms_gate(self, o_ps_t, o_ps, gs, h, o_n, tmp):
        cx = self.cx
        vec = self.vec
        o_f, junk, ssq, rstd = tmp
        cx.op("act", lambda e: e.activation(out=o_f[:, :], in_=o_ps, func=AF.Copy), [o_ps_t], [o_f])
        if ck(4.71):
            return
        cx.op("act", lambda e: e.activation(out=junk[:, :], in_=o_f[:, :], func=AF.Square, scale=1.0 / 16.0,
                                            accum_out=ssq[:, 0:1]), [o_f], [junk, ssq])
        if ck(4.72):
            return
        cx.op("act", lambda e: e.activation(out=rstd[:, 0:1], in_=ssq[:, 0:1], func=AF.Ln,
                                            bias=self.cst[:, C_EPSR:C_EPSR + 1]), [ssq, self.cst], [rstd])
        if ck(4.73):
            return
        cx.op("act", lambda e: e.activation(out=rstd[:, 0:1], in_=rstd[:, 0:1], func=AF.Exp, scale=-0.5),
              [rstd], [rstd])
        if ck(4.74):
            return
        cx.op("dve", lambda e: e.scalar_tensor_tensor(out=o_f[:, :], in0=o_f[:, :], scalar=rstd[:, 0:1],
                                                      in1=vec[:, V_NW:V_NW + 256], op0=ALU.mult, op1=ALU.mult),
              [o_f, rstd, vec], [o_f])
        if ck(4.75):
            return
        cx.op("dve", lambda e: e.tensor_tensor(out=o_n[:, h * 256:(h + 1) * 256], in0=o_f[:, :], in1=gs,
                                               op=ALU.mult), [o_f] + self._gs_t, [o_n])

    def gla_phase_a(self, li, src, srcn):
        cx, dr, cst, vec = self.cx, self.dr, self.cst, self.vec
        pb = self.pb
        nt = self.nt
        M1 = cst[:, C_M1:C_M1 + 128]
        M2 = cst[:, C_M2:C_M2 + 128]
        tks = []
        self.li = li
        nh = self.nh("gla")
        NK = nh * 128
        NV = nh * 256
        CIN = nh * 768 + 16
        win = self.wbuf["win_gla"]
        wgk = cx.sb("wgk", [16, NK], F32)
        cx.dma(wgk[:, :], dr[f"wgk{li}"][:, :], writes=[wgk], key="wgk")
        wgk_h = cx.sb("wgk_h", [16, NK], BF16)
        wgk_l = cx.sb("wgk_l", [16, NK], BF16)
        self.split(wgk, wgk[:, :], wgk_h, wgk_l)
        m1b, m2b = self.m1b, self.m2b
        S_f = cx.sb("S_f", [128, nh, 256], F32)
        S_b = [cx.sb(f"S_b{h}", [128, 256], BF16) for h in range(nh)]
        cx.dma(S_f[:, :, :], dr[f"sin{li}"].rearrange("p (h v) -> p h v", h=nh), writes=[S_f], key="S")
        for h in range(nh):
            cx.op("dve", lambda e, h=h: e.tensor_copy(out=S_b[h][:, :], in_=S_f[:, h, :]), [S_f], [S_b[h]])
        xts = [cx.sb(f"xt{i}", [128, 1024], F32) for i in range(2)]
        Eq = cx.sb("Eq", [128, 128], F32)
        Ek = cx.sb("Ek", [128, 128], F32)
        qt = cx.sb("qt", [128, 128], BF16)
        kt = cx.sb("kt", [128, 128], BF16)
        ATm = cx.sb("ATm", [128, 128], BF16)
        o_f = cx.sb("o_f", [128, 256], F32)
        junk = cx.sb("junk", [128, 256], F32)
        ssq = cx.sb("ssq", [128, 1], F32)
        rstd = cx.sb("rstd", [128, 1], F32)
        self.onT = cx.sb("onT", [128, 2048], BF16)
        gtmp = cx.sb("gtmp", [128, NV], F32)
        tks = []

        fsets = []
        for k_ in range(3):
            fsets.append(dict(
                xb=cx.sb("xb", [128, 1024], BF16),
                xT=cx.sb("xT", [128, 8, 128], BF16),
                qT_f=cx.sb("qT_f", [128, nh, 128], F32),
                kT_f=cx.sb("kT_f", [128, nh, 128], F32),
                ktok=cx.sb("ktok", [128, NK], F32),
                v_b=cx.sb("v_b", [128, NV], BF16),
                gs=cx.sb("gs", [128, NV], F32),
                gkl=cx.sb("gkl", [16, 128], F32),
                gkl_h=cx.sb("gkl_h", [16, 128], BF16),
                gkl_l=cx.sb("gkl_l", [16, 128], BF16),
                lg=cx.sb("lg", [128, NK], F32),
                lg_h=cx.sb("lg_h", [128, NK], BF16),
                lg_l=cx.sb("lg_l", [128, NK], BF16),
                khat=cx.sb("khat", [128, NK], BF16),
                Er=cx.sb("Er", [128, NK], F32),
                o_n=cx.sb("o_n", [128, NV], BF16),
            ))
        UNP = ('xb', 'xT', 'qT_f', 'kT_f', 'ktok', 'v_b', 'gs', 'gkl', 'gkl_h', 'gkl_l', 'lg', 'lg_h', 'lg_l', 'khat', 'Er', 'o_n')

        aout = [[dict(qt=cx.sb("qt", [128, 128], BF16), ATm=cx.sb("ATm", [128, 128], BF16),
                      Eq=cx.sb("Eq", [128, 128], F32)) for _ in range(2)] for h in range(nh)]

        GT, NGS = 4, 3
        gsets = [dict(xT4=cx.sb("xT4", [128, 8, GT * 128], BF16), qT4=cx.sb("qT4", [128, nh, GT * 128], F32),
                      kT4=cx.sb("kT4", [128, nh, GT * 128], F32), gkl4=cx.sb("gkl4", [16, GT * 128], F32))
                 for _ in range(NGS)]
        xbs = [cx.sb(f"xbg{t}", [128, 1024], BF16) for t in range(2)]
        idb = self.idb

        def G(gi):
            GS = gsets[gi % NGS]
            xT4, qT4, kT4, gkl4 = GS["xT4"], GS["qT4"], GS["kT4"], GS["gkl4"]
            tiles = [i for i in range(gi * GT, min(nt, (gi + 1) * GT))]
            N = len(tiles) * 128
            for t, i in enumerate(tiles):
                xt, xb = xts[t % 2], xbs[t % 2]
                self.load_x_tile(src, srcn, i, xt)
                cx.op("pool", lambda e, xt=xt, xb=xb: e.tensor_copy(out=xb[:, :], in_=xt[:, :]), [xt], [xb])
                pbb = self.pbb[0]
                for kc in range(8):
                    cx.op("pe", lambda e, kc=kc, xb=xb, pbb=pbb: e.transpose(
                        out=pbb[:, kc * 128:(kc + 1) * 128], in_=xb[:, kc * 128:(kc + 1) * 128],
                        identity=idb[:, :]), [xb, idb], [pbb])
                cx.op("act", lambda e, t=t, pbb=pbb: e.activation(
                    out=xT4[:, :, t * 128:(t + 1) * 128], in_=pbb[:, :].rearrange("p (k t) -> p k t", k=8),
                    func=AF.Copy), [pbb], [xT4])
                cx.mark()
            for blk in range(2 * nh):
                dst = qT4 if blk < nh else kT4
                p = pb[blk % 2]
                self.mm_acc(p, p[:, 0:N], [(win[:, kc, blk * 128:(blk + 1) * 128], xT4[:, kc, 0:N]) for kc in range(8)],
                            [win, xT4])
                cx.op("act", lambda e, p=p, dst=dst, blk=blk: e.activation(out=dst[:, blk % nh, 0:N], in_=p[:, 0:N],
                                                                          func=AF.Copy), [p], [dst])
                cx.mark()
            p = pb[2]
            self.mm_acc(p, p[0:16, 0:N], [(win[:, kc, 2 * NK + 2 * NV:2 * NK + 2 * NV + 16], xT4[:, kc, 0:N])
                                          for kc in range(8)], [win, xT4])
            cx.op("act", lambda e, p=p: e.activation(out=gkl4[:, 0:N], in_=p[0:16, 0:N], func=AF.Copy), [p], [gkl4])
            cx.mark()

        def front(i):
            FS = fsets[i % 3]
            xb, xT, qT_f, kT_f, ktok, v_b, gs, gkl, gkl_h, gkl_l, lg, lg_h, lg_l, khat, Er, o_n = (FS[k] for k in UNP)
            GS = gsets[(i // GT) % NGS]
            xT_T, qT4, kT4, gkl4 = GS["xT4"], GS["qT4"], GS["kT4"], GS["gkl4"]
            t0_ = (i % GT) * 128
            xTv = xT_T[:, :, t0_:t0_ + 128]
            p = pb[0]
            self.mm_acc(p, p[:, 0:NK], [(xTv[:, kc, :], win[:, kc, NK:2 * NK]) for kc in range(8)], [win, xT_T])
            cx.op("act", lambda e, p=p: e.activation(out=ktok[:, :], in_=p[:, 0:NK], func=AF.Copy), [p], [ktok])
            for wi, (dstt, fn, c_base) in enumerate(((v_b, AF.Copy, 2 * NK), (gs, AF.Silu, 2 * NK + NV))):
                for ci, c0 in enumerate(range(0, NV, 512)):
                    w = min(512, NV - c0)
                    p = pb[(wi + ci) % 2]
                    self.mm_acc(p, p[:, 0:w], [(xTv[:, kc, :], win[:, kc, c_base + c0:c_base + c0 + w])
                                               for kc in range(8)], [win, xT_T])
                    if fn == AF.Silu:
                        self.silu(p, p[:, 0:w], dstt, dstt[:, c0:c0 + w], gtmp, gtmp[:, c0:c0 + w])
                    else:
                        cx.op("act", lambda e, p=p, dstt=dstt, fn=fn, c0=c0, w=w: e.activation(
                            out=dstt[:, c0:c0 + w], in_=p[:, 0:w], func=fn), [p], [dstt])
            p = pb[3]
            self.split(gkl4, gkl4[:, t0_:t0_ + 128], gkl_h, gkl_l)
            self.mm_acc(p, p[:, 0:NK], [(gkl_h[:, :], wgk_h[:, :]), (gkl_h[:, :], wgk_l[:, :]),
                                        (gkl_l[:, :], wgk_h[:, :])], [gkl_h, gkl_l, wgk_h, wgk_l])
            cx.op("dve", lambda e, p=p: e.tensor_tensor(out=lg[:, :], in0=p[:, 0:NK], in1=vec[:, V_BGK:V_BGK + NK],
                                                        op=ALU.add), [p, vec], [lg])
            cx.op("act", lambda e: e.activation(out=lg[:, :], in_=lg[:, :], func=AF.Exp, scale=-1.0), [lg], [lg])
            cx.op("act", lambda e: e.activation(out=lg[:, :], in_=lg[:, :], func=AF.Ln, bias=cst[:, C_ONE:C_ONE + 1]),
                  [lg, cst], [lg])
            p = pb[3]
            self.split(lg, lg[:, :], lg_h, lg_l)
            self.mm_acc(p, p[:, 0:NK], [(m2b[:, :], lg_h[:, :]), (m2b[:, :], lg_l[:, :])], [m2b, lg_h, lg_l])
            cx.op("act", lambda e, p=p: e.activation(out=Er[:, :], in_=p[:, 0:NK], func=AF.Exp,
                                                     scale=-1.0 / 16.0), [p], [Er])
            cx.op("dve", lambda e: e.tensor_tensor(out=khat[:, :], in0=ktok[:, :], in1=Er[:, :], op=ALU.mult),
                  [ktok, Er], [khat])

        def heads(i):
            GSh = gsets[(i // GT) % NGS]
            qT4, kT4 = GSh["qT4"], GSh["kT4"]
            t0_ = (i % GT) * 128
            FS = fsets[i % 3]
            xb, xT, qT_f, kT_f, ktok, v_b, gs, gkl, gkl_h, gkl_l, lg, lg_h, lg_l, khat, Er, o_n = (FS[k] for k in UNP)
            self._gs_t = [gs]
            for h in range(nh):
                qt, ATm, Eq = (aout[h][i % 2][k] for k in ("qt", "ATm", "Eq"))
                pq = pb[4]
                self.mm_acc(pq, pq[:, 0:128], [(lg_h[:, h * 128:(h + 1) * 128], m1b[:, :]),
                                               (lg_l[:, h * 128:(h + 1) * 128], m1b[:, :])], [lg_h, lg_l, m1b])
                cx.op("act", lambda e: e.activation(out=Eq[:, :], in_=pq[:, 0:128], func=AF.Exp, scale=-1.0 / 16.0),
                      [pq], [Eq])
                cx.op("act", lambda e: e.activation(out=Ek[:, :], in_=pq[:, 0:128], func=AF.Exp, scale=1.0 / 16.0),
                      [pq], [Ek])
                cx.op("dve", lambda e, h=h: e.scalar_tensor_tensor(out=qt[:, :], in0=qT4[:, h, t0_:t0_ + 128], scalar=SCALE,
                                                                   in1=Eq[:, :], op0=ALU.mult, op1=ALU.mult),
                      [qT4, Eq], [qt])
                cx.op("dve", lambda e, h=h: e.tensor_tensor(out=kt[:, :], in0=kT4[:, h, t0_:t0_ + 128], in1=Ek[:, :],
                                                            op=ALU.mult), [kT4, Ek], [kt])
                self.mm_acc(pq, pq[:, 128:256], [(kt[:, :], qt[:, :])], [kt, qt])
                cx.op("dve", lambda e: e.tensor_tensor(out=ATm[:, :], in0=pq[:, 128:256], in1=M1, op=ALU.mult),
                      [pq, cst], [ATm])
            cx.mark()
            for h in range(nh):
                qt, ATm, Eq = (aout[h][i % 2][k] for k in ("qt", "ATm", "Eq"))
                po = pb[5]
                self.mm_acc(po, po[:, 0:256], [(qt[:, :], S_b[h][:, :]), (ATm[:, :], v_b[:, h * 256:(h + 1) * 256])],
                            [qt, S_b[h], ATm, v_b])
                self.mm_acc(po, po[:, 256:512], [(khat[:, h * 128:(h + 1) * 128], v_b[:, h * 256:(h + 1) * 256])],
                            [khat, v_b])
                cx.op("dve", lambda e, h=h: e.scalar_tensor_tensor(out=S_f[:, h, :], in0=S_f[:, h, :],
                                                                   scalar=Eq[:, 127:128], in1=po[:, 256:512],
                                                                   op0=ALU.mult, op1=ALU.add), [S_f, Eq, po], [S_f])
                cx.op("pool", lambda e, h=h: e.tensor_copy(out=S_b[h][:, :], in_=S_f[:, h, :]), [S_f], [S_b[h]])
                self.rms_gate(po, po[:, 0:256], gs[:, h * 256:(h + 1) * 256], h, o_n, (o_f, junk, ssq, rstd))

        def segs(lst):
            out, cur = [], []
            for it in lst:
                if it[0] == "mark":
                    if cur:
                        out.append(cur)
                    cur = []
                else:
                    cur.append(it)
            if cur:
                out.append(cur)
            return out

        ngrp = (nt + GT - 1) // GT
        cx.play([cx.rec(G, 0)])
        gq = segs(cx.rec(G, 1)) if ngrp > 1 else []
        nextg = 2

        def front_list(j):
            nonlocal gq, nextg
            out = []
            if j % GT == 0 and j > 0:
                for sg in gq:
                    out += sg
                gq = segs(cx.rec(G, nextg)) if nextg < ngrp else []
                nextg += 1
                out += cx.rec(front, j)
            else:
                out += cx.rec(front, j)
                left = GT - (j % GT)
                n = (len(gq) + left - 1) // max(1, left)
                for sg in gq[:n]:
                    out += sg
                gq = gq[n:]
            return out

        AB = {}
        cx.play([front_list(0)])
        AB[0] = cx.split_at_mark(cx.rec(heads, 0))
        lists = [AB[0][0]]
        if nt > 1:
            lists.append(front_list(1))
        cx.play(lists)
        for i in range(nt):
            lists = [AB[i][1]]
            if i + 1 < nt:
                AB[i + 1] = cx.split_at_mark(cx.rec(heads, i + 1))
                lists.append(AB[i + 1][0])
            if i + 2 < nt:
                lists.append(front_list(i + 2))
            lists.append(self.take_pref(nt - i))
            cx.play(lists)
            del AB[i]
            tks.append(self.store_onT(fsets[i % 3]["o_n"], 2 * nh, i))
        tks.append(cx.dma(dr[f"sout{li}"].rearrange("p (h v) -> p h v", h=nh), S_f[:, :, :], reads=[S_f],
                          key="Sst"))
        return tks

    def gdn_phase_a(self, li, src, srcn):
        cx, dr, cst, vec = self.cx, self.dr, self.cst, self.vec
        pb = self.pb
        nt = self.nt
        m1b, m2b, oneb, idb = self.m1b, self.m2b, self.oneb, self.idb
        ID = cst[:, C_ID:C_ID + 128]
        M1 = cst[:, C_M1:C_M1 + 128]
        tks = []
        self.li = li
        nh = self.nh("gdn")
        NB = 4 * nh
        CG = NB * 128
        CB = CG + nh * 256
        CIN = CB + 2 * nh
        win = self.wbuf["win_gdn"]
        S_f = [cx.sb(f"S_f{h}", [128, 256], F32) for h in range(nh)]
        S_b = [cx.sb(f"S_b{h}", [128, 256], BF16) for h in range(nh)]
        for h in range(nh):
            cx.dma(S_f[h][:, :], dr[f"sin{li}"][:, h * 256:(h + 1) * 256], writes=[S_f[h]], key=f"S{h}")
            cx.op("dve", lambda e, h=h: e.tensor_copy(out=S_b[h][:, :], in_=S_f[h][:, :]), [S_f[h]], [S_b[h]])
        halo = cx.sb("halo", [128, NB, 3], F32)
        css = [cx.sb(f"cs{i}", [128, 131], F32) for i in range(2)]
        f32t = {n: cx.sb(n, [128, 128], F32) for n in ["acc", "sl", "rs", "zero"]}
        b16t = {n: cx.sb(n, [128, 128], BF16) for n in ["sq"]}
        ff32, fb16 = f32t, b16t
        negA = cx.sb("negA", [128, nh], F32)
        fsets = []
        for k_ in range(3):
            FS = dict(xb=None, xT=None,
                      qkvT=None, ktok=cx.sb("ktok", [128, nh * 128], BF16),
                      vtok=cx.sb("vtok", [128, nh * 256], BF16), ba=cx.sb("ba", [128, 2 * nh], F32),
                      E3=cx.sb("E3", [128, 3 * nh], F32), o_n=cx.sb("o_n", [128, nh * 256], BF16))
            FS["sm"] = {n: cx.sb(n, [128, nh], F32) for n in
                        ["beta", "lnb", "z", "g", "g_hf", "g_lf", "l_hf", "l_lf", "be"]}
            FS["sm"]["negA"] = negA
            FS["smb"] = {n: cx.sb(n, [128, nh], BF16) for n in ["g_hb", "g_lb", "l_hb"]}
            fsets.append(FS)
        hsets = []
        for k_ in range(min(2, nh)):
            HB = dict(f32t={n: cx.sb(n, [128, 128], F32) for n in ["D", "DT", "BDT", "EgcB", "A_row", "A_col", "tmpq"]},
                      b16t={n: cx.sb(n, [128, 128], BF16) for n in
                            ["rR_h", "rR_l", "rC_h", "rC_l", "rL_h", "rL_l", "qd", "QKTm", "X0", "X1", "R0", "R1",
                             "Offn", "On", "W1", "W2", "wT", "kgb", "kd"]},
                      bv=cx.sb("bv", [128, 256], BF16), u_f=cx.sb("u_f", [128, 256], F32),
                      vnew=cx.sb("vnew", [128, 256], BF16), gsh=cx.sb("gsh", [128, 256], F32),
                      o_f=cx.sb("o_f", [128, 256], F32), junk=cx.sb("junk", [128, 256], F32),
                      sgt=cx.sb("sgt", [128, 256], F32), Offa=cx.sb("Offa", [128, 6, 128], BF16),
                      Ona=cx.sb("Ona", [128, 6, 128], BF16),
                      ssq=cx.sb("ssq", [128, 1], F32), rstd=cx.sb("rstd", [128, 1], F32),
                      pdg=pb[2 * k_], pw=pb[2 * k_ + 1])
            HB["out"] = [dict(u_f=cx.sb("u_f", [128, 256], F32), wT=cx.sb("wT", [128, 128], BF16),
                              qd=cx.sb("qd", [128, 128], BF16), QKTm=cx.sb("QKTm", [128, 128], BF16),
                              kd=cx.sb("kd", [128, 128], BF16)) for _ in range(2)]
            hsets.append(HB)
        self.onT = cx.sb("onT", [128, 2048], BF16)
        sm = {"negA": negA}
        zero = f32t["zero"]
        cx.op("dve", lambda e: e.memset(zero[:, :], 0.0), [], [zero])
        cx.op("act", lambda e: e.activation(out=negA[:, :], in_=vec[:, V_A:V_A + nh], func=AF.Exp), [vec],
              [negA])
        cx.op("act", lambda e: e.activation(out=negA[:, :], in_=negA[:, :], func=AF.Copy, scale=-1.0),
              [negA], [negA])
        xh = cx.sb("xh", [3, 1024], F32)
        xhb = cx.sb("xhb", [3, 1024], BF16)
        xhT = cx.sb("xhT", [128, 8, 4], BF16)
        cx.dma(xh[:, :], dr["xhalo"][li], writes=[xh], key="xh")
        cx.op("dve", lambda e: e.tensor_copy(out=xhb[:, :], in_=xh[:, :]), [xh], [xhb])
        pbb = self.pbb[0]
        for kc in range(8):
            cx.op("pe", lambda e, kc=kc: e.transpose(out=pbb[:, kc * 4:kc * 4 + 3], in_=xhb[0:3, kc * 128:(kc + 1) * 128],
                                                     identity=idb[0:3, 0:3]), [xhb, idb], [pbb])
        cx.op("act", lambda e: e.activation(out=xhT[:, :, 0:3],
                                            in_=pbb[:, 0:32].rearrange("p (k t) -> p k t", k=8)[:, :, 0:3],
                                            func=AF.Copy), [pbb], [xhT])
        for blk in range(NB):
            p = pb[blk % 2]
            self.mm_acc(p, p[:, 0:3], [(win[:, kc, blk * 128:(blk + 1) * 128], xhT[:, kc, 0:3]) for kc in range(8)],
                        [win, xhT])
            cx.op("act", lambda e, p=p, blk=blk: e.activation(out=halo[:, blk, :], in_=p[:, 0:3], func=AF.Copy),
                  [p], [halo])
        UNP = ("xb", "xT", "qkvT", "ktok", "vtok", "ba", "E3", "o_n", "sm", "smb")

        pB0, pB1 = pb[4], pb[5]
        pF = pb[6]
        GT = 4
        NGS = 3
        gsets = [dict(xT4=cx.sb("xT4", [128, 8, GT * 128], BF16), qkvT4=cx.sb("qkvT4", [128, NB, GT * 128], BF16))
                 for _ in range(NGS)]
        cs4 = [cx.sb(f"cs4_{k}", [128, 3 + GT * 128], F32) for k in range(2)]
        acc4 = cx.sb("acc4", [128, GT * 128], F32)
        sl4 = cx.sb("sl4", [128, GT * 128], F32)
        rs4 = cx.sb("rs4", [128, GT * 128], F32)
        sq4 = cx.sb("sq4", [128, GT * 128], BF16)
        zero4 = cx.sb("zero4", [128, GT * 128], F32)
        cx.op("pool", lambda e: e.memset(zero4[:, :], 0.0), [], [zero4])
        xt4 = [cx.sb(f"xtg{t}", [128, 1024], F32) for t in range(2)]
        xbs = [cx.sb(f"xbg{t}", [128, 1024], BF16) for t in range(2)]

        def G(gi):
            GS = gsets[gi % NGS]
            xT4, qkvT4 = GS["xT4"], GS["qkvT4"]
            tiles = [i for i in range(gi * GT, min(nt, (gi + 1) * GT))]
            N = len(tiles) * 128
            for t, i in enumerate(tiles):
                xt, xb = xt4[t % 2], xbs[t % 2]
                self.load_x_tile(src, srcn, i, xt)
                cx.op("pool", lambda e, xt=xt, xb=xb: e.tensor_copy(out=xb[:, :], in_=xt[:, :]), [xt], [xb])
                pbb = self.pbb[0]
                for kc in range(8):
                    cx.op("pe", lambda e, kc=kc, xb=xb, pbb=pbb: e.transpose(
                        out=pbb[:, kc * 128:(kc + 1) * 128], in_=xb[:, kc * 128:(kc + 1) * 128],
                        identity=idb[:, :]), [xb, idb], [pbb])
                cx.op("act", lambda e, t=t, pbb=pbb: e.activation(
                    out=xT4[:, :, t * 128:(t + 1) * 128], in_=pbb[:, :].rearrange("p (k t) -> p k t", k=8),
                    func=AF.Copy), [pbb], [xT4])
                cx.mark()
            for blk in range(NB):
                self.mm_acc(pF, pF[:, 0:N], [(win[:, kc, blk * 128:(blk + 1) * 128], xT4[:, kc, 0:N])
                                             for kc in range(8)], [win, xT4])
                cs = cs4[blk % 2]
                cx.op("pool", lambda e, cs=cs, blk=blk: e.tensor_copy(out=cs[:, 0:3], in_=halo[:, blk, :]), [halo], [cs])
                cx.op("act", lambda e, cs=cs: e.activation(out=cs[:, 3:3 + N], in_=pF[:, 0:N], func=AF.Copy), [pF], [cs])
                cx.op("pool", lambda e, cs=cs, blk=blk: e.tensor_copy(out=halo[:, blk, :], in_=cs[:, N:N + 3]), [cs], [halo])
                for j in range(4):
                    cx.op("dve", lambda e, cs=cs, blk=blk, j=j: e.scalar_tensor_tensor(
                        out=acc4[:, 0:N], in0=cs[:, j:j + N], scalar=vec[:, V_CW + blk * 4 + j:V_CW + blk * 4 + j + 1],
                        in1=(zero4[:, 0:N] if j == 0 else acc4[:, 0:N]), op0=ALU.mult, op1=ALU.add),
                        [cs, vec, zero4, acc4], [acc4])
                if blk >= 2 * nh:
                    self.silu(acc4, acc4[:, 0:N], qkvT4, qkvT4[:, blk, 0:N], rs4, rs4[:, 0:N])
                else:
                    self.silu(acc4, acc4[:, 0:N], sl4, sl4[:, 0:N], rs4, rs4[:, 0:N])
                    cx.op("act", lambda e: e.activation(out=sq4[:, 0:N], in_=sl4[:, 0:N], func=AF.Square), [sl4], [sq4])
                    self.mm_acc(pF, pF[:, 0:N], [(oneb[:, :], sq4[:, 0:N])], [oneb, sq4])
                    cx.op("act", lambda e: e.activation(out=rs4[:, 0:N], in_=pF[:, 0:N], func=AF.Ln,
                                                        bias=cst[:, C_EPSR:C_EPSR + 1]), [pF, cst], [rs4])
                    cx.op("act", lambda e: e.activation(out=rs4[:, 0:N], in_=rs4[:, 0:N], func=AF.Exp, scale=-0.5),
                          [rs4], [rs4])
                    cx.op("dve", lambda e, blk=blk: e.tensor_tensor(out=qkvT4[:, blk, 0:N], in0=sl4[:, 0:N],
                                                                    in1=rs4[:, 0:N], op=ALU.mult), [sl4, rs4], [qkvT4])
                cx.mark()

        def front(i):
            FS = fsets[i % 3]
            xb, xT, qkvT, ktok, vtok, ba, E3, o_n, sm, smb = (FS[k] for k in UNP)
            f32t, b16t = ff32, fb16
            GS = gsets[(i // GT) % NGS]
            xT_T, qkvT_T = GS["xT4"], GS["qkvT4"]
            t0_ = (i % GT) * 128
            xTv = xT_T[:, :, t0_:t0_ + 128]
            tb = list(range(nh, NB))
            for g0 in range(0, len(tb), 8):
                grp = tb[g0:g0 + 8]
                pbb = self.pbb[0]
                for k, blk in enumerate(grp):
                    cx.op("pe", lambda e, blk=blk, k=k: e.transpose(out=pbb[:, k * 128:(k + 1) * 128],
                                                                    in_=qkvT_T[:, blk, t0_:t0_ + 128],
                                                                    identity=idb[:, :]),
                          [qkvT_T, idb], [pbb])
                kb = [b for b in grp if b < 2 * nh]
                vb = [b for b in grp if b >= 2 * nh]
                if kb:
                    cx.op("act", lambda e, kb=kb, grp=grp, pbb=pbb: e.activation(
                        out=ktok[:, (kb[0] - nh) * 128:(kb[-1] - nh + 1) * 128],
                        in_=pbb[:, grp.index(kb[0]) * 128:(grp.index(kb[-1]) + 1) * 128], func=AF.Copy), [pbb], [ktok])
                if vb:
                    cx.op("act", lambda e, vb=vb, grp=grp, pbb=pbb: e.activation(
                        out=vtok[:, (vb[0] - 2 * nh) * 128:(vb[-1] - 2 * nh + 1) * 128],
                        in_=pbb[:, grp.index(vb[0]) * 128:(grp.index(vb[-1]) + 1) * 128], func=AF.Copy), [pbb], [vtok])
            p = pF
            self.mm_acc(p, p[:, 0:2 * nh], [(xTv[:, kc, :], win[:, kc, CB:CB + 2 * nh]) for kc in range(8)], [win, xT_T])
            cx.op("act", lambda e, p=p: e.activation(out=ba[:, :], in_=p[:, 0:2 * nh], func=AF.Copy), [p], [ba])
            beta, lnb, z, g = sm["beta"], sm["lnb"], sm["z"], sm["g"]
            cx.op("act", lambda e: e.activation(out=lnb[:, :], in_=ba[:, 0:nh], func=AF.Exp, scale=-1.0), [ba], [lnb])
            cx.op("act", lambda e: e.activation(out=lnb[:, :], in_=lnb[:, :], func=AF.Ln, bias=cst[:, C_ONE:C_ONE + 1]),
                  [lnb, cst], [lnb])
            cx.op("act", lambda e: e.activation(out=beta[:, :], in_=lnb[:, :], func=AF.Exp, scale=-1.0), [lnb], [beta])
            cx.op("act", lambda e: e.activation(out=lnb[:, :], in_=lnb[:, :], func=AF.Copy, scale=-1.0), [lnb], [lnb])
            cx.op("dve", lambda e: e.tensor_tensor(out=z[:, :], in0=ba[:, nh:2 * nh], in1=vec[:, V_DT:V_DT + nh], op=ALU.add),
                  [ba, vec], [z])
            cx.op("act", lambda e: e.activation(out=z[:, :], in_=z[:, :], func=AF.Exp), [z], [z])
            cx.op("act", lambda e: e.activation(out=z[:, :], in_=z[:, :], func=AF.Ln, bias=cst[:, C_ONE:C_ONE + 1]),
                  [z, cst], [z])
            cx.op("dve", lambda e: e.tensor_tensor(out=g[:, :], in0=z[:, :], in1=sm["negA"][:, :], op=ALU.mult),
                  [z, sm["negA"]], [g])
            for srcv, hb, hf, lf in ((g, smb["g_hb"], sm["g_hf"], sm["g_lf"]), (lnb, smb["l_hb"], sm["l_hf"], sm["l_lf"])):
                cx.op("dve", lambda e, srcv=srcv, hb=hb: e.tensor_copy(out=hb[:, :], in_=srcv[:, :]), [srcv], [hb])
                cx.op("dve", lambda e, hb=hb, hf=hf: e.tensor_copy(out=hf[:, :], in_=hb[:, :]), [hb], [hf])
                cx.op("dve", lambda e, srcv=srcv, hf=hf, lf=lf: e.tensor_tensor(out=lf[:, :], in0=srcv[:, :], in1=hf[:, :],
                                                                              op=ALU.subtract), [srcv, hf], [lf])
            cx.op("dve", lambda e: e.tensor_copy(out=smb["g_lb"][:, :], in_=sm["g_lf"][:, :]), [sm["g_lf"]], [smb["g_lb"]])
            p = pF
            self.mm_acc(p, p[:, 0:nh], [(m1b[:, :], smb["g_hb"][:, :]), (m1b[:, :], smb["g_lb"][:, :])],
                        [m1b, smb["g_hb"], smb["g_lb"]])
            self.mm_acc(p, p[:, nh:2 * nh], [(m2b[:, :], smb["g_hb"][:, :]), (m2b[:, :], smb["g_lb"][:, :])],
                        [m2b, smb["g_hb"], smb["g_lb"]])
            self.mm_acc(p, p[:, 2 * nh:3 * nh], [(oneb[:, :], smb["g_hb"][:, :]), (oneb[:, :], smb["g_lb"][:, :])],
                        [oneb, smb["g_hb"], smb["g_lb"]])
            cx.op("act", lambda e, p=p: e.activation(out=E3[:, :], in_=p[:, 0:3 * nh], func=AF.Exp), [p], [E3])
            be = sm["be"]
            cx.op("dve", lambda e: e.tensor_tensor(out=be[:, :], in0=beta[:, :], in1=E3[:, 0:nh], op=ALU.mult),
                  [beta, E3], [be])

        def head(i, h):
            FS = fsets[i % 3]
            xb, xT, qkvT, ktok, vtok, ba, E3, o_n, sm, smb = (FS[k] for k in UNP)
            HB = hsets[h % len(hsets)]
            xT_T = gsets[(i // GT) % NGS]["xT4"]
            qkvT_T = gsets[(i // GT) % NGS]["qkvT4"]
            t0_ = (i % GT) * 128
            xTv = xT_T[:, :, t0_:t0_ + 128]
            HO = HB["out"][i % 2]
            f32t, b16t = HB["f32t"], dict(HB["b16t"])
            b16t.update({k: HO[k] for k in ("wT", "qd", "QKTm", "kd")})
            u_f = HO["u_f"]
            bv, vnew, gsh, o_f, junk, ssq, rstd = (HB[k] for k in
                                                   ("bv", "vnew", "gsh", "o_f", "junk", "ssq", "rstd"))
            beta, be = sm["beta"], sm["be"]
            self._gs_t = [gsh]
            if True:
                qT = qkvT_T[:, h, t0_:t0_ + 128]
                kT = qkvT_T[:, nh + h, t0_:t0_ + 128]
                D, DT, BDT, EgcB = f32t["D"], f32t["DT"], f32t["BDT"], f32t["EgcB"]
                A_row, A_col, tmpq = f32t["A_row"], f32t["A_col"], f32t["tmpq"]
                rb = {}
                for nm, mask, sc in (("rR_h", C_M2, sm["g_hf"]), ("rR_l", C_M2, sm["g_lf"]), ("rC_h", C_M1, sm["g_hf"]),
                                     ("rC_l", C_M1, sm["g_lf"]), ("rL_h", C_ID, sm["l_hf"]), ("rL_l", C_ID, sm["l_lf"])):
                    t = b16t[nm]
                    rb[nm] = t
                    cx.op("dve", lambda e, t=t, mask=mask, sc=sc, h=h: e.scalar_tensor_tensor(
                        out=t[:, :], in0=cst[:, mask:mask + 128], scalar=sc[:, h:h + 1], in1=cst[:, mask:mask + 128],
                        op0=ALU.mult, op1=ALU.mult), [cst, sc], [t])
                pd = HB["pdg"]
                rr = [rb["rR_h"], rb["rR_l"]]
                rc = [rb["rC_h"], rb["rC_l"]]
                rl = [rb["rL_h"], rb["rL_l"]]
                self.mm_acc(pd, pd[:, 0:128], [(m1b[:, :], t[:, :]) for t in rr], [m1b] + rr)
                self.mm_acc(pd, pd[:, 128:256], [(m2b[:, :], t[:, :]) for t in rc], [m2b] + rc)
                self.mm_acc(pd, pd[:, 256:384], [(m2b[:, :], t[:, :]) for t in rc] + [(oneb[:, :], t[:, :]) for t in rl],
                            [m2b, oneb] + rc + rl)
                self.mm_acc(pd, pd[:, 384:512], [(oneb[:, :], t[:, :]) for t in rc], [oneb] + rc)
                for k, dt_ in enumerate((D, DT, BDT, EgcB)):
                    cx.op("act", lambda e, k=k, dt_=dt_: e.activation(out=dt_[:, :], in_=pd[:, k * 128:(k + 1) * 128],
                                                                    func=AF.Exp), [pd], [dt_])
                qd = b16t["qd"]
                cx.op("dve", lambda e, qT=qT: e.scalar_tensor_tensor(out=qd[:, :], in0=qT, scalar=SCALE, in1=EgcB[:, :],
                                                                     op0=ALU.mult, op1=ALU.mult), [qkvT_T, EgcB], [qd])
                pg = HB["pdg"]
                self.mm_acc(pg, pg[:, 0:128], [(kT, kT)], [qkvT_T])
                self.mm_acc(pg, pg[:, 128:256], [(kT, qT)], [qkvT_T])
                cx.op("dve", lambda e, h=h: e.scalar_tensor_tensor(out=A_row[:, :], in0=pg[:, 0:128],
                                                                   scalar=beta[:, h:h + 1], in1=D[:, :], op0=ALU.mult,
                                                                   op1=ALU.mult), [pg, beta, D], [A_row])
                cx.op("dve", lambda e: e.tensor_tensor(out=A_col[:, :], in0=pg[:, 0:128], in1=BDT[:, :], op=ALU.mult),
                      [pg, BDT], [A_col])
                cx.op("dve", lambda e: e.scalar_tensor_tensor(out=tmpq[:, :], in0=pg[:, 128:256], scalar=SCALE,
                                                              in1=DT[:, :], op0=ALU.mult, op1=ALU.mult), [pg, DT], [tmpq])
                QKTm = b16t["QKTm"]
                cx.op("dve", lambda e: e.tensor_tensor(out=QKTm[:, :], in0=tmpq[:, :], in1=M1, op=ALU.mult),
                      [tmpq, cst], [QKTm])
                X, R = b16t["X0"], b16t["R0"]
                Xn, Rn = b16t["X1"], b16t["R1"]
                cx.op("dve", lambda e: e.tensor_tensor(out=tmpq[:, :], in0=A_col[:, :], in1=cst[:, C_NMC:C_NMC + 128],
                                                       op=ALU.mult), [A_col, cst], [tmpq])
                cx.op("dve", lambda e, X=X: e.tensor_tensor(out=X[:, :], in0=tmpq[:, :], in1=ID, op=ALU.add),
                      [tmpq, cst], [X])
                cx.op("dve", lambda e: e.tensor_tensor(out=tmpq[:, :], in0=A_row[:, :], in1=cst[:, C_NMR:C_NMR + 128],
                                                       op=ALU.mult), [A_row, cst], [tmpq])
                cx.op("dve", lambda e, R=R: e.tensor_tensor(out=R[:, :], in0=tmpq[:, :], in1=ID, op=ALU.add),
                      [tmpq, cst], [R])
                W1, W2 = b16t["W1"], b16t["W2"]
                Offa, Ona = HB["Offa"], HB["Ona"]
                cx.op("dve", lambda e: e.tensor_tensor(
                    out=Offa[:, :, :], in0=A_row[:, :].unsqueeze(1).broadcast_to([128, 6, 128]),
                    in1=cst[:, C_NMR + 128:C_NMR + 7 * 128].rearrange("p (l s) -> p l s", l=6), op=ALU.mult),
                    [A_row, cst], [Offa])
                cx.op("dve", lambda e: e.tensor_tensor(
                    out=Ona[:, :, :], in0=A_col[:, :].unsqueeze(1).broadcast_to([128, 6, 128]),
                    in1=cst[:, C_NMC + 128:C_NMC + 7 * 128].rearrange("p (l s) -> p l s", l=6), op=ALU.mult),
                    [A_col, cst], [Ona])
                pw = HB["pw"]
                for l in range(1, 7):
                    lastl = (l == 6)
                    self.mm_acc(pw, pw[:, 0:128], [(Offa[:, l - 1, :], X[:, :])], [Offa, X])
                    cx.op("act", lambda e: e.activation(out=W1[:, :], in_=pw[:, 0:128], func=AF.Copy), [pw], [W1])
                    if not lastl:
                        self.mm_acc(pw, pw[:, 128:256], [(Ona[:, l - 1, :], R[:, :])], [Ona, R])
                        cx.op("act", lambda e: e.activation(out=W2[:, :], in_=pw[:, 128:256], func=AF.Copy), [pw], [W2])
                    self.mm_acc(pw, pw[:, 256:384], [(R[:, :], W1[:, :])], [R, W1])
                    cx.op("dve", lambda e, X=X, Xn=Xn: e.tensor_tensor(out=Xn[:, :], in0=X[:, :], in1=pw[:, 256:384],
                                                                       op=ALU.add), [X, pw], [Xn])
                    if not lastl:
                        self.mm_acc(pw, pw[:, 384:512], [(X[:, :], W2[:, :])], [X, W2])
                        cx.op("dve", lambda e, R=R, Rn=Rn: e.tensor_tensor(out=Rn[:, :], in0=R[:, :], in1=pw[:, 384:512],
                                                                           op=ALU.add), [R, pw], [Rn])
                    X, Xn = Xn, X
                    R, Rn = Rn, R
                kgb, kd, wT = b16t["kgb"], b16t["kd"], b16t["wT"]
                cx.op("act", lambda e, h=h: e.activation(out=bv[:, :], in_=vtok[:, h * 256:(h + 1) * 256], func=AF.Copy,
                                                         scale=beta[:, h:h + 1]), [vtok, beta], [bv])
                cx.op("act", lambda e, h=h: e.activation(out=kgb[:, :], in_=ktok[:, h * 128:(h + 1) * 128], func=AF.Copy,
                                                         scale=be[:, h:h + 1]), [ktok, be], [kgb])
                cx.op("act", lambda e, h=h: e.activation(out=kd[:, :], in_=ktok[:, h * 128:(h + 1) * 128], func=AF.Copy,
                                                         scale=E3[:, nh + h:nh + h + 1]), [ktok, E3], [kd])
                self.mm_acc(pw, pw[:, 0:256], [(X[:, :], bv[:, :])], [X, bv])
                cx.op("act", lambda e: e.activation(out=u_f[:, :], in_=pw[:, 0:256], func=AF.Copy), [pw], [u_f])
                self.mm_acc(pg, pg[:, 256:384], [(kgb[:, :], X[:, :])], [kgb, X])
                cx.op("act", lambda e: e.activation(out=wT[:, :], in_=pg[:, 256:384], func=AF.Copy), [pg], [wT])
                cx.mark()
                self.mm_acc(pB0, pB0[:, 0:256], [(wT[:, :], S_b[h][:, :])], [wT, S_b[h]])
                cx.op("dve", lambda e: e.tensor_tensor(out=vnew[:, :], in0=u_f[:, :], in1=pB0[:, 0:256], op=ALU.subtract),
                      [u_f, pB0], [vnew])
                self.mm_acc(pB0, pB0[:, 256:512], [(qd[:, :], S_b[h][:, :]), (QKTm[:, :], vnew[:, :])],
                            [qd, S_b[h], QKTm, vnew])
                self.mm_acc(pB1, pB1[:, 0:256], [(kd[:, :], vnew[:, :])], [kd, vnew])
                cx.op("dve", lambda e, h=h: e.scalar_tensor_tensor(out=S_f[h][:, :], in0=S_f[h][:, :],
                                                                   scalar=E3[:, 2 * nh + h:2 * nh + h + 1], in1=pB1[:, 0:256],
                                                                   op0=ALU.mult, op1=ALU.add), [S_f[h], E3, pB1], [S_f[h]])
                cx.op("pool", lambda e, h=h: e.tensor_copy(out=S_b[h][:, :], in_=S_f[h][:, :]), [S_f[h]], [S_b[h]])
                self.mm_acc(pB1, pB1[:, 256:512], [(xTv[:, kc, :], win[:, kc, CG + h * 256:CG + (h + 1) * 256])
                                                    for kc in range(8)], [xT_T, win])
                self.silu(pB1, pB1[:, 256:512], gsh, gsh[:, :], HB["sgt"], HB["sgt"][:, :])
                self.rms_gate(pB0, pB0[:, 256:512], gsh[:, :], h, o_n, (o_f, junk, ssq, rstd))

        assert nh <= len(hsets)
        AB = {}

        def rec_heads(i):
            for h in range(nh):
                AB[(i, h)] = cx.split_at_mark(cx.rec(head, i, h))

        def segs(lst):
            out, cur = [], []
            for it in lst:
                if it[0] == "mark":
                    if cur:
                        out.append(cur)
                    cur = []
                else:
                    cur.append(it)
            if cur:
                out.append(cur)
            return out

        gq = []
        ngrp = (nt + GT - 1) // GT
        cx.play([cx.rec(G, 0)])
        if ngrp > 1:
            gq = segs(cx.rec(G, 1))
        nextg = 2

        def front_list(j):
            nonlocal gq, nextg
            out = []
            if j % GT == 0 and j > 0:
                for sg in gq:
                    out += sg
                gq = segs(cx.rec(G, nextg)) if nextg < ngrp else []
                nextg += 1
                out += cx.rec(front, j)
            else:
                out += cx.rec(front, j)
                left = GT - (j % GT)
                n = (len(gq) + left - 1) // max(1, left)
                for sg in gq[:n]:
                    out += sg
                gq = gq[n:]
            return out

        cx.play([front_list(0)])
        rec_heads(0)
        lists = [AB[(0, h)][0] for h in range(nh)]
        if nt > 1:
            lists.append(front_list(1))
        cx.play(lists)
        for i in range(nt):
            lists = [sum((AB[(i, h)][1] for h in range(nh)), [])]
            if i + 1 < nt:
                rec_heads(i + 1)
                lists += [AB[(i + 1, h)][0] for h in range(nh)]
            if i + 2 < nt:
                lists.append(front_list(i + 2))
            lists.append(self.take_pref(nt - i))
            cx.play(lists)
            for h in range(nh):
                del AB[(i, h)]
            tks.append(self.store_onT(fsets[i % 3]["o_n"], 2 * nh, i))
        for h in range(nh):
            tks.append(cx.dma(dr[f"sout{li}"][:, h * 256:(h + 1) * 256], S_f[h][:, :], reads=[S_f[h]], key=f"Sst{h}"))
        return tks

    def phase_b(self, li, kind, src, srcn, dst, dstn, last):
        cx, dr, vec = self.cx, self.dr, self.vec
        pb = self.pb
        nt = self.nt
        nkc = 16 if kind == "gdn" else 8
        wout = self.wbuf[f"wout_{kind}"]
        NS = 3
        bsets = [dict(xt=cx.sb(f"xtb{k}", [128, 1024], F32), on=cx.sb(f"onb{k}", [128, 2048], BF16),
                      z=cx.sb(f"z{k}", [128, 1024], F32), st=cx.sb("bnst", [128, 2, 6], F32), mv=cx.sb("mv", [128, 2], F32),
                      rstd=cx.sb("rstdb", [128, 1], F32), p=[pb[2 * (k % NS)], pb[2 * (k % NS) + 1]])
                 for k in range(2 * NS)]
        tks = []
        R = self.ranks
        tpc = self.tpc[li]

        def loads(i):
            B = bsets[i % (2 * NS)]
            xt, on = B["xt"], B["on"]
            self.load_x_tile(src, srcn, i, xt)
            q, t = i // tpc, i % tpc
            if R == 1:
                cx.dma(on[:, 0:nkc * 128], self.drt[f"ont{li}_{q}"].ap()[t * 128:(t + 1) * 128, :],
                       reads=[self.dtile("ont", i)], writes=[on], key=on.name)
            else:
                srcap = self.drt[f"onta{li}_{q}"].ap().rearrange("(r t p) c -> t p r c", r=R, p=128)[t]
                cx.dma(on[:, 0:nkc * 128].rearrange("p (r c) -> p r c", r=R), srcap,
                       reads=[self.dtile("onta", q)], writes=[on], key=on.name)

        def tile(i):
            B = bsets[i % (2 * NS)]
            xt, on, z, st, mv, rstd = (B[k] for k in ("xt", "on", "z", "st", "mv", "rstd"))
            for hb in range(2):
                p = B["p"][hb]
                self.mm_acc(p, p[:, :], [(on[:, kc * 128:(kc + 1) * 128], wout[:, kc, hb * 512:(hb + 1) * 512])
                                         for kc in range(nkc)], [on, wout])
                cx.op("dve", lambda e, p=p, hb=hb: e.scalar_tensor_tensor(
                    out=z[:, hb * 512:(hb + 1) * 512], in0=xt[:, hb * 512:(hb + 1) * 512], scalar=DEEP_ALPHA,
                    in1=p[:, :], op0=ALU.mult, op1=ALU.add), [xt, p], [z])
            for hb in range(2):
                cx.op("dve", lambda e, hb=hb: e.bn_stats(out=st[:, hb, :], in_=z[:, hb * 512:(hb + 1) * 512]),
                      [z], [st])
            cx.op("dve", lambda e: e.bn_aggr(out=mv[:, :], in_=st[:, :, :]), [st], [mv])
            cx.op("act", lambda e: e.activation(out=rstd[:, 0:1], in_=mv[:, 1:2], func=AF.Ln,
                                                bias=self.cst[:, C_EPSL:C_EPSL + 1]), [mv, self.cst], [rstd])
            cx.op("act", lambda e: e.activation(out=rstd[:, 0:1], in_=rstd[:, 0:1], func=AF.Exp, scale=-0.5),
                  [rstd], [rstd])
            cx.op("dve", lambda e: e.tensor_scalar(out=z[:, :], in0=z[:, :], scalar1=mv[:, 0:1], scalar2=rstd[:, 0:1],
                                                   op0=ALU.subtract, op1=ALU.mult), [z, mv, rstd], [z])
            cx.op("pool", lambda e: e.tensor_tensor(out=z[:, :], in0=z[:, :], in1=vec[:, V_LNG:V_LNG + 1024],
                                                    op=ALU.mult), [z, vec], [z])
            cx.op("pool", lambda e: e.tensor_tensor(out=z[:, :], in0=z[:, :], in1=vec[:, V_LNB:V_LNB + 1024],
                                                    op=ALU.add), [z, vec], [z])
            cx.dma(dst[i * 128:(i + 1) * 128, :], z[:, :], reads=[z], writes=[self.dtile(dstn, i)],
                   key=z.name + "st")

        for i in range(0, min(nt, NS)):
            loads(i)
        for i0_ in range(0, nt, NS):
            lists = [cx.rec(tile, i) for i in range(i0_, min(nt, i0_ + NS))]
            nxt = list(range(i0_ + NS, min(nt, i0_ + 2 * NS)))
            if nxt:
                lists.append(cx.rec(lambda: [loads(i) for i in nxt]))
            lists.append(self.take_pref((nt - i0_ + NS - 1) // NS))
            cx.play(lists)
        return tks


def gdn_cols(r, R):
    nh = 8 // R
    h0 = r * nh
    return np.concatenate([np.arange(h0 * 128, (h0 + nh) * 128), 1024 + np.arange(h0 * 128, (h0 + nh) * 128),
                           2048 + np.arange(h0 * 256, (h0 + nh) * 256), 4096 + np.arange(h0 * 256, (h0 + nh) * 256),
                           6144 + np.arange(h0, h0 + nh), 6152 + np.arange(h0, h0 + nh)])


def gla_cols(r, R):
    nh = 4 // R
    h0 = r * nh
    return np.concatenate([np.arange(h0 * 128, (h0 + nh) * 128), 512 + np.arange(h0 * 128, (h0 + nh) * 128),
                           1024 + np.arange(h0 * 256, (h0 + nh) * 256), 2048 + np.arange(h0 * 256, (h0 + nh) * 256),
                           3072 + np.arange(16)])


def pack_vecs(kind, j, i, inp, r=0, R=1):
    v = np.zeros((128, VTOT), np.float32)
    v[:, V_LNG:V_LNG + 1024] = inp["ln_g"][i][None, :]
    v[:, V_LNB:V_LNB + 1024] = inp["ln_b"][i][None, :]
    if kind == "gdn":
        v[:, V_NW:V_NW + 256] = inp["gdn_norm_w"][j][None, :]
        nh = 8 // R
        v[:, V_A:V_A + nh] = inp["gdn_a_log"][j][None, r * nh:(r + 1) * nh]
        v[:, V_DT:V_DT + nh] = inp["gdn_dt_bias"][j][None, r * nh:(r + 1) * nh]
        cw = inp["gdn_conv_w"][j][:, gdn_cols(r, R)[:4 * nh * 128]]
        v[:, V_CW:V_CW + 4 * nh * 4] = cw.reshape(4, 4 * nh, 128).transpose(2, 1, 0).reshape(128, 4 * nh * 4)
    else:
        v[:, V_NW:V_NW + 256] = inp["gla_norm_w"][j][None, :]
        nh = 4 // R
        v[:, V_BGK:V_BGK + nh * 128] = inp["gla_b_gk"][j][None, r * nh * 128:(r + 1) * nh * 128]
    return v


LAYERS = ["gdn", "gla", "gdn", "gla"]


def make_in_map(xb, inp, nt, r=0, R=1):
    m = {"x": np.ascontiguousarray(xb, dtype=np.float32), "consts": make_consts(),
         "xhalo": np.zeros((len(LAYERS), 3, D_MODEL), np.float32)}
    vecs = []
    for i, kind in enumerate(LAYERS):
        j = i // 2
        vecs.append(pack_vecs(kind, j, i, inp, r, R))
        nh = (8 if kind == "gdn" else 4) // R
        cols = gdn_cols(r, R) if kind == "gdn" else gla_cols(r, R)
        m[f"win{i}"] = np.ascontiguousarray(inp[f"{kind}_w_in"][j][:, cols], dtype=np.float32)
        m[f"wout{i}"] = np.ascontiguousarray(inp[f"{kind}_w_out"][j], dtype=np.float32)
        m[f"sin{i}"] = np.zeros((128, nh * 256), np.float32)
        if kind == "gla":
            m[f"wgk{i}"] = np.ascontiguousarray(inp["gla_w_gk_up"][j][:, r * nh * 128:(r + 1) * nh * 128],
                                                dtype=np.float32)
    m["vecs"] = np.stack(vecs, 0)
    return m


_PROG = {}
RANKS = 4


def kernel(**inputs):
    inp = {k: np.asarray(v) for k, v in inputs.items()}
    x = inp["x"]
    B, T, _ = x.shape
    nt = T // 128
    R = RANKS
    if nt not in _PROG:
        _PROG[nt] = Prog(LAYERS, nt, ranks=R)
    P = _PROG[nt]
    in_maps = [make_in_map(x[c // R], inp, nt, c % R, R) for c in range(8)]
    res = run_bass_kernel_spmd(P.nc, in_maps, core_ids=list(range(8)))
    out = np.stack([np.asarray(res.results[b * R]["y"], dtype=np.float32).reshape(T, D_MODEL) for b in range(B)], 0)
    return out
```

```python
import re as _re
import numpy as np
from contextlib import ExitStack
import concourse.bass as bass
import concourse.mybir as mybir
from concourse.bass_utils import run_bass_kernel_spmd

F32 = mybir.dt.float32
BF16 = mybir.dt.bfloat16
AF = mybir.ActivationFunctionType
ALU = mybir.AluOpType

D_MODEL = 1024
SEQ = 8192
BATCH = 2
DEPTH = 4
GDN_IN = 6160
GLA_IN = 3088
DEEP_ALPHA = (2.0 * DEPTH) ** 0.25
LN_EPS = 1e-5
RMS_EPS = 1e-6
L2_EPS = 1e-6
SCALE = 128.0 ** -0.5
GEN = 8000


class Tk:
    __slots__ = ("key", "gen", "val", "sem")

    def __init__(self, key, gen, val, sem):
        self.key, self.gen, self.val, self.sem = key, gen, val, sem


class T:
    __slots__ = ("ap", "w", "r", "name", "excl")

    def __init__(self, ap, name=""):
        self.excl = False
        self.ap = ap
        self.w = None
        self.r = {}
        self.name = name

    def __getitem__(self, idx):
        return self.ap[idx]


class Ctx:
    def __init__(self, nc, stack):
        self.nc = nc
        self.stack = stack
        self.sem_stack = stack
        self.eng = {"pe": nc.tensor, "act": nc.scalar, "dve": nc.vector, "pool": nc.gpsimd, "sp": nc.sync}
        self.seq = {k: 0 for k in self.eng}
        self.sems = {k: [] for k in self.eng}
        self.known = {k: {} for k in self.eng}
        self.dma_sems = {}
        self.dma_cnt = {}
        self.nsem = 0
        self.ninst = 0
        self._rec = None
        self._win = []
        self._wcount = 0
        self._emitting = False

    def new_sem(self, name):
        self.nsem += 1
        return self.sem_stack.enter_context(self.nc.semaphore(f"{name}_{self.nsem}"))

    def sb(self, name, shape, dt):
        self.nsem += 1
        name = f"{name}_u{self.nsem}"
        return T(self.stack.enter_context(self.nc.sbuf_tensor(name, list(shape), dt)), name)

    def ps(self, name, shape, dt):
        t = T(self.stack.enter_context(self.nc.psum_tensor(name, list(shape), dt)), name)
        t.excl = True
        return t

    def _wait(self, e, tk):
        if tk is None:
            return
        if tk.key == e and e == "pe":
            return
        kn = self.known[e].get(tk.key)
        if kn is not None and kn >= (tk.gen, tk.val):
            return
        self.eng[e].wait_ge(tk.sem, tk.val)
        self.known[e][tk.key] = (tk.gen, tk.val)
        self.ninst += 1

    def _deps(self, e, reads, writes, defer=False):
        need = []
        for t in reads:
            if t.w is not None:
                need.append(t.w)
        for t in writes:
            if t.w is not None:
                need.append(t.w)
            need.extend(t.r.values())
        best = {}
        for tk in need:
            if tk.key == e and e == "pe":
                continue
            kn = self.known[e].get(tk.key)
            if kn is not None and kn >= (tk.gen, tk.val):
                continue
            b = best.get(tk.key)
            if b is None or (b.gen, b.val) < (tk.gen, tk.val):
                best[tk.key] = tk
        toks = list(best.values())
        last = None
        if defer and toks and EMBED:
            last = toks.pop()
        for tk in toks:
            self._wait(e, tk)
        return last

    def _mark(self, tk, reads, writes):
        for t in reads:
            t.r[tk.key] = tk
        for t in writes:
            t.w = tk
            t.r = {}

    def mark(self):
        if self._rec is not None:
            self._rec.append(("mark", None))

    @staticmethod
    def split_at_mark(lst):
        for k, it in enumerate(lst):
            if it[0] == "mark":
                return lst[:k], lst[k + 1:]
        return lst, []

    def rec(self, fn, *args):
        assert self._rec is None
        self._rec = []
        fn(*args)
        out, self._rec = self._rec, None
        return out

    def play(self, lists):
        self._win += [l for l in lists if l]
        self._wcount += 1
        if self._wcount >= WIN:
            self.flush()

    def flush(self):
        if self._win:
            w, self._win = self._win, []
            self._emitting = True
            try:
                self.play_sched(w)
            finally:
                self._emitting = False
        self._wcount = 0

    def _direct(self):
        if self._rec is None and not self._emitting and self._win:
            self.flush()

    @staticmethod
    def _dur(kind, e, writes):
        if kind == "dma":
            return 0.06, 2.5
        if kind == "collective":
            return 0.3, 25.0
        if e == "pe":
            return 0.22, 0.35
        n = 128
        for t in writes:
            try:
                sh = t.ap.shape
                m = 1
                for d in sh[1:]:
                    m *= int(d)
                n = max(n, min(m, 512))
            except Exception:
                pass
        d = 0.1 + n / 960.0
        return d, d + 0.25

    def play_sched(self, lists):
        ops = []
        for l in lists:
            ops += [it for it in l if it[0] != "mark"]
        n = len(ops)
        if n == 0:
            return
        lw, rd = {}, {}
        preds = [None] * n
        meta = [None] * n
        for k, (kind, a) in enumerate(ops):
            if kind == "op":
                e, fn, reads, writes = a[0]
            elif kind == "dma":
                e, reads, writes = a[1]["e"], a[1]["reads"], a[1]["writes"]
            else:
                e, reads, writes = "pool", a[1]["reads"], a[1]["writes"]
            ex = [t for t in reads if t.excl]
            if ex:
                reads = [t for t in reads if not t.excl]
                writes = list(writes) + ex
            ps = set()
            for t in reads:
                if id(t) in lw:
                    ps.add(lw[id(t)])
            for t in writes:
                if id(t) in lw:
                    ps.add(lw[id(t)])
                ps.update(rd.get(id(t), ()))
            ps.discard(k)
            preds[k] = ps
            for t in reads:
                rd.setdefault(id(t), []).append(k)
            for t in writes:
                lw[id(t)] = k
                rd[id(t)] = []
            meta[k] = (e,) + self._dur(kind, e, writes)
        succs = [[] for _ in range(n)]
        npred = [len(p) for p in preds]
        for k in range(n):
            for p in preds[k]:
                succs[p].append(k)
        fin = [0.0] * n
        clock = {}
        ready = [k for k in range(n) if npred[k] == 0]
        rt = {k: 0.0 for k in ready}
        order = []
        while ready:
            best, bs = None, None
            for k in ready:
                st = max(rt[k], clock.get(meta[k][0], 0.0))
                key = (st, k)
                if bs is None or key < bs:
                    best, bs = k, key
            ready.remove(best)
            e, busy, lat = meta[best]
            st = bs[0]
            clock[e] = st + busy
            fin[best] = st + lat
            order.append(best)
            for q in succs[best]:
                npred[q] -= 1
                rt[q] = max(rt.get(q, 0.0), fin[best])
                if npred[q] == 0:
                    ready.append(q)
        assert len(order) == n
        for k in order:
            kind, a = ops[k]
            getattr(self, kind)(*a[0], **a[1])

    def play_prop(self, lists):
        lists = [l for l in lists if l]
        pos = [0] * len(lists)
        total = sum(len(l) for l in lists)
        for _ in range(total):
            k = min((i for i in range(len(lists)) if pos[i] < len(lists[i])),
                    key=lambda i: (pos[i] + 0.5) / len(lists[i]))
            kind, a = lists[k][pos[k]]
            pos[k] += 1
            if kind == "mark":
                continue
            getattr(self, kind)(*a[0], **a[1])

    def op(self, e, fn, reads=(), writes=()):
        self._direct()
        if self._rec is not None:
            self._rec.append(("op", ((e, fn, list(reads), list(writes)), {})))
            return None
        ex = [t for t in reads if t.excl]
        if ex:
            reads = [t for t in reads if not t.excl]
            writes = list(writes) + ex
        last = self._deps(e, reads, writes, defer=True)
        ins = fn(self.eng[e])
        if last is not None:
            ins._wait_ge(last.sem, last.val)
            self.known[e][last.key] = (last.gen, last.val)
        n = self.seq[e]
        gen, val = n // GEN, n % GEN + 1
        if gen >= len(self.sems[e]):
            self.sems[e].append(self.new_sem(f"s_{e}"))
        sem = self.sems[e][gen]
        ins.then_inc(sem, 1)
        self.seq[e] = n + 1
        self.ninst += 1
        tk = Tk(e, gen, val, sem)
        self._mark(tk, reads, writes)
        return tk

    def dma(self, out_ap, in_ap, reads=(), writes=(), key=None, e="sp", slow=False):
        self._direct()
        if self._rec is not None:
            self._rec.append(("dma", ((out_ap, in_ap), dict(reads=list(reads), writes=list(writes), key=key, e=e,
                                                              slow=slow))))
            return None
        key = _re.sub(r"_u\d+", "", str(key))
        if key not in self.dma_sems:
            self.dma_sems[key] = self.new_sem("d")
            self.dma_cnt[key] = 0
        self._deps(e, reads, writes)
        sem = self.dma_sems[key]
        if slow:
            ins = self.eng[e].dma_start(out=out_ap, in_=in_ap, allow_slow_non_contiguous=True)
        else:
            ins = self.eng[e].dma_start(out=out_ap, in_=in_ap)
        self.dma_cnt[key] += 16
        ins.then_inc(sem, 16)
        self.ninst += 1
        tk = Tk(("dma", key), 0, self.dma_cnt[key], sem)
        self._mark(tk, reads, writes)
        return tk

    def collective(self, kind, in_ap, out_ap, groups, reads=(), writes=()):
        self._direct()
        if self._rec is not None:
            self._rec.append(("collective", ((kind, in_ap, out_ap, groups), dict(reads=list(reads),
                                                                                 writes=list(writes)))))
            return None
        e = "pool"
        self._deps(e, reads, writes)
        if not hasattr(self, "cc_sem"):
            self.cc_sem = self.new_sem("cc")
            self.cc_cnt = 0
        sem = self.cc_sem
        ins = self.eng[e].collective_compute(kind, ALU.bypass, replica_groups=groups, ins=[in_ap], outs=[out_ap])
        ins.then_inc(sem, 1)
        self.cc_cnt += 1
        self.ninst += 1
        tk = Tk(("cc", 0), 0, self.cc_cnt, sem)
        self._mark(tk, reads, writes)
        return tk

    def barrier(self):
        self.flush()
        toks = []
        for e2, n in self.seq.items():
            if n > 0:
                m = n - 1
                toks.append(Tk(e2, m // GEN, m % GEN + 1, self.sems[e2][m // GEN]))
        for key, cnt in self.dma_cnt.items():
            if cnt > 0:
                toks.append(Tk(("dma", key), 0, cnt, self.dma_sems[key]))
        for e in self.eng:
            for tk in toks:
                if tk.key != e:
                    self._wait(e, tk)

    def finish(self, tks):
        for tk in tks:
            if tk is not None:
                self._wait("sp", tk)
        self.barrier()


C_ID, C_M1, C_M2, C_ONE = 0, 128, 256, 384
C_NMR = 512
C_NMC = 512 + 7 * 128
C_EPSR = 512 + 14 * 128
C_EPSL = C_EPSR + 1
C_TOT = C_EPSR + 8


def make_consts():
    j = np.arange(128)
    c = np.zeros((128, C_TOT), np.float32)
    c[:, C_ID:C_ID + 128] = np.eye(128)
    c[:, C_M1:C_M1 + 128] = (j[:, None] <= j[None, :])
    c[:, C_M2:C_M2 + 128] = (j[:, None] > j[None, :])
    c[:, C_ONE:C_ONE + 128] = 1.0
    c[:, C_EPSR] = RMS_EPS
    c[:, C_EPSL] = LN_EPS
    for l in range(7):
        b = 1 << l
        row = ((j[:, None] // (2 * b)) == (j[None, :] // (2 * b))) & ((j[:, None] // b) > (j[None, :] // b))
        c[:, C_NMR + l * 128:C_NMR + (l + 1) * 128] = -row.astype(np.float32)
        c[:, C_NMC + l * 128:C_NMC + (l + 1) * 128] = -row.T.astype(np.float32)
    return c


V_LNG, V_LNB, V_NW = 0, 1024, 2048
V_A = 2304
V_DT = 2312
V_CW = 2320
V_BGK = 2448
VTOT = 2960


class StopEmit(Exception):
    pass


import os as _os
_KSTOP = float(_os.environ.get("KSTOP", "99"))


STOPPED = [False]
SCHED = int(_os.environ.get("KSCHED", "1"))
EMBED = int(_os.environ.get("KEMBED", "1"))
WIN = int(_os.environ.get("KWIN", "2"))


def ck(k):
    if _KSTOP <= k:
        STOPPED[0] = True
        return True
    return False


class Prog:
    def __init__(self, layers, ntiles, ranks=1):
        self.layers = layers
        self.nt = ntiles
        self.ranks = ranks
        self.build()

    def nh(self, kind):
        return (8 if kind == "gdn" else 4) // self.ranks

    def build(self):
        from contextlib import ExitStack
        nc = bass.Bass("TRN2", target_bir_lowering=False)
        self.nc = nc
        nt = self.nt
        L = len(self.layers)
        ntok = nt * 128
        dr = {}
        dr["x"] = nc.dram_tensor("x", [ntok, D_MODEL], F32, kind="ExternalInput").ap()
        if "gdn" in self.layers:
            dr["xhalo"] = nc.dram_tensor("xhalo", [L, 3, D_MODEL], F32, kind="ExternalInput").ap()
        dr["consts"] = nc.dram_tensor("consts", [128, C_TOT], F32, kind="ExternalInput").ap()
        dr["vecs"] = nc.dram_tensor("vecs", [L, 128, VTOT], F32, kind="ExternalInput").ap()
        for li, kind in enumerate(self.layers):
            nh = self.nh(kind)
            cin = nh * 768 + (2 * nh if kind == "gdn" else 16)
            val = 2048 if kind == "gdn" else 1024
            dr[f"win{li}"] = nc.dram_tensor(f"win{li}", [D_MODEL, cin], F32, kind="ExternalInput").ap()
            dr[f"wout{li}"] = nc.dram_tensor(f"wout{li}", [val, D_MODEL], F32, kind="ExternalInput").ap()
            dr[f"sin{li}"] = nc.dram_tensor(f"sin{li}", [128, nh * 256], F32, kind="ExternalInput").ap()
            dr[f"sout{li}"] = nc.dram_tensor(f"sout{li}", [128, nh * 256], F32, kind="ExternalOutput").ap()
            if kind == "gla":
                dr[f"wgk{li}"] = nc.dram_tensor(f"wgk{li}", [16, nh * 128], F32, kind="ExternalInput").ap()
            self.drt = getattr(self, "drt", {})
            self.tpc = getattr(self, "tpc", {})
            tpc = max(1, (1 << 20) // (128 * nh * 256 * 2))
            self.tpc[li] = tpc
            for q in range((nt + tpc - 1) // tpc):
                tq = min(tpc, nt - q * tpc)
                self.drt[f"ont{li}_{q}"] = nc.dram_tensor(f"ont{li}_{q}", [tq * 128, nh * 256], BF16)
                if self.ranks > 1:
                    self.drt[f"onta{li}_{q}"] = nc.dram_tensor(f"onta{li}_{q}", [self.ranks * tq * 128, nh * 256], BF16)
        dr["y"] = nc.dram_tensor("y", [ntok, D_MODEL], F32, kind="ExternalOutput").ap()
        for i in range(min(2, L - 1)):
            dr[f"act{i}"] = nc.dram_tensor(f"act{i}", [ntok, D_MODEL], F32).ap()
        self.dr = dr
        self.dt = {}
        with ExitStack() as stack:
            cx = Ctx(nc, stack)
            self.cx = cx
            self.emit(stack)
        return nc

    def dtile(self, name, i):
        k = (name, i)
        if k not in self.dt:
            self.dt[k] = T(None, f"{name}{i}")
        return self.dt[k]

    def emit(self, stack):
        from contextlib import ExitStack
        cx, nc, dr = self.cx, self.nc, self.dr
        L = len(self.layers)
        cst = cx.sb("cst", [128, C_TOT], F32)
        cx.dma(cst[:, :], dr["consts"][:, :], writes=[cst], key="cst")
        idb = cx.sb("idb", [128, 128], BF16)
        oneb = cx.sb("oneb", [128, 128], BF16)
        cx.op("dve", lambda e: e.tensor_copy(out=idb[:, :], in_=cst[:, C_ID:C_ID + 128]), [cst], [idb])
        cx.op("dve", lambda e: e.tensor_copy(out=oneb[:, :], in_=cst[:, C_ONE:C_ONE + 128]), [cst], [oneb])
        m1b = cx.sb("m1b", [128, 128], BF16)
        m2b = cx.sb("m2b", [128, 128], BF16)
        cx.op("dve", lambda e: e.tensor_copy(out=m1b[:, :], in_=cst[:, C_M1:C_M1 + 128]), [cst], [m1b])
        cx.op("dve", lambda e: e.tensor_copy(out=m2b[:, :], in_=cst[:, C_M2:C_M2 + 128]), [cst], [m2b])
        self.m1b, self.m2b = m1b, m2b
        self.cst, self.idb, self.oneb = cst, idb, oneb
        self.pb = [cx.ps(f"pb{i}", [128, 512], F32) for i in range(7)]
        pbb0 = cx.ps("pbb0", [128, 1024], BF16)
        self.pbb = [pbb0, pbb0]
        self.wbuf = {}
        self.pref = []
        out_tks = []
        try:
            self.emit_layers(stack, out_tks)
        except StopEmit:
            cx.sem_stack = stack
            cx.stack = stack
        cx.finish(out_tks)

    def emit_layers(self, stack, out_tks):
        cx, nc, dr = self.cx, self.nc, self.dr
        L = len(self.layers)
        for li, kind in enumerate(self.layers):
            src = dr["x"] if li == 0 else dr[f"act{(li - 1) % 2}"]
            srcn = "x" if li == 0 else f"act{(li - 1) % 2}"
            dst = dr["y"] if li == L - 1 else dr[f"act{li % 2}"]
            dstn = "y" if li == L - 1 else f"act{li % 2}"
            with ExitStack() as ls:
                cx.stack = ls
                vec = cx.sb(f"vec{li}", [128, VTOT], F32)
                cx.dma(vec[:, :], dr["vecs"][li], writes=[vec], key="vec")
                self.vec = vec
                with ExitStack() as pa:
                    cx.stack = pa
                    self.alloc_w("win", li)
                    self.weight_ops("win", li)
                    self.pref = []
                    if kind == "gla":
                        tks = self.gla_phase_a(li, src, srcn)
                    else:
                        tks = self.gdn_phase_a(li, src, srcn)
                    out_tks += tks
                    cx.play([self.take_pref(1)])
                    cx.barrier()
                with ExitStack() as pbk:
                    cx.stack = pbk
                    if STOPPED[0]:
                        continue
                    self.alloc_w("wout", li)
                    self.weight_ops("wout", li)
                    self.pref = []
                    tks = self.phase_b(li, kind, src, srcn, dst, dstn, last=(li == L - 1))
                    out_tks += tks
                    cx.play([self.take_pref(1)])
                    cx.barrier()
            cx.stack = stack

    def alloc_w(self, which, li):
        cx = self.cx
        kind = self.layers[li]
        nh = self.nh(kind)
        if which == "win":
            shp = [128, 8, nh * 768 + (2 * nh if kind == "gdn" else 16)]
        else:
            shp = [128, (2048 if kind == "gdn" else 1024) // 128, 1024]
        self.wbuf[f"{which}_{kind}"] = cx.sb(f"{which}_{kind}", shp, BF16)
        self.wstg = [cx.sb(f"wstg{i}", [128, 1024], F32) for i in range(2)]

    def weight_ops(self, which, li):
        kind = self.layers[li]
        nh = self.nh(kind)
        if which == "win":
            rows, cols = 1024, nh * 768 + (2 * nh if kind == "gdn" else 16)
        else:
            rows, cols = (2048 if kind == "gdn" else 1024), 1024
        self.load_weight(self.dr[f"{which}{li}"], self.wbuf[f"{which}_{kind}"], rows, cols, self.wstg, pool_only=False)

    def take_pref(self, nleft):
        n = (len(self.pref) + nleft - 1) // max(1, nleft)
        n += n % 2
        out, self.pref = self.pref[:n], self.pref[n:]
        return out

    def load_weight(self, wdram, wsb, rows, cols, stg, pool_only=False):
        cx = self.cx
        kcn = rows // 128
        cb = 1024
        n = 0
        for kc in range(kcn):
            for c0 in range(0, cols, cb):
                c1 = min(cols, c0 + cb)
                s = stg[n % 2]
                cx.dma(s[:, 0:c1 - c0], wdram[kc * 128:(kc + 1) * 128, c0:c1], writes=[s], key=f"wstg{n % 2}",
                       e="sp")
                eng = "pool" if (n % 2 == 0 or pool_only) else "dve"
                cx.op(eng, lambda e, s=s, kc=kc, c0=c0, c1=c1: e.tensor_copy(out=wsb[:, kc, c0:c1],
                                                                             in_=s[:, 0:c1 - c0]), [s], [wsb])
                n += 1

    def load_x_tile(self, src, srcn, i, xt):
        cx = self.cx
        cx.dma(xt[:, :], src[i * 128:(i + 1) * 128, :], reads=[self.dtile(srcn, i)], writes=[xt],
               key=xt.name)

    def make_xT(self, xt, xb, xT):
        cx = self.cx
        pbb = self.pbb[0]
        cx.op("pool", lambda e: e.tensor_copy(out=xb[:, :], in_=xt[:, :]), [xt], [xb])
        for kc in range(8):
            cx.op("pe", lambda e, kc=kc: e.transpose(out=pbb[:, kc * 128:(kc + 1) * 128],
                                                     in_=xb[:, kc * 128:(kc + 1) * 128], identity=self.idb[:, :]),
                  [xb, self.idb], [pbb])
        cx.op("act", lambda e: e.activation(out=xT[:, :, :], in_=pbb[:, :].rearrange("p (k t) -> p k t", k=8),
                                            func=AF.Copy), [pbb], [xT])

    def split(self, src_t, src_ap, hi, lo, hi_ap=None, lo_ap=None):
        cx = self.cx
        hi_ap = hi[:, :] if hi_ap is None else hi_ap
        lo_ap = lo[:, :] if lo_ap is None else lo_ap
        cx.op("dve", lambda e: e.tensor_copy(out=hi_ap, in_=src_ap), [src_t], [hi])
        cx.op("dve", lambda e: e.tensor_tensor(out=lo_ap, in0=src_ap, in1=hi_ap, op=ALU.subtract), [src_t, hi], [lo])

    def silu(self, src_t, src_ap, dst_t, dst_ap, tmp_t, tmp_ap):
        cx = self.cx
        one = self.cst[:, C_ONE:C_ONE + 1]
        cx.op("act", lambda e: e.activation(out=tmp_ap, in_=src_ap, func=AF.Exp, scale=-1.0), [src_t], [tmp_t])
        cx.op("act", lambda e: e.activation(out=tmp_ap, in_=tmp_ap, func=AF.Ln, bias=one), [tmp_t, self.cst], [tmp_t])
        cx.op("act", lambda e: e.activation(out=tmp_ap, in_=tmp_ap, func=AF.Exp, scale=-1.0), [tmp_t], [tmp_t])
        cx.op("dve", lambda e: e.tensor_tensor(out=dst_ap, in0=src_ap, in1=tmp_ap, op=ALU.mult), [src_t, tmp_t], [dst_t])

    def mm_acc(self, ps_t, ps_ap, pairs, reads):
        cx = self.cx
        n = len(pairs)
        for k, (l, r) in enumerate(pairs):
            cx.op("pe", lambda e, l=l, r=r, k=k: e.matmul(ps_ap, lhsT=l, rhs=r, start=(k == 0), stop=(k == n - 1)),
                  reads, [ps_t])

    def store_onT(self, o_n, nkc, i):
        cx = self.cx
        onT = self.onT
        for g0 in range(0, nkc, 8):
            n = min(8, nkc - g0)
            pbb = self.pbb[1]
            for k in range(n):
                kc = g0 + k
                cx.op("pe", lambda e, kc=kc, k=k: e.transpose(out=pbb[:, k * 128:(k + 1) * 128],
                                                              in_=o_n[:, kc * 128:(kc + 1) * 128],
                                                              identity=self.idb[:, :]), [o_n, self.idb], [pbb])
            cx.op("act", lambda e, g0=g0, n=n: e.activation(out=onT[:, g0 * 128:(g0 + n) * 128],
                                                           in_=pbb[:, 0:n * 128], func=AF.Copy), [pbb], [onT])
        li = self.li
        tpc = self.tpc[li]
        q, t = i // tpc, i % tpc
        tk = cx.dma(self.drt[f"ont{li}_{q}"].ap()[t * 128:(t + 1) * 128, :], onT[:, 0:nkc * 128], reads=[onT],
                    writes=[self.dtile("ont", i)], key="ontst")
        if self.ranks > 1 and (t == tpc - 1 or i == self.nt - 1):
            R = self.ranks
            groups = [list(range(g * R, (g + 1) * R)) for g in range(8 // R)]
            tk = cx.collective("AllGather", self.drt[f"ont{li}_{q}"].ap().opt(), self.drt[f"onta{li}_{q}"].ap().opt(),
                               groups, reads=[self.dtile("ont", k) for k in range(q * tpc, i + 1)],
                               writes=[self.dtile("onta", q)])
        return tk

    def rms_gate(self, o_ps_t, o_ps, gs, h, o_n, tmp):
        cx = self.cx
        vec = self.vec
        o_f, junk, ssq, rstd = tmp
        cx.op("act", lambda e: e.activation(out=o_f[:, :], in_=o_ps, func=AF.Copy), [o_ps_t], [o_f])
        if ck(4.71):
            return
        cx.op("act", lambda e: e.activation(out=junk[:, :], in_=o_f[:, :], func=AF.Square, scale=1.0 / 16.0,
                                            accum_out=ssq[:, 0:1]), [o_f], [junk, ssq])
        if ck(4.72):
            return
        cx.op("act", lambda e: e.activation(out=rstd[:, 0:1], in_=ssq[:, 0:1], func=AF.Ln,
                                            bias=self.cst[:, C_EPSR:C_EPSR + 1]), [ssq, self.cst], [rstd])
        if ck(4.73):
            return
        cx.op("act", lambda e: e.activation(out=rstd[:, 0:1], in_=rstd[:, 0:1], func=AF.Exp, scale=-0.5),
              [rstd], [rstd])
        if ck(4.74):
            return
        cx.op("dve", lambda e: e.scalar_tensor_tensor(out=o_f[:, :], in0=o_f[:, :], scalar=rstd[:, 0:1],
                                                      in1=vec[:, V_NW:V_NW + 256], op0=ALU.mult, op1=ALU.mult),
              [o_f, rstd, vec], [o_f])
        if ck(4.75):
            return
        cx.op("dve", lambda e: e.tensor_tensor(out=o_n[:, h * 256:(h + 1) * 256], in0=o_f[:, :], in1=gs,
                                               op=ALU.mult), [o_f] + self._gs_t, [o_n])

    def gla_phase_a(self, li, src, srcn):
        cx, dr, cst, vec = self.cx, self.dr, self.cst, self.vec
        pb = self.pb
        nt = self.nt
        M1 = cst[:, C_M1:C_M1 + 128]
        M2 = cst[:, C_M2:C_M2 + 128]
        tks = []
        self.li = li
        nh = self.nh("gla")
        NK = nh * 128
        NV = nh * 256
        CIN = nh * 768 + 16
        win = self.wbuf["win_gla"]
        wgk = cx.sb("wgk", [16, NK], F32)
        cx.dma(wgk[:, :], dr[f"wgk{li}"][:, :], writes=[wgk], key="wgk")
        wgk_h = cx.sb("wgk_h", [16, NK], BF16)
        wgk_l = cx.sb("wgk_l", [16, NK], BF16)
        self.split(wgk, wgk[:, :], wgk_h, wgk_l)
        m1b, m2b = self.m1b, self.m2b
        S_f = cx.sb("S_f", [128, nh, 256], F32)
        S_b = [cx.sb(f"S_b{h}", [128, 256], BF16) for h in range(nh)]
        cx.dma(S_f[:, :, :], dr[f"sin{li}"].rearrange("p (h v) -> p h v", h=nh), writes=[S_f], key="S")
        for h in range(nh):
            cx.op("dve", lambda e, h=h: e.tensor_copy(out=S_b[h][:, :], in_=S_f[:, h, :]), [S_f], [S_b[h]])
        xts = [cx.sb(f"xt{i}", [128, 1024], F32) for i in range(2)]
        Eq = cx.sb("Eq", [128, 128], F32)
        Ek = cx.sb("Ek", [128, 128], F32)
        qt = cx.sb("qt", [128, 128], BF16)
        kt = cx.sb("kt", [128, 128], BF16)
        ATm = cx.sb("ATm", [128, 128], BF16)
        o_f = cx.sb("o_f", [128, 256], F32)
        junk = cx.sb("junk", [128, 256], F32)
        ssq = cx.sb("ssq", [128, 1], F32)
        rstd = cx.sb("rstd", [128, 1], F32)
        self.onT = cx.sb("onT", [128, 2048], BF16)
        gtmp = cx.sb("gtmp", [128, NV], F32)
        tks = []

        fsets = []
        for k_ in range(3):
            fsets.append(dict(
                xb=cx.sb("xb", [128, 1024], BF16),
                xT=cx.sb("xT", [128, 8, 128], BF16),
                qT_f=cx.sb("qT_f", [128, nh, 128], F32),
                kT_f=cx.sb("kT_f", [128, nh, 128], F32),
                ktok=cx.sb("ktok", [128, NK], F32),
                v_b=cx.sb("v_b", [128, NV], BF16),
                gs=cx.sb("gs", [128, NV], F32),
                gkl=cx.sb("gkl", [16, 128], F32),
                gkl_h=cx.sb("gkl_h", [16, 128], BF16),
                gkl_l=cx.sb("gkl_l", [16, 128], BF16),
                lg=cx.sb("lg", [128, NK], F32),
                lg_h=cx.sb("lg_h", [128, NK], BF16),
                lg_l=cx.sb("lg_l", [128, NK], BF16),
                khat=cx.sb("khat", [128, NK], BF16),
                Er=cx.sb("Er", [128, NK], F32),
                o_n=cx.sb("o_n", [128, NV], BF16),
            ))
        UNP = ('xb', 'xT', 'qT_f', 'kT_f', 'ktok', 'v_b', 'gs', 'gkl', 'gkl_h', 'gkl_l', 'lg', 'lg_h', 'lg_l', 'khat', 'Er', 'o_n')

        aout = [[dict(qt=cx.sb("qt", [128, 128], BF16), ATm=cx.sb("ATm", [128, 128], BF16),
                      Eq=cx.sb("Eq", [128, 128], F32)) for _ in range(2)] for h in range(nh)]

        GT, NGS = 4, 2
        gsets = [dict(xT4=cx.sb("xT4", [128, 8, GT * 128], BF16), qT4=cx.sb("qT4", [128, nh, GT * 128], F32),
                      kT4=cx.sb("kT4", [128, nh, GT * 128], F32), gkl4=cx.sb("gkl4", [16, GT * 128], F32))
                 for _ in range(NGS)]
        xbs = [cx.sb(f"xbg{t}", [128, 1024], BF16) for t in range(2)]
        idb = self.idb

        def G(gi):
            GS = gsets[gi % NGS]
            xT4, qT4, kT4, gkl4 = GS["xT4"], GS["qT4"], GS["kT4"], GS["gkl4"]
            tiles = [i for i in range(gi * GT, min(nt, (gi + 1) * GT))]
            N = len(tiles) * 128
            for t, i in enumerate(tiles):
                xt, xb = xts[t % 2], xbs[t % 2]
                self.load_x_tile(src, srcn, i, xt)
                cx.op("pool", lambda e, xt=xt, xb=xb: e.tensor_copy(out=xb[:, :], in_=xt[:, :]), [xt], [xb])
                pbb = self.pbb[0]
                for kc in range(8):
                    cx.op("pe", lambda e, kc=kc, xb=xb, pbb=pbb: e.transpose(
                        out=pbb[:, kc * 128:(kc + 1) * 128], in_=xb[:, kc * 128:(kc + 1) * 128],
                        identity=idb[:, :]), [xb, idb], [pbb])
                cx.op("act", lambda e, t=t, pbb=pbb: e.activation(
                    out=xT4[:, :, t * 128:(t + 1) * 128], in_=pbb[:, :].rearrange("p (k t) -> p k t", k=8),
                    func=AF.Copy), [pbb], [xT4])
                cx.mark()
            for blk in range(2 * nh):
                dst = qT4 if blk < nh else kT4
                p = pb[blk % 2]
                self.mm_acc(p, p[:, 0:N], [(win[:, kc, blk * 128:(blk + 1) * 128], xT4[:, kc, 0:N]) for kc in range(8)],
                            [win, xT4])
                cx.op("act", lambda e, p=p, dst=dst, blk=blk: e.activation(out=dst[:, blk % nh, 0:N], in_=p[:, 0:N],
                                                                          func=AF.Copy), [p], [dst])
                cx.mark()
            p = pb[2]
            self.mm_acc(p, p[0:16, 0:N], [(win[:, kc, 2 * NK + 2 * NV:2 * NK + 2 * NV + 16], xT4[:, kc, 0:N])
                                          for kc in range(8)], [win, xT4])
            cx.op("act", lambda e, p=p: e.activation(out=gkl4[:, 0:N], in_=p[0:16, 0:N], func=AF.Copy), [p], [gkl4])
            cx.mark()

        def front(i):
            FS = fsets[i % 3]
            xb, xT, qT_f, kT_f, ktok, v_b, gs, gkl, gkl_h, gkl_l, lg, lg_h, lg_l, khat, Er, o_n = (FS[k] for k in UNP)
            GS = gsets[(i // GT) % NGS]
            xT_T, qT4, kT4, gkl4 = GS["xT4"], GS["qT4"], GS["kT4"], GS["gkl4"]
            t0_ = (i % GT) * 128
            xTv = xT_T[:, :, t0_:t0_ + 128]
            p = pb[0]
            self.mm_acc(p, p[:, 0:NK], [(xTv[:, kc, :], win[:, kc, NK:2 * NK]) for kc in range(8)], [win, xT_T])
            cx.op("act", lambda e, p=p: e.activation(out=ktok[:, :], in_=p[:, 0:NK], func=AF.Copy), [p], [ktok])
            for wi, (dstt, fn, c_base) in enumerate(((v_b, AF.Copy, 2 * NK), (gs, AF.Silu, 2 * NK + NV))):
                for ci, c0 in enumerate(range(0, NV, 512)):
                    w = min(512, NV - c0)
                    p = pb[(wi + ci) % 2]
                    self.mm_acc(p, p[:, 0:w], [(xTv[:, kc, :], win[:, kc, c_base + c0:c_base + c0 + w])
                                               for kc in range(8)], [win, xT_T])
                    if fn == AF.Silu:
                        self.silu(p, p[:, 0:w], dstt, dstt[:, c0:c0 + w], gtmp, gtmp[:, c0:c0 + w])
                    else:
                        cx.op("act", lambda e, p=p, dstt=dstt, fn=fn, c0=c0, w=w: e.activation(
                            out=dstt[:, c0:c0 + w], in_=p[:, 0:w], func=fn), [p], [dstt])
            p = pb[3]
            self.split(gkl4, gkl4[:, t0_:t0_ + 128], gkl_h, gkl_l)
            self.mm_acc(p, p[:, 0:NK], [(gkl_h[:, :], wgk_h[:, :]), (gkl_h[:, :], wgk_l[:, :]),
                                        (gkl_l[:, :], wgk_h[:, :])], [gkl_h, gkl_l, wgk_h, wgk_l])
            cx.op("dve", lambda e, p=p: e.tensor_tensor(out=lg[:, :], in0=p[:, 0:NK], in1=vec[:, V_BGK:V_BGK + NK],
                                                        op=ALU.add), [p, vec], [lg])
            cx.op("act", lambda e: e.activation(out=lg[:, :], in_=lg[:, :], func=AF.Exp, scale=-1.0), [lg], [lg])
            cx.op("act", lambda e: e.activation(out=lg[:, :], in_=lg[:, :], func=AF.Ln, bias=cst[:, C_ONE:C_ONE + 1]),
                  [lg, cst], [lg])
            p = pb[3]
            self.split(lg, lg[:, :], lg_h, lg_l)
            self.mm_acc(p, p[:, 0:NK], [(m2b[:, :], lg_h[:, :]), (m2b[:, :], lg_l[:, :])], [m2b, lg_h, lg_l])
            cx.op("act", lambda e, p=p: e.activation(out=Er[:, :], in_=p[:, 0:NK], func=AF.Exp,
                                                     scale=-1.0 / 16.0), [p], [Er])
            cx.op("dve", lambda e: e.tensor_tensor(out=khat[:, :], in0=ktok[:, :], in1=Er[:, :], op=ALU.mult),
                  [ktok, Er], [khat])

        def heads(i):
            GSh = gsets[(i // GT) % NGS]
            qT4, kT4 = GSh["qT4"], GSh["kT4"]
            t0_ = (i % GT) * 128
            FS = fsets[i % 3]
            xb, xT, qT_f, kT_f, ktok, v_b, gs, gkl, gkl_h, gkl_l, lg, lg_h, lg_l, khat, Er, o_n = (FS[k] for k in UNP)
            self._gs_t = [gs]
            for h in range(nh):
                qt, ATm, Eq = (aout[h][i % 2][k] for k in ("qt", "ATm", "Eq"))
                pq = pb[4]
                self.mm_acc(pq, pq[:, 0:128], [(lg_h[:, h * 128:(h + 1) * 128], m1b[:, :]),
                                               (lg_l[:, h * 128:(h + 1) * 128], m1b[:, :])], [lg_h, lg_l, m1b])
                cx.op("act", lambda e: e.activation(out=Eq[:, :], in_=pq[:, 0:128], func=AF.Exp, scale=-1.0 / 16.0),
                      [pq], [Eq])
                cx.op("act", lambda e: e.activation(out=Ek[:, :], in_=pq[:, 0:128], func=AF.Exp, scale=1.0 / 16.0),
                      [pq], [Ek])
                cx.op("dve", lambda e, h=h: e.scalar_tensor_tensor(out=qt[:, :], in0=qT4[:, h, t0_:t0_ + 128], scalar=SCALE,
                                                                   in1=Eq[:, :], op0=ALU.mult, op1=ALU.mult),
                      [qT4, Eq], [qt])
                cx.op("dve", lambda e, h=h: e.tensor_tensor(out=kt[:, :], in0=kT4[:, h, t0_:t0_ + 128], in1=Ek[:, :],
                                                            op=ALU.mult), [kT4, Ek], [kt])
                self.mm_acc(pq, pq[:, 128:256], [(kt[:, :], qt[:, :])], [kt, qt])
                cx.op("dve", lambda e: e.tensor_tensor(out=ATm[:, :], in0=pq[:, 128:256], in1=M1, op=ALU.mult),
                      [pq, cst], [ATm])
            cx.mark()
            for h in range(nh):
                qt, ATm, Eq = (aout[h][i % 2][k] for k in ("qt", "ATm", "Eq"))
                po = pb[5]
                self.mm_acc(po, po[:, 0:256], [(qt[:, :], S_b[h][:, :]), (ATm[:, :], v_b[:, h * 256:(h + 1) * 256])],
                            [qt, S_b[h], ATm, v_b])
                self.mm_acc(po, po[:, 256:512], [(khat[:, h * 128:(h + 1) * 128], v_b[:, h * 256:(h + 1) * 256])],
                            [khat, v_b])
                cx.op("dve", lambda e, h=h: e.scalar_tensor_tensor(out=S_f[:, h, :], in0=S_f[:, h, :],
                                                                   scalar=Eq[:, 127:128], in1=po[:, 256:512],
                                                                   op0=ALU.mult, op1=ALU.add), [S_f, Eq, po], [S_f])
                cx.op("pool", lambda e, h=h: e.tensor_copy(out=S_b[h][:, :], in_=S_f[:, h, :]), [S_f], [S_b[h]])
                self.rms_gate(po, po[:, 0:256], gs[:, h * 256:(h + 1) * 256], h, o_n, (o_f, junk, ssq, rstd))

        def segs(lst):
            out, cur = [], []
            for it in lst:
                if it[0] == "mark":
                    if cur:
                        out.append(cur)
                    cur = []
                else:
                    cur.append(it)
            if cur:
                out.append(cur)
            return out

        ngrp = (nt + GT - 1) // GT
        cx.play([cx.rec(G, 0)])
        gq = segs(cx.rec(G, 1)) if ngrp > 1 else []
        nextg = 2

        def front_list(j):
            nonlocal gq, nextg
            out = []
            if j % GT == 0 and j > 0:
                for sg in gq:
                    out += sg
                gq = segs(cx.rec(G, nextg)) if nextg < ngrp else []
                nextg += 1
                out += cx.rec(front, j)
            else:
                out += cx.rec(front, j)
                left = GT - (j % GT)
                n = (len(gq) + left - 1) // max(1, left)
                for sg in gq[:n]:
                    out += sg
                gq = gq[n:]
            return out

        AB = {}
        cx.play([front_list(0)])
        AB[0] = cx.split_at_mark(cx.rec(heads, 0))
        lists = [AB[0][0]]
        if nt > 1:
            lists.append(front_list(1))
        cx.play(lists)
        for i in range(nt):
            lists = [AB[i][1]]
            if i + 1 < nt:
                AB[i + 1] = cx.split_at_mark(cx.rec(heads, i + 1))
                lists.append(AB[i + 1][0])
            st_ = cx.rec(self.store_onT, fsets[i % 3]["o_n"], 2 * nh, i)
            if i + 2 < nt:
                lists.append(front_list(i + 2) + st_)
            else:
                lists.append(st_)
            lists.append(self.take_pref(nt - i))
            cx.play(lists)
            del AB[i]
        tks.append(cx.dma(dr[f"sout{li}"].rearrange("p (h v) -> p h v", h=nh), S_f[:, :, :], reads=[S_f],
                          key="Sst"))
        return tks

    def gdn_phase_a(self, li, src, srcn):
        cx, dr, cst, vec = self.cx, self.dr, self.cst, self.vec
        pb = self.pb
        nt = self.nt
        m1b, m2b, oneb, idb = self.m1b, self.m2b, self.oneb, self.idb
        ID = cst[:, C_ID:C_ID + 128]
        M1 = cst[:, C_M1:C_M1 + 128]
        tks = []
        self.li = li
        nh = self.nh("gdn")
        NB = 4 * nh
        CG = NB * 128
        CB = CG + nh * 256
        CIN = CB + 2 * nh
        win = self.wbuf["win_gdn"]
        S_f = [cx.sb(f"S_f{h}", [128, 256], F32) for h in range(nh)]
        S_b = [cx.sb(f"S_b{h}", [128, 256], BF16) for h in range(nh)]
        for h in range(nh):
            cx.dma(S_f[h][:, :], dr[f"sin{li}"][:, h * 256:(h + 1) * 256], writes=[S_f[h]], key=f"S{h}")
            cx.op("dve", lambda e, h=h: e.tensor_copy(out=S_b[h][:, :], in_=S_f[h][:, :]), [S_f[h]], [S_b[h]])
        halo = cx.sb("halo", [128, NB, 3], F32)
        css = [cx.sb(f"cs{i}", [128, 131], F32) for i in range(2)]
        f32t = {n: cx.sb(n, [128, 128], F32) for n in ["acc", "sl", "rs", "zero"]}
        b16t = {n: cx.sb(n, [128, 128], BF16) for n in ["sq"]}
        ff32, fb16 = f32t, b16t
        negA = cx.sb("negA", [128, nh], F32)
        fsets = []
        for k_ in range(3):
            FS = dict(xb=None, xT=None,
                      qkvT=None, ktok=cx.sb("ktok", [128, nh * 128], BF16),
                      vtok=cx.sb("vtok", [128, nh * 256], BF16), ba=cx.sb("ba", [128, 2 * nh], F32),
                      E3=cx.sb("E3", [128, 3 * nh], F32), o_n=cx.sb("o_n", [128, nh * 256], BF16))
            FS["sm"] = {n: cx.sb(n, [128, nh], F32) for n in
                        ["beta", "lnb", "z", "g", "g_hf", "g_lf", "l_hf", "l_lf", "be"]}
            FS["sm"]["negA"] = negA
            FS["smb"] = {n: cx.sb(n, [128, nh], BF16) for n in ["g_hb", "g_lb", "l_hb"]}
            fsets.append(FS)
        hsets = []
        for k_ in range(min(2, nh)):
            HB = dict(f32t={n: cx.sb(n, [128, 128], F32) for n in ["D", "DT", "BDT", "EgcB", "A_row", "A_col", "tmpq"]},
                      b16t={n: cx.sb(n, [128, 128], BF16) for n in
                            ["rR_h", "rR_l", "rC_h", "rC_l", "rL_h", "rL_l", "qd", "QKTm", "X0", "X1", "R0", "R1",
                             "Offn", "On", "W1", "W2", "wT", "kgb", "kd"]},
                      bv=cx.sb("bv", [128, 256], BF16), u_f=cx.sb("u_f", [128, 256], F32),
                      vnew=cx.sb("vnew", [128, 256], BF16), gsh=cx.sb("gsh", [128, 256], F32),
                      o_f=cx.sb("o_f", [128, 256], F32), junk=cx.sb("junk", [128, 256], F32),
                      sgt=cx.sb("sgt", [128, 256], F32), Offa=cx.sb("Offa", [128, 6, 128], BF16),
                      Ona=cx.sb("Ona", [128, 6, 128], BF16),
                      ssq=cx.sb("ssq", [128, 1], F32), rstd=cx.sb("rstd", [128, 1], F32),
                      pdg=pb[2 * k_], pw=pb[2 * k_ + 1])
            HB["out"] = [dict(u_f=cx.sb("u_f", [128, 256], F32), wT=cx.sb("wT", [128, 128], BF16),
                              qd=cx.sb("qd", [128, 128], BF16), QKTm=cx.sb("QKTm", [128, 128], BF16),
                              kd=cx.sb("kd", [128, 128], BF16)) for _ in range(2)]
            hsets.append(HB)
        self.onT = cx.sb("onT", [128, 2048], BF16)
        sm = {"negA": negA}
        zero = f32t["zero"]
        cx.op("dve", lambda e: e.memset(zero[:, :], 0.0), [], [zero])
        cx.op("act", lambda e: e.activation(out=negA[:, :], in_=vec[:, V_A:V_A + nh], func=AF.Exp), [vec],
              [negA])
        cx.op("act", lambda e: e.activation(out=negA[:, :], in_=negA[:, :], func=AF.Copy, scale=-1.0),
              [negA], [negA])
        xh = cx.sb("xh", [3, 1024], F32)
        xhb = cx.sb("xhb", [3, 1024], BF16)
        xhT = cx.sb("xhT", [128, 8, 4], BF16)
        cx.dma(xh[:, :], dr["xhalo"][li], writes=[xh], key="xh")
        cx.op("dve", lambda e: e.tensor_copy(out=xhb[:, :], in_=xh[:, :]), [xh], [xhb])
        pbb = self.pbb[0]
        for kc in range(8):
            cx.op("pe", lambda e, kc=kc: e.transpose(out=pbb[:, kc * 4:kc * 4 + 3], in_=xhb[0:3, kc * 128:(kc + 1) * 128],
                                                     identity=idb[0:3, 0:3]), [xhb, idb], [pbb])
        cx.op("act", lambda e: e.activation(out=xhT[:, :, 0:3],
                                            in_=pbb[:, 0:32].rearrange("p (k t) -> p k t", k=8)[:, :, 0:3],
                                            func=AF.Copy), [pbb], [xhT])
        for blk in range(NB):
            p = pb[blk % 2]
            self.mm_acc(p, p[:, 0:3], [(win[:, kc, blk * 128:(blk + 1) * 128], xhT[:, kc, 0:3]) for kc in range(8)],
                        [win, xhT])
            cx.op("act", lambda e, p=p, blk=blk: e.activation(out=halo[:, blk, :], in_=p[:, 0:3], func=AF.Copy),
                  [p], [halo])
        UNP = ("xb", "xT", "qkvT", "ktok", "vtok", "ba", "E3", "o_n", "sm", "smb")

        pB0, pB1 = pb[4], pb[5]
        pF = pb[6]
        GT = 4
        NGS = 2
        gsets = [dict(xT4=cx.sb("xT4", [128, 8, GT * 128], BF16), qkvT4=cx.sb("qkvT4", [128, NB, GT * 128], BF16))
                 for _ in range(NGS)]
        cs4 = [cx.sb(f"cs4_{k}", [128, 3 + GT * 128], F32) for k in range(2)]
        acc4 = cx.sb("acc4", [128, GT * 128], F32)
        sl4 = cx.sb("sl4", [128, GT * 128], F32)
        rs4 = cx.sb("rs4", [128, GT * 128], F32)
        sq4 = cx.sb("sq4", [128, GT * 128], BF16)
        zero4 = cx.sb("zero4", [128, GT * 128], F32)
        cx.op("pool", lambda e: e.memset(zero4[:, :], 0.0), [], [zero4])
        xt4 = [cx.sb(f"xtg{t}", [128, 1024], F32) for t in range(2)]
        xbs = [cx.sb(f"xbg{t}", [128, 1024], BF16) for t in range(2)]

        def G(gi):
            GS = gsets[gi % NGS]
            xT4, qkvT4 = GS["xT4"], GS["qkvT4"]
            tiles = [i for i in range(gi * GT, min(nt, (gi + 1) * GT))]
            N = len(tiles) * 128
            for t, i in enumerate(tiles):
                xt, xb = xt4[t % 2], xbs[t % 2]
                self.load_x_tile(src, srcn, i, xt)
                cx.op("pool", lambda e, xt=xt, xb=xb: e.tensor_copy(out=xb[:, :], in_=xt[:, :]), [xt], [xb])
                pbb = self.pbb[0]
                for kc in range(8):
                    cx.op("pe", lambda e, kc=kc, xb=xb, pbb=pbb: e.transpose(
                        out=pbb[:, kc * 128:(kc + 1) * 128], in_=xb[:, kc * 128:(kc + 1) * 128],
                        identity=idb[:, :]), [xb, idb], [pbb])
                cx.op("act", lambda e, t=t, pbb=pbb: e.activation(
                    out=xT4[:, :, t * 128:(t + 1) * 128], in_=pbb[:, :].rearrange("p (k t) -> p k t", k=8),
                    func=AF.Copy), [pbb], [xT4])
                cx.mark()
            for blk in range(NB):
                self.mm_acc(pF, pF[:, 0:N], [(win[:, kc, blk * 128:(blk + 1) * 128], xT4[:, kc, 0:N])
                                             for kc in range(8)], [win, xT4])
                cs = cs4[blk % 2]
                cx.op("pool", lambda e, cs=cs, blk=blk: e.tensor_copy(out=cs[:, 0:3], in_=halo[:, blk, :]), [halo], [cs])
                cx.op("act", lambda e, cs=cs: e.activation(out=cs[:, 3:3 + N], in_=pF[:, 0:N], func=AF.Copy), [pF], [cs])
                cx.op("pool", lambda e, cs=cs, blk=blk: e.tensor_copy(out=halo[:, blk, :], in_=cs[:, N:N + 3]), [cs], [halo])
                for j in range(4):
                    cx.op("dve", lambda e, cs=cs, blk=blk, j=j: e.scalar_tensor_tensor(
                        out=acc4[:, 0:N], in0=cs[:, j:j + N], scalar=vec[:, V_CW + blk * 4 + j:V_CW + blk * 4 + j + 1],
                        in1=(zero4[:, 0:N] if j == 0 else acc4[:, 0:N]), op0=ALU.mult, op1=ALU.add),
                        [cs, vec, zero4, acc4], [acc4])
                if blk >= 2 * nh:
                    self.silu(acc4, acc4[:, 0:N], qkvT4, qkvT4[:, blk, 0:N], rs4, rs4[:, 0:N])
                else:
                    self.silu(acc4, acc4[:, 0:N], sl4, sl4[:, 0:N], rs4, rs4[:, 0:N])
                    cx.op("act", lambda e: e.activation(out=sq4[:, 0:N], in_=sl4[:, 0:N], func=AF.Square), [sl4], [sq4])
                    self.mm_acc(pF, pF[:, 0:N], [(oneb[:, :], sq4[:, 0:N])], [oneb, sq4])
                    cx.op("act", lambda e: e.activation(out=rs4[:, 0:N], in_=pF[:, 0:N], func=AF.Ln,
                                                        bias=cst[:, C_EPSR:C_EPSR + 1]), [pF, cst], [rs4])
                    cx.op("act", lambda e: e.activation(out=rs4[:, 0:N], in_=rs4[:, 0:N], func=AF.Exp, scale=-0.5),
                          [rs4], [rs4])
                    cx.op("dve", lambda e, blk=blk: e.tensor_tensor(out=qkvT4[:, blk, 0:N], in0=sl4[:, 0:N],
                                                                    in1=rs4[:, 0:N], op=ALU.mult), [sl4, rs4], [qkvT4])
                cx.mark()

        def front(i):
            FS = fsets[i % 3]
            xb, xT, qkvT, ktok, vtok, ba, E3, o_n, sm, smb = (FS[k] for k in UNP)
            f32t, b16t = ff32, fb16
            GS = gsets[(i // GT) % NGS]
            xT_T, qkvT_T = GS["xT4"], GS["qkvT4"]
            t0_ = (i % GT) * 128
            xTv = xT_T[:, :, t0_:t0_ + 128]
            tb = list(range(nh, NB))
            for g0 in range(0, len(tb), 8):
                grp = tb[g0:g0 + 8]
                pbb = self.pbb[0]
                for k, blk in enumerate(grp):
                    cx.op("pe", lambda e, blk=blk, k=k: e.transpose(out=pbb[:, k * 128:(k + 1) * 128],
                                                                    in_=qkvT_T[:, blk, t0_:t0_ + 128],
                                                                    identity=idb[:, :]),
                          [qkvT_T, idb], [pbb])
                kb = [b for b in grp if b < 2 * nh]
                vb = [b for b in grp if b >= 2 * nh]
                if kb:
                    cx.op("act", lambda e, kb=kb, grp=grp, pbb=pbb: e.activation(
                        out=ktok[:, (kb[0] - nh) * 128:(kb[-1] - nh + 1) * 128],
                        in_=pbb[:, grp.index(kb[0]) * 128:(grp.index(kb[-1]) + 1) * 128], func=AF.Copy), [pbb], [ktok])
                if vb:
                    cx.op("act", lambda e, vb=vb, grp=grp, pbb=pbb: e.activation(
                        out=vtok[:, (vb[0] - 2 * nh) * 128:(vb[-1] - 2 * nh + 1) * 128],
                        in_=pbb[:, grp.index(vb[0]) * 128:(grp.index(vb[-1]) + 1) * 128], func=AF.Copy), [pbb], [vtok])
            p = pF
            self.mm_acc(p, p[:, 0:2 * nh], [(xTv[:, kc, :], win[:, kc, CB:CB + 2 * nh]) for kc in range(8)], [win, xT_T])
            cx.op("act", lambda e, p=p: e.activation(out=ba[:, :], in_=p[:, 0:2 * nh], func=AF.Copy), [p], [ba])
            beta, lnb, z, g = sm["beta"], sm["lnb"], sm["z"], sm["g"]
            cx.op("act", lambda e: e.activation(out=lnb[:, :], in_=ba[:, 0:nh], func=AF.Exp, scale=-1.0), [ba], [lnb])
            cx.op("act", lambda e: e.activation(out=lnb[:, :], in_=lnb[:, :], func=AF.Ln, bias=cst[:, C_ONE:C_ONE + 1]),
                  [lnb, cst], [lnb])
            cx.op("act", lambda e: e.activation(out=beta[:, :], in_=lnb[:, :], func=AF.Exp, scale=-1.0), [lnb], [beta])
            cx.op("act", lambda e: e.activation(out=lnb[:, :], in_=lnb[:, :], func=AF.Copy, scale=-1.0), [lnb], [lnb])
            cx.op("dve", lambda e: e.tensor_tensor(out=z[:, :], in0=ba[:, nh:2 * nh], in1=vec[:, V_DT:V_DT + nh], op=ALU.add),
                  [ba, vec], [z])
            cx.op("act", lambda e: e.activation(out=z[:, :], in_=z[:, :], func=AF.Exp), [z], [z])
            cx.op("act", lambda e: e.activation(out=z[:, :], in_=z[:, :], func=AF.Ln, bias=cst[:, C_ONE:C_ONE + 1]),
                  [z, cst], [z])
            cx.op("dve", lambda e: e.tensor_tensor(out=g[:, :], in0=z[:, :], in1=sm["negA"][:, :], op=ALU.mult),
                  [z, sm["negA"]], [g])
            for srcv, hb, hf, lf in ((g, smb["g_hb"], sm["g_hf"], sm["g_lf"]), (lnb, smb["l_hb"], sm["l_hf"], sm["l_lf"])):
                cx.op("dve", lambda e, srcv=srcv, hb=hb: e.tensor_copy(out=hb[:, :], in_=srcv[:, :]), [srcv], [hb])
                cx.op("dve", lambda e, hb=hb, hf=hf: e.tensor_copy(out=hf[:, :], in_=hb[:, :]), [hb], [hf])
                cx.op("dve", lambda e, srcv=srcv, hf=hf, lf=lf: e.tensor_tensor(out=lf[:, :], in0=srcv[:, :], in1=hf[:, :],
                                                                              op=ALU.subtract), [srcv, hf], [lf])
            cx.op("dve", lambda e: e.tensor_copy(out=smb["g_lb"][:, :], in_=sm["g_lf"][:, :]), [sm["g_lf"]], [smb["g_lb"]])
            p = pF
            self.mm_acc(p, p[:, 0:nh], [(m1b[:, :], smb["g_hb"][:, :]), (m1b[:, :], smb["g_lb"][:, :])],
                        [m1b, smb["g_hb"], smb["g_lb"]])
            self.mm_acc(p, p[:, nh:2 * nh], [(m2b[:, :], smb["g_hb"][:, :]), (m2b[:, :], smb["g_lb"][:, :])],
                        [m2b, smb["g_hb"], smb["g_lb"]])
            self.mm_acc(p, p[:, 2 * nh:3 * nh], [(oneb[:, :], smb["g_hb"][:, :]), (oneb[:, :], smb["g_lb"][:, :])],
                        [oneb, smb["g_hb"], smb["g_lb"]])
            cx.op("act", lambda e, p=p: e.activation(out=E3[:, :], in_=p[:, 0:3 * nh], func=AF.Exp), [p], [E3])
            be = sm["be"]
            cx.op("dve", lambda e: e.tensor_tensor(out=be[:, :], in0=beta[:, :], in1=E3[:, 0:nh], op=ALU.mult),
                  [beta, E3], [be])

        def head(i, h):
            FS = fsets[i % 3]
            xb, xT, qkvT, ktok, vtok, ba, E3, o_n, sm, smb = (FS[k] for k in UNP)
            HB = hsets[h % len(hsets)]
            xT_T = gsets[(i // GT) % NGS]["xT4"]
            qkvT_T = gsets[(i // GT) % NGS]["qkvT4"]
            t0_ = (i % GT) * 128
            xTv = xT_T[:, :, t0_:t0_ + 128]
            HO = HB["out"][i % 2]
            f32t, b16t = HB["f32t"], dict(HB["b16t"])
            b16t.update({k: HO[k] for k in ("wT", "qd", "QKTm", "kd")})
            u_f = HO["u_f"]
            bv, vnew, gsh, o_f, junk, ssq, rstd = (HB[k] for k in
                                                   ("bv", "vnew", "gsh", "o_f", "junk", "ssq", "rstd"))
            beta, be = sm["beta"], sm["be"]
            self._gs_t = [gsh]
            if True:
                qT = qkvT_T[:, h, t0_:t0_ + 128]
                kT = qkvT_T[:, nh + h, t0_:t0_ + 128]
                D, DT, BDT, EgcB = f32t["D"], f32t["DT"], f32t["BDT"], f32t["EgcB"]
                A_row, A_col, tmpq = f32t["A_row"], f32t["A_col"], f32t["tmpq"]
                rb = {}
                for nm, mask, sc in (("rR_h", C_M2, sm["g_hf"]), ("rR_l", C_M2, sm["g_lf"]), ("rC_h", C_M1, sm["g_hf"]),
                                     ("rC_l", C_M1, sm["g_lf"]), ("rL_h", C_ID, sm["l_hf"]), ("rL_l", C_ID, sm["l_lf"])):
                    t = b16t[nm]
                    rb[nm] = t
                    cx.op("dve", lambda e, t=t, mask=mask, sc=sc, h=h: e.scalar_tensor_tensor(
                        out=t[:, :], in0=cst[:, mask:mask + 128], scalar=sc[:, h:h + 1], in1=cst[:, mask:mask + 128],
                        op0=ALU.mult, op1=ALU.mult), [cst, sc], [t])
                pd = HB["pdg"]
                rr = [rb["rR_h"], rb["rR_l"]]
                rc = [rb["rC_h"], rb["rC_l"]]
                rl = [rb["rL_h"], rb["rL_l"]]
                self.mm_acc(pd, pd[:, 0:128], [(m1b[:, :], t[:, :]) for t in rr], [m1b] + rr)
                self.mm_acc(pd, pd[:, 128:256], [(m2b[:, :], t[:, :]) for t in rc], [m2b] + rc)
                self.mm_acc(pd, pd[:, 256:384], [(m2b[:, :], t[:, :]) for t in rc] + [(oneb[:, :], t[:, :]) for t in rl],
                            [m2b, oneb] + rc + rl)
                self.mm_acc(pd, pd[:, 384:512], [(oneb[:, :], t[:, :]) for t in rc], [oneb] + rc)
                for k, dt_ in enumerate((D, DT, BDT, EgcB)):
                    cx.op("act", lambda e, k=k, dt_=dt_: e.activation(out=dt_[:, :], in_=pd[:, k * 128:(k + 1) * 128],
                                                                    func=AF.Exp), [pd], [dt_])
                qd = b16t["qd"]
                cx.op("dve", lambda e, qT=qT: e.scalar_tensor_tensor(out=qd[:, :], in0=qT, scalar=SCALE, in1=EgcB[:, :],
                                                                     op0=ALU.mult, op1=ALU.mult), [qkvT_T, EgcB], [qd])
                pg = HB["pdg"]
                self.mm_acc(pg, pg[:, 0:128], [(kT, kT)], [qkvT_T])
                self.mm_acc(pg, pg[:, 128:256], [(kT, qT)], [qkvT_T])
                cx.op("dve", lambda e, h=h: e.scalar_tensor_tensor(out=A_row[:, :], in0=pg[:, 0:128],
                                                                   scalar=beta[:, h:h + 1], in1=D[:, :], op0=ALU.mult,
                                                                   op1=ALU.mult), [pg, beta, D], [A_row])
                cx.op("dve", lambda e: e.tensor_tensor(out=A_col[:, :], in0=pg[:, 0:128], in1=BDT[:, :], op=ALU.mult),
                      [pg, BDT], [A_col])
                cx.op("dve", lambda e: e.scalar_tensor_tensor(out=tmpq[:, :], in0=pg[:, 128:256], scalar=SCALE,
                                                              in1=DT[:, :], op0=ALU.mult, op1=ALU.mult), [pg, DT], [tmpq])
                QKTm = b16t["QKTm"]
                cx.op("dve", lambda e: e.tensor_tensor(out=QKTm[:, :], in0=tmpq[:, :], in1=M1, op=ALU.mult),
                      [tmpq, cst], [QKTm])
                X, R = b16t["X0"], b16t["R0"]
                Xn, Rn = b16t["X1"], b16t["R1"]
                cx.op("dve", lambda e: e.tensor_tensor(out=tmpq[:, :], in0=A_col[:, :], in1=cst[:, C_NMC:C_NMC + 128],
                                                       op=ALU.mult), [A_col, cst], [tmpq])
                cx.op("dve", lambda e, X=X: e.tensor_tensor(out=X[:, :], in0=tmpq[:, :], in1=ID, op=ALU.add),
                      [tmpq, cst], [X])
                cx.op("dve", lambda e: e.tensor_tensor(out=tmpq[:, :], in0=A_row[:, :], in1=cst[:, C_NMR:C_NMR + 128],
                                                       op=ALU.mult), [A_row, cst], [tmpq])
                cx.op("dve", lambda e, R=R: e.tensor_tensor(out=R[:, :], in0=tmpq[:, :], in1=ID, op=ALU.add),
                      [tmpq, cst], [R])
                W1, W2 = b16t["W1"], b16t["W2"]
                Offa, Ona = HB["Offa"], HB["Ona"]
                cx.op("dve", lambda e: e.tensor_tensor(
                    out=Offa[:, :, :], in0=A_row[:, :].unsqueeze(1).broadcast_to([128, 6, 128]),
                    in1=cst[:, C_NMR + 128:C_NMR + 7 * 128].rearrange("p (l s) -> p l s", l=6), op=ALU.mult),
                    [A_row, cst], [Offa])
                cx.op("dve", lambda e: e.tensor_tensor(
                    out=Ona[:, :, :], in0=A_col[:, :].unsqueeze(1).broadcast_to([128, 6, 128]),
                    in1=cst[:, C_NMC + 128:C_NMC + 7 * 128].rearrange("p (l s) -> p l s", l=6), op=ALU.mult),
                    [A_col, cst], [Ona])
                pw = HB["pw"]
                for l in range(1, 7):
                    lastl = (l == 6)
                    self.mm_acc(pw, pw[:, 0:128], [(Offa[:, l - 1, :], X[:, :])], [Offa, X])
                    cx.op("act", lambda e: e.activation(out=W1[:, :], in_=pw[:, 0:128], func=AF.Copy), [pw], [W1])
                    if not lastl:
                        self.mm_acc(pw, pw[:, 128:256], [(Ona[:, l - 1, :], R[:, :])], [Ona, R])
                        cx.op("act", lambda e: e.activation(out=W2[:, :], in_=pw[:, 128:256], func=AF.Copy), [pw], [W2])
                    self.mm_acc(pw, pw[:, 256:384], [(R[:, :], W1[:, :])], [R, W1])
                    cx.op("dve", lambda e, X=X, Xn=Xn: e.tensor_tensor(out=Xn[:, :], in0=X[:, :], in1=pw[:, 256:384],
                                                                       op=ALU.add), [X, pw], [Xn])
                    if not lastl:
                        self.mm_acc(pw, pw[:, 384:512], [(X[:, :], W2[:, :])], [X, W2])
                        cx.op("dve", lambda e, R=R, Rn=Rn: e.tensor_tensor(out=Rn[:, :], in0=R[:, :], in1=pw[:, 384:512],
                                                                           op=ALU.add), [R, pw], [Rn])
                    X, Xn = Xn, X
                    R, Rn = Rn, R
                kgb, kd, wT = b16t["kgb"], b16t["kd"], b16t["wT"]
                cx.op("act", lambda e, h=h: e.activation(out=bv[:, :], in_=vtok[:, h * 256:(h + 1) * 256], func=AF.Copy,
                                                         scale=beta[:, h:h + 1]), [vtok, beta], [bv])
                cx.op("act", lambda e, h=h: e.activation(out=kgb[:, :], in_=ktok[:, h * 128:(h + 1) * 128], func=AF.Copy,
                                                         scale=be[:, h:h + 1]), [ktok, be], [kgb])
                cx.op("act", lambda e, h=h: e.activation(out=kd[:, :], in_=ktok[:, h * 128:(h + 1) * 128], func=AF.Copy,
                                                         scale=E3[:, nh + h:nh + h + 1]), [ktok, E3], [kd])
                self.mm_acc(pw, pw[:, 0:256], [(X[:, :], bv[:, :])], [X, bv])
                cx.op("act", lambda e: e.activation(out=u_f[:, :], in_=pw[:, 0:256], func=AF.Copy), [pw], [u_f])
                self.mm_acc(pg, pg[:, 256:384], [(kgb[:, :], X[:, :])], [kgb, X])
                cx.op("act", lambda e: e.activation(out=wT[:, :], in_=pg[:, 256:384], func=AF.Copy), [pg], [wT])
                cx.mark()
                self.mm_acc(pB0, pB0[:, 0:256], [(wT[:, :], S_b[h][:, :])], [wT, S_b[h]])
                cx.op("dve", lambda e: e.tensor_tensor(out=vnew[:, :], in0=u_f[:, :], in1=pB0[:, 0:256], op=ALU.subtract),
                      [u_f, pB0], [vnew])
                self.mm_acc(pB0, pB0[:, 256:512], [(qd[:, :], S_b[h][:, :]), (QKTm[:, :], vnew[:, :])],
                            [qd, S_b[h], QKTm, vnew])
                self.mm_acc(pB1, pB1[:, 0:256], [(kd[:, :], vnew[:, :])], [kd, vnew])
                cx.op("dve", lambda e, h=h: e.scalar_tensor_tensor(out=S_f[h][:, :], in0=S_f[h][:, :],
                                                                   scalar=E3[:, 2 * nh + h:2 * nh + h + 1], in1=pB1[:, 0:256],
                                                                   op0=ALU.mult, op1=ALU.add), [S_f[h], E3, pB1], [S_f[h]])
                cx.op("pool", lambda e, h=h: e.tensor_copy(out=S_b[h][:, :], in_=S_f[h][:, :]), [S_f[h]], [S_b[h]])
                self.mm_acc(pB1, pB1[:, 256:512], [(xTv[:, kc, :], win[:, kc, CG + h * 256:CG + (h + 1) * 256])
                                                    for kc in range(8)], [xT_T, win])
                self.silu(pB1, pB1[:, 256:512], gsh, gsh[:, :], HB["sgt"], HB["sgt"][:, :])
                self.rms_gate(pB0, pB0[:, 256:512], gsh[:, :], h, o_n, (o_f, junk, ssq, rstd))

        assert nh <= len(hsets)
        AB = {}

        def rec_heads(i):
            for h in range(nh):
                AB[(i, h)] = cx.split_at_mark(cx.rec(head, i, h))

        def segs(lst):
            out, cur = [], []
            for it in lst:
                if it[0] == "mark":
                    if cur:
                        out.append(cur)
                    cur = []
                else:
                    cur.append(it)
            if cur:
                out.append(cur)
            return out

        gq = []
        ngrp = (nt + GT - 1) // GT
        cx.play([cx.rec(G, 0)])
        if ngrp > 1:
            gq = segs(cx.rec(G, 1))
        nextg = 2

        def front_list(j):
            nonlocal gq, nextg
            out = []
            if j % GT == 0 and j > 0:
                for sg in gq:
                    out += sg
                gq = segs(cx.rec(G, nextg)) if nextg < ngrp else []
                nextg += 1
                out += cx.rec(front, j)
            else:
                out += cx.rec(front, j)
                left = GT - (j % GT)
                n = (len(gq) + left - 1) // max(1, left)
                for sg in gq[:n]:
                    out += sg
                gq = gq[n:]
            return out

        cx.play([front_list(0)])
        rec_heads(0)
        lists = [AB[(0, h)][0] for h in range(nh)]
        if nt > 1:
            lists.append(front_list(1))
        cx.play(lists)
        for i in range(nt):
            lists = [sum((AB[(i, h)][1] for h in range(nh)), [])]
            if i + 1 < nt:
                rec_heads(i + 1)
                lists += [AB[(i + 1, h)][0] for h in range(nh)]
            st_ = cx.rec(self.store_onT, fsets[i % 3]["o_n"], 2 * nh, i)
            if i + 2 < nt:
                lists.append(front_list(i + 2) + st_)
            else:
                lists.append(st_)
            lists.append(self.take_pref(nt - i))
            cx.play(lists)
            for h in range(nh):
                del AB[(i, h)]
        for h in range(nh):
            tks.append(cx.dma(dr[f"sout{li}"][:, h * 256:(h + 1) * 256], S_f[h][:, :], reads=[S_f[h]], key=f"Sst{h}"))
        return tks

    def phase_b(self, li, kind, src, srcn, dst, dstn, last):
        cx, dr, vec = self.cx, self.dr, self.vec
        pb = self.pb
        nt = self.nt
        nkc = 16 if kind == "gdn" else 8
        wout = self.wbuf[f"wout_{kind}"]
        NS = 3
        bsets = [dict(xt=cx.sb(f"xtb{k}", [128, 1024], F32), on=cx.sb(f"onb{k}", [128, 2048], BF16),
                      z=cx.sb(f"z{k}", [128, 1024], F32), st=cx.sb("bnst", [128, 2, 6], F32), mv=cx.sb("mv", [128, 2], F32),
                      rstd=cx.sb("rstdb", [128, 1], F32), p=[pb[2 * (k % NS)], pb[2 * (k % NS) + 1]])
                 for k in range(2 * NS)]
        tks = []
        R = self.ranks
        tpc = self.tpc[li]

        def loads(i):
            B = bsets[i % (2 * NS)]
            xt, on = B["xt"], B["on"]
            self.load_x_tile(src, srcn, i, xt)
            q, t = i // tpc, i % tpc
            if R == 1:
                cx.dma(on[:, 0:nkc * 128], self.drt[f"ont{li}_{q}"].ap()[t * 128:(t + 1) * 128, :],
                       reads=[self.dtile("ont", i)], writes=[on], key=on.name)
            else:
                srcap = self.drt[f"onta{li}_{q}"].ap().rearrange("(r t p) c -> t p r c", r=R, p=128)[t]
                cx.dma(on[:, 0:nkc * 128].rearrange("p (r c) -> p r c", r=R), srcap,
                       reads=[self.dtile("onta", q)], writes=[on], key=on.name)

        def tile(i):
            B = bsets[i % (2 * NS)]
            xt, on, z, st, mv, rstd = (B[k] for k in ("xt", "on", "z", "st", "mv", "rstd"))
            for hb in range(2):
                p = B["p"][hb]
                self.mm_acc(p, p[:, :], [(on[:, kc * 128:(kc + 1) * 128], wout[:, kc, hb * 512:(hb + 1) * 512])
                                         for kc in range(nkc)], [on, wout])
                cx.op("dve", lambda e, p=p, hb=hb: e.scalar_tensor_tensor(
                    out=z[:, hb * 512:(hb + 1) * 512], in0=xt[:, hb * 512:(hb + 1) * 512], scalar=DEEP_ALPHA,
                    in1=p[:, :], op0=ALU.mult, op1=ALU.add), [xt, p], [z])
            for hb in range(2):
                cx.op("dve", lambda e, hb=hb: e.bn_stats(out=st[:, hb, :], in_=z[:, hb * 512:(hb + 1) * 512]),
                      [z], [st])
            cx.op("dve", lambda e: e.bn_aggr(out=mv[:, :], in_=st[:, :, :]), [st], [mv])
            cx.op("act", lambda e: e.activation(out=rstd[:, 0:1], in_=mv[:, 1:2], func=AF.Ln,
                                                bias=self.cst[:, C_EPSL:C_EPSL + 1]), [mv, self.cst], [rstd])
            cx.op("act", lambda e: e.activation(out=rstd[:, 0:1], in_=rstd[:, 0:1], func=AF.Exp, scale=-0.5),
                  [rstd], [rstd])
            cx.op("dve", lambda e: e.tensor_scalar(out=z[:, :], in0=z[:, :], scalar1=mv[:, 0:1], scalar2=rstd[:, 0:1],
                                                   op0=ALU.subtract, op1=ALU.mult), [z, mv, rstd], [z])
            cx.op("pool", lambda e: e.tensor_tensor(out=z[:, :], in0=z[:, :], in1=vec[:, V_LNG:V_LNG + 1024],
                                                    op=ALU.mult), [z, vec], [z])
            cx.op("pool", lambda e: e.tensor_tensor(out=z[:, :], in0=z[:, :], in1=vec[:, V_LNB:V_LNB + 1024],
                                                    op=ALU.add), [z, vec], [z])
            cx.dma(dst[i * 128:(i + 1) * 128, :], z[:, :], reads=[z], writes=[self.dtile(dstn, i)],
                   key=z.name + "st")

        for i in range(0, min(nt, NS)):
            loads(i)
        for i0_ in range(0, nt, NS):
            lists = [cx.rec(tile, i) for i in range(i0_, min(nt, i0_ + NS))]
            nxt = list(range(i0_ + NS, min(nt, i0_ + 2 * NS)))
            if nxt:
                lists.append(cx.rec(lambda: [loads(i) for i in nxt]))
            lists.append(self.take_pref((nt - i0_ + NS - 1) // NS))
            cx.play(lists)
        return tks


def gdn_cols(r, R):
    nh = 8 // R
    h0 = r * nh
    return np.concatenate([np.arange(h0 * 128, (h0 + nh) * 128), 1024 + np.arange(h0 * 128, (h0 + nh) * 128),
                           2048 + np.arange(h0 * 256, (h0 + nh) * 256), 4096 + np.arange(h0 * 256, (h0 + nh) * 256),
                           6144 + np.arange(h0, h0 + nh), 6152 + np.arange(h0, h0 + nh)])


def gla_cols(r, R):
    nh = 4 // R
    h0 = r * nh
    return np.concatenate([np.arange(h0 * 128, (h0 + nh) * 128), 512 + np.arange(h0 * 128, (h0 + nh) * 128),
                           1024 + np.arange(h0 * 256, (h0 + nh) * 256), 2048 + np.arange(h0 * 256, (h0 + nh) * 256),
                           3072 + np.arange(16)])


def pack_vecs(kind, j, i, inp, r=0, R=1):
    v = np.zeros((128, VTOT), np.float32)
    v[:, V_LNG:V_LNG + 1024] = inp["ln_g"][i][None, :]
    v[:, V_LNB:V_LNB + 1024] = inp["ln_b"][i][None, :]
    if kind == "gdn":
        v[:, V_NW:V_NW + 256] = inp["gdn_norm_w"][j][None, :]
        nh = 8 // R
        v[:, V_A:V_A + nh] = inp["gdn_a_log"][j][None, r * nh:(r + 1) * nh]
        v[:, V_DT:V_DT + nh] = inp["gdn_dt_bias"][j][None, r * nh:(r + 1) * nh]
        cw = inp["gdn_conv_w"][j][:, gdn_cols(r, R)[:4 * nh * 128]]
        v[:, V_CW:V_CW + 4 * nh * 4] = cw.reshape(4, 4 * nh, 128).transpose(2, 1, 0).reshape(128, 4 * nh * 4)
    else:
        v[:, V_NW:V_NW + 256] = inp["gla_norm_w"][j][None, :]
        nh = 4 // R
        v[:, V_BGK:V_BGK + nh * 128] = inp["gla_b_gk"][j][None, r * nh * 128:(r + 1) * nh * 128]
    return v


LAYERS = ["gdn", "gla", "gdn", "gla"]


def make_in_map(xb, inp, nt, r=0, R=1):
    m = {"x": np.ascontiguousarray(xb, dtype=np.float32), "consts": make_consts(),
         "xhalo": np.zeros((len(LAYERS), 3, D_MODEL), np.float32)}
    vecs = []
    for i, kind in enumerate(LAYERS):
        j = i // 2
        vecs.append(pack_vecs(kind, j, i, inp, r, R))
        nh = (8 if kind == "gdn" else 4) // R
        cols = gdn_cols(r, R) if kind == "gdn" else gla_cols(r, R)
        m[f"win{i}"] = np.ascontiguousarray(inp[f"{kind}_w_in"][j][:, cols], dtype=np.float32)
        m[f"wout{i}"] = np.ascontiguousarray(inp[f"{kind}_w_out"][j], dtype=np.float32)
        m[f"sin{i}"] = np.zeros((128, nh * 256), np.float32)
        if kind == "gla":
            m[f"wgk{i}"] = np.ascontiguousarray(inp["gla_w_gk_up"][j][:, r * nh * 128:(r + 1) * nh * 128],
                                                dtype=np.float32)
    m["vecs"] = np.stack(vecs, 0)
    return m


_PROG = {}
RANKS = 4


def kernel(**inputs):
    inp = {k: np.asarray(v) for k, v in inputs.items()}
    x = inp["x"]
    B, T, _ = x.shape
    nt = T // 128
    R = RANKS
    if nt not in _PROG:
        _PROG[nt] = Prog(LAYERS, nt, ranks=R)
    P = _PROG[nt]
    in_maps = [make_in_map(x[c // R], inp, nt, c % R, R) for c in range(8)]
    res = run_bass_kernel_spmd(P.nc, in_maps, core_ids=list(range(8)))
    out = np.stack([np.asarray(res.results[b * R]["y"], dtype=np.float32).reshape(T, D_MODEL) for b in range(B)], 0)
    return out
```

```python
import re as _re
import numpy as np
from contextlib import ExitStack
import concourse.bass as bass
import concourse.mybir as mybir
from concourse.bass_utils import run_bass_kernel_spmd

F32 = mybir.dt.float32
BF16 = mybir.dt.bfloat16
AF = mybir.ActivationFunctionType
ALU = mybir.AluOpType

D_MODEL = 1024
SEQ = 8192
BATCH = 2
DEPTH = 4
GDN_IN = 6160
GLA_IN = 3088
DEEP_ALPHA = (2.0 * DEPTH) ** 0.25
LN_EPS = 1e-5
RMS_EPS = 1e-6
L2_EPS = 1e-6
SCALE = 128.0 ** -0.5
GEN = 8000


class Tk:
    __slots__ = ("key", "gen", "val", "sem")

    def __init__(self, key, gen, val, sem):
        self.key, self.gen, self.val, self.sem = key, gen, val, sem


class T:
    __slots__ = ("ap", "w", "r", "name", "excl")

    def __init__(self, ap, name=""):
        self.excl = False
        self.ap = ap
        self.w = None
        self.r = {}
        self.name = name

    def __getitem__(self, idx):
        return self.ap[idx]


class Ctx:
    def __init__(self, nc, stack):
        self.nc = nc
        self.stack = stack
        self.sem_stack = stack
        self.eng = {"pe": nc.tensor, "act": nc.scalar, "dve": nc.vector, "pool": nc.gpsimd, "sp": nc.sync}
        self.seq = {k: 0 for k in self.eng}
        self.sems = {k: [] for k in self.eng}
        self.known = {k: {} for k in self.eng}
        self.dma_sems = {}
        self.dma_cnt = {}
        self.nsem = 0
        self.ninst = 0
        self._rec = None
        self._win = []
        self._wcount = 0
        self._emitting = False

    def new_sem(self, name):
        self.nsem += 1
        return self.sem_stack.enter_context(self.nc.semaphore(f"{name}_{self.nsem}"))

    def sb(self, name, shape, dt):
        self.nsem += 1
        name = f"{name}_u{self.nsem}"
        return T(self.stack.enter_context(self.nc.sbuf_tensor(name, list(shape), dt)), name)

    def ps(self, name, shape, dt):
        t = T(self.stack.enter_context(self.nc.psum_tensor(name, list(shape), dt)), name)
        t.excl = True
        return t

    def _wait(self, e, tk):
        if tk is None:
            return
        if tk.key == e and e == "pe":
            return
        kn = self.known[e].get(tk.key)
        if kn is not None and kn >= (tk.gen, tk.val):
            return
        self.eng[e].wait_ge(tk.sem, tk.val)
        self.known[e][tk.key] = (tk.gen, tk.val)
        self.ninst += 1

    def _deps(self, e, reads, writes, defer=False):
        need = []
        for t in reads:
            if t.w is not None:
                need.append(t.w)
        for t in writes:
            if t.w is not None:
                need.append(t.w)
            need.extend(t.r.values())
        best = {}
        for tk in need:
            if tk.key == e and e == "pe":
                continue
            kn = self.known[e].get(tk.key)
            if kn is not None and kn >= (tk.gen, tk.val):
                continue
            b = best.get(tk.key)
            if b is None or (b.gen, b.val) < (tk.gen, tk.val):
                best[tk.key] = tk
        toks = list(best.values())
        last = None
        if defer and toks and EMBED:
            last = toks.pop()
        for tk in toks:
            self._wait(e, tk)
        return last

    def _mark(self, tk, reads, writes):
        for t in reads:
            t.r[tk.key] = tk
        for t in writes:
            t.w = tk
            t.r = {}

    def mark(self):
        if self._rec is not None:
            self._rec.append(("mark", None))

    @staticmethod
    def split_at_mark(lst):
        for k, it in enumerate(lst):
            if it[0] == "mark":
                return lst[:k], lst[k + 1:]
        return lst, []

    def rec(self, fn, *args):
        assert self._rec is None
        self._rec = []
        fn(*args)
        out, self._rec = self._rec, None
        return out

    def play(self, lists):
        self._win += [l for l in lists if l]
        self._wcount += 1
        if self._wcount >= WIN:
            self.flush()

    def flush(self):
        if self._win:
            w, self._win = self._win, []
            self._emitting = True
            try:
                self.play_sched(w)
            finally:
                self._emitting = False
        self._wcount = 0

    def _direct(self):
        if self._rec is None and not self._emitting and self._win:
            self.flush()

    @staticmethod
    def _dur(kind, e, writes):
        if kind == "dma":
            return 0.06, 2.5
        if kind == "collective":
            return 0.3, 25.0
        if e == "pe":
            return 0.22, 0.35
        n = 128
        for t in writes:
            try:
                sh = t.ap.shape
                m = 1
                for d in sh[1:]:
                    m *= int(d)
                n = max(n, min(m, 512))
            except Exception:
                pass
        d = 0.1 + n / 960.0
        return d, d + 0.25

    def play_sched(self, lists):
        ops = []
        for l in lists:
            ops += [it for it in l if it[0] != "mark"]
        n = len(ops)
        if n == 0:
            return
        lw, rd = {}, {}
        preds = [None] * n
        meta = [None] * n
        for k, (kind, a) in enumerate(ops):
            if kind == "op":
                e, fn, reads, writes = a[0]
            elif kind == "dma":
                e, reads, writes = a[1]["e"], a[1]["reads"], a[1]["writes"]
            else:
                e, reads, writes = "pool", a[1]["reads"], a[1]["writes"]
            ex = [t for t in reads if t.excl]
            if ex:
                reads = [t for t in reads if not t.excl]
                writes = list(writes) + ex
            ps = set()
            for t in reads:
                if id(t) in lw:
                    ps.add(lw[id(t)])
            for t in writes:
                if id(t) in lw:
                    ps.add(lw[id(t)])
                ps.update(rd.get(id(t), ()))
            ps.discard(k)
            preds[k] = ps
            for t in reads:
                rd.setdefault(id(t), []).append(k)
            for t in writes:
                lw[id(t)] = k
                rd[id(t)] = []
            meta[k] = (e,) + self._dur(kind, e, writes)
        succs = [[] for _ in range(n)]
        npred = [len(p) for p in preds]
        for k in range(n):
            for p in preds[k]:
                succs[p].append(k)
        fin = [0.0] * n
        clock = {}
        ready = [k for k in range(n) if npred[k] == 0]
        rt = {k: 0.0 for k in ready}
        order = []
        while ready:
            best, bs = None, None
            for k in ready:
                st = max(rt[k], clock.get(meta[k][0], 0.0))
                key = (st, k)
                if bs is None or key < bs:
                    best, bs = k, key
            ready.remove(best)
            e, busy, lat = meta[best]
            st = bs[0]
            clock[e] = st + busy
            fin[best] = st + lat
            order.append(best)
            for q in succs[best]:
                npred[q] -= 1
                rt[q] = max(rt.get(q, 0.0), fin[best])
                if npred[q] == 0:
                    ready.append(q)
        assert len(order) == n
        for k in order:
            kind, a = ops[k]
            getattr(self, kind)(*a[0], **a[1])

    def play_prop(self, lists):
        lists = [l for l in lists if l]
        pos = [0] * len(lists)
        total = sum(len(l) for l in lists)
        for _ in range(total):
            k = min((i for i in range(len(lists)) if pos[i] < len(lists[i])),
                    key=lambda i: (pos[i] + 0.5) / len(lists[i]))
            kind, a = lists[k][pos[k]]
            pos[k] += 1
            if kind == "mark":
                continue
            getattr(self, kind)(*a[0], **a[1])

    def op(self, e, fn, reads=(), writes=()):
        self._direct()
        if self._rec is not None:
            self._rec.append(("op", ((e, fn, list(reads), list(writes)), {})))
            return None
        ex = [t for t in reads if t.excl]
        if ex:
            reads = [t for t in reads if not t.excl]
            writes = list(writes) + ex
        last = self._deps(e, reads, writes, defer=True)
        ins = fn(self.eng[e])
        if last is not None:
            ins._wait_ge(last.sem, last.val)
            self.known[e][last.key] = (last.gen, last.val)
        n = self.seq[e]
        gen, val = n // GEN, n % GEN + 1
        if gen >= len(self.sems[e]):
            self.sems[e].append(self.new_sem(f"s_{e}"))
        sem = self.sems[e][gen]
        ins.then_inc(sem, 1)
        self.seq[e] = n + 1
        self.ninst += 1
        tk = Tk(e, gen, val, sem)
        self._mark(tk, reads, writes)
        return tk

    def dma(self, out_ap, in_ap, reads=(), writes=(), key=None, e="sp", slow=False):
        self._direct()
        if self._rec is not None:
            self._rec.append(("dma", ((out_ap, in_ap), dict(reads=list(reads), writes=list(writes), key=key, e=e,
                                                              slow=slow))))
            return None
        key = _re.sub(r"_u\d+", "", str(key))
        if key not in self.dma_sems:
            self.dma_sems[key] = self.new_sem("d")
            self.dma_cnt[key] = 0
        self._deps(e, reads, writes)
        sem = self.dma_sems[key]
        if slow:
            ins = self.eng[e].dma_start(out=out_ap, in_=in_ap, allow_slow_non_contiguous=True)
        else:
            ins = self.eng[e].dma_start(out=out_ap, in_=in_ap)
        self.dma_cnt[key] += 16
        ins.then_inc(sem, 16)
        self.ninst += 1
        tk = Tk(("dma", key), 0, self.dma_cnt[key], sem)
        self._mark(tk, reads, writes)
        return tk

    def collective(self, kind, in_ap, out_ap, groups, reads=(), writes=()):
        self._direct()
        if self._rec is not None:
            self._rec.append(("collective", ((kind, in_ap, out_ap, groups), dict(reads=list(reads),
                                                                                 writes=list(writes)))))
            return None
        e = "pool"
        self._deps(e, reads, writes)
        if not hasattr(self, "cc_sem"):
            self.cc_sem = self.new_sem("cc")
            self.cc_cnt = 0
        sem = self.cc_sem
        ins = self.eng[e].collective_compute(kind, ALU.bypass, replica_groups=groups, ins=[in_ap], outs=[out_ap])
        ins.then_inc(sem, 1)
        self.cc_cnt += 1
        self.ninst += 1
        tk = Tk(("cc", 0), 0, self.cc_cnt, sem)
        self._mark(tk, reads, writes)
        return tk

    def barrier(self):
        self.flush()
        toks = []
        for e2, n in self.seq.items():
            if n > 0:
                m = n - 1
                toks.append(Tk(e2, m // GEN, m % GEN + 1, self.sems[e2][m // GEN]))
        for key, cnt in self.dma_cnt.items():
            if cnt > 0:
                toks.append(Tk(("dma", key), 0, cnt, self.dma_sems[key]))
        for e in self.eng:
            for tk in toks:
                if tk.key != e:
                    self._wait(e, tk)

    def finish(self, tks):
        for tk in tks:
            if tk is not None:
                self._wait("sp", tk)
        self.barrier()


C_ID, C_M1, C_M2, C_ONE = 0, 128, 256, 384
C_NMR = 512
C_NMC = 512 + 7 * 128
C_EPSR = 512 + 14 * 128
C_EPSL = C_EPSR + 1
C_TOT = C_EPSR + 8


def make_consts():
    j = np.arange(128)
    c = np.zeros((128, C_TOT), np.float32)
    c[:, C_ID:C_ID + 128] = np.eye(128)
    c[:, C_M1:C_M1 + 128] = (j[:, None] <= j[None, :])
    c[:, C_M2:C_M2 + 128] = (j[:, None] > j[None, :])
    c[:, C_ONE:C_ONE + 128] = 1.0
    c[:, C_EPSR] = RMS_EPS
    c[:, C_EPSL] = LN_EPS
    for l in range(7):
        b = 1 << l
        row = ((j[:, None] // (2 * b)) == (j[None, :] // (2 * b))) & ((j[:, None] // b) > (j[None, :] // b))
        c[:, C_NMR + l * 128:C_NMR + (l + 1) * 128] = -row.astype(np.float32)
        c[:, C_NMC + l * 128:C_NMC + (l + 1) * 128] = -row.T.astype(np.float32)
    return c


V_LNG, V_LNB, V_NW = 0, 1024, 2048
V_A = 2304
V_DT = 2312
V_CW = 2320
V_BGK = 2448
VTOT = 2960


class StopEmit(Exception):
    pass


import os as _os
_KSTOP = float(_os.environ.get("KSTOP", "99"))


STOPPED = [False]
SCHED = int(_os.environ.get("KSCHED", "1"))
EMBED = int(_os.environ.get("KEMBED", "1"))
WIN = int(_os.environ.get("KWIN", "4"))


def ck(k):
    if _KSTOP <= k:
        STOPPED[0] = True
        return True
    return False


class Prog:
    def __init__(self, layers, ntiles, ranks=1):
        self.layers = layers
        self.nt = ntiles
        self.ranks = ranks
        self.build()

    def nh(self, kind):
        return (8 if kind == "gdn" else 4) // self.ranks

    def build(self):
        from contextlib import ExitStack
        nc = bass.Bass("TRN2", target_bir_lowering=False)
        self.nc = nc
        nt = self.nt
        L = len(self.layers)
        ntok = nt * 128
        dr = {}
        dr["x"] = nc.dram_tensor("x", [ntok, D_MODEL], F32, kind="ExternalInput").ap()
        if "gdn" in self.layers:
            dr["xhalo"] = nc.dram_tensor("xhalo", [L, 3, D_MODEL], F32, kind="ExternalInput").ap()
        dr["consts"] = nc.dram_tensor("consts", [128, C_TOT], F32, kind="ExternalInput").ap()
        dr["vecs"] = nc.dram_tensor("vecs", [L, 128, VTOT], F32, kind="ExternalInput").ap()
        for li, kind in enumerate(self.layers):
            nh = self.nh(kind)
            cin = nh * 768 + (2 * nh if kind == "gdn" else 16)
            val = 2048 if kind == "gdn" else 1024
            dr[f"win{li}"] = nc.dram_tensor(f"win{li}", [D_MODEL, cin], F32, kind="ExternalInput").ap()
            dr[f"wout{li}"] = nc.dram_tensor(f"wout{li}", [val, D_MODEL], F32, kind="ExternalInput").ap()
            dr[f"sin{li}"] = nc.dram_tensor(f"sin{li}", [128, nh * 256], F32, kind="ExternalInput").ap()
            dr[f"sout{li}"] = nc.dram_tensor(f"sout{li}", [128, nh * 256], F32, kind="ExternalOutput").ap()
            if kind == "gla":
                dr[f"wgk{li}"] = nc.dram_tensor(f"wgk{li}", [16, nh * 128], F32, kind="ExternalInput").ap()
            self.drt = getattr(self, "drt", {})
            self.tpc = getattr(self, "tpc", {})
            tpc = max(1, (1 << 20) // (128 * nh * 256 * 2))
            self.tpc[li] = tpc
            for q in range((nt + tpc - 1) // tpc):
                tq = min(tpc, nt - q * tpc)
                self.drt[f"ont{li}_{q}"] = nc.dram_tensor(f"ont{li}_{q}", [tq * 128, nh * 256], BF16)
                if self.ranks > 1:
                    self.drt[f"onta{li}_{q}"] = nc.dram_tensor(f"onta{li}_{q}", [self.ranks * tq * 128, nh * 256], BF16)
        dr["y"] = nc.dram_tensor("y", [ntok, D_MODEL], F32, kind="ExternalOutput").ap()
        for i in range(min(2, L - 1)):
            dr[f"act{i}"] = nc.dram_tensor(f"act{i}", [ntok, D_MODEL], F32).ap()
        self.dr = dr
        self.dt = {}
        with ExitStack() as stack:
            cx = Ctx(nc, stack)
            self.cx = cx
            self.emit(stack)
        return nc

    def dtile(self, name, i):
        k = (name, i)
        if k not in self.dt:
            self.dt[k] = T(None, f"{name}{i}")
        return self.dt[k]

    def emit(self, stack):
        from contextlib import ExitStack
        cx, nc, dr = self.cx, self.nc, self.dr
        L = len(self.layers)
        cst = cx.sb("cst", [128, C_TOT], F32)
        cx.dma(cst[:, :], dr["consts"][:, :], writes=[cst], key="cst")
        idb = cx.sb("idb", [128, 128], BF16)
        oneb = cx.sb("oneb", [128, 128], BF16)
        cx.op("dve", lambda e: e.tensor_copy(out=idb[:, :], in_=cst[:, C_ID:C_ID + 128]), [cst], [idb])
        cx.op("dve", lambda e: e.tensor_copy(out=oneb[:, :], in_=cst[:, C_ONE:C_ONE + 128]), [cst], [oneb])
        m1b = cx.sb("m1b", [128, 128], BF16)
        m2b = cx.sb("m2b", [128, 128], BF16)
        cx.op("dve", lambda e: e.tensor_copy(out=m1b[:, :], in_=cst[:, C_M1:C_M1 + 128]), [cst], [m1b])
        cx.op("dve", lambda e: e.tensor_copy(out=m2b[:, :], in_=cst[:, C_M2:C_M2 + 128]), [cst], [m2b])
        self.m1b, self.m2b = m1b, m2b
        self.cst, self.idb, self.oneb = cst, idb, oneb
        self.pb = [cx.ps(f"pb{i}", [128, 512], F32) for i in range(7)]
        pbb0 = cx.ps("pbb0", [128, 1024], BF16)
        self.pbb = [pbb0, pbb0]
        self.wbuf = {}
        self.pref = []
        out_tks = []
        try:
            self.emit_layers(stack, out_tks)
        except StopEmit:
            cx.sem_stack = stack
            cx.stack = stack
        cx.finish(out_tks)

    def emit_layers(self, stack, out_tks):
        cx, nc, dr = self.cx, self.nc, self.dr
        L = len(self.layers)
        for li, kind in enumerate(self.layers):
            src = dr["x"] if li == 0 else dr[f"act{(li - 1) % 2}"]
            srcn = "x" if li == 0 else f"act{(li - 1) % 2}"
            dst = dr["y"] if li == L - 1 else dr[f"act{li % 2}"]
            dstn = "y" if li == L - 1 else f"act{li % 2}"
            with ExitStack() as ls:
                cx.stack = ls
                vec = cx.sb(f"vec{li}", [128, VTOT], F32)
                cx.dma(vec[:, :], dr["vecs"][li], writes=[vec], key="vec")
                self.vec = vec
                with ExitStack() as pa:
                    cx.stack = pa
                    self.alloc_w("win", li)
                    self.weight_ops("win", li)
                    self.pref = []
                    if kind == "gla":
                        tks = self.gla_phase_a(li, src, srcn)
                    else:
                        tks = self.gdn_phase_a(li, src, srcn)
                    out_tks += tks
                    cx.play([self.take_pref(1)])
                    cx.barrier()
                with ExitStack() as pbk:
                    cx.stack = pbk
                    if STOPPED[0]:
                        continue
                    self.alloc_w("wout", li)
                    self.weight_ops("wout", li)
                    self.pref = []
                    tks = self.phase_b(li, kind, src, srcn, dst, dstn, last=(li == L - 1))
                    out_tks += tks
                    cx.play([self.take_pref(1)])
                    cx.barrier()
            cx.stack = stack

    def alloc_w(self, which, li):
        cx = self.cx
        kind = self.layers[li]
        nh = self.nh(kind)
        if which == "win":
            shp = [128, 8, nh * 768 + (2 * nh if kind == "gdn" else 16)]
        else:
            shp = [128, (2048 if kind == "gdn" else 1024) // 128, 1024]
        self.wbuf[f"{which}_{kind}"] = cx.sb(f"{which}_{kind}", shp, BF16)
        self.wstg = [cx.sb(f"wstg{i}", [128, 1024], F32) for i in range(2)]

    def weight_ops(self, which, li):
        kind = self.layers[li]
        nh = self.nh(kind)
        if which == "win":
            rows, cols = 1024, nh * 768 + (2 * nh if kind == "gdn" else 16)
        else:
            rows, cols = (2048 if kind == "gdn" else 1024), 1024
        self.load_weight(self.dr[f"{which}{li}"], self.wbuf[f"{which}_{kind}"], rows, cols, self.wstg, pool_only=False)

    def take_pref(self, nleft):
        n = (len(self.pref) + nleft - 1) // max(1, nleft)
        n += n % 2
        out, self.pref = self.pref[:n], self.pref[n:]
        return out

    def load_weight(self, wdram, wsb, rows, cols, stg, pool_only=False):
        cx = self.cx
        kcn = rows // 128
        cb = 1024
        n = 0
        for kc in range(kcn):
            for c0 in range(0, cols, cb):
                c1 = min(cols, c0 + cb)
                s = stg[n % 2]
                cx.dma(s[:, 0:c1 - c0], wdram[kc * 128:(kc + 1) * 128, c0:c1], writes=[s], key=f"wstg{n % 2}",
                       e="sp")
                eng = "pool" if (n % 2 == 0 or pool_only) else "dve"
                cx.op(eng, lambda e, s=s, kc=kc, c0=c0, c1=c1: e.tensor_copy(out=wsb[:, kc, c0:c1],
                                                                             in_=s[:, 0:c1 - c0]), [s], [wsb])
                n += 1

    def load_x_tile(self, src, srcn, i, xt):
        cx = self.cx
        cx.dma(xt[:, :], src[i * 128:(i + 1) * 128, :], reads=[self.dtile(srcn, i)], writes=[xt],
               key=xt.name)

    def make_xT(self, xt, xb, xT):
        cx = self.cx
        pbb = self.pbb[0]
        cx.op("pool", lambda e: e.tensor_copy(out=xb[:, :], in_=xt[:, :]), [xt], [xb])
        for kc in range(8):
            cx.op("pe", lambda e, kc=kc: e.transpose(out=pbb[:, kc * 128:(kc + 1) * 128],
                                                     in_=xb[:, kc * 128:(kc + 1) * 128], identity=self.idb[:, :]),
                  [xb, self.idb], [pbb])
        cx.op("act", lambda e: e.activation(out=xT[:, :, :], in_=pbb[:, :].rearrange("p (k t) -> p k t", k=8),
                                            func=AF.Copy), [pbb], [xT])

    def split(self, src_t, src_ap, hi, lo, hi_ap=None, lo_ap=None):
        cx = self.cx
        hi_ap = hi[:, :] if hi_ap is None else hi_ap
        lo_ap = lo[:, :] if lo_ap is None else lo_ap
        cx.op("dve", lambda e: e.tensor_copy(out=hi_ap, in_=src_ap), [src_t], [hi])
        cx.op("dve", lambda e: e.tensor_tensor(out=lo_ap, in0=src_ap, in1=hi_ap, op=ALU.subtract), [src_t, hi], [lo])

    def silu(self, src_t, src_ap, dst_t, dst_ap, tmp_t, tmp_ap):
        cx = self.cx
        one = self.cst[:, C_ONE:C_ONE + 1]
        cx.op("act", lambda e: e.activation(out=tmp_ap, in_=src_ap, func=AF.Exp, scale=-1.0), [src_t], [tmp_t])
        cx.op("act", lambda e: e.activation(out=tmp_ap, in_=tmp_ap, func=AF.Ln, bias=one), [tmp_t, self.cst], [tmp_t])
        cx.op("act", lambda e: e.activation(out=tmp_ap, in_=tmp_ap, func=AF.Exp, scale=-1.0), [tmp_t], [tmp_t])
        cx.op("dve", lambda e: e.tensor_tensor(out=dst_ap, in0=src_ap, in1=tmp_ap, op=ALU.mult), [src_t, tmp_t], [dst_t])

    def mm_acc(self, ps_t, ps_ap, pairs, reads):
        cx = self.cx
        n = len(pairs)
        for k, (l, r) in enumerate(pairs):
            cx.op("pe", lambda e, l=l, r=r, k=k: e.matmul(ps_ap, lhsT=l, rhs=r, start=(k == 0), stop=(k == n - 1)),
                  reads, [ps_t])

    def store_onT(self, o_n, nkc, i):
        cx = self.cx
        onT = self.onT
        for g0 in range(0, nkc, 8):
            n = min(8, nkc - g0)
            pbb = self.pbb[1]
            for k in range(n):
                kc = g0 + k
                cx.op("pe", lambda e, kc=kc, k=k: e.transpose(out=pbb[:, k * 128:(k + 1) * 128],
                                                              in_=o_n[:, kc * 128:(kc + 1) * 128],
                                                              identity=self.idb[:, :]), [o_n, self.idb], [pbb])
            cx.op("act", lambda e, g0=g0, n=n: e.activation(out=onT[:, g0 * 128:(g0 + n) * 128],
                                                           in_=pbb[:, 0:n * 128], func=AF.Copy), [pbb], [onT])
        li = self.li
        tpc = self.tpc[li]
        q, t = i // tpc, i % tpc
        tk = cx.dma(self.drt[f"ont{li}_{q}"].ap()[t * 128:(t + 1) * 128, :], onT[:, 0:nkc * 128], reads=[onT],
                    writes=[self.dtile("ont", i)], key="ontst")
        if self.ranks > 1 and (t == tpc - 1 or i == self.nt - 1):
            R = self.ranks
            groups = [list(range(g * R, (g + 1) * R)) for g in range(8 // R)]
            tk = cx.collective("AllGather", self.drt[f"ont{li}_{q}"].ap().opt(), self.drt[f"onta{li}_{q}"].ap().opt(),
                               groups, reads=[self.dtile("ont", k) for k in range(q * tpc, i + 1)],
                               writes=[self.dtile("onta", q)])
        return tk

    def rms_gate(self, o_ps_t, o_ps, gs, h, o_n, tmp):
        cx = self.cx
        vec = self.vec
        o_f, junk, ssq, rstd = tmp
        cx.op("act", lambda e: e.activation(out=o_f[:, :], in_=o_ps, func=AF.Copy), [o_ps_t], [o_f])
        if ck(4.71):
            return
        cx.op("act", lambda e: e.activation(out=junk[:, :], in_=o_f[:, :], func=AF.Square, scale=1.0 / 16.0,
                                            accum_out=ssq[:, 0:1]), [o_f], [junk, ssq])
        if ck(4.72):
            return
        cx.op("act", lambda e: e.activation(out=rstd[:, 0:1], in_=ssq[:, 0:1], func=AF.Ln,
                                            bias=self.cst[:, C_EPSR:C_EPSR + 1]), [ssq, self.cst], [rstd])
        if ck(4.73):
            return
        cx.op("act", lambda e: e.activation(out=rstd[:, 0:1], in_=rstd[:, 0:1], func=AF.Exp, scale=-0.5),
              [rstd], [rstd])
        if ck(4.74):
            return
        cx.op("dve", lambda e: e.scalar_tensor_tensor(out=o_f[:, :], in0=o_f[:, :], scalar=rstd[:, 0:1],
                                                      in1=vec[:, V_NW:V_NW + 256], op0=ALU.mult, op1=ALU.mult),
              [o_f, rstd, vec], [o_f])
        if ck(4.75):
            return
        cx.op("dve", lambda e: e.tensor_tensor(out=o_n[:, h * 256:(h + 1) * 256], in0=o_f[:, :], in1=gs,
                                               op=ALU.mult), [o_f] + self._gs_t, [o_n])

    def gla_phase_a(self, li, src, srcn):
        cx, dr, cst, vec = self.cx, self.dr, self.cst, self.vec
        pb = self.pb
        nt = self.nt
        M1 = cst[:, C_M1:C_M1 + 128]
        M2 = cst[:, C_M2:C_M2 + 128]
        tks = []
        self.li = li
        nh = self.nh("gla")
        NK = nh * 128
        NV = nh * 256
        CIN = nh * 768 + 16
        win = self.wbuf["win_gla"]
        wgk = cx.sb("wgk", [16, NK], F32)
        cx.dma(wgk[:, :], dr[f"wgk{li}"][:, :], writes=[wgk], key="wgk")
        wgk_h = cx.sb("wgk_h", [16, NK], BF16)
        wgk_l = cx.sb("wgk_l", [16, NK], BF16)
        self.split(wgk, wgk[:, :], wgk_h, wgk_l)
        m1b, m2b = self.m1b, self.m2b
        S_f = cx.sb("S_f", [128, nh, 256], F32)
        S_b = [cx.sb(f"S_b{h}", [128, 256], BF16) for h in range(nh)]
        cx.dma(S_f[:, :, :], dr[f"sin{li}"].rearrange("p (h v) -> p h v", h=nh), writes=[S_f], key="S")
        for h in range(nh):
            cx.op("dve", lambda e, h=h: e.tensor_copy(out=S_b[h][:, :], in_=S_f[:, h, :]), [S_f], [S_b[h]])
        xts = [cx.sb(f"xt{i}", [128, 1024], F32) for i in range(2)]
        Eq = cx.sb("Eq", [128, 128], F32)
        Ek = cx.sb("Ek", [128, 128], F32)
        qt = cx.sb("qt", [128, 128], BF16)
        kt = cx.sb("kt", [128, 128], BF16)
        ATm = cx.sb("ATm", [128, 128], BF16)
        o_f = cx.sb("o_f", [128, 256], F32)
        junk = cx.sb("junk", [128, 256], F32)
        ssq = cx.sb("ssq", [128, 1], F32)
        rstd = cx.sb("rstd", [128, 1], F32)
        self.onT = cx.sb("onT", [128, 2048], BF16)
        gtmp = cx.sb("gtmp", [128, NV], F32)
        tks = []

        fsets = []
        for k_ in range(3):
            fsets.append(dict(
                xb=cx.sb("xb", [128, 1024], BF16),
                xT=cx.sb("xT", [128, 8, 128], BF16),
                qT_f=cx.sb("qT_f", [128, nh, 128], F32),
                kT_f=cx.sb("kT_f", [128, nh, 128], F32),
                ktok=cx.sb("ktok", [128, NK], F32),
                v_b=cx.sb("v_b", [128, NV], BF16),
                gs=cx.sb("gs", [128, NV], F32),
                gkl=cx.sb("gkl", [16, 128], F32),
                gkl_h=cx.sb("gkl_h", [16, 128], BF16),
                gkl_l=cx.sb("gkl_l", [16, 128], BF16),
                lg=cx.sb("lg", [128, NK], F32),
                lg_h=cx.sb("lg_h", [128, NK], BF16),
                lg_l=cx.sb("lg_l", [128, NK], BF16),
                khat=cx.sb("khat", [128, NK], BF16),
                Er=cx.sb("Er", [128, NK], F32),
                o_n=cx.sb("o_n", [128, NV], BF16),
            ))
        UNP = ('xb', 'xT', 'qT_f', 'kT_f', 'ktok', 'v_b', 'gs', 'gkl', 'gkl_h', 'gkl_l', 'lg', 'lg_h', 'lg_l', 'khat', 'Er', 'o_n')

        aout = [[dict(qt=cx.sb("qt", [128, 128], BF16), ATm=cx.sb("ATm", [128, 128], BF16),
                      Eq=cx.sb("Eq", [128, 128], F32)) for _ in range(2)] for h in range(nh)]

        GT, NGS = 4, 2
        gsets = [dict(xT4=cx.sb("xT4", [128, 8, GT * 128], BF16), qT4=cx.sb("qT4", [128, nh, GT * 128], F32),
                      kT4=cx.sb("kT4", [128, nh, GT * 128], F32), gkl4=cx.sb("gkl4", [16, GT * 128], F32))
                 for _ in range(NGS)]
        xbs = [cx.sb(f"xbg{t}", [128, 1024], BF16) for t in range(2)]
        idb = self.idb

        def G(gi):
            GS = gsets[gi % NGS]
            xT4, qT4, kT4, gkl4 = GS["xT4"], GS["qT4"], GS["kT4"], GS["gkl4"]
            tiles = [i for i in range(gi * GT, min(nt, (gi + 1) * GT))]
            N = len(tiles) * 128
            for t, i in enumerate(tiles):
                xt, xb = xts[t % 2], xbs[t % 2]
                self.load_x_tile(src, srcn, i, xt)
                cx.op("pool", lambda e, xt=xt, xb=xb: e.tensor_copy(out=xb[:, :], in_=xt[:, :]), [xt], [xb])
                pbb = self.pbb[0]
                for kc in range(8):
                    cx.op("pe", lambda e, kc=kc, xb=xb, pbb=pbb: e.transpose(
                        out=pbb[:, kc * 128:(kc + 1) * 128], in_=xb[:, kc * 128:(kc + 1) * 128],
                        identity=idb[:, :]), [xb, idb], [pbb])
                cx.op("act", lambda e, t=t, pbb=pbb: e.activation(
                    out=xT4[:, :, t * 128:(t + 1) * 128], in_=pbb[:, :].rearrange("p (k t) -> p k t", k=8),
                    func=AF.Copy), [pbb], [xT4])
                cx.mark()
            for blk in range(2 * nh):
                dst = qT4 if blk < nh else kT4
                p = pb[blk % 2]
                self.mm_acc(p, p[:, 0:N], [(win[:, kc, blk * 128:(blk + 1) * 128], xT4[:, kc, 0:N]) for kc in range(8)],
                            [win, xT4])
                cx.op("act", lambda e, p=p, dst=dst, blk=blk: e.activation(out=dst[:, blk % nh, 0:N], in_=p[:, 0:N],
                                                                          func=AF.Copy), [p], [dst])
                cx.mark()
            p = pb[2]
            self.mm_acc(p, p[0:16, 0:N], [(win[:, kc, 2 * NK + 2 * NV:2 * NK + 2 * NV + 16], xT4[:, kc, 0:N])
                                          for kc in range(8)], [win, xT4])
            cx.op("act", lambda e, p=p: e.activation(out=gkl4[:, 0:N], in_=p[0:16, 0:N], func=AF.Copy), [p], [gkl4])
            cx.mark()

        def front(i):
            FS = fsets[i % 3]
            xb, xT, qT_f, kT_f, ktok, v_b, gs, gkl, gkl_h, gkl_l, lg, lg_h, lg_l, khat, Er, o_n = (FS[k] for k in UNP)
            GS = gsets[(i // GT) % NGS]
            xT_T, qT4, kT4, gkl4 = GS["xT4"], GS["qT4"], GS["kT4"], GS["gkl4"]
            t0_ = (i % GT) * 128
            xTv = xT_T[:, :, t0_:t0_ + 128]
            p = pb[0]
            self.mm_acc(p, p[:, 0:NK], [(xTv[:, kc, :], win[:, kc, NK:2 * NK]) for kc in range(8)], [win, xT_T])
            cx.op("act", lambda e, p=p: e.activation(out=ktok[:, :], in_=p[:, 0:NK], func=AF.Copy), [p], [ktok])
            for wi, (dstt, fn, c_base) in enumerate(((v_b, AF.Copy, 2 * NK), (gs, AF.Silu, 2 * NK + NV))):
                for ci, c0 in enumerate(range(0, NV, 512)):
                    w = min(512, NV - c0)
                    p = pb[(wi + ci) % 2]
                    self.mm_acc(p, p[:, 0:w], [(xTv[:, kc, :], win[:, kc, c_base + c0:c_base + c0 + w])
                                               for kc in range(8)], [win, xT_T])
                    if fn == AF.Silu:
                        self.silu(p, p[:, 0:w], dstt, dstt[:, c0:c0 + w], gtmp, gtmp[:, c0:c0 + w])
                    else:
                        cx.op("act", lambda e, p=p, dstt=dstt, fn=fn, c0=c0, w=w: e.activation(
                            out=dstt[:, c0:c0 + w], in_=p[:, 0:w], func=fn), [p], [dstt])
            p = pb[3]
            self.split(gkl4, gkl4[:, t0_:t0_ + 128], gkl_h, gkl_l)
            self.mm_acc(p, p[:, 0:NK], [(gkl_h[:, :], wgk_h[:, :]), (gkl_h[:, :], wgk_l[:, :]),
                                        (gkl_l[:, :], wgk_h[:, :])], [gkl_h, gkl_l, wgk_h, wgk_l])
            cx.op("dve", lambda e, p=p: e.tensor_tensor(out=lg[:, :], in0=p[:, 0:NK], in1=vec[:, V_BGK:V_BGK + NK],
                                                        op=ALU.add), [p, vec], [lg])
            cx.op("act", lambda e: e.activation(out=lg[:, :], in_=lg[:, :], func=AF.Exp, scale=-1.0), [lg], [lg])
            cx.op("act", lambda e: e.activation(out=lg[:, :], in_=lg[:, :], func=AF.Ln, bias=cst[:, C_ONE:C_ONE + 1]),
                  [lg, cst], [lg])
            p = pb[3]
            self.split(lg, lg[:, :], lg_h, lg_l)
            self.mm_acc(p, p[:, 0:NK], [(m2b[:, :], lg_h[:, :]), (m2b[:, :], lg_l[:, :])], [m2b, lg_h, lg_l])
            cx.op("act", lambda e, p=p: e.activation(out=Er[:, :], in_=p[:, 0:NK], func=AF.Exp,
                                                     scale=-1.0 / 16.0), [p], [Er])
            cx.op("dve", lambda e: e.tensor_tensor(out=khat[:, :], in0=ktok[:, :], in1=Er[:, :], op=ALU.mult),
                  [ktok, Er], [khat])

        def heads(i):
            GSh = gsets[(i // GT) % NGS]
            qT4, kT4 = GSh["qT4"], GSh["kT4"]
            t0_ = (i % GT) * 128
            FS = fsets[i % 3]
            xb, xT, qT_f, kT_f, ktok, v_b, gs, gkl, gkl_h, gkl_l, lg, lg_h, lg_l, khat, Er, o_n = (FS[k] for k in UNP)
            self._gs_t = [gs]
            for h in range(nh):
                qt, ATm, Eq = (aout[h][i % 2][k] for k in ("qt", "ATm", "Eq"))
                pq = pb[4]
                self.mm_acc(pq, pq[:, 0:128], [(lg_h[:, h * 128:(h + 1) * 128], m1b[:, :]),
                                               (lg_l[:, h * 128:(h + 1) * 128], m1b[:, :])], [lg_h, lg_l, m1b])
                cx.op("act", lambda e: e.activation(out=Eq[:, :], in_=pq[:, 0:128], func=AF.Exp, scale=-1.0 / 16.0),
                      [pq], [Eq])
                cx.op("act", lambda e: e.activation(out=Ek[:, :], in_=pq[:, 0:128], func=AF.Exp, scale=1.0 / 16.0),
                      [pq], [Ek])
                cx.op("dve", lambda e, h=h: e.scalar_tensor_tensor(out=qt[:, :], in0=qT4[:, h, t0_:t0_ + 128], scalar=SCALE,
                                                                   in1=Eq[:, :], op0=ALU.mult, op1=ALU.mult),
                      [qT4, Eq], [qt])
                cx.op("dve", lambda e, h=h: e.tensor_tensor(out=kt[:, :], in0=kT4[:, h, t0_:t0_ + 128], in1=Ek[:, :],
                                                            op=ALU.mult), [kT4, Ek], [kt])
                self.mm_acc(pq, pq[:, 128:256], [(kt[:, :], qt[:, :])], [kt, qt])
                cx.op("dve", lambda e: e.tensor_tensor(out=ATm[:, :], in0=pq[:, 128:256], in1=M1, op=ALU.mult),
                      [pq, cst], [ATm])
            cx.mark()
            for h in range(nh):
                qt, ATm, Eq = (aout[h][i % 2][k] for k in ("qt", "ATm", "Eq"))
                po = pb[5]
                self.mm_acc(po, po[:, 0:256], [(qt[:, :], S_b[h][:, :]), (ATm[:, :], v_b[:, h * 256:(h + 1) * 256])],
                            [qt, S_b[h], ATm, v_b])
                self.mm_acc(po, po[:, 256:512], [(khat[:, h * 128:(h + 1) * 128], v_b[:, h * 256:(h + 1) * 256])],
                            [khat, v_b])
                cx.op("dve", lambda e, h=h: e.scalar_tensor_tensor(out=S_f[:, h, :], in0=S_f[:, h, :],
                                                                   scalar=Eq[:, 127:128], in1=po[:, 256:512],
                                                                   op0=ALU.mult, op1=ALU.add), [S_f, Eq, po], [S_f])
                cx.op("pool", lambda e, h=h: e.tensor_copy(out=S_b[h][:, :], in_=S_f[:, h, :]), [S_f], [S_b[h]])
                self.rms_gate(po, po[:, 0:256], gs[:, h * 256:(h + 1) * 256], h, o_n, (o_f, junk, ssq, rstd))

        def segs(lst):
            out, cur = [], []
            for it in lst:
                if it[0] == "mark":
                    if cur:
                        out.append(cur)
                    cur = []
                else:
                    cur.append(it)
            if cur:
                out.append(cur)
            return out

        ngrp = (nt + GT - 1) // GT
        cx.play([cx.rec(G, 0)])
        gq = segs(cx.rec(G, 1)) if ngrp > 1 else []
        nextg = 2

        def front_list(j):
            nonlocal gq, nextg
            out = []
            if j % GT == 0 and j > 0:
                for sg in gq:
                    out += sg
                gq = segs(cx.rec(G, nextg)) if nextg < ngrp else []
                nextg += 1
                out += cx.rec(front, j)
            else:
                out += cx.rec(front, j)
                left = GT - (j % GT)
                n = (len(gq) + left - 1) // max(1, left)
                for sg in gq[:n]:
                    out += sg
                gq = gq[n:]
            return out

        AB = {}
        cx.play([front_list(0)])
        AB[0] = cx.split_at_mark(cx.rec(heads, 0))
        lists = [AB[0][0]]
        if nt > 1:
            lists.append(front_list(1))
        cx.play(lists)
        for i in range(nt):
            lists = [AB[i][1]]
            if i + 1 < nt:
                AB[i + 1] = cx.split_at_mark(cx.rec(heads, i + 1))
                lists.append(AB[i + 1][0])
            st_ = cx.rec(self.store_onT, fsets[i % 3]["o_n"], 2 * nh, i)
            if i + 2 < nt:
                lists.append(front_list(i + 2) + st_)
            else:
                lists.append(st_)
            lists.append(self.take_pref(nt - i))
            cx.play(lists)
            del AB[i]
        tks.append(cx.dma(dr[f"sout{li}"].rearrange("p (h v) -> p h v", h=nh), S_f[:, :, :], reads=[S_f],
                          key="Sst"))
        return tks

    def gdn_phase_a(self, li, src, srcn):
        cx, dr, cst, vec = self.cx, self.dr, self.cst, self.vec
        pb = self.pb
        nt = self.nt
        m1b, m2b, oneb, idb = self.m1b, self.m2b, self.oneb, self.idb
        ID = cst[:, C_ID:C_ID + 128]
        M1 = cst[:, C_M1:C_M1 + 128]
        tks = []
        self.li = li
        nh = self.nh("gdn")
        NB = 4 * nh
        CG = NB * 128
        CB = CG + nh * 256
        CIN = CB + 2 * nh
        win = self.wbuf["win_gdn"]
        S_f = [cx.sb(f"S_f{h}", [128, 256], F32) for h in range(nh)]
        S_b = [cx.sb(f"S_b{h}", [128, 256], BF16) for h in range(nh)]
        for h in range(nh):
            cx.dma(S_f[h][:, :], dr[f"sin{li}"][:, h * 256:(h + 1) * 256], writes=[S_f[h]], key=f"S{h}")
            cx.op("dve", lambda e, h=h: e.tensor_copy(out=S_b[h][:, :], in_=S_f[h][:, :]), [S_f[h]], [S_b[h]])
        halo = cx.sb("halo", [128, NB, 3], F32)
        css = [cx.sb(f"cs{i}", [128, 131], F32) for i in range(2)]
        f32t = {n: cx.sb(n, [128, 128], F32) for n in ["acc", "sl", "rs", "zero"]}
        b16t = {n: cx.sb(n, [128, 128], BF16) for n in ["sq"]}
        ff32, fb16 = f32t, b16t
        negA = cx.sb("negA", [128, nh], F32)
        fsets = []
        for k_ in range(3):
            FS = dict(xb=None, xT=None,
                      qkvT=None, ktok=cx.sb("ktok", [128, nh * 128], BF16),
                      vtok=cx.sb("vtok", [128, nh * 256], BF16), ba=cx.sb("ba", [128, 2 * nh], F32),
                      E3=cx.sb("E3", [128, 3 * nh], F32), o_n=cx.sb("o_n", [128, nh * 256], BF16))
            FS["sm"] = {n: cx.sb(n, [128, nh], F32) for n in
                        ["beta", "lnb", "z", "g", "g_hf", "g_lf", "l_hf", "l_lf", "be"]}
            FS["sm"]["negA"] = negA
            FS["smb"] = {n: cx.sb(n, [128, nh], BF16) for n in ["g_hb", "g_lb", "l_hb"]}
            fsets.append(FS)
        hsets = []
        for k_ in range(min(2, nh)):
            HB = dict(f32t={n: cx.sb(n, [128, 128], F32) for n in ["D", "DT", "BDT", "EgcB", "A_row", "A_col", "tmpq"]},
                      b16t={n: cx.sb(n, [128, 128], BF16) for n in
                            ["rR_h", "rR_l", "rC_h", "rC_l", "rL_h", "rL_l", "qd", "QKTm", "X0", "X1", "R0", "R1",
                             "Offn", "On", "W1", "W2", "wT", "kgb", "kd"]},
                      bv=cx.sb("bv", [128, 256], BF16), u_f=cx.sb("u_f", [128, 256], F32),
                      vnew=cx.sb("vnew", [128, 256], BF16), gsh=cx.sb("gsh", [128, 256], F32),
                      o_f=cx.sb("o_f", [128, 256], F32), junk=cx.sb("junk", [128, 256], F32),
                      sgt=cx.sb("sgt", [128, 256], F32), Offa=cx.sb("Offa", [128, 6, 128], BF16),
                      Ona=cx.sb("Ona", [128, 6, 128], BF16),
                      ssq=cx.sb("ssq", [128, 1], F32), rstd=cx.sb("rstd", [128, 1], F32),
                      pdg=pb[2 * k_], pw=pb[2 * k_ + 1])
            HB["out"] = [dict(u_f=cx.sb("u_f", [128, 256], F32), wT=cx.sb("wT", [128, 128], BF16),
                              qd=cx.sb("qd", [128, 128], BF16), QKTm=cx.sb("QKTm", [128, 128], BF16),
                              kd=cx.sb("kd", [128, 128], BF16)) for _ in range(2)]
            hsets.append(HB)
        self.onT = cx.sb("onT", [128, 2048], BF16)
        sm = {"negA": negA}
        zero = f32t["zero"]
        cx.op("dve", lambda e: e.memset(zero[:, :], 0.0), [], [zero])
        cx.op("act", lambda e: e.activation(out=negA[:, :], in_=vec[:, V_A:V_A + nh], func=AF.Exp), [vec],
              [negA])
        cx.op("act", lambda e: e.activation(out=negA[:, :], in_=negA[:, :], func=AF.Copy, scale=-1.0),
              [negA], [negA])
        xh = cx.sb("xh", [3, 1024], F32)
        xhb = cx.sb("xhb", [3, 1024], BF16)
        xhT = cx.sb("xhT", [128, 8, 4], BF16)
        cx.dma(xh[:, :], dr["xhalo"][li], writes=[xh], key="xh")
        cx.op("dve", lambda e: e.tensor_copy(out=xhb[:, :], in_=xh[:, :]), [xh], [xhb])
        pbb = self.pbb[0]
        for kc in range(8):
            cx.op("pe", lambda e, kc=kc: e.transpose(out=pbb[:, kc * 4:kc * 4 + 3], in_=xhb[0:3, kc * 128:(kc + 1) * 128],
                                                     identity=idb[0:3, 0:3]), [xhb, idb], [pbb])
        cx.op("act", lambda e: e.activation(out=xhT[:, :, 0:3],
                                            in_=pbb[:, 0:32].rearrange("p (k t) -> p k t", k=8)[:, :, 0:3],
                                            func=AF.Copy), [pbb], [xhT])
        for blk in range(NB):
            p = pb[blk % 2]
            self.mm_acc(p, p[:, 0:3], [(win[:, kc, blk * 128:(blk + 1) * 128], xhT[:, kc, 0:3]) for kc in range(8)],
                        [win, xhT])
            cx.op("act", lambda e, p=p, blk=blk: e.activation(out=halo[:, blk, :], in_=p[:, 0:3], func=AF.Copy),
                  [p], [halo])
        UNP = ("xb", "xT", "qkvT", "ktok", "vtok", "ba", "E3", "o_n", "sm", "smb")

        pB0, pB1 = pb[4], pb[5]
        pF = pb[6]
        GT = 4
        NGS = 2
        gsets = [dict(xT4=cx.sb("xT4", [128, 8, GT * 128], BF16), qkvT4=cx.sb("qkvT4", [128, NB, GT * 128], BF16))
                 for _ in range(NGS)]
        cs4 = [cx.sb(f"cs4_{k}", [128, 3 + GT * 128], F32) for k in range(2)]
        acc4 = cx.sb("acc4", [128, GT * 128], F32)
        sl4 = cx.sb("sl4", [128, GT * 128], F32)
        rs4 = cx.sb("rs4", [128, GT * 128], F32)
        sq4 = cx.sb("sq4", [128, GT * 128], BF16)
        zero4 = cx.sb("zero4", [128, GT * 128], F32)
        cx.op("pool", lambda e: e.memset(zero4[:, :], 0.0), [], [zero4])
        xt4 = [cx.sb(f"xtg{t}", [128, 1024], F32) for t in range(2)]
        xbs = [cx.sb(f"xbg{t}", [128, 1024], BF16) for t in range(2)]

        def G(gi):
            GS = gsets[gi % NGS]
            xT4, qkvT4 = GS["xT4"], GS["qkvT4"]
            tiles = [i for i in range(gi * GT, min(nt, (gi + 1) * GT))]
            N = len(tiles) * 128
            for t, i in enumerate(tiles):
                xt, xb = xt4[t % 2], xbs[t % 2]
                self.load_x_tile(src, srcn, i, xt)
                cx.op("pool", lambda e, xt=xt, xb=xb: e.tensor_copy(out=xb[:, :], in_=xt[:, :]), [xt], [xb])
                pbb = self.pbb[0]
                for kc in range(8):
                    cx.op("pe", lambda e, kc=kc, xb=xb, pbb=pbb: e.transpose(
                        out=pbb[:, kc * 128:(kc + 1) * 128], in_=xb[:, kc * 128:(kc + 1) * 128],
                        identity=idb[:, :]), [xb, idb], [pbb])
                cx.op("act", lambda e, t=t, pbb=pbb: e.activation(
                    out=xT4[:, :, t * 128:(t + 1) * 128], in_=pbb[:, :].rearrange("p (k t) -> p k t", k=8),
                    func=AF.Copy), [pbb], [xT4])
                cx.mark()
            for blk in range(NB):
                self.mm_acc(pF, pF[:, 0:N], [(win[:, kc, blk * 128:(blk + 1) * 128], xT4[:, kc, 0:N])
                                             for kc in range(8)], [win, xT4])
                cs = cs4[blk % 2]
                cx.op("pool", lambda e, cs=cs, blk=blk: e.tensor_copy(out=cs[:, 0:3], in_=halo[:, blk, :]), [halo], [cs])
                cx.op("act", lambda e, cs=cs: e.activation(out=cs[:, 3:3 + N], in_=pF[:, 0:N], func=AF.Copy), [pF], [cs])
                cx.op("pool", lambda e, cs=cs, blk=blk: e.tensor_copy(out=halo[:, blk, :], in_=cs[:, N:N + 3]), [cs], [halo])
                for j in range(4):
                    cx.op("dve", lambda e, cs=cs, blk=blk, j=j: e.scalar_tensor_tensor(
                        out=acc4[:, 0:N], in0=cs[:, j:j + N], scalar=vec[:, V_CW + blk * 4 + j:V_CW + blk * 4 + j + 1],
                        in1=(zero4[:, 0:N] if j == 0 else acc4[:, 0:N]), op0=ALU.mult, op1=ALU.add),
                        [cs, vec, zero4, acc4], [acc4])
                if blk >= 2 * nh:
                    self.silu(acc4, acc4[:, 0:N], qkvT4, qkvT4[:, blk, 0:N], rs4, rs4[:, 0:N])
                else:
                    self.silu(acc4, acc4[:, 0:N], sl4, sl4[:, 0:N], rs4, rs4[:, 0:N])
                    cx.op("act", lambda e: e.activation(out=sq4[:, 0:N], in_=sl4[:, 0:N], func=AF.Square), [sl4], [sq4])
                    self.mm_acc(pF, pF[:, 0:N], [(oneb[:, :], sq4[:, 0:N])], [oneb, sq4])
                    cx.op("act", lambda e: e.activation(out=rs4[:, 0:N], in_=pF[:, 0:N], func=AF.Ln,
                                                        bias=cst[:, C_EPSR:C_EPSR + 1]), [pF, cst], [rs4])
                    cx.op("act", lambda e: e.activation(out=rs4[:, 0:N], in_=rs4[:, 0:N], func=AF.Exp, scale=-0.5),
                          [rs4], [rs4])
                    cx.op("dve", lambda e, blk=blk: e.tensor_tensor(out=qkvT4[:, blk, 0:N], in0=sl4[:, 0:N],
                                                                    in1=rs4[:, 0:N], op=ALU.mult), [sl4, rs4], [qkvT4])
                cx.mark()

        def front(i):
            FS = fsets[i % 3]
            xb, xT, qkvT, ktok, vtok, ba, E3, o_n, sm, smb = (FS[k] for k in UNP)
            f32t, b16t = ff32, fb16
            GS = gsets[(i // GT) % NGS]
            xT_T, qkvT_T = GS["xT4"], GS["qkvT4"]
            t0_ = (i % GT) * 128
            xTv = xT_T[:, :, t0_:t0_ + 128]
            tb = list(range(nh, NB))
            for g0 in range(0, len(tb), 8):
                grp = tb[g0:g0 + 8]
                pbb = self.pbb[0]
                for k, blk in enumerate(grp):
                    cx.op("pe", lambda e, blk=blk, k=k: e.transpose(out=pbb[:, k * 128:(k + 1) * 128],
                                                                    in_=qkvT_T[:, blk, t0_:t0_ + 128],
                                                                    identity=idb[:, :]),
                          [qkvT_T, idb], [pbb])
                kb = [b for b in grp if b < 2 * nh]
                vb = [b for b in grp if b >= 2 * nh]
                if kb:
                    cx.op("act", lambda e, kb=kb, grp=grp, pbb=pbb: e.activation(
                        out=ktok[:, (kb[0] - nh) * 128:(kb[-1] - nh + 1) * 128],
                        in_=pbb[:, grp.index(kb[0]) * 128:(grp.index(kb[-1]) + 1) * 128], func=AF.Copy), [pbb], [ktok])
                if vb:
                    cx.op("act", lambda e, vb=vb, grp=grp, pbb=pbb: e.activation(
                        out=vtok[:, (vb[0] - 2 * nh) * 128:(vb[-1] - 2 * nh + 1) * 128],
                        in_=pbb[:, grp.index(vb[0]) * 128:(grp.index(vb[-1]) + 1) * 128], func=AF.Copy), [pbb], [vtok])
            p = pF
            self.mm_acc(p, p[:, 0:2 * nh], [(xTv[:, kc, :], win[:, kc, CB:CB + 2 * nh]) for kc in range(8)], [win, xT_T])
            cx.op("act", lambda e, p=p: e.activation(out=ba[:, :], in_=p[:, 0:2 * nh], func=AF.Copy), [p], [ba])
            beta, lnb, z, g = sm["beta"], sm["lnb"], sm["z"], sm["g"]
            cx.op("act", lambda e: e.activation(out=lnb[:, :], in_=ba[:, 0:nh], func=AF.Exp, scale=-1.0), [ba], [lnb])
            cx.op("act", lambda e: e.activation(out=lnb[:, :], in_=lnb[:, :], func=AF.Ln, bias=cst[:, C_ONE:C_ONE + 1]),
                  [lnb, cst], [lnb])
            cx.op("act", lambda e: e.activation(out=beta[:, :], in_=lnb[:, :], func=AF.Exp, scale=-1.0), [lnb], [beta])
            cx.op("act", lambda e: e.activation(out=lnb[:, :], in_=lnb[:, :], func=AF.Copy, scale=-1.0), [lnb], [lnb])
            cx.op("dve", lambda e: e.tensor_tensor(out=z[:, :], in0=ba[:, nh:2 * nh], in1=vec[:, V_DT:V_DT + nh], op=ALU.add),
                  [ba, vec], [z])
            cx.op("act", lambda e: e.activation(out=z[:, :], in_=z[:, :], func=AF.Exp), [z], [z])
            cx.op("act", lambda e: e.activation(out=z[:, :], in_=z[:, :], func=AF.Ln, bias=cst[:, C_ONE:C_ONE + 1]),
                  [z, cst], [z])
            cx.op("dve", lambda e: e.tensor_tensor(out=g[:, :], in0=z[:, :], in1=sm["negA"][:, :], op=ALU.mult),
                  [z, sm["negA"]], [g])
            for srcv, hb, hf, lf in ((g, smb["g_hb"], sm["g_hf"], sm["g_lf"]), (lnb, smb["l_hb"], sm["l_hf"], sm["l_lf"])):
                cx.op("dve", lambda e, srcv=srcv, hb=hb: e.tensor_copy(out=hb[:, :], in_=srcv[:, :]), [srcv], [hb])
                cx.op("dve", lambda e, hb=hb, hf=hf: e.tensor_copy(out=hf[:, :], in_=hb[:, :]), [hb], [hf])
                cx.op("dve", lambda e, srcv=srcv, hf=hf, lf=lf: e.tensor_tensor(out=lf[:, :], in0=srcv[:, :], in1=hf[:, :],
                                                                              op=ALU.subtract), [srcv, hf], [lf])
            cx.op("dve", lambda e: e.tensor_copy(out=smb["g_lb"][:, :], in_=sm["g_lf"][:, :]), [sm["g_lf"]], [smb["g_lb"]])
            p = pF
            self.mm_acc(p, p[:, 0:nh], [(m1b[:, :], smb["g_hb"][:, :]), (m1b[:, :], smb["g_lb"][:, :])],
                        [m1b, smb["g_hb"], smb["g_lb"]])
            self.mm_acc(p, p[:, nh:2 * nh], [(m2b[:, :], smb["g_hb"][:, :]), (m2b[:, :], smb["g_lb"][:, :])],
                        [m2b, smb["g_hb"], smb["g_lb"]])
            self.mm_acc(p, p[:, 2 * nh:3 * nh], [(oneb[:, :], smb["g_hb"][:, :]), (oneb[:, :], smb["g_lb"][:, :])],
                        [oneb, smb["g_hb"], smb["g_lb"]])
            cx.op("act", lambda e, p=p: e.activation(out=E3[:, :], in_=p[:, 0:3 * nh], func=AF.Exp), [p], [E3])
            be = sm["be"]
            cx.op("dve", lambda e: e.tensor_tensor(out=be[:, :], in0=beta[:, :], in1=E3[:, 0:nh], op=ALU.mult),
                  [beta, E3], [be])

        def head(i, h):
            FS = fsets[i % 3]
            xb, xT, qkvT, ktok, vtok, ba, E3, o_n, sm, smb = (FS[k] for k in UNP)
            HB = hsets[h % len(hsets)]
            xT_T = gsets[(i // GT) % NGS]["xT4"]
            qkvT_T = gsets[(i // GT) % NGS]["qkvT4"]
            t0_ = (i % GT) * 128
            xTv = xT_T[:, :, t0_:t0_ + 128]
            HO = HB["out"][i % 2]
            f32t, b16t = HB["f32t"], dict(HB["b16t"])
            b16t.update({k: HO[k] for k in ("wT", "qd", "QKTm", "kd")})
            u_f = HO["u_f"]
            bv, vnew, gsh, o_f, junk, ssq, rstd = (HB[k] for k in
                                                   ("bv", "vnew", "gsh", "o_f", "junk", "ssq", "rstd"))
            beta, be = sm["beta"], sm["be"]
            self._gs_t = [gsh]
            if True:
                qT = qkvT_T[:, h, t0_:t0_ + 128]
                kT = qkvT_T[:, nh + h, t0_:t0_ + 128]
                D, DT, BDT, EgcB = f32t["D"], f32t["DT"], f32t["BDT"], f32t["EgcB"]
                A_row, A_col, tmpq = f32t["A_row"], f32t["A_col"], f32t["tmpq"]
                rb = {}
                for nm, mask, sc in (("rR_h", C_M2, sm["g_hf"]), ("rR_l", C_M2, sm["g_lf"]), ("rC_h", C_M1, sm["g_hf"]),
                                     ("rC_l", C_M1, sm["g_lf"]), ("rL_h", C_ID, sm["l_hf"]), ("rL_l", C_ID, sm["l_lf"])):
                    t = b16t[nm]
                    rb[nm] = t
                    cx.op("dve", lambda e, t=t, mask=mask, sc=sc, h=h: e.scalar_tensor_tensor(
                        out=t[:, :], in0=cst[:, mask:mask + 128], scalar=sc[:, h:h + 1], in1=cst[:, mask:mask + 128],
                        op0=ALU.mult, op1=ALU.mult), [cst, sc], [t])
                pd = HB["pdg"]
                rr = [rb["rR_h"], rb["rR_l"]]
                rc = [rb["rC_h"], rb["rC_l"]]
                rl = [rb["rL_h"], rb["rL_l"]]
                self.mm_acc(pd, pd[:, 0:128], [(m1b[:, :], t[:, :]) for t in rr], [m1b] + rr)
                self.mm_acc(pd, pd[:, 128:256], [(m2b[:, :], t[:, :]) for t in rc], [m2b] + rc)
                self.mm_acc(pd, pd[:, 256:384], [(m2b[:, :], t[:, :]) for t in rc] + [(oneb[:, :], t[:, :]) for t in rl],
                            [m2b, oneb] + rc + rl)
                self.mm_acc(pd, pd[:, 384:512], [(oneb[:, :], t[:, :]) for t in rc], [oneb] + rc)
                for k, dt_ in enumerate((D, DT, BDT, EgcB)):
                    cx.op("act", lambda e, k=k, dt_=dt_: e.activation(out=dt_[:, :], in_=pd[:, k * 128:(k + 1) * 128],
                                                                    func=AF.Exp), [pd], [dt_])
                qd = b16t["qd"]
                cx.op("dve", lambda e, qT=qT: e.scalar_tensor_tensor(out=qd[:, :], in0=qT, scalar=SCALE, in1=EgcB[:, :],
                                                                     op0=ALU.mult, op1=ALU.mult), [qkvT_T, EgcB], [qd])
                pg = HB["pdg"]
                self.mm_acc(pg, pg[:, 0:128], [(kT, kT)], [qkvT_T])
                self.mm_acc(pg, pg[:, 128:256], [(kT, qT)], [qkvT_T])
                cx.op("dve", lambda e, h=h: e.scalar_tensor_tensor(out=A_row[:, :], in0=pg[:, 0:128],
                                                                   scalar=beta[:, h:h + 1], in1=D[:, :], op0=ALU.mult,
                                                                   op1=ALU.mult), [pg, beta, D], [A_row])
                cx.op("dve", lambda e: e.tensor_tensor(out=A_col[:, :], in0=pg[:, 0:128], in1=BDT[:, :], op=ALU.mult),
                      [pg, BDT], [A_col])
                cx.op("dve", lambda e: e.scalar_tensor_tensor(out=tmpq[:, :], in0=pg[:, 128:256], scalar=SCALE,
                                                              in1=DT[:, :], op0=ALU.mult, op1=ALU.mult), [pg, DT], [tmpq])
                QKTm = b16t["QKTm"]
                cx.op("dve", lambda e: e.tensor_tensor(out=QKTm[:, :], in0=tmpq[:, :], in1=M1, op=ALU.mult),
                      [tmpq, cst], [QKTm])
                X, R = b16t["X0"], b16t["R0"]
                Xn, Rn = b16t["X1"], b16t["R1"]
                cx.op("dve", lambda e: e.tensor_tensor(out=tmpq[:, :], in0=A_col[:, :], in1=cst[:, C_NMC:C_NMC + 128],
                                                       op=ALU.mult), [A_col, cst], [tmpq])
                cx.op("dve", lambda e, X=X: e.tensor_tensor(out=X[:, :], in0=tmpq[:, :], in1=ID, op=ALU.add),
                      [tmpq, cst], [X])
                cx.op("dve", lambda e: e.tensor_tensor(out=tmpq[:, :], in0=A_row[:, :], in1=cst[:, C_NMR:C_NMR + 128],
                                                       op=ALU.mult), [A_row, cst], [tmpq])
                cx.op("dve", lambda e, R=R: e.tensor_tensor(out=R[:, :], in0=tmpq[:, :], in1=ID, op=ALU.add),
                      [tmpq, cst], [R])
                W1, W2 = b16t["W1"], b16t["W2"]
                Offa, Ona = HB["Offa"], HB["Ona"]
                cx.op("dve", lambda e: e.tensor_tensor(
                    out=Offa[:, :, :], in0=A_row[:, :].unsqueeze(1).broadcast_to([128, 6, 128]),
                    in1=cst[:, C_NMR + 128:C_NMR + 7 * 128].rearrange("p (l s) -> p l s", l=6), op=ALU.mult),
                    [A_row, cst], [Offa])
                cx.op("dve", lambda e: e.tensor_tensor(
                    out=Ona[:, :, :], in0=A_col[:, :].unsqueeze(1).broadcast_to([128, 6, 128]),
                    in1=cst[:, C_NMC + 128:C_NMC + 7 * 128].rearrange("p (l s) -> p l s", l=6), op=ALU.mult),
                    [A_col, cst], [Ona])
                pw = HB["pw"]
                for l in range(1, 7):
                    lastl = (l == 6)
                    self.mm_acc(pw, pw[:, 0:128], [(Offa[:, l - 1, :], X[:, :])], [Offa, X])
                    cx.op("act", lambda e: e.activation(out=W1[:, :], in_=pw[:, 0:128], func=AF.Copy), [pw], [W1])
                    if not lastl:
                        self.mm_acc(pw, pw[:, 128:256], [(Ona[:, l - 1, :], R[:, :])], [Ona, R])
                        cx.op("act", lambda e: e.activation(out=W2[:, :], in_=pw[:, 128:256], func=AF.Copy), [pw], [W2])
                    self.mm_acc(pw, pw[:, 256:384], [(R[:, :], W1[:, :])], [R, W1])
                    cx.op("dve", lambda e, X=X, Xn=Xn: e.tensor_tensor(out=Xn[:, :], in0=X[:, :], in1=pw[:, 256:384],
                                                                       op=ALU.add), [X, pw], [Xn])
                    if not lastl:
                        self.mm_acc(pw, pw[:, 384:512], [(X[:, :], W2[:, :])], [X, W2])
                        cx.op("dve", lambda e, R=R, Rn=Rn: e.tensor_tensor(out=Rn[:, :], in0=R[:, :], in1=pw[:, 384:512],
                                                                           op=ALU.add), [R, pw], [Rn])
                    X, Xn = Xn, X
                    R, Rn = Rn, R
                kgb, kd, wT = b16t["kgb"], b16t["kd"], b16t["wT"]
                cx.op("act", lambda e, h=h: e.activation(out=bv[:, :], in_=vtok[:, h * 256:(h + 1) * 256], func=AF.Copy,
                                                         scale=beta[:, h:h + 1]), [vtok, beta], [bv])
                cx.op("act", lambda e, h=h: e.activation(out=kgb[:, :], in_=ktok[:, h * 128:(h + 1) * 128], func=AF.Copy,
                                                         scale=be[:, h:h + 1]), [ktok, be], [kgb])
                cx.op("act", lambda e, h=h: e.activation(out=kd[:, :], in_=ktok[:, h * 128:(h + 1) * 128], func=AF.Copy,
                                                         scale=E3[:, nh + h:nh + h + 1]), [ktok, E3], [kd])
                self.mm_acc(pw, pw[:, 0:256], [(X[:, :], bv[:, :])], [X, bv])
                cx.op("act", lambda e: e.activation(out=u_f[:, :], in_=pw[:, 0:256], func=AF.Copy), [pw], [u_f])
                self.mm_acc(pg, pg[:, 256:384], [(kgb[:, :], X[:, :])], [kgb, X])
                cx.op("act", lambda e: e.activation(out=wT[:, :], in_=pg[:, 256:384], func=AF.Copy), [pg], [wT])
                cx.mark()
                self.mm_acc(pB0, pB0[:, 0:256], [(wT[:, :], S_b[h][:, :])], [wT, S_b[h]])
                cx.op("dve", lambda e: e.tensor_tensor(out=vnew[:, :], in0=u_f[:, :], in1=pB0[:, 0:256], op=ALU.subtract),
                      [u_f, pB0], [vnew])
                self.mm_acc(pB0, pB0[:, 256:512], [(qd[:, :], S_b[h][:, :]), (QKTm[:, :], vnew[:, :])],
                            [qd, S_b[h], QKTm, vnew])
                self.mm_acc(pB1, pB1[:, 0:256], [(kd[:, :], vnew[:, :])], [kd, vnew])
                cx.op("dve", lambda e, h=h: e.scalar_tensor_tensor(out=S_f[h][:, :], in0=S_f[h][:, :],
                                                                   scalar=E3[:, 2 * nh + h:2 * nh + h + 1], in1=pB1[:, 0:256],
                                                                   op0=ALU.mult, op1=ALU.add), [S_f[h], E3, pB1], [S_f[h]])
                cx.op("pool", lambda e, h=h: e.tensor_copy(out=S_b[h][:, :], in_=S_f[h][:, :]), [S_f[h]], [S_b[h]])
                self.mm_acc(pB1, pB1[:, 256:512], [(xTv[:, kc, :], win[:, kc, CG + h * 256:CG + (h + 1) * 256])
                                                    for kc in range(8)], [xT_T, win])
                self.silu(pB1, pB1[:, 256:512], gsh, gsh[:, :], HB["sgt"], HB["sgt"][:, :])
                self.rms_gate(pB0, pB0[:, 256:512], gsh[:, :], h, o_n, (o_f, junk, ssq, rstd))

        assert nh <= len(hsets)
        AB = {}

        def rec_heads(i):
            for h in range(nh):
                AB[(i, h)] = cx.split_at_mark(cx.rec(head, i, h))

        def segs(lst):
            out, cur = [], []
            for it in lst:
                if it[0] == "mark":
                    if cur:
                        out.append(cur)
                    cur = []
                else:
                    cur.append(it)
            if cur:
                out.append(cur)
            return out

        gq = []
        ngrp = (nt + GT - 1) // GT
        cx.play([cx.rec(G, 0)])
        if ngrp > 1:
            gq = segs(cx.rec(G, 1))
        nextg = 2

        def front_list(j):
            nonlocal gq, nextg
            out = []
            if j % GT == 0 and j > 0:
                for sg in gq:
                    out += sg
                gq = segs(cx.rec(G, nextg)) if nextg < ngrp else []
                nextg += 1
                out += cx.rec(front, j)
            else:
                out += cx.rec(front, j)
                left = GT - (j % GT)
                n = (len(gq) + left - 1) // max(1, left)
                for sg in gq[:n]:
                    out += sg
                gq = gq[n:]
            return out

        cx.play([front_list(0)])
        rec_heads(0)
        lists = [AB[(0, h)][0] for h in range(nh)]
        if nt > 1:
            lists.append(front_list(1))
        cx.play(lists)
        for i in range(nt):
            lists = [sum((AB[(i, h)][1] for h in range(nh)), [])]
            if i + 1 < nt:
                rec_heads(i + 1)
                lists += [AB[(i + 1, h)][0] for h in range(nh)]
            st_ = cx.rec(self.store_onT, fsets[i % 3]["o_n"], 2 * nh, i)
            if i + 2 < nt:
                lists.append(front_list(i + 2) + st_)
            else:
                lists.append(st_)
            lists.append(self.take_pref(nt - i))
            cx.play(lists)
            for h in range(nh):
                del AB[(i, h)]
        for h in range(nh):
            tks.append(cx.dma(dr[f"sout{li}"][:, h * 256:(h + 1) * 256], S_f[h][:, :], reads=[S_f[h]], key=f"Sst{h}"))
        return tks

    def phase_b(self, li, kind, src, srcn, dst, dstn, last):
        cx, dr, vec = self.cx, self.dr, self.vec
        pb = self.pb
        nt = self.nt
        nkc = 16 if kind == "gdn" else 8
        wout = self.wbuf[f"wout_{kind}"]
        NS = 3
        bsets = [dict(xt=cx.sb(f"xtb{k}", [128, 1024], F32), on=cx.sb(f"onb{k}", [128, 2048], BF16),
                      z=cx.sb(f"z{k}", [128, 1024], F32), st=cx.sb("bnst", [128, 2, 6], F32), mv=cx.sb("mv", [128, 2], F32),
                      rstd=cx.sb("rstdb", [128, 1], F32), p=[pb[2 * (k % NS)], pb[2 * (k % NS) + 1]])
                 for k in range(2 * NS)]
        tks = []
        R = self.ranks
        tpc = self.tpc[li]

        def loads(i):
            B = bsets[i % (2 * NS)]
            xt, on = B["xt"], B["on"]
            self.load_x_tile(src, srcn, i, xt)
            q, t = i // tpc, i % tpc
            if R == 1:
                cx.dma(on[:, 0:nkc * 128], self.drt[f"ont{li}_{q}"].ap()[t * 128:(t + 1) * 128, :],
                       reads=[self.dtile("ont", i)], writes=[on], key=on.name)
            else:
                srcap = self.drt[f"onta{li}_{q}"].ap().rearrange("(r t p) c -> t p r c", r=R, p=128)[t]
                cx.dma(on[:, 0:nkc * 128].rearrange("p (r c) -> p r c", r=R), srcap,
                       reads=[self.dtile("onta", q)], writes=[on], key=on.name)

        def tile(i):
            B = bsets[i % (2 * NS)]
            xt, on, z, st, mv, rstd = (B[k] for k in ("xt", "on", "z", "st", "mv", "rstd"))
            for hb in range(2):
                p = B["p"][hb]
                self.mm_acc(p, p[:, :], [(on[:, kc * 128:(kc + 1) * 128], wout[:, kc, hb * 512:(hb + 1) * 512])
                                         for kc in range(nkc)], [on, wout])
                cx.op("dve", lambda e, p=p, hb=hb: e.scalar_tensor_tensor(
                    out=z[:, hb * 512:(hb + 1) * 512], in0=xt[:, hb * 512:(hb + 1) * 512], scalar=DEEP_ALPHA,
                    in1=p[:, :], op0=ALU.mult, op1=ALU.add), [xt, p], [z])
            for hb in range(2):
                cx.op("dve", lambda e, hb=hb: e.bn_stats(out=st[:, hb, :], in_=z[:, hb * 512:(hb + 1) * 512]),
                      [z], [st])
            cx.op("dve", lambda e: e.bn_aggr(out=mv[:, :], in_=st[:, :, :]), [st], [mv])
            cx.op("act", lambda e: e.activation(out=rstd[:, 0:1], in_=mv[:, 1:2], func=AF.Ln,
                                                bias=self.cst[:, C_EPSL:C_EPSL + 1]), [mv, self.cst], [rstd])
            cx.op("act", lambda e: e.activation(out=rstd[:, 0:1], in_=rstd[:, 0:1], func=AF.Exp, scale=-0.5),
                  [rstd], [rstd])
            cx.op("dve", lambda e: e.tensor_scalar(out=z[:, :], in0=z[:, :], scalar1=mv[:, 0:1], scalar2=rstd[:, 0:1],
                                                   op0=ALU.subtract, op1=ALU.mult), [z, mv, rstd], [z])
            cx.op("pool", lambda e: e.tensor_tensor(out=z[:, :], in0=z[:, :], in1=vec[:, V_LNG:V_LNG + 1024],
                                                    op=ALU.mult), [z, vec], [z])
            cx.op("pool", lambda e: e.tensor_tensor(out=z[:, :], in0=z[:, :], in1=vec[:, V_LNB:V_LNB + 1024],
                                                    op=ALU.add), [z, vec], [z])
            cx.dma(dst[i * 128:(i + 1) * 128, :], z[:, :], reads=[z], writes=[self.dtile(dstn, i)],
                   key=z.name + "st")

        for i in range(0, min(nt, NS)):
            loads(i)
        for i0_ in range(0, nt, NS):
            lists = [cx.rec(tile, i) for i in range(i0_, min(nt, i0_ + NS))]
            nxt = list(range(i0_ + NS, min(nt, i0_ + 2 * NS)))
            if nxt:
                lists.append(cx.rec(lambda: [loads(i) for i in nxt]))
            lists.append(self.take_pref((nt - i0_ + NS - 1) // NS))
            cx.play(lists)
        return tks


def gdn_cols(r, R):
    nh = 8 // R
    h0 = r * nh
    return np.concatenate([np.arange(h0 * 128, (h0 + nh) * 128), 1024 + np.arange(h0 * 128, (h0 + nh) * 128),
                           2048 + np.arange(h0 * 256, (h0 + nh) * 256), 4096 + np.arange(h0 * 256, (h0 + nh) * 256),
                           6144 + np.arange(h0, h0 + nh), 6152 + np.arange(h0, h0 + nh)])


def gla_cols(r, R):
    nh = 4 // R
    h0 = r * nh
    return np.concatenate([np.arange(h0 * 128, (h0 + nh) * 128), 512 + np.arange(h0 * 128, (h0 + nh) * 128),
                           1024 + np.arange(h0 * 256, (h0 + nh) * 256), 2048 + np.arange(h0 * 256, (h0 + nh) * 256),
                           3072 + np.arange(16)])


def pack_vecs(kind, j, i, inp, r=0, R=1):
    v = np.zeros((128, VTOT), np.float32)
    v[:, V_LNG:V_LNG + 1024] = inp["ln_g"][i][None, :]
    v[:, V_LNB:V_LNB + 1024] = inp["ln_b"][i][None, :]
    if kind == "gdn":
        v[:, V_NW:V_NW + 256] = inp["gdn_norm_w"][j][None, :]
        nh = 8 // R
        v[:, V_A:V_A + nh] = inp["gdn_a_log"][j][None, r * nh:(r + 1) * nh]
        v[:, V_DT:V_DT + nh] = inp["gdn_dt_bias"][j][None, r * nh:(r + 1) * nh]
        cw = inp["gdn_conv_w"][j][:, gdn_cols(r, R)[:4 * nh * 128]]
        v[:, V_CW:V_CW + 4 * nh * 4] = cw.reshape(4, 4 * nh, 128).transpose(2, 1, 0).reshape(128, 4 * nh * 4)
    else:
        v[:, V_NW:V_NW + 256] = inp["gla_norm_w"][j][None, :]
        nh = 4 // R
        v[:, V_BGK:V_BGK + nh * 128] = inp["gla_b_gk"][j][None, r * nh * 128:(r + 1) * nh * 128]
    return v


LAYERS = ["gdn", "gla", "gdn", "gla"]


def make_in_map(xb, inp, nt, r=0, R=1):
    m = {"x": np.ascontiguousarray(xb, dtype=np.float32), "consts": make_consts(),
         "xhalo": np.zeros((len(LAYERS), 3, D_MODEL), np.float32)}
    vecs = []
    for i, kind in enumerate(LAYERS):
        j = i // 2
        vecs.append(pack_vecs(kind, j, i, inp, r, R))
        nh = (8 if kind == "gdn" else 4) // R
        cols = gdn_cols(r, R) if kind == "gdn" else gla_cols(r, R)
        m[f"win{i}"] = np.ascontiguousarray(inp[f"{kind}_w_in"][j][:, cols], dtype=np.float32)
        m[f"wout{i}"] = np.ascontiguousarray(inp[f"{kind}_w_out"][j], dtype=np.float32)
        m[f"sin{i}"] = np.zeros((128, nh * 256), np.float32)
        if kind == "gla":
            m[f"wgk{i}"] = np.ascontiguousarray(inp["gla_w_gk_up"][j][:, r * nh * 128:(r + 1) * nh * 128],
                                                dtype=np.float32)
    m["vecs"] = np.stack(vecs, 0)
    return m


_PROG = {}
RANKS = 4


def kernel(**inputs):
    inp = {k: np.asarray(v) for k, v in inputs.items()}
    x = inp["x"]
    B, T, _ = x.shape
    nt = T // 128
    R = RANKS
    if nt not in _PROG:
        _PROG[nt] = Prog(LAYERS, nt, ranks=R)
    P = _PROG[nt]
    in_maps = [make_in_map(x[c // R], inp, nt, c % R, R) for c in range(8)]
    res = run_bass_kernel_spmd(P.nc, in_maps, core_ids=list(range(8)))
    out = np.stack([np.asarray(res.results[b * R]["y"], dtype=np.float32).reshape(T, D_MODEL) for b in range(B)], 0)
    return out
```

```python
import re as _re
import numpy as np
from contextlib import ExitStack
import concourse.bass as bass
import concourse.mybir as mybir
from concourse.bass_utils import run_bass_kernel_spmd

F32 = mybir.dt.float32
BF16 = mybir.dt.bfloat16
AF = mybir.ActivationFunctionType
ALU = mybir.AluOpType

D_MODEL = 1024
SEQ = 8192
BATCH = 2
DEPTH = 4
GDN_IN = 6160
GLA_IN = 3088
DEEP_ALPHA = (2.0 * DEPTH) ** 0.25
LN_EPS = 1e-5
RMS_EPS = 1e-6
L2_EPS = 1e-6
SCALE = 128.0 ** -0.5
GEN = 8000


class Tk:
    __slots__ = ("key", "gen", "val", "sem")

    def __init__(self, key, gen, val, sem):
        self.key, self.gen, self.val, self.sem = key, gen, val, sem


class T:
    __slots__ = ("ap", "w", "r", "name", "excl")

    def __init__(self, ap, name=""):
        self.excl = False
        self.ap = ap
        self.w = None
        self.r = {}
        self.name = name

    def __getitem__(self, idx):
        return self.ap[idx]


class Ctx:
    def __init__(self, nc, stack):
        self.nc = nc
        self.stack = stack
        self.sem_stack = stack
        self.eng = {"pe": nc.tensor, "act": nc.scalar, "dve": nc.vector, "pool": nc.gpsimd, "sp": nc.sync}
        self.seq = {k: 0 for k in self.eng}
        self.sems = {k: [] for k in self.eng}
        self.known = {k: {} for k in self.eng}
        self.dma_sems = {}
        self.dma_cnt = {}
        self.nsem = 0
        self.ninst = 0
        self._rec = None
        self._win = []
        self._wcount = 0
        self._emitting = False

    def new_sem(self, name):
        self.nsem += 1
        return self.sem_stack.enter_context(self.nc.semaphore(f"{name}_{self.nsem}"))

    def sb(self, name, shape, dt):
        self.nsem += 1
        name = f"{name}_u{self.nsem}"
        return T(self.stack.enter_context(self.nc.sbuf_tensor(name, list(shape), dt)), name)

    def ps(self, name, shape, dt):
        t = T(self.stack.enter_context(self.nc.psum_tensor(name, list(shape), dt)), name)
        t.excl = True
        return t

    def _wait(self, e, tk):
        if tk is None:
            return
        if tk.key == e and e == "pe":
            return
        kn = self.known[e].get(tk.key)
        if kn is not None and kn >= (tk.gen, tk.val):
            return
        self.eng[e].wait_ge(tk.sem, tk.val)
        self.known[e][tk.key] = (tk.gen, tk.val)
        self.ninst += 1

    def _deps(self, e, reads, writes, defer=False):
        need = []
        for t in reads:
            if t.w is not None:
                need.append(t.w)
        for t in writes:
            if t.w is not None:
                need.append(t.w)
            need.extend(t.r.values())
        best = {}
        for tk in need:
            if tk.key == e and e == "pe":
                continue
            kn = self.known[e].get(tk.key)
            if kn is not None and kn >= (tk.gen, tk.val):
                continue
            b = best.get(tk.key)
            if b is None or (b.gen, b.val) < (tk.gen, tk.val):
                best[tk.key] = tk
        toks = list(best.values())
        last = None
        if defer and toks and EMBED:
            last = toks.pop()
        for tk in toks:
            self._wait(e, tk)
        return last

    def _mark(self, tk, reads, writes):
        for t in reads:
            t.r[tk.key] = tk
        for t in writes:
            t.w = tk
            t.r = {}

    def mark(self):
        if self._rec is not None:
            self._rec.append(("mark", None))

    @staticmethod
    def split_at_mark(lst):
        for k, it in enumerate(lst):
            if it[0] == "mark":
                return lst[:k], lst[k + 1:]
        return lst, []

    def rec(self, fn, *args):
        assert self._rec is None
        self._rec = []
        fn(*args)
        out, self._rec = self._rec, None
        return out

    def play(self, lists):
        self._win += [l for l in lists if l]
        self._wcount += 1
        if self._wcount >= WIN:
            self.flush()

    def flush(self):
        if self._win:
            w, self._win = self._win, []
            self._emitting = True
            try:
                self.play_sched(w)
            finally:
                self._emitting = False
        self._wcount = 0

    def _direct(self):
        if self._rec is None and not self._emitting and self._win:
            self.flush()

    @staticmethod
    def _dur(kind, e, writes):
        if kind == "dma":
            return 0.06, 2.5
        if kind == "collective":
            return 0.3, 25.0
        if e == "pe":
            return 0.22, 0.35
        n = 128
        for t in writes:
            try:
                sh = t.ap.shape
                m = 1
                for d in sh[1:]:
                    m *= int(d)
                n = max(n, min(m, 512))
            except Exception:
                pass
        d = 0.1 + n / 960.0
        return d, d + 0.25

    def play_sched(self, lists):
        ops = []
        for l in lists:
            ops += [it for it in l if it[0] != "mark"]
        n = len(ops)
        if n == 0:
            return
        lw, rd = {}, {}
        preds = [None] * n
        meta = [None] * n
        for k, (kind, a) in enumerate(ops):
            if kind == "op":
                e, fn, reads, writes = a[0]
            elif kind == "dma":
                e, reads, writes = a[1]["e"], a[1]["reads"], a[1]["writes"]
            else:
                e, reads, writes = "pool", a[1]["reads"], a[1]["writes"]
            ex = [t for t in reads if t.excl]
            if ex:
                reads = [t for t in reads if not t.excl]
                writes = list(writes) + ex
            ps = set()
            for t in reads:
                if id(t) in lw:
                    ps.add(lw[id(t)])
            for t in writes:
                if id(t) in lw:
                    ps.add(lw[id(t)])
                ps.update(rd.get(id(t), ()))
            ps.discard(k)
            preds[k] = ps
            for t in reads:
                rd.setdefault(id(t), []).append(k)
            for t in writes:
                lw[id(t)] = k
                rd[id(t)] = []
            meta[k] = (e,) + self._dur(kind, e, writes)
        succs = [[] for _ in range(n)]
        npred = [len(p) for p in preds]
        for k in range(n):
            for p in preds[k]:
                succs[p].append(k)
        fin = [0.0] * n
        clock = {}
        ready = [k for k in range(n) if npred[k] == 0]
        rt = {k: 0.0 for k in ready}
        order = []
        while ready:
            best, bs = None, None
            for k in ready:
                st = max(rt[k], clock.get(meta[k][0], 0.0))
                key = (st, k)
                if bs is None or key < bs:
                    best, bs = k, key
            ready.remove(best)
            e, busy, lat = meta[best]
            st = bs[0]
            clock[e] = st + busy
            fin[best] = st + lat
            order.append(best)
            for q in succs[best]:
                npred[q] -= 1
                rt[q] = max(rt.get(q, 0.0), fin[best])
                if npred[q] == 0:
                    ready.append(q)
        assert len(order) == n
        for k in order:
            kind, a = ops[k]
            getattr(self, kind)(*a[0], **a[1])

    def play_prop(self, lists):
        lists = [l for l in lists if l]
        pos = [0] * len(lists)
        total = sum(len(l) for l in lists)
        for _ in range(total):
            k = min((i for i in range(len(lists)) if pos[i] < len(lists[i])),
                    key=lambda i: (pos[i] + 0.5) / len(lists[i]))
            kind, a = lists[k][pos[k]]
            pos[k] += 1
            if kind == "mark":
                continue
            getattr(self, kind)(*a[0], **a[1])

    def op(self, e, fn, reads=(), writes=()):
        self._direct()
        if self._rec is not None:
            self._rec.append(("op", ((e, fn, list(reads), list(writes)), {})))
            return None
        ex = [t for t in reads if t.excl]
        if ex:
            reads = [t for t in reads if not t.excl]
            writes = list(writes) + ex
        last = self._deps(e, reads, writes, defer=True)
        ins = fn(self.eng[e])
        if last is not None:
            ins._wait_ge(last.sem, last.val)
            self.known[e][last.key] = (last.gen, last.val)
        n = self.seq[e]
        gen, val = n // GEN, n % GEN + 1
        if gen >= len(self.sems[e]):
            self.sems[e].append(self.new_sem(f"s_{e}"))
        sem = self.sems[e][gen]
        ins.then_inc(sem, 1)
        self.seq[e] = n + 1
        self.ninst += 1
        tk = Tk(e, gen, val, sem)
        self._mark(tk, reads, writes)
        return tk

    def dma(self, out_ap, in_ap, reads=(), writes=(), key=None, e="sp", slow=False):
        self._direct()
        if self._rec is not None:
            self._rec.append(("dma", ((out_ap, in_ap), dict(reads=list(reads), writes=list(writes), key=key, e=e,
                                                              slow=slow))))
            return None
        key = _re.sub(r"_u\d+", "", str(key))
        if key not in self.dma_sems:
            self.dma_sems[key] = self.new_sem("d")
            self.dma_cnt[key] = 0
        self._deps(e, reads, writes)
        sem = self.dma_sems[key]
        if slow:
            ins = self.eng[e].dma_start(out=out_ap, in_=in_ap, allow_slow_non_contiguous=True)
        else:
            ins = self.eng[e].dma_start(out=out_ap, in_=in_ap)
        self.dma_cnt[key] += 16
        ins.then_inc(sem, 16)
        self.ninst += 1
        tk = Tk(("dma", key), 0, self.dma_cnt[key], sem)
        self._mark(tk, reads, writes)
        return tk

    def collective(self, kind, in_ap, out_ap, groups, reads=(), writes=()):
        self._direct()
        if self._rec is not None:
            self._rec.append(("collective", ((kind, in_ap, out_ap, groups), dict(reads=list(reads),
                                                                                 writes=list(writes)))))
            return None
        e = "pool"
        self._deps(e, reads, writes)
        if not hasattr(self, "cc_sem"):
            self.cc_sem = self.new_sem("cc")
            self.cc_cnt = 0
        sem = self.cc_sem
        ins = self.eng[e].collective_compute(kind, ALU.bypass, replica_groups=groups, ins=[in_ap], outs=[out_ap])
        ins.then_inc(sem, 1)
        self.cc_cnt += 1
        self.ninst += 1
        tk = Tk(("cc", 0), 0, self.cc_cnt, sem)
        self._mark(tk, reads, writes)
        return tk

    def barrier(self):
        self.flush()
        toks = []
        for e2, n in self.seq.items():
            if n > 0:
                m = n - 1
                toks.append(Tk(e2, m // GEN, m % GEN + 1, self.sems[e2][m // GEN]))
        for key, cnt in self.dma_cnt.items():
            if cnt > 0:
                toks.append(Tk(("dma", key), 0, cnt, self.dma_sems[key]))
        for e in self.eng:
            for tk in toks:
                if tk.key != e:
                    self._wait(e, tk)

    def finish(self, tks):
        for tk in tks:
            if tk is not None:
                self._wait("sp", tk)
        self.barrier()


C_ID, C_M1, C_M2, C_ONE = 0, 128, 256, 384
C_NMR = 512
C_NMC = 512 + 7 * 128
C_EPSR = 512 + 14 * 128
C_EPSL = C_EPSR + 1
C_TOT = C_EPSR + 8


def make_consts():
    j = np.arange(128)
    c = np.zeros((128, C_TOT), np.float32)
    c[:, C_ID:C_ID + 128] = np.eye(128)
    c[:, C_M1:C_M1 + 128] = (j[:, None] <= j[None, :])
    c[:, C_M2:C_M2 + 128] = (j[:, None] > j[None, :])
    c[:, C_ONE:C_ONE + 128] = 1.0
    c[:, C_EPSR] = RMS_EPS
    c[:, C_EPSL] = LN_EPS
    for l in range(7):
        b = 1 << l
        row = ((j[:, None] // (2 * b)) == (j[None, :] // (2 * b))) & ((j[:, None] // b) > (j[None, :] // b))
        c[:, C_NMR + l * 128:C_NMR + (l + 1) * 128] = -row.astype(np.float32)
        c[:, C_NMC + l * 128:C_NMC + (l + 1) * 128] = -row.T.astype(np.float32)
    return c


V_LNG, V_LNB, V_NW = 0, 1024, 2048
V_A = 2304
V_DT = 2312
V_CW = 2320
V_BGK = 2448
VTOT = 2960


class StopEmit(Exception):
    pass


import os as _os
_KSTOP = float(_os.environ.get("KSTOP", "99"))


STOPPED = [False]
SCHED = int(_os.environ.get("KSCHED", "1"))
EMBED = int(_os.environ.get("KEMBED", "1"))
WIN = int(_os.environ.get("KWIN", "8"))


def ck(k):
    if _KSTOP <= k:
        STOPPED[0] = True
        return True
    return False


class Prog:
    def __init__(self, layers, ntiles, ranks=1):
        self.layers = layers
        self.nt = ntiles
        self.ranks = ranks
        self.build()

    def nh(self, kind):
        return (8 if kind == "gdn" else 4) // self.ranks

    def build(self):
        from contextlib import ExitStack
        nc = bass.Bass("TRN2", target_bir_lowering=False)
        self.nc = nc
        nt = self.nt
        L = len(self.layers)
        ntok = nt * 128
        dr = {}
        dr["x"] = nc.dram_tensor("x", [ntok, D_MODEL], F32, kind="ExternalInput").ap()
        if "gdn" in self.layers:
            dr["xhalo"] = nc.dram_tensor("xhalo", [L, 3, D_MODEL], F32, kind="ExternalInput").ap()
        dr["consts"] = nc.dram_tensor("consts", [128, C_TOT], F32, kind="ExternalInput").ap()
        dr["vecs"] = nc.dram_tensor("vecs", [L, 128, VTOT], F32, kind="ExternalInput").ap()
        for li, kind in enumerate(self.layers):
            nh = self.nh(kind)
            cin = nh * 768 + (2 * nh if kind == "gdn" else 16)
            val = 2048 if kind == "gdn" else 1024
            dr[f"win{li}"] = nc.dram_tensor(f"win{li}", [D_MODEL, cin], F32, kind="ExternalInput").ap()
            dr[f"wout{li}"] = nc.dram_tensor(f"wout{li}", [val, D_MODEL], F32, kind="ExternalInput").ap()
            dr[f"sin{li}"] = nc.dram_tensor(f"sin{li}", [128, nh * 256], F32, kind="ExternalInput").ap()
            dr[f"sout{li}"] = nc.dram_tensor(f"sout{li}", [128, nh * 256], F32, kind="ExternalOutput").ap()
            if kind == "gla":
                dr[f"wgk{li}"] = nc.dram_tensor(f"wgk{li}", [16, nh * 128], F32, kind="ExternalInput").ap()
            self.drt = getattr(self, "drt", {})
            self.tpc = getattr(self, "tpc", {})
            tpc = max(1, (1 << 20) // (128 * nh * 256 * 2))
            self.tpc[li] = tpc
            for q in range((nt + tpc - 1) // tpc):
                tq = min(tpc, nt - q * tpc)
                self.drt[f"ont{li}_{q}"] = nc.dram_tensor(f"ont{li}_{q}", [tq * 128, nh * 256], BF16)
                if self.ranks > 1:
                    self.drt[f"onta{li}_{q}"] = nc.dram_tensor(f"onta{li}_{q}", [self.ranks * tq * 128, nh * 256], BF16)
        dr["y"] = nc.dram_tensor("y", [ntok, D_MODEL], F32, kind="ExternalOutput").ap()
        for i in range(min(2, L - 1)):
            dr[f"act{i}"] = nc.dram_tensor(f"act{i}", [ntok, D_MODEL], F32).ap()
        self.dr = dr
        self.dt = {}
        with ExitStack() as stack:
            cx = Ctx(nc, stack)
            self.cx = cx
            self.emit(stack)
        return nc

    def dtile(self, name, i):
        k = (name, i)
        if k not in self.dt:
            self.dt[k] = T(None, f"{name}{i}")
        return self.dt[k]

    def emit(self, stack):
        from contextlib import ExitStack
        cx, nc, dr = self.cx, self.nc, self.dr
        L = len(self.layers)
        cst = cx.sb("cst", [128, C_TOT], F32)
        cx.dma(cst[:, :], dr["consts"][:, :], writes=[cst], key="cst")
        idb = cx.sb("idb", [128, 128], BF16)
        oneb = cx.sb("oneb", [128, 128], BF16)
        cx.op("dve", lambda e: e.tensor_copy(out=idb[:, :], in_=cst[:, C_ID:C_ID + 128]), [cst], [idb])
        cx.op("dve", lambda e: e.tensor_copy(out=oneb[:, :], in_=cst[:, C_ONE:C_ONE + 128]), [cst], [oneb])
        m1b = cx.sb("m1b", [128, 128], BF16)
        m2b = cx.sb("m2b", [128, 128], BF16)
        cx.op("dve", lambda e: e.tensor_copy(out=m1b[:, :], in_=cst[:, C_M1:C_M1 + 128]), [cst], [m1b])
        cx.op("dve", lambda e: e.tensor_copy(out=m2b[:, :], in_=cst[:, C_M2:C_M2 + 128]), [cst], [m2b])
        self.m1b, self.m2b = m1b, m2b
        self.cst, self.idb, self.oneb = cst, idb, oneb
        self.pb = [cx.ps(f"pb{i}", [128, 512], F32) for i in range(7)]
        pbb0 = cx.ps("pbb0", [128, 1024], BF16)
        self.pbb = [pbb0, pbb0]
        self.wbuf = {}
        self.pref = []
        out_tks = []
        try:
            self.emit_layers(stack, out_tks)
        except StopEmit:
            cx.sem_stack = stack
            cx.stack = stack
        cx.finish(out_tks)

    def emit_layers(self, stack, out_tks):
        cx, nc, dr = self.cx, self.nc, self.dr
        L = len(self.layers)
        for li, kind in enumerate(self.layers):
            src = dr["x"] if li == 0 else dr[f"act{(li - 1) % 2}"]
            srcn = "x" if li == 0 else f"act{(li - 1) % 2}"
            dst = dr["y"] if li == L - 1 else dr[f"act{li % 2}"]
            dstn = "y" if li == L - 1 else f"act{li % 2}"
            with ExitStack() as ls:
                cx.stack = ls
                vec = cx.sb(f"vec{li}", [128, VTOT], F32)
                cx.dma(vec[:, :], dr["vecs"][li], writes=[vec], key="vec")
                self.vec = vec
                with ExitStack() as pa:
                    cx.stack = pa
                    self.alloc_w("win", li)
                    self.weight_ops("win", li)
                    self.pref = []
                    if kind == "gla":
                        tks = self.gla_phase_a(li, src, srcn)
                    else:
                        tks = self.gdn_phase_a(li, src, srcn)
                    out_tks += tks
                    cx.play([self.take_pref(1)])
                    cx.barrier()
                with ExitStack() as pbk:
                    cx.stack = pbk
                    if STOPPED[0]:
                        continue
                    self.alloc_w("wout", li)
                    self.weight_ops("wout", li)
                    self.pref = []
                    tks = self.phase_b(li, kind, src, srcn, dst, dstn, last=(li == L - 1))
                    out_tks += tks
                    cx.play([self.take_pref(1)])
                    cx.barrier()
            cx.stack = stack

    def alloc_w(self, which, li):
        cx = self.cx
        kind = self.layers[li]
        nh = self.nh(kind)
        if which == "win":
            shp = [128, 8, nh * 768 + (2 * nh if kind == "gdn" else 16)]
        else:
            shp = [128, (2048 if kind == "gdn" else 1024) // 128, 1024]
        self.wbuf[f"{which}_{kind}"] = cx.sb(f"{which}_{kind}", shp, BF16)
        self.wstg = [cx.sb(f"wstg{i}", [128, 1024], F32) for i in range(2)]

    def weight_ops(self, which, li):
        kind = self.layers[li]
        nh = self.nh(kind)
        if which == "win":
            rows, cols = 1024, nh * 768 + (2 * nh if kind == "gdn" else 16)
        else:
            rows, cols = (2048 if kind == "gdn" else 1024), 1024
        self.load_weight(self.dr[f"{which}{li}"], self.wbuf[f"{which}_{kind}"], rows, cols, self.wstg, pool_only=False)

    def take_pref(self, nleft):
        n = (len(self.pref) + nleft - 1) // max(1, nleft)
        n += n % 2
        out, self.pref = self.pref[:n], self.pref[n:]
        return out

    def load_weight(self, wdram, wsb, rows, cols, stg, pool_only=False):
        cx = self.cx
        kcn = rows // 128
        cb = 1024
        n = 0
        for kc in range(kcn):
            for c0 in range(0, cols, cb):
                c1 = min(cols, c0 + cb)
                s = stg[n % 2]
                cx.dma(s[:, 0:c1 - c0], wdram[kc * 128:(kc + 1) * 128, c0:c1], writes=[s], key=f"wstg{n % 2}",
                       e="sp")
                eng = "pool" if (n % 2 == 0 or pool_only) else "dve"
                cx.op(eng, lambda e, s=s, kc=kc, c0=c0, c1=c1: e.tensor_copy(out=wsb[:, kc, c0:c1],
                                                                             in_=s[:, 0:c1 - c0]), [s], [wsb])
                n += 1

    def load_x_tile(self, src, srcn, i, xt):
        cx = self.cx
        cx.dma(xt[:, :], src[i * 128:(i + 1) * 128, :], reads=[self.dtile(srcn, i)], writes=[xt],
               key=xt.name)

    def make_xT(self, xt, xb, xT):
        cx = self.cx
        pbb = self.pbb[0]
        cx.op("pool", lambda e: e.tensor_copy(out=xb[:, :], in_=xt[:, :]), [xt], [xb])
        for kc in range(8):
            cx.op("pe", lambda e, kc=kc: e.transpose(out=pbb[:, kc * 128:(kc + 1) * 128],
                                                     in_=xb[:, kc * 128:(kc + 1) * 128], identity=self.idb[:, :]),
                  [xb, self.idb], [pbb])
        cx.op("act", lambda e: e.activation(out=xT[:, :, :], in_=pbb[:, :].rearrange("p (k t) -> p k t", k=8),
                                            func=AF.Copy), [pbb], [xT])

    def split(self, src_t, src_ap, hi, lo, hi_ap=None, lo_ap=None):
        cx = self.cx
        hi_ap = hi[:, :] if hi_ap is None else hi_ap
        lo_ap = lo[:, :] if lo_ap is None else lo_ap
        cx.op("dve", lambda e: e.tensor_copy(out=hi_ap, in_=src_ap), [src_t], [hi])
        cx.op("dve", lambda e: e.tensor_tensor(out=lo_ap, in0=src_ap, in1=hi_ap, op=ALU.subtract), [src_t, hi], [lo])

    def silu(self, src_t, src_ap, dst_t, dst_ap, tmp_t, tmp_ap):
        cx = self.cx
        one = self.cst[:, C_ONE:C_ONE + 1]
        cx.op("act", lambda e: e.activation(out=tmp_ap, in_=src_ap, func=AF.Exp, scale=-1.0), [src_t], [tmp_t])
        cx.op("act", lambda e: e.activation(out=tmp_ap, in_=tmp_ap, func=AF.Ln, bias=one), [tmp_t, self.cst], [tmp_t])
        cx.op("act", lambda e: e.activation(out=tmp_ap, in_=tmp_ap, func=AF.Exp, scale=-1.0), [tmp_t], [tmp_t])
        cx.op("dve", lambda e: e.tensor_tensor(out=dst_ap, in0=src_ap, in1=tmp_ap, op=ALU.mult), [src_t, tmp_t], [dst_t])

    def mm_acc(self, ps_t, ps_ap, pairs, reads):
        cx = self.cx
        n = len(pairs)
        for k, (l, r) in enumerate(pairs):
            cx.op("pe", lambda e, l=l, r=r, k=k: e.matmul(ps_ap, lhsT=l, rhs=r, start=(k == 0), stop=(k == n - 1)),
                  reads, [ps_t])

    def store_onT(self, o_n, nkc, i):
        cx = self.cx
        onT = self.onT
        for g0 in range(0, nkc, 8):
            n = min(8, nkc - g0)
            pbb = self.pbb[1]
            for k in range(n):
                kc = g0 + k
                cx.op("pe", lambda e, kc=kc, k=k: e.transpose(out=pbb[:, k * 128:(k + 1) * 128],
                                                              in_=o_n[:, kc * 128:(kc + 1) * 128],
                                                              identity=self.idb[:, :]), [o_n, self.idb], [pbb])
            cx.op("act", lambda e, g0=g0, n=n: e.activation(out=onT[:, g0 * 128:(g0 + n) * 128],
                                                           in_=pbb[:, 0:n * 128], func=AF.Copy), [pbb], [onT])
        li = self.li
        tpc = self.tpc[li]
        q, t = i // tpc, i % tpc
        tk = cx.dma(self.drt[f"ont{li}_{q}"].ap()[t * 128:(t + 1) * 128, :], onT[:, 0:nkc * 128], reads=[onT],
                    writes=[self.dtile("ont", i)], key="ontst")
        if self.ranks > 1 and (t == tpc - 1 or i == self.nt - 1):
            R = self.ranks
            groups = [list(range(g * R, (g + 1) * R)) for g in range(8 // R)]
            tk = cx.collective("AllGather", self.drt[f"ont{li}_{q}"].ap().opt(), self.drt[f"onta{li}_{q}"].ap().opt(),
                               groups, reads=[self.dtile("ont", k) for k in range(q * tpc, i + 1)],
                               writes=[self.dtile("onta", q)])
        return tk

    def rms_gate(self, o_ps_t, o_ps, gs, h, o_n, tmp):
        cx = self.cx
        vec = self.vec
        o_f, junk, ssq, rstd = tmp
        cx.op("act", lambda e: e.activation(out=o_f[:, :], in_=o_ps, func=AF.Copy), [o_ps_t], [o_f])
        if ck(4.71):
            return
        cx.op("act", lambda e: e.activation(out=junk[:, :], in_=o_f[:, :], func=AF.Square, scale=1.0 / 16.0,
                                            accum_out=ssq[:, 0:1]), [o_f], [junk, ssq])
        if ck(4.72):
            return
        cx.op("act", lambda e: e.activation(out=rstd[:, 0:1], in_=ssq[:, 0:1], func=AF.Ln,
                                            bias=self.cst[:, C_EPSR:C_EPSR + 1]), [ssq, self.cst], [rstd])
        if ck(4.73):
            return
        cx.op("act", lambda e: e.activation(out=rstd[:, 0:1], in_=rstd[:, 0:1], func=AF.Exp, scale=-0.5),
              [rstd], [rstd])
        if ck(4.74):
            return
        cx.op("dve", lambda e: e.scalar_tensor_tensor(out=o_f[:, :], in0=o_f[:, :], scalar=rstd[:, 0:1],
                                                      in1=vec[:, V_NW:V_NW + 256], op0=ALU.mult, op1=ALU.mult),
              [o_f, rstd, vec], [o_f])
        if ck(4.75):
            return
        cx.op("dve", lambda e: e.tensor_tensor(out=o_n[:, h * 256:(h + 1) * 256], in0=o_f[:, :], in1=gs,
                                               op=ALU.mult), [o_f] + self._gs_t, [o_n])

    def gla_phase_a(self, li, src, srcn):
        cx, dr, cst, vec = self.cx, self.dr, self.cst, self.vec
        pb = self.pb
        nt = self.nt
        M1 = cst[:, C_M1:C_M1 + 128]
        M2 = cst[:, C_M2:C_M2 + 128]
        tks = []
        self.li = li
        nh = self.nh("gla")
        NK = nh * 128
        NV = nh * 256
        CIN = nh * 768 + 16
        win = self.wbuf["win_gla"]
        wgk = cx.sb("wgk", [16, NK], F32)
        cx.dma(wgk[:, :], dr[f"wgk{li}"][:, :], writes=[wgk], key="wgk")
        wgk_h = cx.sb("wgk_h", [16, NK], BF16)
        wgk_l = cx.sb("wgk_l", [16, NK], BF16)
        self.split(wgk, wgk[:, :], wgk_h, wgk_l)
        m1b, m2b = self.m1b, self.m2b
        S_f = cx.sb("S_f", [128, nh, 256], F32)
        S_b = [cx.sb(f"S_b{h}", [128, 256], BF16) for h in range(nh)]
        cx.dma(S_f[:, :, :], dr[f"sin{li}"].rearrange("p (h v) -> p h v", h=nh), writes=[S_f], key="S")
        for h in range(nh):
            cx.op("dve", lambda e, h=h: e.tensor_copy(out=S_b[h][:, :], in_=S_f[:, h, :]), [S_f], [S_b[h]])
        xts = [cx.sb(f"xt{i}", [128, 1024], F32) for i in range(2)]
        Eq = cx.sb("Eq", [128, 128], F32)
        Ek = cx.sb("Ek", [128, 128], F32)
        qt = cx.sb("qt", [128, 128], BF16)
        kt = cx.sb("kt", [128, 128], BF16)
        ATm = cx.sb("ATm", [128, 128], BF16)
        o_f = cx.sb("o_f", [128, 256], F32)
        junk = cx.sb("junk", [128, 256], F32)
        ssq = cx.sb("ssq", [128, 1], F32)
        rstd = cx.sb("rstd", [128, 1], F32)
        self.onT = cx.sb("onT", [128, 2048], BF16)
        gtmp = cx.sb("gtmp", [128, NV], F32)
        tks = []

        fsets = []
        for k_ in range(3):
            fsets.append(dict(
                xb=cx.sb("xb", [128, 1024], BF16),
                xT=cx.sb("xT", [128, 8, 128], BF16),
                qT_f=cx.sb("qT_f", [128, nh, 128], F32),
                kT_f=cx.sb("kT_f", [128, nh, 128], F32),
                ktok=cx.sb("ktok", [128, NK], F32),
                v_b=cx.sb("v_b", [128, NV], BF16),
                gs=cx.sb("gs", [128, NV], F32),
                gkl=cx.sb("gkl", [16, 128], F32),
                gkl_h=cx.sb("gkl_h", [16, 128], BF16),
                gkl_l=cx.sb("gkl_l", [16, 128], BF16),
                lg=cx.sb("lg", [128, NK], F32),
                lg_h=cx.sb("lg_h", [128, NK], BF16),
                lg_l=cx.sb("lg_l", [128, NK], BF16),
                khat=cx.sb("khat", [128, NK], BF16),
                Er=cx.sb("Er", [128, NK], F32),
                o_n=cx.sb("o_n", [128, NV], BF16),
            ))
        UNP = ('xb', 'xT', 'qT_f', 'kT_f', 'ktok', 'v_b', 'gs', 'gkl', 'gkl_h', 'gkl_l', 'lg', 'lg_h', 'lg_l', 'khat', 'Er', 'o_n')

        aout = [[dict(qt=cx.sb("qt", [128, 128], BF16), ATm=cx.sb("ATm", [128, 128], BF16),
                      Eq=cx.sb("Eq", [128, 128], F32)) for _ in range(2)] for h in range(nh)]

        GT, NGS = 4, 2
        gsets = [dict(xT4=cx.sb("xT4", [128, 8, GT * 128], BF16), qT4=cx.sb("qT4", [128, nh, GT * 128], F32),
                      kT4=cx.sb("kT4", [128, nh, GT * 128], F32), gkl4=cx.sb("gkl4", [16, GT * 128], F32))
                 for _ in range(NGS)]
        xbs = [cx.sb(f"xbg{t}", [128, 1024], BF16) for t in range(2)]
        idb = self.idb

        def G(gi):
            GS = gsets[gi % NGS]
            xT4, qT4, kT4, gkl4 = GS["xT4"], GS["qT4"], GS["kT4"], GS["gkl4"]
            tiles = [i for i in range(gi * GT, min(nt, (gi + 1) * GT))]
            N = len(tiles) * 128
            for t, i in enumerate(tiles):
                xt, xb = xts[t % 2], xbs[t % 2]
                self.load_x_tile(src, srcn, i, xt)
                cx.op("pool", lambda e, xt=xt, xb=xb: e.tensor_copy(out=xb[:, :], in_=xt[:, :]), [xt], [xb])
                pbb = self.pbb[0]
                for kc in range(8):
                    cx.op("pe", lambda e, kc=kc, xb=xb, pbb=pbb: e.transpose(
                        out=pbb[:, kc * 128:(kc + 1) * 128], in_=xb[:, kc * 128:(kc + 1) * 128],
                        identity=idb[:, :]), [xb, idb], [pbb])
                cx.op("act", lambda e, t=t, pbb=pbb: e.activation(
                    out=xT4[:, :, t * 128:(t + 1) * 128], in_=pbb[:, :].rearrange("p (k t) -> p k t", k=8),
                    func=AF.Copy), [pbb], [xT4])
                cx.mark()
            for blk in range(2 * nh):
                dst = qT4 if blk < nh else kT4
                p = pb[blk % 2]
                self.mm_acc(p, p[:, 0:N], [(win[:, kc, blk * 128:(blk + 1) * 128], xT4[:, kc, 0:N]) for kc in range(8)],
                            [win, xT4])
                cx.op("act", lambda e, p=p, dst=dst, blk=blk: e.activation(out=dst[:, blk % nh, 0:N], in_=p[:, 0:N],
                                                                          func=AF.Copy), [p], [dst])
                cx.mark()
            p = pb[2]
            self.mm_acc(p, p[0:16, 0:N], [(win[:, kc, 2 * NK + 2 * NV:2 * NK + 2 * NV + 16], xT4[:, kc, 0:N])
                                          for kc in range(8)], [win, xT4])
            cx.op("act", lambda e, p=p: e.activation(out=gkl4[:, 0:N], in_=p[0:16, 0:N], func=AF.Copy), [p], [gkl4])
            cx.mark()

        def front(i):
            FS = fsets[i % 3]
            xb, xT, qT_f, kT_f, ktok, v_b, gs, gkl, gkl_h, gkl_l, lg, lg_h, lg_l, khat, Er, o_n = (FS[k] for k in UNP)
            GS = gsets[(i // GT) % NGS]
            xT_T, qT4, kT4, gkl4 = GS["xT4"], GS["qT4"], GS["kT4"], GS["gkl4"]
            t0_ = (i % GT) * 128
            xTv = xT_T[:, :, t0_:t0_ + 128]
            p = pb[0]
            self.mm_acc(p, p[:, 0:NK], [(xTv[:, kc, :], win[:, kc, NK:2 * NK]) for kc in range(8)], [win, xT_T])
            cx.op("act", lambda e, p=p: e.activation(out=ktok[:, :], in_=p[:, 0:NK], func=AF.Copy), [p], [ktok])
            for wi, (dstt, fn, c_base) in enumerate(((v_b, AF.Copy, 2 * NK), (gs, AF.Silu, 2 * NK + NV))):
                for ci, c0 in enumerate(range(0, NV, 512)):
                    w = min(512, NV - c0)
                    p = pb[(wi + ci) % 2]
                    self.mm_acc(p, p[:, 0:w], [(xTv[:, kc, :], win[:, kc, c_base + c0:c_base + c0 + w])
                                               for kc in range(8)], [win, xT_T])
                    if fn == AF.Silu:
                        self.silu(p, p[:, 0:w], dstt, dstt[:, c0:c0 + w], gtmp, gtmp[:, c0:c0 + w])
                    else:
                        cx.op("act", lambda e, p=p, dstt=dstt, fn=fn, c0=c0, w=w: e.activation(
                            out=dstt[:, c0:c0 + w], in_=p[:, 0:w], func=fn), [p], [dstt])
            p = pb[3]
            self.split(gkl4, gkl4[:, t0_:t0_ + 128], gkl_h, gkl_l)
            self.mm_acc(p, p[:, 0:NK], [(gkl_h[:, :], wgk_h[:, :]), (gkl_h[:, :], wgk_l[:, :]),
                                        (gkl_l[:, :], wgk_h[:, :])], [gkl_h, gkl_l, wgk_h, wgk_l])
            cx.op("dve", lambda e, p=p: e.tensor_tensor(out=lg[:, :], in0=p[:, 0:NK], in1=vec[:, V_BGK:V_BGK + NK],
                                                        op=ALU.add), [p, vec], [lg])
            cx.op("act", lambda e: e.activation(out=lg[:, :], in_=lg[:, :], func=AF.Exp, scale=-1.0), [lg], [lg])
            cx.op("act", lambda e: e.activation(out=lg[:, :], in_=lg[:, :], func=AF.Ln, bias=cst[:, C_ONE:C_ONE + 1]),
                  [lg, cst], [lg])
            p = pb[3]
            self.split(lg, lg[:, :], lg_h, lg_l)
            self.mm_acc(p, p[:, 0:NK], [(m2b[:, :], lg_h[:, :]), (m2b[:, :], lg_l[:, :])], [m2b, lg_h, lg_l])
            cx.op("act", lambda e, p=p: e.activation(out=Er[:, :], in_=p[:, 0:NK], func=AF.Exp,
                                                     scale=-1.0 / 16.0), [p], [Er])
            cx.op("dve", lambda e: e.tensor_tensor(out=khat[:, :], in0=ktok[:, :], in1=Er[:, :], op=ALU.mult),
                  [ktok, Er], [khat])

        def heads(i):
            GSh = gsets[(i // GT) % NGS]
            qT4, kT4 = GSh["qT4"], GSh["kT4"]
            t0_ = (i % GT) * 128
            FS = fsets[i % 3]
            xb, xT, qT_f, kT_f, ktok, v_b, gs, gkl, gkl_h, gkl_l, lg, lg_h, lg_l, khat, Er, o_n = (FS[k] for k in UNP)
            self._gs_t = [gs]
            for h in range(nh):
                qt, ATm, Eq = (aout[h][i % 2][k] for k in ("qt", "ATm", "Eq"))
                pq = pb[4]
                self.mm_acc(pq, pq[:, 0:128], [(lg_h[:, h * 128:(h + 1) * 128], m1b[:, :]),
                                               (lg_l[:, h * 128:(h + 1) * 128], m1b[:, :])], [lg_h, lg_l, m1b])
                cx.op("act", lambda e: e.activation(out=Eq[:, :], in_=pq[:, 0:128], func=AF.Exp, scale=-1.0 / 16.0),
                      [pq], [Eq])
                cx.op("act", lambda e: e.activation(out=Ek[:, :], in_=pq[:, 0:128], func=AF.Exp, scale=1.0 / 16.0),
                      [pq], [Ek])
                cx.op("dve", lambda e, h=h: e.scalar_tensor_tensor(out=qt[:, :], in0=qT4[:, h, t0_:t0_ + 128], scalar=SCALE,
                                                                   in1=Eq[:, :], op0=ALU.mult, op1=ALU.mult),
                      [qT4, Eq], [qt])
                cx.op("dve", lambda e, h=h: e.tensor_tensor(out=kt[:, :], in0=kT4[:, h, t0_:t0_ + 128], in1=Ek[:, :],
                                                            op=ALU.mult), [kT4, Ek], [kt])
                self.mm_acc(pq, pq[:, 128:256], [(kt[:, :], qt[:, :])], [kt, qt])
                cx.op("dve", lambda e: e.tensor_tensor(out=ATm[:, :], in0=pq[:, 128:256], in1=M1, op=ALU.mult),
                      [pq, cst], [ATm])
            cx.mark()
            for h in range(nh):
                qt, ATm, Eq = (aout[h][i % 2][k] for k in ("qt", "ATm", "Eq"))
                po = pb[5]
                self.mm_acc(po, po[:, 0:256], [(qt[:, :], S_b[h][:, :]), (ATm[:, :], v_b[:, h * 256:(h + 1) * 256])],
                            [qt, S_b[h], ATm, v_b])
                self.mm_acc(po, po[:, 256:512], [(khat[:, h * 128:(h + 1) * 128], v_b[:, h * 256:(h + 1) * 256])],
                            [khat, v_b])
                cx.op("dve", lambda e, h=h: e.scalar_tensor_tensor(out=S_f[:, h, :], in0=S_f[:, h, :],
                                                                   scalar=Eq[:, 127:128], in1=po[:, 256:512],
                                                                   op0=ALU.mult, op1=ALU.add), [S_f, Eq, po], [S_f])
                cx.op("pool", lambda e, h=h: e.tensor_copy(out=S_b[h][:, :], in_=S_f[:, h, :]), [S_f], [S_b[h]])
                self.rms_gate(po, po[:, 0:256], gs[:, h * 256:(h + 1) * 256], h, o_n, (o_f, junk, ssq, rstd))

        def segs(lst):
            out, cur = [], []
            for it in lst:
                if it[0] == "mark":
                    if cur:
                        out.append(cur)
                    cur = []
                else:
                    cur.append(it)
            if cur:
                out.append(cur)
            return out

        ngrp = (nt + GT - 1) // GT
        cx.play([cx.rec(G, 0)])
        gq = segs(cx.rec(G, 1)) if ngrp > 1 else []
        nextg = 2

        def front_list(j):
            nonlocal gq, nextg
            out = []
            if j % GT == 0 and j > 0:
                for sg in gq:
                    out += sg
                gq = segs(cx.rec(G, nextg)) if nextg < ngrp else []
                nextg += 1
                out += cx.rec(front, j)
            else:
                out += cx.rec(front, j)
                left = GT - (j % GT)
                n = (len(gq) + left - 1) // max(1, left)
                for sg in gq[:n]:
                    out += sg
                gq = gq[n:]
            return out

        AB = {}
        cx.play([front_list(0)])
        AB[0] = cx.split_at_mark(cx.rec(heads, 0))
        lists = [AB[0][0]]
        if nt > 1:
            lists.append(front_list(1))
        cx.play(lists)
        for i in range(nt):
            lists = [AB[i][1]]
            if i + 1 < nt:
                AB[i + 1] = cx.split_at_mark(cx.rec(heads, i + 1))
                lists.append(AB[i + 1][0])
            st_ = cx.rec(self.store_onT, fsets[i % 3]["o_n"], 2 * nh, i)
            if i + 2 < nt:
                lists.append(front_list(i + 2) + st_)
            else:
                lists.append(st_)
            lists.append(self.take_pref(nt - i))
            cx.play(lists)
            del AB[i]
        tks.append(cx.dma(dr[f"sout{li}"].rearrange("p (h v) -> p h v", h=nh), S_f[:, :, :], reads=[S_f],
                          key="Sst"))
        return tks

    def gdn_phase_a(self, li, src, srcn):
        cx, dr, cst, vec = self.cx, self.dr, self.cst, self.vec
        pb = self.pb
        nt = self.nt
        m1b, m2b, oneb, idb = self.m1b, self.m2b, self.oneb, self.idb
        ID = cst[:, C_ID:C_ID + 128]
        M1 = cst[:, C_M1:C_M1 + 128]
        tks = []
        self.li = li
        nh = self.nh("gdn")
        NB = 4 * nh
        CG = NB * 128
        CB = CG + nh * 256
        CIN = CB + 2 * nh
        win = self.wbuf["win_gdn"]
        S_f = [cx.sb(f"S_f{h}", [128, 256], F32) for h in range(nh)]
        S_b = [cx.sb(f"S_b{h}", [128, 256], BF16) for h in range(nh)]
        for h in range(nh):
            cx.dma(S_f[h][:, :], dr[f"sin{li}"][:, h * 256:(h + 1) * 256], writes=[S_f[h]], key=f"S{h}")
            cx.op("dve", lambda e, h=h: e.tensor_copy(out=S_b[h][:, :], in_=S_f[h][:, :]), [S_f[h]], [S_b[h]])
        halo = cx.sb("halo", [128, NB, 3], F32)
        css = [cx.sb(f"cs{i}", [128, 131], F32) for i in range(2)]
        f32t = {n: cx.sb(n, [128, 128], F32) for n in ["acc", "sl", "rs", "zero"]}
        b16t = {n: cx.sb(n, [128, 128], BF16) for n in ["sq"]}
        ff32, fb16 = f32t, b16t
        negA = cx.sb("negA", [128, nh], F32)
        fsets = []
        for k_ in range(3):
            FS = dict(xb=None, xT=None,
                      qkvT=None, ktok=cx.sb("ktok", [128, nh * 128], BF16),
                      vtok=cx.sb("vtok", [128, nh * 256], BF16), ba=cx.sb("ba", [128, 2 * nh], F32),
                      E3=cx.sb("E3", [128, 3 * nh], F32), o_n=cx.sb("o_n", [128, nh * 256], BF16))
            FS["sm"] = {n: cx.sb(n, [128, nh], F32) for n in
                        ["beta", "lnb", "z", "g", "g_hf", "g_lf", "l_hf", "l_lf", "be"]}
            FS["sm"]["negA"] = negA
            FS["smb"] = {n: cx.sb(n, [128, nh], BF16) for n in ["g_hb", "g_lb", "l_hb"]}
            fsets.append(FS)
        hsets = []
        for k_ in range(min(2, nh)):
            HB = dict(f32t={n: cx.sb(n, [128, 128], F32) for n in ["D", "DT", "BDT", "EgcB", "A_row", "A_col", "tmpq"]},
                      b16t={n: cx.sb(n, [128, 128], BF16) for n in
                            ["rR_h", "rR_l", "rC_h", "rC_l", "rL_h", "rL_l", "qd", "QKTm", "X0", "X1", "R0", "R1",
                             "Offn", "On", "W1", "W2", "wT", "kgb", "kd"]},
                      bv=cx.sb("bv", [128, 256], BF16), u_f=cx.sb("u_f", [128, 256], F32),
                      vnew=cx.sb("vnew", [128, 256], BF16), gsh=cx.sb("gsh", [128, 256], F32),
                      o_f=cx.sb("o_f", [128, 256], F32), junk=cx.sb("junk", [128, 256], F32),
                      sgt=cx.sb("sgt", [128, 256], F32), Offa=cx.sb("Offa", [128, 6, 128], BF16),
                      Ona=cx.sb("Ona", [128, 6, 128], BF16),
                      ssq=cx.sb("ssq", [128, 1], F32), rstd=cx.sb("rstd", [128, 1], F32),
                      pdg=pb[2 * k_], pw=pb[2 * k_ + 1])
            HB["out"] = [dict(u_f=cx.sb("u_f", [128, 256], F32), wT=cx.sb("wT", [128, 128], BF16),
                              qd=cx.sb("qd", [128, 128], BF16), QKTm=cx.sb("QKTm", [128, 128], BF16),
                              kd=cx.sb("kd", [128, 128], BF16)) for _ in range(2)]
            hsets.append(HB)
        self.onT = cx.sb("onT", [128, 2048], BF16)
        sm = {"negA": negA}
        zero = f32t["zero"]
        cx.op("dve", lambda e: e.memset(zero[:, :], 0.0), [], [zero])
        cx.op("act", lambda e: e.activation(out=negA[:, :], in_=vec[:, V_A:V_A + nh], func=AF.Exp), [vec],
              [negA])
        cx.op("act", lambda e: e.activation(out=negA[:, :], in_=negA[:, :], func=AF.Copy, scale=-1.0),
              [negA], [negA])
        xh = cx.sb("xh", [3, 1024], F32)
        xhb = cx.sb("xhb", [3, 1024], BF16)
        xhT = cx.sb("xhT", [128, 8, 4], BF16)
        cx.dma(xh[:, :], dr["xhalo"][li], writes=[xh], key="xh")
        cx.op("dve", lambda e: e.tensor_copy(out=xhb[:, :], in_=xh[:, :]), [xh], [xhb])
        pbb = self.pbb[0]
        for kc in range(8):
            cx.op("pe", lambda e, kc=kc: e.transpose(out=pbb[:, kc * 4:kc * 4 + 3], in_=xhb[0:3, kc * 128:(kc + 1) * 128],
                                                     identity=idb[0:3, 0:3]), [xhb, idb], [pbb])
        cx.op("act", lambda e: e.activation(out=xhT[:, :, 0:3],
                                            in_=pbb[:, 0:32].rearrange("p (k t) -> p k t", k=8)[:, :, 0:3],
                                            func=AF.Copy), [pbb], [xhT])
        for blk in range(NB):
            p = pb[blk % 2]
            self.mm_acc(p, p[:, 0:3], [(win[:, kc, blk * 128:(blk + 1) * 128], xhT[:, kc, 0:3]) for kc in range(8)],
                        [win, xhT])
            cx.op("act", lambda e, p=p, blk=blk: e.activation(out=halo[:, blk, :], in_=p[:, 0:3], func=AF.Copy),
                  [p], [halo])
        UNP = ("xb", "xT", "qkvT", "ktok", "vtok", "ba", "E3", "o_n", "sm", "smb")

        pB0, pB1 = pb[4], pb[5]
        pF = pb[6]
        GT = 4
        NGS = 2
        gsets = [dict(xT4=cx.sb("xT4", [128, 8, GT * 128], BF16), qkvT4=cx.sb("qkvT4", [128, NB, GT * 128], BF16))
                 for _ in range(NGS)]
        cs4 = [cx.sb(f"cs4_{k}", [128, 3 + GT * 128], F32) for k in range(2)]
        acc4 = cx.sb("acc4", [128, GT * 128], F32)
        sl4 = cx.sb("sl4", [128, GT * 128], F32)
        rs4 = cx.sb("rs4", [128, GT * 128], F32)
        sq4 = cx.sb("sq4", [128, GT * 128], BF16)
        zero4 = cx.sb("zero4", [128, GT * 128], F32)
        cx.op("pool", lambda e: e.memset(zero4[:, :], 0.0), [], [zero4])
        xt4 = [cx.sb(f"xtg{t}", [128, 1024], F32) for t in range(2)]
        xbs = [cx.sb(f"xbg{t}", [128, 1024], BF16) for t in range(2)]

        def G(gi):
            GS = gsets[gi % NGS]
            xT4, qkvT4 = GS["xT4"], GS["qkvT4"]
            tiles = [i for i in range(gi * GT, min(nt, (gi + 1) * GT))]
            N = len(tiles) * 128
            for t, i in enumerate(tiles):
                xt, xb = xt4[t % 2], xbs[t % 2]
                self.load_x_tile(src, srcn, i, xt)
                cx.op("pool", lambda e, xt=xt, xb=xb: e.tensor_copy(out=xb[:, :], in_=xt[:, :]), [xt], [xb])
                pbb = self.pbb[0]
                for kc in range(8):
                    cx.op("pe", lambda e, kc=kc, xb=xb, pbb=pbb: e.transpose(
                        out=pbb[:, kc * 128:(kc + 1) * 128], in_=xb[:, kc * 128:(kc + 1) * 128],
                        identity=idb[:, :]), [xb, idb], [pbb])
                cx.op("act", lambda e, t=t, pbb=pbb: e.activation(
                    out=xT4[:, :, t * 128:(t + 1) * 128], in_=pbb[:, :].rearrange("p (k t) -> p k t", k=8),
                    func=AF.Copy), [pbb], [xT4])
                cx.mark()
            for blk in range(NB):
                self.mm_acc(pF, pF[:, 0:N], [(win[:, kc, blk * 128:(blk + 1) * 128], xT4[:, kc, 0:N])
                                             for kc in range(8)], [win, xT4])
                cs = cs4[blk % 2]
                cx.op("pool", lambda e, cs=cs, blk=blk: e.tensor_copy(out=cs[:, 0:3], in_=halo[:, blk, :]), [halo], [cs])
                cx.op("act", lambda e, cs=cs: e.activation(out=cs[:, 3:3 + N], in_=pF[:, 0:N], func=AF.Copy), [pF], [cs])
                cx.op("pool", lambda e, cs=cs, blk=blk: e.tensor_copy(out=halo[:, blk, :], in_=cs[:, N:N + 3]), [cs], [halo])
                for j in range(4):
                    cx.op("dve", lambda e, cs=cs, blk=blk, j=j: e.scalar_tensor_tensor(
                        out=acc4[:, 0:N], in0=cs[:, j:j + N], scalar=vec[:, V_CW + blk * 4 + j:V_CW + blk * 4 + j + 1],
                        in1=(zero4[:, 0:N] if j == 0 else acc4[:, 0:N]), op0=ALU.mult, op1=ALU.add),
                        [cs, vec, zero4, acc4], [acc4])
                if blk >= 2 * nh:
                    self.silu(acc4, acc4[:, 0:N], qkvT4, qkvT4[:, blk, 0:N], rs4, rs4[:, 0:N])
                else:
                    self.silu(acc4, acc4[:, 0:N], sl4, sl4[:, 0:N], rs4, rs4[:, 0:N])
                    cx.op("act", lambda e: e.activation(out=sq4[:, 0:N], in_=sl4[:, 0:N], func=AF.Square), [sl4], [sq4])
                    self.mm_acc(pF, pF[:, 0:N], [(oneb[:, :], sq4[:, 0:N])], [oneb, sq4])
                    cx.op("act", lambda e: e.activation(out=rs4[:, 0:N], in_=pF[:, 0:N], func=AF.Ln,
                                                        bias=cst[:, C_EPSR:C_EPSR + 1]), [pF, cst], [rs4])
                    cx.op("act", lambda e: e.activation(out=rs4[:, 0:N], in_=rs4[:, 0:N], func=AF.Exp, scale=-0.5),
                          [rs4], [rs4])
                    cx.op("dve", lambda e, blk=blk: e.tensor_tensor(out=qkvT4[:, blk, 0:N], in0=sl4[:, 0:N],
                                                                    in1=rs4[:, 0:N], op=ALU.mult), [sl4, rs4], [qkvT4])
                cx.mark()

        def front(i):
            FS = fsets[i % 3]
            xb, xT, qkvT, ktok, vtok, ba, E3, o_n, sm, smb = (FS[k] for k in UNP)
            f32t, b16t = ff32, fb16
            GS = gsets[(i // GT) % NGS]
            xT_T, qkvT_T = GS["xT4"], GS["qkvT4"]
            t0_ = (i % GT) * 128
            xTv = xT_T[:, :, t0_:t0_ + 128]
            tb = list(range(nh, NB))
            for g0 in range(0, len(tb), 8):
                grp = tb[g0:g0 + 8]
                pbb = self.pbb[0]
                for k, blk in enumerate(grp):
                    cx.op("pe", lambda e, blk=blk, k=k: e.transpose(out=pbb[:, k * 128:(k + 1) * 128],
                                                                    in_=qkvT_T[:, blk, t0_:t0_ + 128],
                                                                    identity=idb[:, :]),
                          [qkvT_T, idb], [pbb])
                kb = [b for b in grp if b < 2 * nh]
                vb = [b for b in grp if b >= 2 * nh]
                if kb:
                    cx.op("act", lambda e, kb=kb, grp=grp, pbb=pbb: e.activation(
                        out=ktok[:, (kb[0] - nh) * 128:(kb[-1] - nh + 1) * 128],
                        in_=pbb[:, grp.index(kb[0]) * 128:(grp.index(kb[-1]) + 1) * 128], func=AF.Copy), [pbb], [ktok])
                if vb:
                    cx.op("act", lambda e, vb=vb, grp=grp, pbb=pbb: e.activation(
                        out=vtok[:, (vb[0] - 2 * nh) * 128:(vb[-1] - 2 * nh + 1) * 128],
                        in_=pbb[:, grp.index(vb[0]) * 128:(grp.index(vb[-1]) + 1) * 128], func=AF.Copy), [pbb], [vtok])
            p = pF
            self.mm_acc(p, p[:, 0:2 * nh], [(xTv[:, kc, :], win[:, kc, CB:CB + 2 * nh]) for kc in range(8)], [win, xT_T])
            cx.op("act", lambda e, p=p: e.activation(out=ba[:, :], in_=p[:, 0:2 * nh], func=AF.Copy), [p], [ba])
            beta, lnb, z, g = sm["beta"], sm["lnb"], sm["z"], sm["g"]
            cx.op("act", lambda e: e.activation(out=lnb[:, :], in_=ba[:, 0:nh], func=AF.Exp, scale=-1.0), [ba], [lnb])
            cx.op("act", lambda e: e.activation(out=lnb[:, :], in_=lnb[:, :], func=AF.Ln, bias=cst[:, C_ONE:C_ONE + 1]),
                  [lnb, cst], [lnb])
            cx.op("act", lambda e: e.activation(out=beta[:, :], in_=lnb[:, :], func=AF.Exp, scale=-1.0), [lnb], [beta])
            cx.op("act", lambda e: e.activation(out=lnb[:, :], in_=lnb[:, :], func=AF.Copy, scale=-1.0), [lnb], [lnb])
            cx.op("dve", lambda e: e.tensor_tensor(out=z[:, :], in0=ba[:, nh:2 * nh], in1=vec[:, V_DT:V_DT + nh], op=ALU.add),
                  [ba, vec], [z])
            cx.op("act", lambda e: e.activation(out=z[:, :], in_=z[:, :], func=AF.Exp), [z], [z])
            cx.op("act", lambda e: e.activation(out=z[:, :], in_=z[:, :], func=AF.Ln, bias=cst[:, C_ONE:C_ONE + 1]),
                  [z, cst], [z])
            cx.op("dve", lambda e: e.tensor_tensor(out=g[:, :], in0=z[:, :], in1=sm["negA"][:, :], op=ALU.mult),
                  [z, sm["negA"]], [g])
            for srcv, hb, hf, lf in ((g, smb["g_hb"], sm["g_hf"], sm["g_lf"]), (lnb, smb["l_hb"], sm["l_hf"], sm["l_lf"])):
                cx.op("dve", lambda e, srcv=srcv, hb=hb: e.tensor_copy(out=hb[:, :], in_=srcv[:, :]), [srcv], [hb])
                cx.op("dve", lambda e, hb=hb, hf=hf: e.tensor_copy(out=hf[:, :], in_=hb[:, :]), [hb], [hf])
                cx.op("dve", lambda e, srcv=srcv, hf=hf, lf=lf: e.tensor_tensor(out=lf[:, :], in0=srcv[:, :], in1=hf[:, :],
                                                                              op=ALU.subtract), [srcv, hf], [lf])
            cx.op("dve", lambda e: e.tensor_copy(out=smb["g_lb"][:, :], in_=sm["g_lf"][:, :]), [sm["g_lf"]], [smb["g_lb"]])
            p = pF
            self.mm_acc(p, p[:, 0:nh], [(m1b[:, :], smb["g_hb"][:, :]), (m1b[:, :], smb["g_lb"][:, :])],
                        [m1b, smb["g_hb"], smb["g_lb"]])
            self.mm_acc(p, p[:, nh:2 * nh], [(m2b[:, :], smb["g_hb"][:, :]), (m2b[:, :], smb["g_lb"][:, :])],
                        [m2b, smb["g_hb"], smb["g_lb"]])
            self.mm_acc(p, p[:, 2 * nh:3 * nh], [(oneb[:, :], smb["g_hb"][:, :]), (oneb[:, :], smb["g_lb"][:, :])],
                        [oneb, smb["g_hb"], smb["g_lb"]])
            cx.op("act", lambda e, p=p: e.activation(out=E3[:, :], in_=p[:, 0:3 * nh], func=AF.Exp), [p], [E3])
            be = sm["be"]
            cx.op("dve", lambda e: e.tensor_tensor(out=be[:, :], in0=beta[:, :], in1=E3[:, 0:nh], op=ALU.mult),
                  [beta, E3], [be])

        def head(i, h):
            FS = fsets[i % 3]
            xb, xT, qkvT, ktok, vtok, ba, E3, o_n, sm, smb = (FS[k] for k in UNP)
            HB = hsets[h % len(hsets)]
            xT_T = gsets[(i // GT) % NGS]["xT4"]
            qkvT_T = gsets[(i // GT) % NGS]["qkvT4"]
            t0_ = (i % GT) * 128
            xTv = xT_T[:, :, t0_:t0_ + 128]
            HO = HB["out"][i % 2]
            f32t, b16t = HB["f32t"], dict(HB["b16t"])
            b16t.update({k: HO[k] for k in ("wT", "qd", "QKTm", "kd")})
            u_f = HO["u_f"]
            bv, vnew, gsh, o_f, junk, ssq, rstd = (HB[k] for k in
                                                   ("bv", "vnew", "gsh", "o_f", "junk", "ssq", "rstd"))
            beta, be = sm["beta"], sm["be"]
            self._gs_t = [gsh]
            if True:
                qT = qkvT_T[:, h, t0_:t0_ + 128]
                kT = qkvT_T[:, nh + h, t0_:t0_ + 128]
                D, DT, BDT, EgcB = f32t["D"], f32t["DT"], f32t["BDT"], f32t["EgcB"]
                A_row, A_col, tmpq = f32t["A_row"], f32t["A_col"], f32t["tmpq"]
                rb = {}
                for nm, mask, sc in (("rR_h", C_M2, sm["g_hf"]), ("rR_l", C_M2, sm["g_lf"]), ("rC_h", C_M1, sm["g_hf"]),
                                     ("rC_l", C_M1, sm["g_lf"]), ("rL_h", C_ID, sm["l_hf"]), ("rL_l", C_ID, sm["l_lf"])):
                    t = b16t[nm]
                    rb[nm] = t
                    cx.op("dve", lambda e, t=t, mask=mask, sc=sc, h=h: e.scalar_tensor_tensor(
                        out=t[:, :], in0=cst[:, mask:mask + 128], scalar=sc[:, h:h + 1], in1=cst[:, mask:mask + 128],
                        op0=ALU.mult, op1=ALU.mult), [cst, sc], [t])
                pd = HB["pdg"]
                rr = [rb["rR_h"], rb["rR_l"]]
                rc = [rb["rC_h"], rb["rC_l"]]
                rl = [rb["rL_h"], rb["rL_l"]]
                self.mm_acc(pd, pd[:, 0:128], [(m1b[:, :], t[:, :]) for t in rr], [m1b] + rr)
                self.mm_acc(pd, pd[:, 128:256], [(m2b[:, :], t[:, :]) for t in rc], [m2b] + rc)
                self.mm_acc(pd, pd[:, 256:384], [(m2b[:, :], t[:, :]) for t in rc] + [(oneb[:, :], t[:, :]) for t in rl],
                            [m2b, oneb] + rc + rl)
                self.mm_acc(pd, pd[:, 384:512], [(oneb[:, :], t[:, :]) for t in rc], [oneb] + rc)
                for k, dt_ in enumerate((D, DT, BDT, EgcB)):
                    cx.op("act", lambda e, k=k, dt_=dt_: e.activation(out=dt_[:, :], in_=pd[:, k * 128:(k + 1) * 128],
                                                                    func=AF.Exp), [pd], [dt_])
                qd = b16t["qd"]
                cx.op("dve", lambda e, qT=qT: e.scalar_tensor_tensor(out=qd[:, :], in0=qT, scalar=SCALE, in1=EgcB[:, :],
                                                                     op0=ALU.mult, op1=ALU.mult), [qkvT_T, EgcB], [qd])
                pg = HB["pdg"]
                self.mm_acc(pg, pg[:, 0:128], [(kT, kT)], [qkvT_T])
                self.mm_acc(pg, pg[:, 128:256], [(kT, qT)], [qkvT_T])
                cx.op("dve", lambda e, h=h: e.scalar_tensor_tensor(out=A_row[:, :], in0=pg[:, 0:128],
                                                                   scalar=beta[:, h:h + 1], in1=D[:, :], op0=ALU.mult,
                                                                   op1=ALU.mult), [pg, beta, D], [A_row])
                cx.op("dve", lambda e: e.tensor_tensor(out=A_col[:, :], in0=pg[:, 0:128], in1=BDT[:, :], op=ALU.mult),
                      [pg, BDT], [A_col])
                cx.op("dve", lambda e: e.scalar_tensor_tensor(out=tmpq[:, :], in0=pg[:, 128:256], scalar=SCALE,
                                                              in1=DT[:, :], op0=ALU.mult, op1=ALU.mult), [pg, DT], [tmpq])
                QKTm = b16t["QKTm"]
                cx.op("dve", lambda e: e.tensor_tensor(out=QKTm[:, :], in0=tmpq[:, :], in1=M1, op=ALU.mult),
                      [tmpq, cst], [QKTm])
                X, R = b16t["X0"], b16t["R0"]
                Xn, Rn = b16t["X1"], b16t["R1"]
                cx.op("dve", lambda e: e.tensor_tensor(out=tmpq[:, :], in0=A_col[:, :], in1=cst[:, C_NMC:C_NMC + 128],
                                                       op=ALU.mult), [A_col, cst], [tmpq])
                cx.op("dve", lambda e, X=X: e.tensor_tensor(out=X[:, :], in0=tmpq[:, :], in1=ID, op=ALU.add),
                      [tmpq, cst], [X])
                cx.op("dve", lambda e: e.tensor_tensor(out=tmpq[:, :], in0=A_row[:, :], in1=cst[:, C_NMR:C_NMR + 128],
                                                       op=ALU.mult), [A_row, cst], [tmpq])
                cx.op("dve", lambda e, R=R: e.tensor_tensor(out=R[:, :], in0=tmpq[:, :], in1=ID, op=ALU.add),
                      [tmpq, cst], [R])
                W1, W2 = b16t["W1"], b16t["W2"]
                Offa, Ona = HB["Offa"], HB["Ona"]
                cx.op("dve", lambda e: e.tensor_tensor(
                    out=Offa[:, :, :], in0=A_row[:, :].unsqueeze(1).broadcast_to([128, 6, 128]),
                    in1=cst[:, C_NMR + 128:C_NMR + 7 * 128].rearrange("p (l s) -> p l s", l=6), op=ALU.mult),
                    [A_row, cst], [Offa])
                cx.op("dve", lambda e: e.tensor_tensor(
                    out=Ona[:, :, :], in0=A_col[:, :].unsqueeze(1).broadcast_to([128, 6, 128]),
                    in1=cst[:, C_NMC + 128:C_NMC + 7 * 128].rearrange("p (l s) -> p l s", l=6), op=ALU.mult),
                    [A_col, cst], [Ona])
                pw = HB["pw"]
                for l in range(1, 7):
                    lastl = (l == 6)
                    self.mm_acc(pw, pw[:, 0:128], [(Offa[:, l - 1, :], X[:, :])], [Offa, X])
                    cx.op("act", lambda e: e.activation(out=W1[:, :], in_=pw[:, 0:128], func=AF.Copy), [pw], [W1])
                    if not lastl:
                        self.mm_acc(pw, pw[:, 128:256], [(Ona[:, l - 1, :], R[:, :])], [Ona, R])
                        cx.op("act", lambda e: e.activation(out=W2[:, :], in_=pw[:, 128:256], func=AF.Copy), [pw], [W2])
                    self.mm_acc(pw, pw[:, 256:384], [(R[:, :], W1[:, :])], [R, W1])
                    cx.op("dve", lambda e, X=X, Xn=Xn: e.tensor_tensor(out=Xn[:, :], in0=X[:, :], in1=pw[:, 256:384],
                                                                       op=ALU.add), [X, pw], [Xn])
                    if not lastl:
                        self.mm_acc(pw, pw[:, 384:512], [(X[:, :], W2[:, :])], [X, W2])
                        cx.op("dve", lambda e, R=R, Rn=Rn: e.tensor_tensor(out=Rn[:, :], in0=R[:, :], in1=pw[:, 384:512],
                                                                           op=ALU.add), [R, pw], [Rn])
                    X, Xn = Xn, X
                    R, Rn = Rn, R
                kgb, kd, wT = b16t["kgb"], b16t["kd"], b16t["wT"]
                cx.op("act", lambda e, h=h: e.activation(out=bv[:, :], in_=vtok[:, h * 256:(h + 1) * 256], func=AF.Copy,
                                                         scale=beta[:, h:h + 1]), [vtok, beta], [bv])
                cx.op("act", lambda e, h=h: e.activation(out=kgb[:, :], in_=ktok[:, h * 128:(h + 1) * 128], func=AF.Copy,
                                                         scale=be[:, h:h + 1]), [ktok, be], [kgb])
                cx.op("act", lambda e, h=h: e.activation(out=kd[:, :], in_=ktok[:, h * 128:(h + 1) * 128], func=AF.Copy,
                                                         scale=E3[:, nh + h:nh + h + 1]), [ktok, E3], [kd])
                self.mm_acc(pw, pw[:, 0:256], [(X[:, :], bv[:, :])], [X, bv])
                cx.op("act", lambda e: e.activation(out=u_f[:, :], in_=pw[:, 0:256], func=AF.Copy), [pw], [u_f])
                self.mm_acc(pg, pg[:, 256:384], [(kgb[:, :], X[:, :])], [kgb, X])
                cx.op("act", lambda e: e.activation(out=wT[:, :], in_=pg[:, 256:384], func=AF.Copy), [pg], [wT])
                cx.mark()
                self.mm_acc(pB0, pB0[:, 0:256], [(wT[:, :], S_b[h][:, :])], [wT, S_b[h]])
                cx.op("dve", lambda e: e.tensor_tensor(out=vnew[:, :], in0=u_f[:, :], in1=pB0[:, 0:256], op=ALU.subtract),
                      [u_f, pB0], [vnew])
                self.mm_acc(pB0, pB0[:, 256:512], [(qd[:, :], S_b[h][:, :]), (QKTm[:, :], vnew[:, :])],
                            [qd, S_b[h], QKTm, vnew])
                self.mm_acc(pB1, pB1[:, 0:256], [(kd[:, :], vnew[:, :])], [kd, vnew])
                cx.op("dve", lambda e, h=h: e.scalar_tensor_tensor(out=S_f[h][:, :], in0=S_f[h][:, :],
                                                                   scalar=E3[:, 2 * nh + h:2 * nh + h + 1], in1=pB1[:, 0:256],
                                                                   op0=ALU.mult, op1=ALU.add), [S_f[h], E3, pB1], [S_f[h]])
                cx.op("pool", lambda e, h=h: e.tensor_copy(out=S_b[h][:, :], in_=S_f[h][:, :]), [S_f[h]], [S_b[h]])
                self.mm_acc(pB1, pB1[:, 256:512], [(xTv[:, kc, :], win[:, kc, CG + h * 256:CG + (h + 1) * 256])
                                                    for kc in range(8)], [xT_T, win])
                self.silu(pB1, pB1[:, 256:512], gsh, gsh[:, :], HB["sgt"], HB["sgt"][:, :])
                self.rms_gate(pB0, pB0[:, 256:512], gsh[:, :], h, o_n, (o_f, junk, ssq, rstd))

        assert nh <= len(hsets)
        AB = {}

        def rec_heads(i):
            for h in range(nh):
                AB[(i, h)] = cx.split_at_mark(cx.rec(head, i, h))

        def segs(lst):
            out, cur = [], []
            for it in lst:
                if it[0] == "mark":
                    if cur:
                        out.append(cur)
                    cur = []
                else:
                    cur.append(it)
            if cur:
                out.append(cur)
            return out

        gq = []
        ngrp = (nt + GT - 1) // GT
        cx.play([cx.rec(G, 0)])
        if ngrp > 1:
            gq = segs(cx.rec(G, 1))
        nextg = 2

        def front_list(j):
            nonlocal gq, nextg
            out = []
            if j % GT == 0 and j > 0:
                for sg in gq:
                    out += sg
                gq = segs(cx.rec(G, nextg)) if nextg < ngrp else []
                nextg += 1
                out += cx.rec(front, j)
            else:
                out += cx.rec(front, j)
                left = GT - (j % GT)
                n = (len(gq) + left - 1) // max(1, left)
                for sg in gq[:n]:
                    out += sg
                gq = gq[n:]
            return out

        cx.play([front_list(0)])
        rec_heads(0)
        lists = [AB[(0, h)][0] for h in range(nh)]
        if nt > 1:
            lists.append(front_list(1))
        cx.play(lists)
        for i in range(nt):
            lists = [sum((AB[(i, h)][1] for h in range(nh)), [])]
            if i + 1 < nt:
                rec_heads(i + 1)
                lists += [AB[(i + 1, h)][0] for h in range(nh)]
            st_ = cx.rec(self.store_onT, fsets[i % 3]["o_n"], 2 * nh, i)
            if i + 2 < nt:
                lists.append(front_list(i + 2) + st_)
            else:
                lists.append(st_)
            lists.append(self.take_pref(nt - i))
            cx.play(lists)
            for h in range(nh):
                del AB[(i, h)]
        for h in range(nh):
            tks.append(cx.dma(dr[f"sout{li}"][:, h * 256:(h + 1) * 256], S_f[h][:, :], reads=[S_f[h]], key=f"Sst{h}"))
        return tks

    def phase_b(self, li, kind, src, srcn, dst, dstn, last):
        cx, dr, vec = self.cx, self.dr, self.vec
        pb = self.pb
        nt = self.nt
        nkc = 16 if kind == "gdn" else 8
        wout = self.wbuf[f"wout_{kind}"]
        NS = 3
        bsets = [dict(xt=cx.sb(f"xtb{k}", [128, 1024], F32), on=cx.sb(f"onb{k}", [128, 2048], BF16),
                      z=cx.sb(f"z{k}", [128, 1024], F32), st=cx.sb("bnst", [128, 2, 6], F32), mv=cx.sb("mv", [128, 2], F32),
                      rstd=cx.sb("rstdb", [128, 1], F32), p=[pb[2 * (k % NS)], pb[2 * (k % NS) + 1]])
                 for k in range(2 * NS)]
        tks = []
        R = self.ranks
        tpc = self.tpc[li]

        def loads(i):
            B = bsets[i % (2 * NS)]
            xt, on = B["xt"], B["on"]
            self.load_x_tile(src, srcn, i, xt)
            q, t = i // tpc, i % tpc
            if R == 1:
                cx.dma(on[:, 0:nkc * 128], self.drt[f"ont{li}_{q}"].ap()[t * 128:(t + 1) * 128, :],
                       reads=[self.dtile("ont", i)], writes=[on], key=on.name)
            else:
                srcap = self.drt[f"onta{li}_{q}"].ap().rearrange("(r t p) c -> t p r c", r=R, p=128)[t]
                cx.dma(on[:, 0:nkc * 128].rearrange("p (r c) -> p r c", r=R), srcap,
                       reads=[self.dtile("onta", q)], writes=[on], key=on.name)

        def tile(i):
            B = bsets[i % (2 * NS)]
            xt, on, z, st, mv, rstd = (B[k] for k in ("xt", "on", "z", "st", "mv", "rstd"))
            for hb in range(2):
                p = B["p"][hb]
                self.mm_acc(p, p[:, :], [(on[:, kc * 128:(kc + 1) * 128], wout[:, kc, hb * 512:(hb + 1) * 512])
                                         for kc in range(nkc)], [on, wout])
                cx.op("dve", lambda e, p=p, hb=hb: e.scalar_tensor_tensor(
                    out=z[:, hb * 512:(hb + 1) * 512], in0=xt[:, hb * 512:(hb + 1) * 512], scalar=DEEP_ALPHA,
                    in1=p[:, :], op0=ALU.mult, op1=ALU.add), [xt, p], [z])
            for hb in range(2):
                cx.op("dve", lambda e, hb=hb: e.bn_stats(out=st[:, hb, :], in_=z[:, hb * 512:(hb + 1) * 512]),
                      [z], [st])
            cx.op("dve", lambda e: e.bn_aggr(out=mv[:, :], in_=st[:, :, :]), [st], [mv])
            cx.op("act", lambda e: e.activation(out=rstd[:, 0:1], in_=mv[:, 1:2], func=AF.Ln,
                                                bias=self.cst[:, C_EPSL:C_EPSL + 1]), [mv, self.cst], [rstd])
            cx.op("act", lambda e: e.activation(out=rstd[:, 0:1], in_=rstd[:, 0:1], func=AF.Exp, scale=-0.5),
                  [rstd], [rstd])
            cx.op("dve", lambda e: e.tensor_scalar(out=z[:, :], in0=z[:, :], scalar1=mv[:, 0:1], scalar2=rstd[:, 0:1],
                                                   op0=ALU.subtract, op1=ALU.mult), [z, mv, rstd], [z])
            cx.op("pool", lambda e: e.tensor_tensor(out=z[:, :], in0=z[:, :], in1=vec[:, V_LNG:V_LNG + 1024],
                                                    op=ALU.mult), [z, vec], [z])
            cx.op("pool", lambda e: e.tensor_tensor(out=z[:, :], in0=z[:, :], in1=vec[:, V_LNB:V_LNB + 1024],
                                                    op=ALU.add), [z, vec], [z])
            cx.dma(dst[i * 128:(i + 1) * 128, :], z[:, :], reads=[z], writes=[self.dtile(dstn, i)],
                   key=z.name + "st")

        for i in range(0, min(nt, NS)):
            loads(i)
        for i0_ in range(0, nt, NS):
            lists = [cx.rec(tile, i) for i in range(i0_, min(nt, i0_ + NS))]
            nxt = list(range(i0_ + NS, min(nt, i0_ + 2 * NS)))
            if nxt:
                lists.append(cx.rec(lambda: [loads(i) for i in nxt]))
            lists.append(self.take_pref((nt - i0_ + NS - 1) // NS))
            cx.play(lists)
        return tks


def gdn_cols(r, R):
    nh = 8 // R
    h0 = r * nh
    return np.concatenate([np.arange(h0 * 128, (h0 + nh) * 128), 1024 + np.arange(h0 * 128, (h0 + nh) * 128),
                           2048 + np.arange(h0 * 256, (h0 + nh) * 256), 4096 + np.arange(h0 * 256, (h0 + nh) * 256),
                           6144 + np.arange(h0, h0 + nh), 6152 + np.arange(h0, h0 + nh)])


def gla_cols(r, R):
    nh = 4 // R
    h0 = r * nh
    return np.concatenate([np.arange(h0 * 128, (h0 + nh) * 128), 512 + np.arange(h0 * 128, (h0 + nh) * 128),
                           1024 + np.arange(h0 * 256, (h0 + nh) * 256), 2048 + np.arange(h0 * 256, (h0 + nh) * 256),
                           3072 + np.arange(16)])


def pack_vecs(kind, j, i, inp, r=0, R=1):
    v = np.zeros((128, VTOT), np.float32)
    v[:, V_LNG:V_LNG + 1024] = inp["ln_g"][i][None, :]
    v[:, V_LNB:V_LNB + 1024] = inp["ln_b"][i][None, :]
    if kind == "gdn":
        v[:, V_NW:V_NW + 256] = inp["gdn_norm_w"][j][None, :]
        nh = 8 // R
        v[:, V_A:V_A + nh] = inp["gdn_a_log"][j][None, r * nh:(r + 1) * nh]
        v[:, V_DT:V_DT + nh] = inp["gdn_dt_bias"][j][None, r * nh:(r + 1) * nh]
        cw = inp["gdn_conv_w"][j][:, gdn_cols(r, R)[:4 * nh * 128]]
        v[:, V_CW:V_CW + 4 * nh * 4] = cw.reshape(4, 4 * nh, 128).transpose(2, 1, 0).reshape(128, 4 * nh * 4)
    else:
        v[:, V_NW:V_NW + 256] = inp["gla_norm_w"][j][None, :]
        nh = 4 // R
        v[:, V_BGK:V_BGK + nh * 128] = inp["gla_b_gk"][j][None, r * nh * 128:(r + 1) * nh * 128]
    return v


LAYERS = ["gdn", "gla", "gdn", "gla"]


def make_in_map(xb, inp, nt, r=0, R=1):
    m = {"x": np.ascontiguousarray(xb, dtype=np.float32), "consts": make_consts(),
         "xhalo": np.zeros((len(LAYERS), 3, D_MODEL), np.float32)}
    vecs = []
    for i, kind in enumerate(LAYERS):
        j = i // 2
        vecs.append(pack_vecs(kind, j, i, inp, r, R))
        nh = (8 if kind == "gdn" else 4) // R
        cols = gdn_cols(r, R) if kind == "gdn" else gla_cols(r, R)
        m[f"win{i}"] = np.ascontiguousarray(inp[f"{kind}_w_in"][j][:, cols], dtype=np.float32)
        m[f"wout{i}"] = np.ascontiguousarray(inp[f"{kind}_w_out"][j], dtype=np.float32)
        m[f"sin{i}"] = np.zeros((128, nh * 256), np.float32)
        if kind == "gla":
            m[f"wgk{i}"] = np.ascontiguousarray(inp["gla_w_gk_up"][j][:, r * nh * 128:(r + 1) * nh * 128],
                                                dtype=np.float32)
    m["vecs"] = np.stack(vecs, 0)
    return m


_PROG = {}
RANKS = 4


def kernel(**inputs):
    inp = {k: np.asarray(v) for k, v in inputs.items()}
    x = inp["x"]
    B, T, _ = x.shape
    nt = T // 128
    R = RANKS
    if nt not in _PROG:
        _PROG[nt] = Prog(LAYERS, nt, ranks=R)
    P = _PROG[nt]
    in_maps = [make_in_map(x[c // R], inp, nt, c % R, R) for c in range(8)]
    res = run_bass_kernel_spmd(P.nc, in_maps, core_ids=list(range(8)))
    out = np.stack([np.asarray(res.results[b * R]["y"], dtype=np.float32).reshape(T, D_MODEL) for b in range(B)], 0)
    return out
```

```python
import re as _re
import numpy as np
from contextlib import ExitStack
import concourse.bass as bass
import concourse.mybir as mybir
from concourse.bass_utils import run_bass_kernel_spmd

F32 = mybir.dt.float32
BF16 = mybir.dt.bfloat16
AF = mybir.ActivationFunctionType
ALU = mybir.AluOpType

D_MODEL = 1024
SEQ = 8192
BATCH = 2
DEPTH = 4
GDN_IN = 6160
GLA_IN = 3088
DEEP_ALPHA = (2.0 * DEPTH) ** 0.25
LN_EPS = 1e-5
RMS_EPS = 1e-6
L2_EPS = 1e-6
SCALE = 128.0 ** -0.5
GEN = 8000


class Tk:
    __slots__ = ("key", "gen", "val", "sem")

    def __init__(self, key, gen, val, sem):
        self.key, self.gen, self.val, self.sem = key, gen, val, sem


class T:
    __slots__ = ("ap", "w", "r", "name", "excl")

    def __init__(self, ap, name=""):
        self.excl = False
        self.ap = ap
        self.w = None
        self.r = {}
        self.name = name

    def __getitem__(self, idx):
        return self.ap[idx]


class Ctx:
    def __init__(self, nc, stack):
        self.nc = nc
        self.stack = stack
        self.sem_stack = stack
        self.eng = {"pe": nc.tensor, "act": nc.scalar, "dve": nc.vector, "pool": nc.gpsimd, "sp": nc.sync}
        self.seq = {k: 0 for k in self.eng}
        self.sems = {k: [] for k in self.eng}
        self.known = {k: {} for k in self.eng}
        self.dma_sems = {}
        self.dma_cnt = {}
        self.nsem = 0
        self.ninst = 0
        self._rec = None
        self._win = []
        self._wcount = 0
        self._emitting = False

    def new_sem(self, name):
        self.nsem += 1
        return self.sem_stack.enter_context(self.nc.semaphore(f"{name}_{self.nsem}"))

    def sb(self, name, shape, dt):
        self.nsem += 1
        name = f"{name}_u{self.nsem}"
        return T(self.stack.enter_context(self.nc.sbuf_tensor(name, list(shape), dt)), name)

    def ps(self, name, shape, dt):
        t = T(self.stack.enter_context(self.nc.psum_tensor(name, list(shape), dt)), name)
        t.excl = True
        return t

    def _wait(self, e, tk):
        if tk is None:
            return
        if tk.key == e and e == "pe":
            return
        kn = self.known[e].get(tk.key)
        if kn is not None and kn >= (tk.gen, tk.val):
            return
        self.eng[e].wait_ge(tk.sem, tk.val)
        self.known[e][tk.key] = (tk.gen, tk.val)
        self.ninst += 1

    def _deps(self, e, reads, writes, defer=False):
        need = []
        for t in reads:
            if t.w is not None:
                need.append(t.w)
        for t in writes:
            if t.w is not None:
                need.append(t.w)
            need.extend(t.r.values())
        best = {}
        for tk in need:
            if tk.key == e and e == "pe":
                continue
            kn = self.known[e].get(tk.key)
            if kn is not None and kn >= (tk.gen, tk.val):
                continue
            b = best.get(tk.key)
            if b is None or (b.gen, b.val) < (tk.gen, tk.val):
                best[tk.key] = tk
        toks = list(best.values())
        last = None
        if defer and toks and EMBED:
            last = toks.pop()
        for tk in toks:
            self._wait(e, tk)
        return last

    def _mark(self, tk, reads, writes):
        for t in reads:
            t.r[tk.key] = tk
        for t in writes:
            t.w = tk
            t.r = {}

    def mark(self):
        if self._rec is not None:
            self._rec.append(("mark", None))

    @staticmethod
    def split_at_mark(lst):
        for k, it in enumerate(lst):
            if it[0] == "mark":
                return lst[:k], lst[k + 1:]
        return lst, []

    def rec(self, fn, *args):
        assert self._rec is None
        self._rec = []
        fn(*args)
        out, self._rec = self._rec, None
        return out

    def play(self, lists):
        self._win += [l for l in lists if l]
        self._wcount += 1
        if self._wcount >= WIN:
            self.flush()

    def flush(self):
        if self._win:
            w, self._win = self._win, []
            self._emitting = True
            try:
                self.play_sched(w)
            finally:
                self._emitting = False
        self._wcount = 0

    def _direct(self):
        if self._rec is None and not self._emitting and self._win:
            self.flush()

    @staticmethod
    def _dur(kind, e, writes):
        if kind == "dma":
            return 0.06, 2.5
        if kind == "collective":
            return 0.3, 25.0
        if e == "pe":
            return 0.22, 0.35
        n = 128
        for t in writes:
            try:
                sh = t.ap.shape
                m = 1
                for d in sh[1:]:
                    m *= int(d)
                n = max(n, min(m, 512))
            except Exception:
                pass
        d = 0.1 + n / 960.0
        return d, d + 0.25

    def play_sched(self, lists):
        ops = []
        for l in lists:
            ops += [it for it in l if it[0] != "mark"]
        n = len(ops)
        if n == 0:
            return
        lw, rd = {}, {}
        preds = [None] * n
        meta = [None] * n
        for k, (kind, a) in enumerate(ops):
            if kind == "op":
                e, fn, reads, writes = a[0]
            elif kind == "dma":
                e, reads, writes = a[1]["e"], a[1]["reads"], a[1]["writes"]
            else:
                e, reads, writes = "pool", a[1]["reads"], a[1]["writes"]
            ex = [t for t in reads if t.excl]
            if ex:
                reads = [t for t in reads if not t.excl]
                writes = list(writes) + ex
            ps = set()
            for t in reads:
                if id(t) in lw:
                    ps.add(lw[id(t)])
            for t in writes:
                if id(t) in lw:
                    ps.add(lw[id(t)])
                ps.update(rd.get(id(t), ()))
            ps.discard(k)
            preds[k] = ps
            for t in reads:
                rd.setdefault(id(t), []).append(k)
            for t in writes:
                lw[id(t)] = k
                rd[id(t)] = []
            meta[k] = (e,) + self._dur(kind, e, writes)
        succs = [[] for _ in range(n)]
        npred = [len(p) for p in preds]
        for k in range(n):
            for p in preds[k]:
                succs[p].append(k)
        fin = [0.0] * n
        clock = {}
        ready = [k for k in range(n) if npred[k] == 0]
        rt = {k: 0.0 for k in ready}
        order = []
        while ready:
            best, bs = None, None
            for k in ready:
                st = max(rt[k], clock.get(meta[k][0], 0.0))
                key = (st, k)
                if bs is None or key < bs:
                    best, bs = k, key
            ready.remove(best)
            e, busy, lat = meta[best]
            st = bs[0]
            clock[e] = st + busy
            fin[best] = st + lat
            order.append(best)
            for q in succs[best]:
                npred[q] -= 1
                rt[q] = max(rt.get(q, 0.0), fin[best])
                if npred[q] == 0:
                    ready.append(q)
        assert len(order) == n
        for k in order:
            kind, a = ops[k]
            getattr(self, kind)(*a[0], **a[1])

    def play_prop(self, lists):
        lists = [l for l in lists if l]
        pos = [0] * len(lists)
        total = sum(len(l) for l in lists)
        for _ in range(total):
            k = min((i for i in range(len(lists)) if pos[i] < len(lists[i])),
                    key=lambda i: (pos[i] + 0.5) / len(lists[i]))
            kind, a = lists[k][pos[k]]
            pos[k] += 1
            if kind == "mark":
                continue
            getattr(self, kind)(*a[0], **a[1])

    def op(self, e, fn, reads=(), writes=()):
        self._direct()
        if self._rec is not None:
            self._rec.append(("op", ((e, fn, list(reads), list(writes)), {})))
            return None
        ex = [t for t in reads if t.excl]
        if ex:
            reads = [t for t in reads if not t.excl]
            writes = list(writes) + ex
        last = self._deps(e, reads, writes, defer=True)
        ins = fn(self.eng[e])
        if last is not None:
            ins._wait_ge(last.sem, last.val)
            self.known[e][last.key] = (last.gen, last.val)
        n = self.seq[e]
        gen, val = n // GEN, n % GEN + 1
        if gen >= len(self.sems[e]):
            self.sems[e].append(self.new_sem(f"s_{e}"))
        sem = self.sems[e][gen]
        ins.then_inc(sem, 1)
        self.seq[e] = n + 1
        self.ninst += 1
        tk = Tk(e, gen, val, sem)
        self._mark(tk, reads, writes)
        return tk

    def dma(self, out_ap, in_ap, reads=(), writes=(), key=None, e="sp", slow=False):
        self._direct()
        if self._rec is not None:
            self._rec.append(("dma", ((out_ap, in_ap), dict(reads=list(reads), writes=list(writes), key=key, e=e,
                                                              slow=slow))))
            return None
        key = _re.sub(r"_u\d+", "", str(key))
        if key not in self.dma_sems:
            self.dma_sems[key] = self.new_sem("d")
            self.dma_cnt[key] = 0
        self._deps(e, reads, writes)
        sem = self.dma_sems[key]
        if slow:
            ins = self.eng[e].dma_start(out=out_ap, in_=in_ap, allow_slow_non_contiguous=True)
        else:
            ins = self.eng[e].dma_start(out=out_ap, in_=in_ap)
        self.dma_cnt[key] += 16
        ins.then_inc(sem, 16)
        self.ninst += 1
        tk = Tk(("dma", key), 0, self.dma_cnt[key], sem)
        self._mark(tk, reads, writes)
        return tk

    def collective(self, kind, in_ap, out_ap, groups, reads=(), writes=()):
        self._direct()
        if self._rec is not None:
            self._rec.append(("collective", ((kind, in_ap, out_ap, groups), dict(reads=list(reads),
                                                                                 writes=list(writes)))))
            return None
        e = "pool"
        self._deps(e, reads, writes)
        if not hasattr(self, "cc_sem"):
            self.cc_sem = self.new_sem("cc")
            self.cc_cnt = 0
        sem = self.cc_sem
        ins = self.eng[e].collective_compute(kind, ALU.bypass, replica_groups=groups, ins=[in_ap], outs=[out_ap])
        ins.then_inc(sem, 1)
        self.cc_cnt += 1
        self.ninst += 1
        tk = Tk(("cc", 0), 0, self.cc_cnt, sem)
        self._mark(tk, reads, writes)
        return tk

    def barrier(self):
        self.flush()
        toks = []
        for e2, n in self.seq.items():
            if n > 0:
                m = n - 1
                toks.append(Tk(e2, m // GEN, m % GEN + 1, self.sems[e2][m // GEN]))
        for key, cnt in self.dma_cnt.items():
            if cnt > 0:
                toks.append(Tk(("dma", key), 0, cnt, self.dma_sems[key]))
        for e in self.eng:
            for tk in toks:
                if tk.key != e:
                    self._wait(e, tk)

    def finish(self, tks):
        for tk in tks:
            if tk is not None:
                self._wait("sp", tk)
        self.barrier()


C_ID, C_M1, C_M2, C_ONE = 0, 128, 256, 384
C_NMR = 512
C_NMC = 512 + 7 * 128
C_EPSR = 512 + 14 * 128
C_EPSL = C_EPSR + 1
C_TOT = C_EPSR + 8


def make_consts():
    j = np.arange(128)
    c = np.zeros((128, C_TOT), np.float32)
    c[:, C_ID:C_ID + 128] = np.eye(128)
    c[:, C_M1:C_M1 + 128] = (j[:, None] <= j[None, :])
    c[:, C_M2:C_M2 + 128] = (j[:, None] > j[None, :])
    c[:, C_ONE:C_ONE + 128] = 1.0
    c[:, C_EPSR] = RMS_EPS
    c[:, C_EPSL] = LN_EPS
    for l in range(7):
        b = 1 << l
        row = ((j[:, None] // (2 * b)) == (j[None, :] // (2 * b))) & ((j[:, None] // b) > (j[None, :] // b))
        c[:, C_NMR + l * 128:C_NMR + (l + 1) * 128] = -row.astype(np.float32)
        c[:, C_NMC + l * 128:C_NMC + (l + 1) * 128] = -row.T.astype(np.float32)
    return c


V_LNG, V_LNB, V_NW = 0, 1024, 2048
V_A = 2304
V_DT = 2312
V_CW = 2320
V_BGK = 2448
VTOT = 2960


class StopEmit(Exception):
    pass


import os as _os
_KSTOP = float(_os.environ.get("KSTOP", "99"))


STOPPED = [False]
SCHED = int(_os.environ.get("KSCHED", "1"))
EMBED = int(_os.environ.get("KEMBED", "1"))
WIN = int(_os.environ.get("KWIN", "16"))


def ck(k):
    if _KSTOP <= k:
        STOPPED[0] = True
        return True
    return False


class Prog:
    def __init__(self, layers, ntiles, ranks=1):
        self.layers = layers
        self.nt = ntiles
        self.ranks = ranks
        self.build()

    def nh(self, kind):
        return (8 if kind == "gdn" else 4) // self.ranks

    def build(self):
        from contextlib import ExitStack
        nc = bass.Bass("TRN2", target_bir_lowering=False)
        self.nc = nc
        nt = self.nt
        L = len(self.layers)
        ntok = nt * 128
        dr = {}
        dr["x"] = nc.dram_tensor("x", [ntok, D_MODEL], F32, kind="ExternalInput").ap()
        if "gdn" in self.layers:
            dr["xhalo"] = nc.dram_tensor("xhalo", [L, 3, D_MODEL], F32, kind="ExternalInput").ap()
        dr["consts"] = nc.dram_tensor("consts", [128, C_TOT], F32, kind="ExternalInput").ap()
        dr["vecs"] = nc.dram_tensor("vecs", [L, 128, VTOT], F32, kind="ExternalInput").ap()
        for li, kind in enumerate(self.layers):
            nh = self.nh(kind)
            cin = nh * 768 + (2 * nh if kind == "gdn" else 16)
            val = 2048 if kind == "gdn" else 1024
            dr[f"win{li}"] = nc.dram_tensor(f"win{li}", [D_MODEL, cin], F32, kind="ExternalInput").ap()
            dr[f"wout{li}"] = nc.dram_tensor(f"wout{li}", [val, D_MODEL], F32, kind="ExternalInput").ap()
            dr[f"sin{li}"] = nc.dram_tensor(f"sin{li}", [128, nh * 256], F32, kind="ExternalInput").ap()
            dr[f"sout{li}"] = nc.dram_tensor(f"sout{li}", [128, nh * 256], F32, kind="ExternalOutput").ap()
            if kind == "gla":
                dr[f"wgk{li}"] = nc.dram_tensor(f"wgk{li}", [16, nh * 128], F32, kind="ExternalInput").ap()
            self.drt = getattr(self, "drt", {})
            self.tpc = getattr(self, "tpc", {})
            tpc = max(1, (1 << 20) // (128 * nh * 256 * 2))
            self.tpc[li] = tpc
            for q in range((nt + tpc - 1) // tpc):
                tq = min(tpc, nt - q * tpc)
                self.drt[f"ont{li}_{q}"] = nc.dram_tensor(f"ont{li}_{q}", [tq * 128, nh * 256], BF16)
                if self.ranks > 1:
                    self.drt[f"onta{li}_{q}"] = nc.dram_tensor(f"onta{li}_{q}", [self.ranks * tq * 128, nh * 256], BF16)
        dr["y"] = nc.dram_tensor("y", [ntok, D_MODEL], F32, kind="ExternalOutput").ap()
        for i in range(min(2, L - 1)):
            dr[f"act{i}"] = nc.dram_tensor(f"act{i}", [ntok, D_MODEL], F32).ap()
        self.dr = dr
        self.dt = {}
        with ExitStack() as stack:
            cx = Ctx(nc, stack)
            self.cx = cx
            self.emit(stack)
        return nc

    def dtile(self, name, i):
        k = (name, i)
        if k not in self.dt:
            self.dt[k] = T(None, f"{name}{i}")
        return self.dt[k]

    def emit(self, stack):
        from contextlib import ExitStack
        cx, nc, dr = self.cx, self.nc, self.dr
        L = len(self.layers)
        cst = cx.sb("cst", [128, C_TOT], F32)
        cx.dma(cst[:, :], dr["consts"][:, :], writes=[cst], key="cst")
        idb = cx.sb("idb", [128, 128], BF16)
        oneb = cx.sb("oneb", [128, 128], BF16)
        cx.op("dve", lambda e: e.tensor_copy(out=idb[:, :], in_=cst[:, C_ID:C_ID + 128]), [cst], [idb])
        cx.op("dve", lambda e: e.tensor_copy(out=oneb[:, :], in_=cst[:, C_ONE:C_ONE + 128]), [cst], [oneb])
        m1b = cx.sb("m1b", [128, 128], BF16)
        m2b = cx.sb("m2b", [128, 128], BF16)
        cx.op("dve", lambda e: e.tensor_copy(out=m1b[:, :], in_=cst[:, C_M1:C_M1 + 128]), [cst], [m1b])
        cx.op("dve", lambda e: e.tensor_copy(out=m2b[:, :], in_=cst[:, C_M2:C_M2 + 128]), [cst], [m2b])
        self.m1b, self.m2b = m1b, m2b
        self.cst, self.idb, self.oneb = cst, idb, oneb
        self.pb = [cx.ps(f"pb{i}", [128, 512], F32) for i in range(7)]
        pbb0 = cx.ps("pbb0", [128, 1024], BF16)
        self.pbb = [pbb0, pbb0]
        self.wbuf = {}
        self.pref = []
        out_tks = []
        try:
            self.emit_layers(stack, out_tks)
        except StopEmit:
            cx.sem_stack = stack
            cx.stack = stack
        cx.finish(out_tks)

    def emit_layers(self, stack, out_tks):
        cx, nc, dr = self.cx, self.nc, self.dr
        L = len(self.layers)
        for li, kind in enumerate(self.layers):
            src = dr["x"] if li == 0 else dr[f"act{(li - 1) % 2}"]
            srcn = "x" if li == 0 else f"act{(li - 1) % 2}"
            dst = dr["y"] if li == L - 1 else dr[f"act{li % 2}"]
            dstn = "y" if li == L - 1 else f"act{li % 2}"
            with ExitStack() as ls:
                cx.stack = ls
                vec = cx.sb(f"vec{li}", [128, VTOT], F32)
                cx.dma(vec[:, :], dr["vecs"][li], writes=[vec], key="vec")
                self.vec = vec
                with ExitStack() as pa:
                    cx.stack = pa
                    self.alloc_w("win", li)
                    self.weight_ops("win", li)
                    self.pref = []
                    if kind == "gla":
                        tks = self.gla_phase_a(li, src, srcn)
                    else:
                        tks = self.gdn_phase_a(li, src, srcn)
                    out_tks += tks
                    cx.play([self.take_pref(1)])
                    cx.barrier()
                with ExitStack() as pbk:
                    cx.stack = pbk
                    if STOPPED[0]:
                        continue
                    self.alloc_w("wout", li)
                    self.weight_ops("wout", li)
                    self.pref = []
                    tks = self.phase_b(li, kind, src, srcn, dst, dstn, last=(li == L - 1))
                    out_tks += tks
                    cx.play([self.take_pref(1)])
                    cx.barrier()
            cx.stack = stack

    def alloc_w(self, which, li):
        cx = self.cx
        kind = self.layers[li]
        nh = self.nh(kind)
        if which == "win":
            shp = [128, 8, nh * 768 + (2 * nh if kind == "gdn" else 16)]
        else:
            shp = [128, (2048 if kind == "gdn" else 1024) // 128, 1024]
        self.wbuf[f"{which}_{kind}"] = cx.sb(f"{which}_{kind}", shp, BF16)
        self.wstg = [cx.sb(f"wstg{i}", [128, 1024], F32) for i in range(2)]

    def weight_ops(self, which, li):
        kind = self.layers[li]
        nh = self.nh(kind)
        if which == "win":
            rows, cols = 1024, nh * 768 + (2 * nh if kind == "gdn" else 16)
        else:
            rows, cols = (2048 if kind == "gdn" else 1024), 1024
        self.load_weight(self.dr[f"{which}{li}"], self.wbuf[f"{which}_{kind}"], rows, cols, self.wstg, pool_only=False)

    def take_pref(self, nleft):
        n = (len(self.pref) + nleft - 1) // max(1, nleft)
        n += n % 2
        out, self.pref = self.pref[:n], self.pref[n:]
        return out

    def load_weight(self, wdram, wsb, rows, cols, stg, pool_only=False):
        cx = self.cx
        kcn = rows // 128
        cb = 1024
        n = 0
        for kc in range(kcn):
            for c0 in range(0, cols, cb):
                c1 = min(cols, c0 + cb)
                s = stg[n % 2]
                cx.dma(s[:, 0:c1 - c0], wdram[kc * 128:(kc + 1) * 128, c0:c1], writes=[s], key=f"wstg{n % 2}",
                       e="sp")
                eng = "pool" if (n % 2 == 0 or pool_only) else "dve"
                cx.op(eng, lambda e, s=s, kc=kc, c0=c0, c1=c1: e.tensor_copy(out=wsb[:, kc, c0:c1],
                                                                             in_=s[:, 0:c1 - c0]), [s], [wsb])
                n += 1

    def load_x_tile(self, src, srcn, i, xt):
        cx = self.cx
        cx.dma(xt[:, :], src[i * 128:(i + 1) * 128, :], reads=[self.dtile(srcn, i)], writes=[xt],
               key=xt.name)

    def make_xT(self, xt, xb, xT):
        cx = self.cx
        pbb = self.pbb[0]
        cx.op("pool", lambda e: e.tensor_copy(out=xb[:, :], in_=xt[:, :]), [xt], [xb])
        for kc in range(8):
            cx.op("pe", lambda e, kc=kc: e.transpose(out=pbb[:, kc * 128:(kc + 1) * 128],
                                                     in_=xb[:, kc * 128:(kc + 1) * 128], identity=self.idb[:, :]),
                  [xb, self.idb], [pbb])
        cx.op("act", lambda e: e.activation(out=xT[:, :, :], in_=pbb[:, :].rearrange("p (k t) -> p k t", k=8),
                                            func=AF.Copy), [pbb], [xT])

    def split(self, src_t, src_ap, hi, lo, hi_ap=None, lo_ap=None):
        cx = self.cx
        hi_ap = hi[:, :] if hi_ap is None else hi_ap
        lo_ap = lo[:, :] if lo_ap is None else lo_ap
        cx.op("dve", lambda e: e.tensor_copy(out=hi_ap, in_=src_ap), [src_t], [hi])
        cx.op("dve", lambda e: e.tensor_tensor(out=lo_ap, in0=src_ap, in1=hi_ap, op=ALU.subtract), [src_t, hi], [lo])

    def silu(self, src_t, src_ap, dst_t, dst_ap, tmp_t, tmp_ap):
        cx = self.cx
        one = self.cst[:, C_ONE:C_ONE + 1]
        cx.op("act", lambda e: e.activation(out=tmp_ap, in_=src_ap, func=AF.Exp, scale=-1.0), [src_t], [tmp_t])
        cx.op("act", lambda e: e.activation(out=tmp_ap, in_=tmp_ap, func=AF.Ln, bias=one), [tmp_t, self.cst], [tmp_t])
        cx.op("act", lambda e: e.activation(out=tmp_ap, in_=tmp_ap, func=AF.Exp, scale=-1.0), [tmp_t], [tmp_t])
        cx.op("dve", lambda e: e.tensor_tensor(out=dst_ap, in0=src_ap, in1=tmp_ap, op=ALU.mult), [src_t, tmp_t], [dst_t])

    def mm_acc(self, ps_t, ps_ap, pairs, reads):
        cx = self.cx
        n = len(pairs)
        for k, (l, r) in enumerate(pairs):
            cx.op("pe", lambda e, l=l, r=r, k=k: e.matmul(ps_ap, lhsT=l, rhs=r, start=(k == 0), stop=(k == n - 1)),
                  reads, [ps_t])

    def store_onT(self, o_n, nkc, i):
        cx = self.cx
        onT = self.onT
        for g0 in range(0, nkc, 8):
            n = min(8, nkc - g0)
            pbb = self.pbb[1]
            for k in range(n):
                kc = g0 + k
                cx.op("pe", lambda e, kc=kc, k=k: e.transpose(out=pbb[:, k * 128:(k + 1) * 128],
                                                              in_=o_n[:, kc * 128:(kc + 1) * 128],
                                                              identity=self.idb[:, :]), [o_n, self.idb], [pbb])
            cx.op("act", lambda e, g0=g0, n=n: e.activation(out=onT[:, g0 * 128:(g0 + n) * 128],
                                                           in_=pbb[:, 0:n * 128], func=AF.Copy), [pbb], [onT])
        li = self.li
        tpc = self.tpc[li]
        q, t = i // tpc, i % tpc
        tk = cx.dma(self.drt[f"ont{li}_{q}"].ap()[t * 128:(t + 1) * 128, :], onT[:, 0:nkc * 128], reads=[onT],
                    writes=[self.dtile("ont", i)], key="ontst")
        if self.ranks > 1 and (t == tpc - 1 or i == self.nt - 1):
            R = self.ranks
            groups = [list(range(g * R, (g + 1) * R)) for g in range(8 // R)]
            tk = cx.collective("AllGather", self.drt[f"ont{li}_{q}"].ap().opt(), self.drt[f"onta{li}_{q}"].ap().opt(),
                               groups, reads=[self.dtile("ont", k) for k in range(q * tpc, i + 1)],
                               writes=[self.dtile("onta", q)])
        return tk

    def rms_gate(self, o_ps_t, o_ps, gs, h, o_n, tmp):
        cx = self.cx
        vec = self.vec
        o_f, junk, ssq, rstd = tmp
        cx.op("act", lambda e: e.activation(out=o_f[:, :], in_=o_ps, func=AF.Copy), [o_ps_t], [o_f])
        if ck(4.71):
            return
        cx.op("act", lambda e: e.activation(out=junk[:, :], in_=o_f[:, :], func=AF.Square, scale=1.0 / 16.0,
                                            accum_out=ssq[:, 0:1]), [o_f], [junk, ssq])
        if ck(4.72):
            return
        cx.op("act", lambda e: e.activation(out=rstd[:, 0:1], in_=ssq[:, 0:1], func=AF.Ln,
                                            bias=self.cst[:, C_EPSR:C_EPSR + 1]), [ssq, self.cst], [rstd])
        if ck(4.73):
            return
        cx.op("act", lambda e: e.activation(out=rstd[:, 0:1], in_=rstd[:, 0:1], func=AF.Exp, scale=-0.5),
              [rstd], [rstd])
        if ck(4.74):
            return
        cx.op("dve", lambda e: e.scalar_tensor_tensor(out=o_f[:, :], in0=o_f[:, :], scalar=rstd[:, 0:1],
                                                      in1=vec[:, V_NW:V_NW + 256], op0=ALU.mult, op1=ALU.mult),
              [o_f, rstd, vec], [o_f])
        if ck(4.75):
            return
        cx.op("dve", lambda e: e.tensor_tensor(out=o_n[:, h * 256:(h + 1) * 256], in0=o_f[:, :], in1=gs,
                                               op=ALU.mult), [o_f] + self._gs_t, [o_n])

    def gla_phase_a(self, li, src, srcn):
        cx, dr, cst, vec = self.cx, self.dr, self.cst, self.vec
        pb = self.pb
        nt = self.nt
        M1 = cst[:, C_M1:C_M1 + 128]
        M2 = cst[:, C_M2:C_M2 + 128]
        tks = []
        self.li = li
        nh = self.nh("gla")
        NK = nh * 128
        NV = nh * 256
        CIN = nh * 768 + 16
        win = self.wbuf["win_gla"]
        wgk = cx.sb("wgk", [16, NK], F32)
        cx.dma(wgk[:, :], dr[f"wgk{li}"][:, :], writes=[wgk], key="wgk")
        wgk_h = cx.sb("wgk_h", [16, NK], BF16)
        wgk_l = cx.sb("wgk_l", [16, NK], BF16)
        self.split(wgk, wgk[:, :], wgk_h, wgk_l)
        m1b, m2b = self.m1b, self.m2b
        S_f = cx.sb("S_f", [128, nh, 256], F32)
        S_b = [cx.sb(f"S_b{h}", [128, 256], BF16) for h in range(nh)]
        cx.dma(S_f[:, :, :], dr[f"sin{li}"].rearrange("p (h v) -> p h v", h=nh), writes=[S_f], key="S")
        for h in range(nh):
            cx.op("dve", lambda e, h=h: e.tensor_copy(out=S_b[h][:, :], in_=S_f[:, h, :]), [S_f], [S_b[h]])
        xts = [cx.sb(f"xt{i}", [128, 1024], F32) for i in range(2)]
        Eq = cx.sb("Eq", [128, 128], F32)
        Ek = cx.sb("Ek", [128, 128], F32)
        qt = cx.sb("qt", [128, 128], BF16)
        kt = cx.sb("kt", [128, 128], BF16)
        ATm = cx.sb("ATm", [128, 128], BF16)
        o_f = cx.sb("o_f", [128, 256], F32)
        junk = cx.sb("junk", [128, 256], F32)
        ssq = cx.sb("ssq", [128, 1], F32)
        rstd = cx.sb("rstd", [128, 1], F32)
        self.onT = cx.sb("onT", [128, 2048], BF16)
        gtmp = cx.sb("gtmp", [128, NV], F32)
        tks = []

        fsets = []
        for k_ in range(3):
            fsets.append(dict(
                xb=cx.sb("xb", [128, 1024], BF16),
                xT=cx.sb("xT", [128, 8, 128], BF16),
                qT_f=cx.sb("qT_f", [128, nh, 128], F32),
                kT_f=cx.sb("kT_f", [128, nh, 128], F32),
                ktok=cx.sb("ktok", [128, NK], F32),
                v_b=cx.sb("v_b", [128, NV], BF16),
                gs=cx.sb("gs", [128, NV], F32),
                gkl=cx.sb("gkl", [16, 128], F32),
                gkl_h=cx.sb("gkl_h", [16, 128], BF16),
                gkl_l=cx.sb("gkl_l", [16, 128], BF16),
                lg=cx.sb("lg", [128, NK], F32),
                lg_h=cx.sb("lg_h", [128, NK], BF16),
                lg_l=cx.sb("lg_l", [128, NK], BF16),
                khat=cx.sb("khat", [128, NK], BF16),
                Er=cx.sb("Er", [128, NK], F32),
                o_n=cx.sb("o_n", [128, NV], BF16),
            ))
        UNP = ('xb', 'xT', 'qT_f', 'kT_f', 'ktok', 'v_b', 'gs', 'gkl', 'gkl_h', 'gkl_l', 'lg', 'lg_h', 'lg_l', 'khat', 'Er', 'o_n')

        aout = [[dict(qt=cx.sb("qt", [128, 128], BF16), ATm=cx.sb("ATm", [128, 128], BF16),
                      Eq=cx.sb("Eq", [128, 128], F32)) for _ in range(2)] for h in range(nh)]

        GT, NGS = 4, 2
        gsets = [dict(xT4=cx.sb("xT4", [128, 8, GT * 128], BF16), qT4=cx.sb("qT4", [128, nh, GT * 128], F32),
                      kT4=cx.sb("kT4", [128, nh, GT * 128], F32), gkl4=cx.sb("gkl4", [16, GT * 128], F32))
                 for _ in range(NGS)]
        xbs = [cx.sb(f"xbg{t}", [128, 1024], BF16) for t in range(2)]
        idb = self.idb

        def G(gi):
            GS = gsets[gi % NGS]
            xT4, qT4, kT4, gkl4 = GS["xT4"], GS["qT4"], GS["kT4"], GS["gkl4"]
            tiles = [i for i in range(gi * GT, min(nt, (gi + 1) * GT))]
            N = len(tiles) * 128
            for t, i in enumerate(tiles):
                xt, xb = xts[t % 2], xbs[t % 2]
                self.load_x_tile(src, srcn, i, xt)
                cx.op("pool", lambda e, xt=xt, xb=xb: e.tensor_copy(out=xb[:, :], in_=xt[:, :]), [xt], [xb])
                pbb = self.pbb[0]
                for kc in range(8):
                    cx.op("pe", lambda e, kc=kc, xb=xb, pbb=pbb: e.transpose(
                        out=pbb[:, kc * 128:(kc + 1) * 128], in_=xb[:, kc * 128:(kc + 1) * 128],
                        identity=idb[:, :]), [xb, idb], [pbb])
                cx.op("act", lambda e, t=t, pbb=pbb: e.activation(
                    out=xT4[:, :, t * 128:(t + 1) * 128], in_=pbb[:, :].rearrange("p (k t) -> p k t", k=8),
                    func=AF.Copy), [pbb], [xT4])
                cx.mark()
            for blk in range(2 * nh):
                dst = qT4 if blk < nh else kT4
                p = pb[blk % 2]
                self.mm_acc(p, p[:, 0:N], [(win[:, kc, blk * 128:(blk + 1) * 128], xT4[:, kc, 0:N]) for kc in range(8)],
                            [win, xT4])
                cx.op("act", lambda e, p=p, dst=dst, blk=blk: e.activation(out=dst[:, blk % nh, 0:N], in_=p[:, 0:N],
                                                                          func=AF.Copy), [p], [dst])
                cx.mark()
            p = pb[2]
            self.mm_acc(p, p[0:16, 0:N], [(win[:, kc, 2 * NK + 2 * NV:2 * NK + 2 * NV + 16], xT4[:, kc, 0:N])
                                          for kc in range(8)], [win, xT4])
            cx.op("act", lambda e, p=p: e.activation(out=gkl4[:, 0:N], in_=p[0:16, 0:N], func=AF.Copy), [p], [gkl4])
            cx.mark()

        def front(i):
            FS = fsets[i % 3]
            xb, xT, qT_f, kT_f, ktok, v_b, gs, gkl, gkl_h, gkl_l, lg, lg_h, lg_l, khat, Er, o_n = (FS[k] for k in UNP)
            GS = gsets[(i // GT) % NGS]
            xT_T, qT4, kT4, gkl4 = GS["xT4"], GS["qT4"], GS["kT4"], GS["gkl4"]
            t0_ = (i % GT) * 128
            xTv = xT_T[:, :, t0_:t0_ + 128]
            p = pb[0]
            self.mm_acc(p, p[:, 0:NK], [(xTv[:, kc, :], win[:, kc, NK:2 * NK]) for kc in range(8)], [win, xT_T])
            cx.op("act", lambda e, p=p: e.activation(out=ktok[:, :], in_=p[:, 0:NK], func=AF.Copy), [p], [ktok])
            for wi, (dstt, fn, c_base) in enumerate(((v_b, AF.Copy, 2 * NK), (gs, AF.Silu, 2 * NK + NV))):
                for ci, c0 in enumerate(range(0, NV, 512)):
                    w = min(512, NV - c0)
                    p = pb[(wi + ci) % 2]
                    self.mm_acc(p, p[:, 0:w], [(xTv[:, kc, :], win[:, kc, c_base + c0:c_base + c0 + w])
                                               for kc in range(8)], [win, xT_T])
                    if fn == AF.Silu:
                        self.silu(p, p[:, 0:w], dstt, dstt[:, c0:c0 + w], gtmp, gtmp[:, c0:c0 + w])
                    else:
                        cx.op("act", lambda e, p=p, dstt=dstt, fn=fn, c0=c0, w=w: e.activation(
                            out=dstt[:, c0:c0 + w], in_=p[:, 0:w], func=fn), [p], [dstt])
            p = pb[3]
            self.split(gkl4, gkl4[:, t0_:t0_ + 128], gkl_h, gkl_l)
            self.mm_acc(p, p[:, 0:NK], [(gkl_h[:, :], wgk_h[:, :]), (gkl_h[:, :], wgk_l[:, :]),
                                        (gkl_l[:, :], wgk_h[:, :])], [gkl_h, gkl_l, wgk_h, wgk_l])
            cx.op("dve", lambda e, p=p: e.tensor_tensor(out=lg[:, :], in0=p[:, 0:NK], in1=vec[:, V_BGK:V_BGK + NK],
                                                        op=ALU.add), [p, vec], [lg])
            cx.op("act", lambda e: e.activation(out=lg[:, :], in_=lg[:, :], func=AF.Exp, scale=-1.0), [lg], [lg])
            cx.op("act", lambda e: e.activation(out=lg[:, :], in_=lg[:, :], func=AF.Ln, bias=cst[:, C_ONE:C_ONE + 1]),
                  [lg, cst], [lg])
            p = pb[3]
            self.split(lg, lg[:, :], lg_h, lg_l)
            self.mm_acc(p, p[:, 0:NK], [(m2b[:, :], lg_h[:, :]), (m2b[:, :], lg_l[:, :])], [m2b, lg_h, lg_l])
            cx.op("act", lambda e, p=p: e.activation(out=Er[:, :], in_=p[:, 0:NK], func=AF.Exp,
                                                     scale=-1.0 / 16.0), [p], [Er])
            cx.op("dve", lambda e: e.tensor_tensor(out=khat[:, :], in0=ktok[:, :], in1=Er[:, :], op=ALU.mult),
                  [ktok, Er], [khat])

        def heads(i):
            GSh = gsets[(i // GT) % NGS]
            qT4, kT4 = GSh["qT4"], GSh["kT4"]
            t0_ = (i % GT) * 128
            FS = fsets[i % 3]
            xb, xT, qT_f, kT_f, ktok, v_b, gs, gkl, gkl_h, gkl_l, lg, lg_h, lg_l, khat, Er, o_n = (FS[k] for k in UNP)
            self._gs_t = [gs]
            for h in range(nh):
                qt, ATm, Eq = (aout[h][i % 2][k] for k in ("qt", "ATm", "Eq"))
                pq = pb[4]
                self.mm_acc(pq, pq[:, 0:128], [(lg_h[:, h * 128:(h + 1) * 128], m1b[:, :]),
                                               (lg_l[:, h * 128:(h + 1) * 128], m1b[:, :])], [lg_h, lg_l, m1b])
                cx.op("act", lambda e: e.activation(out=Eq[:, :], in_=pq[:, 0:128], func=AF.Exp, scale=-1.0 / 16.0),
                      [pq], [Eq])
                cx.op("act", lambda e: e.activation(out=Ek[:, :], in_=pq[:, 0:128], func=AF.Exp, scale=1.0 / 16.0),
                      [pq], [Ek])
                cx.op("dve", lambda e, h=h: e.scalar_tensor_tensor(out=qt[:, :], in0=qT4[:, h, t0_:t0_ + 128], scalar=SCALE,
                                                                   in1=Eq[:, :], op0=ALU.mult, op1=ALU.mult),
                      [qT4, Eq], [qt])
                cx.op("dve", lambda e, h=h: e.tensor_tensor(out=kt[:, :], in0=kT4[:, h, t0_:t0_ + 128], in1=Ek[:, :],
                                                            op=ALU.mult), [kT4, Ek], [kt])
                self.mm_acc(pq, pq[:, 128:256], [(kt[:, :], qt[:, :])], [kt, qt])
                cx.op("dve", lambda e: e.tensor_tensor(out=ATm[:, :], in0=pq[:, 128:256], in1=M1, op=ALU.mult),
                      [pq, cst], [ATm])
            cx.mark()
            for h in range(nh):
                qt, ATm, Eq = (aout[h][i % 2][k] for k in ("qt", "ATm", "Eq"))
                po = pb[5]
                self.mm_acc(po, po[:, 0:256], [(qt[:, :], S_b[h][:, :]), (ATm[:, :], v_b[:, h * 256:(h + 1) * 256])],
                            [qt, S_b[h], ATm, v_b])
                self.mm_acc(po, po[:, 256:512], [(khat[:, h * 128:(h + 1) * 128], v_b[:, h * 256:(h + 1) * 256])],
                            [khat, v_b])
                cx.op("dve", lambda e, h=h: e.scalar_tensor_tensor(out=S_f[:, h, :], in0=S_f[:, h, :],
                                                                   scalar=Eq[:, 127:128], in1=po[:, 256:512],
                                                                   op0=ALU.mult, op1=ALU.add), [S_f, Eq, po], [S_f])
                cx.op("pool", lambda e, h=h: e.tensor_copy(out=S_b[h][:, :], in_=S_f[:, h, :]), [S_f], [S_b[h]])
                self.rms_gate(po, po[:, 0:256], gs[:, h * 256:(h + 1) * 256], h, o_n, (o_f, junk, ssq, rstd))

        def segs(lst):
            out, cur = [], []
            for it in lst:
                if it[0] == "mark":
                    if cur:
                        out.append(cur)
                    cur = []
                else:
                    cur.append(it)
            if cur:
                out.append(cur)
            return out

        ngrp = (nt + GT - 1) // GT
        cx.play([cx.rec(G, 0)])
        gq = segs(cx.rec(G, 1)) if ngrp > 1 else []
        nextg = 2

        def front_list(j):
            nonlocal gq, nextg
            out = []
            if j % GT == 0 and j > 0:
                for sg in gq:
                    out += sg
                gq = segs(cx.rec(G, nextg)) if nextg < ngrp else []
                nextg += 1
                out += cx.rec(front, j)
            else:
                out += cx.rec(front, j)
                left = GT - (j % GT)
                n = (len(gq) + left - 1) // max(1, left)
                for sg in gq[:n]:
                    out += sg
                gq = gq[n:]
            return out

        AB = {}
        cx.play([front_list(0)])
        AB[0] = cx.split_at_mark(cx.rec(heads, 0))
        lists = [AB[0][0]]
        if nt > 1:
            lists.append(front_list(1))
        cx.play(lists)
        for i in range(nt):
            lists = [AB[i][1]]
            if i + 1 < nt:
                AB[i + 1] = cx.split_at_mark(cx.rec(heads, i + 1))
                lists.append(AB[i + 1][0])
            st_ = cx.rec(self.store_onT, fsets[i % 3]["o_n"], 2 * nh, i)
            if i + 2 < nt:
                lists.append(front_list(i + 2) + st_)
            else:
                lists.append(st_)
            lists.append(self.take_pref(nt - i))
            cx.play(lists)
            del AB[i]
        tks.append(cx.dma(dr[f"sout{li}"].rearrange("p (h v) -> p h v", h=nh), S_f[:, :, :], reads=[S_f],
                          key="Sst"))
        return tks

    def gdn_phase_a(self, li, src, srcn):
        cx, dr, cst, vec = self.cx, self.dr, self.cst, self.vec
        pb = self.pb
        nt = self.nt
        m1b, m2b, oneb, idb = self.m1b, self.m2b, self.oneb, self.idb
        ID = cst[:, C_ID:C_ID + 128]
        M1 = cst[:, C_M1:C_M1 + 128]
        tks = []
        self.li = li
        nh = self.nh("gdn")
        NB = 4 * nh
        CG = NB * 128
        CB = CG + nh * 256
        CIN = CB + 2 * nh
        win = self.wbuf["win_gdn"]
        S_f = [cx.sb(f"S_f{h}", [128, 256], F32) for h in range(nh)]
        S_b = [cx.sb(f"S_b{h}", [128, 256], BF16) for h in range(nh)]
        for h in range(nh):
            cx.dma(S_f[h][:, :], dr[f"sin{li}"][:, h * 256:(h + 1) * 256], writes=[S_f[h]], key=f"S{h}")
            cx.op("dve", lambda e, h=h: e.tensor_copy(out=S_b[h][:, :], in_=S_f[h][:, :]), [S_f[h]], [S_b[h]])
        halo = cx.sb("halo", [128, NB, 3], F32)
        css = [cx.sb(f"cs{i}", [128, 131], F32) for i in range(2)]
        f32t = {n: cx.sb(n, [128, 128], F32) for n in ["acc", "sl", "rs", "zero"]}
        b16t = {n: cx.sb(n, [128, 128], BF16) for n in ["sq"]}
        ff32, fb16 = f32t, b16t
        negA = cx.sb("negA", [128, nh], F32)
        fsets = []
        for k_ in range(3):
            FS = dict(xb=None, xT=None,
                      qkvT=None, ktok=cx.sb("ktok", [128, nh * 128], BF16),
                      vtok=cx.sb("vtok", [128, nh * 256], BF16), ba=cx.sb("ba", [128, 2 * nh], F32),
                      E3=cx.sb("E3", [128, 3 * nh], F32), o_n=cx.sb("o_n", [128, nh * 256], BF16))
            FS["sm"] = {n: cx.sb(n, [128, nh], F32) for n in
                        ["beta", "lnb", "z", "g", "g_hf", "g_lf", "l_hf", "l_lf", "be"]}
            FS["sm"]["negA"] = negA
            FS["smb"] = {n: cx.sb(n, [128, nh], BF16) for n in ["g_hb", "g_lb", "l_hb"]}
            fsets.append(FS)
        hsets = []
        for k_ in range(min(2, nh)):
            HB = dict(f32t={n: cx.sb(n, [128, 128], F32) for n in ["D", "DT", "BDT", "EgcB", "A_row", "A_col", "tmpq"]},
                      b16t={n: cx.sb(n, [128, 128], BF16) for n in
                            ["rR_h", "rR_l", "rC_h", "rC_l", "rL_h", "rL_l", "qd", "QKTm", "X0", "X1", "R0", "R1",
                             "Offn", "On", "W1", "W2", "wT", "kgb", "kd"]},
                      bv=cx.sb("bv", [128, 256], BF16), u_f=cx.sb("u_f", [128, 256], F32),
                      vnew=cx.sb("vnew", [128, 256], BF16), gsh=cx.sb("gsh", [128, 256], F32),
                      o_f=cx.sb("o_f", [128, 256], F32), junk=cx.sb("junk", [128, 256], F32),
                      sgt=cx.sb("sgt", [128, 256], F32), Offa=cx.sb("Offa", [128, 6, 128], BF16),
                      Ona=cx.sb("Ona", [128, 6, 128], BF16),
                      ssq=cx.sb("ssq", [128, 1], F32), rstd=cx.sb("rstd", [128, 1], F32),
                      pdg=pb[2 * k_], pw=pb[2 * k_ + 1])
            HB["out"] = [dict(u_f=cx.sb("u_f", [128, 256], F32), wT=cx.sb("wT", [128, 128], BF16),
                              qd=cx.sb("qd", [128, 128], BF16), QKTm=cx.sb("QKTm", [128, 128], BF16),
                              kd=cx.sb("kd", [128, 128], BF16)) for _ in range(2)]
            hsets.append(HB)
        self.onT = cx.sb("onT", [128, 2048], BF16)
        sm = {"negA": negA}
        zero = f32t["zero"]
        cx.op("dve", lambda e: e.memset(zero[:, :], 0.0), [], [zero])
        cx.op("act", lambda e: e.activation(out=negA[:, :], in_=vec[:, V_A:V_A + nh], func=AF.Exp), [vec],
              [negA])
        cx.op("act", lambda e: e.activation(out=negA[:, :], in_=negA[:, :], func=AF.Copy, scale=-1.0),
              [negA], [negA])
        xh = cx.sb("xh", [3, 1024], F32)
        xhb = cx.sb("xhb", [3, 1024], BF16)
        xhT = cx.sb("xhT", [128, 8, 4], BF16)
        cx.dma(xh[:, :], dr["xhalo"][li], writes=[xh], key="xh")
        cx.op("dve", lambda e: e.tensor_copy(out=xhb[:, :], in_=xh[:, :]), [xh], [xhb])
        pbb = self.pbb[0]
        for kc in range(8):
            cx.op("pe", lambda e, kc=kc: e.transpose(out=pbb[:, kc * 4:kc * 4 + 3], in_=xhb[0:3, kc * 128:(kc + 1) * 128],
                                                     identity=idb[0:3, 0:3]), [xhb, idb], [pbb])
        cx.op("act", lambda e: e.activation(out=xhT[:, :, 0:3],
                                            in_=pbb[:, 0:32].rearrange("p (k t) -> p k t", k=8)[:, :, 0:3],
                                            func=AF.Copy), [pbb], [xhT])
        for blk in range(NB):
            p = pb[blk % 2]
            self.mm_acc(p, p[:, 0:3], [(win[:, kc, blk * 128:(blk + 1) * 128], xhT[:, kc, 0:3]) for kc in range(8)],
                        [win, xhT])
            cx.op("act", lambda e, p=p, blk=blk: e.activation(out=halo[:, blk, :], in_=p[:, 0:3], func=AF.Copy),
                  [p], [halo])
        UNP = ("xb", "xT", "qkvT", "ktok", "vtok", "ba", "E3", "o_n", "sm", "smb")

        pB0, pB1 = pb[4], pb[5]
        pF = pb[6]
        GT = 4
        NGS = 2
        gsets = [dict(xT4=cx.sb("xT4", [128, 8, GT * 128], BF16), qkvT4=cx.sb("qkvT4", [128, NB, GT * 128], BF16))
                 for _ in range(NGS)]
        cs4 = [cx.sb(f"cs4_{k}", [128, 3 + GT * 128], F32) for k in range(2)]
        acc4 = cx.sb("acc4", [128, GT * 128], F32)
        sl4 = cx.sb("sl4", [128, GT * 128], F32)
        rs4 = cx.sb("rs4", [128, GT * 128], F32)
        sq4 = cx.sb("sq4", [128, GT * 128], BF16)
        zero4 = cx.sb("zero4", [128, GT * 128], F32)
        cx.op("pool", lambda e: e.memset(zero4[:, :], 0.0), [], [zero4])
        xt4 = [cx.sb(f"xtg{t}", [128, 1024], F32) for t in range(2)]
        xbs = [cx.sb(f"xbg{t}", [128, 1024], BF16) for t in range(2)]

        def G(gi):
            GS = gsets[gi % NGS]
            xT4, qkvT4 = GS["xT4"], GS["qkvT4"]
            tiles = [i for i in range(gi * GT, min(nt, (gi + 1) * GT))]
            N = len(tiles) * 128
            for t, i in enumerate(tiles):
                xt, xb = xt4[t % 2], xbs[t % 2]
                self.load_x_tile(src, srcn, i, xt)
                cx.op("pool", lambda e, xt=xt, xb=xb: e.tensor_copy(out=xb[:, :], in_=xt[:, :]), [xt], [xb])
                pbb = self.pbb[0]
                for kc in range(8):
                    cx.op("pe", lambda e, kc=kc, xb=xb, pbb=pbb: e.transpose(
                        out=pbb[:, kc * 128:(kc + 1) * 128], in_=xb[:, kc * 128:(kc + 1) * 128],
                        identity=idb[:, :]), [xb, idb], [pbb])
                cx.op("act", lambda e, t=t, pbb=pbb: e.activation(
                    out=xT4[:, :, t * 128:(t + 1) * 128], in_=pbb[:, :].rearrange("p (k t) -> p k t", k=8),
                    func=AF.Copy), [pbb], [xT4])
                cx.mark()
            for blk in range(NB):
                self.mm_acc(pF, pF[:, 0:N], [(win[:, kc, blk * 128:(blk + 1) * 128], xT4[:, kc, 0:N])
                                             for kc in range(8)], [win, xT4])
                cs = cs4[blk % 2]
                cx.op("pool", lambda e, cs=cs, blk=blk: e.tensor_copy(out=cs[:, 0:3], in_=halo[:, blk, :]), [halo], [cs])
                cx.op("act", lambda e, cs=cs: e.activation(out=cs[:, 3:3 + N], in_=pF[:, 0:N], func=AF.Copy), [pF], [cs])
                cx.op("pool", lambda e, cs=cs, blk=blk: e.tensor_copy(out=halo[:, blk, :], in_=cs[:, N:N + 3]), [cs], [halo])
                for j in range(4):
                    cx.op("dve", lambda e, cs=cs, blk=blk, j=j: e.scalar_tensor_tensor(
                        out=acc4[:, 0:N], in0=cs[:, j:j + N], scalar=vec[:, V_CW + blk * 4 + j:V_CW + blk * 4 + j + 1],
                        in1=(zero4[:, 0:N] if j == 0 else acc4[:, 0:N]), op0=ALU.mult, op1=ALU.add),
                        [cs, vec, zero4, acc4], [acc4])
                if blk >= 2 * nh:
                    self.silu(acc4, acc4[:, 0:N], qkvT4, qkvT4[:, blk, 0:N], rs4, rs4[:, 0:N])
                else:
                    self.silu(acc4, acc4[:, 0:N], sl4, sl4[:, 0:N], rs4, rs4[:, 0:N])
                    cx.op("act", lambda e: e.activation(out=sq4[:, 0:N], in_=sl4[:, 0:N], func=AF.Square), [sl4], [sq4])
                    self.mm_acc(pF, pF[:, 0:N], [(oneb[:, :], sq4[:, 0:N])], [oneb, sq4])
                    cx.op("act", lambda e: e.activation(out=rs4[:, 0:N], in_=pF[:, 0:N], func=AF.Ln,
                                                        bias=cst[:, C_EPSR:C_EPSR + 1]), [pF, cst], [rs4])
                    cx.op("act", lambda e: e.activation(out=rs4[:, 0:N], in_=rs4[:, 0:N], func=AF.Exp, scale=-0.5),
                          [rs4], [rs4])
                    cx.op("dve", lambda e, blk=blk: e.tensor_tensor(out=qkvT4[:, blk, 0:N], in0=sl4[:, 0:N],
                                                                    in1=rs4[:, 0:N], op=ALU.mult), [sl4, rs4], [qkvT4])
                cx.mark()

        def front(i):
            FS = fsets[i % 3]
            xb, xT, qkvT, ktok, vtok, ba, E3, o_n, sm, smb = (FS[k] for k in UNP)
            f32t, b16t = ff32, fb16
            GS = gsets[(i // GT) % NGS]
            xT_T, qkvT_T = GS["xT4"], GS["qkvT4"]
            t0_ = (i % GT) * 128
            xTv = xT_T[:, :, t0_:t0_ + 128]
            tb = list(range(nh, NB))
            for g0 in range(0, len(tb), 8):
                grp = tb[g0:g0 + 8]
                pbb = self.pbb[0]
                for k, blk in enumerate(grp):
                    cx.op("pe", lambda e, blk=blk, k=k: e.transpose(out=pbb[:, k * 128:(k + 1) * 128],
                                                                    in_=qkvT_T[:, blk, t0_:t0_ + 128],
                                                                    identity=idb[:, :]),
                          [qkvT_T, idb], [pbb])
                kb = [b for b in grp if b < 2 * nh]
                vb = [b for b in grp if b >= 2 * nh]
                if kb:
                    cx.op("act", lambda e, kb=kb, grp=grp, pbb=pbb: e.activation(
                        out=ktok[:, (kb[0] - nh) * 128:(kb[-1] - nh + 1) * 128],
                        in_=pbb[:, grp.index(kb[0]) * 128:(grp.index(kb[-1]) + 1) * 128], func=AF.Copy), [pbb], [ktok])
                if vb:
                    cx.op("act", lambda e, vb=vb, grp=grp, pbb=pbb: e.activation(
                        out=vtok[:, (vb[0] - 2 * nh) * 128:(vb[-1] - 2 * nh + 1) * 128],
                        in_=pbb[:, grp.index(vb[0]) * 128:(grp.index(vb[-1]) + 1) * 128], func=AF.Copy), [pbb], [vtok])
            p = pF
            self.mm_acc(p, p[:, 0:2 * nh], [(xTv[:, kc, :], win[:, kc, CB:CB + 2 * nh]) for kc in range(8)], [win, xT_T])
            cx.op("act", lambda e, p=p: e.activation(out=ba[:, :], in_=p[:, 0:2 * nh], func=AF.Copy), [p], [ba])
            beta, lnb, z, g = sm["beta"], sm["lnb"], sm["z"], sm["g"]
            cx.op("act", lambda e: e.activation(out=lnb[:, :], in_=ba[:, 0:nh], func=AF.Exp, scale=-1.0), [ba], [lnb])
            cx.op("act", lambda e: e.activation(out=lnb[:, :], in_=lnb[:, :], func=AF.Ln, bias=cst[:, C_ONE:C_ONE + 1]),
                  [lnb, cst], [lnb])
            cx.op("act", lambda e: e.activation(out=beta[:, :], in_=lnb[:, :], func=AF.Exp, scale=-1.0), [lnb], [beta])
            cx.op("act", lambda e: e.activation(out=lnb[:, :], in_=lnb[:, :], func=AF.Copy, scale=-1.0), [lnb], [lnb])
            cx.op("dve", lambda e: e.tensor_tensor(out=z[:, :], in0=ba[:, nh:2 * nh], in1=vec[:, V_DT:V_DT + nh], op=ALU.add),
                  [ba, vec], [z])
            cx.op("act", lambda e: e.activation(out=z[:, :], in_=z[:, :], func=AF.Exp), [z], [z])
            cx.op("act", lambda e: e.activation(out=z[:, :], in_=z[:, :], func=AF.Ln, bias=cst[:, C_ONE:C_ONE + 1]),
                  [z, cst], [z])
            cx.op("dve", lambda e: e.tensor_tensor(out=g[:, :], in0=z[:, :], in1=sm["negA"][:, :], op=ALU.mult),
                  [z, sm["negA"]], [g])
            for srcv, hb, hf, lf in ((g, smb["g_hb"], sm["g_hf"], sm["g_lf"]), (lnb, smb["l_hb"], sm["l_hf"], sm["l_lf"])):
                cx.op("dve", lambda e, srcv=srcv, hb=hb: e.tensor_copy(out=hb[:, :], in_=srcv[:, :]), [srcv], [hb])
                cx.op("dve", lambda e, hb=hb, hf=hf: e.tensor_copy(out=hf[:, :], in_=hb[:, :]), [hb], [hf])
                cx.op("dve", lambda e, srcv=srcv, hf=hf, lf=lf: e.tensor_tensor(out=lf[:, :], in0=srcv[:, :], in1=hf[:, :],
                                                                              op=ALU.subtract), [srcv, hf], [lf])
            cx.op("dve", lambda e: e.tensor_copy(out=smb["g_lb"][:, :], in_=sm["g_lf"][:, :]), [sm["g_lf"]], [smb["g_lb"]])
            p = pF
            self.mm_acc(p, p[:, 0:nh], [(m1b[:, :], smb["g_hb"][:, :]), (m1b[:, :], smb["g_lb"][:, :])],
                        [m1b, smb["g_hb"], smb["g_lb"]])
            self.mm_acc(p, p[:, nh:2 * nh], [(m2b[:, :], smb["g_hb"][:, :]), (m2b[:, :], smb["g_lb"][:, :])],
                        [m2b, smb["g_hb"], smb["g_lb"]])
            self.mm_acc(p, p[:, 2 * nh:3 * nh], [(oneb[:, :], smb["g_hb"][:, :]), (oneb[:, :], smb["g_lb"][:, :])],
                        [oneb, smb["g_hb"], smb["g_lb"]])
            cx.op("act", lambda e, p=p: e.activation(out=E3[:, :], in_=p[:, 0:3 * nh], func=AF.Exp), [p], [E3])
            be = sm["be"]
            cx.op("dve", lambda e: e.tensor_tensor(out=be[:, :], in0=beta[:, :], in1=E3[:, 0:nh], op=ALU.mult),
                  [beta, E3], [be])

        def head(i, h):
            FS = fsets[i % 3]
            xb, xT, qkvT, ktok, vtok, ba, E3, o_n, sm, smb = (FS[k] for k in UNP)
            HB = hsets[h % len(hsets)]
            xT_T = gsets[(i // GT) % NGS]["xT4"]
            qkvT_T = gsets[(i // GT) % NGS]["qkvT4"]
            t0_ = (i % GT) * 128
            xTv = xT_T[:, :, t0_:t0_ + 128]
            HO = HB["out"][i % 2]
            f32t, b16t = HB["f32t"], dict(HB["b16t"])
            b16t.update({k: HO[k] for k in ("wT", "qd", "QKTm", "kd")})
            u_f = HO["u_f"]
            bv, vnew, gsh, o_f, junk, ssq, rstd = (HB[k] for k in
                                                   ("bv", "vnew", "gsh", "o_f", "junk", "ssq", "rstd"))
            beta, be = sm["beta"], sm["be"]
            self._gs_t = [gsh]
            if True:
                qT = qkvT_T[:, h, t0_:t0_ + 128]
                kT = qkvT_T[:, nh + h, t0_:t0_ + 128]
                D, DT, BDT, EgcB = f32t["D"], f32t["DT"], f32t["BDT"], f32t["EgcB"]
                A_row, A_col, tmpq = f32t["A_row"], f32t["A_col"], f32t["tmpq"]
                rb = {}
                for nm, mask, sc in (("rR_h", C_M2, sm["g_hf"]), ("rR_l", C_M2, sm["g_lf"]), ("rC_h", C_M1, sm["g_hf"]),
                                     ("rC_l", C_M1, sm["g_lf"]), ("rL_h", C_ID, sm["l_hf"]), ("rL_l", C_ID, sm["l_lf"])):
                    t = b16t[nm]
                    rb[nm] = t
                    cx.op("dve", lambda e, t=t, mask=mask, sc=sc, h=h: e.scalar_tensor_tensor(
                        out=t[:, :], in0=cst[:, mask:mask + 128], scalar=sc[:, h:h + 1], in1=cst[:, mask:mask + 128],
                        op0=ALU.mult, op1=ALU.mult), [cst, sc], [t])
                pd = HB["pdg"]
                rr = [rb["rR_h"], rb["rR_l"]]
                rc = [rb["rC_h"], rb["rC_l"]]
                rl = [rb["rL_h"], rb["rL_l"]]
                self.mm_acc(pd, pd[:, 0:128], [(m1b[:, :], t[:, :]) for t in rr], [m1b] + rr)
                self.mm_acc(pd, pd[:, 128:256], [(m2b[:, :], t[:, :]) for t in rc], [m2b] + rc)
                self.mm_acc(pd, pd[:, 256:384], [(m2b[:, :], t[:, :]) for t in rc] + [(oneb[:, :], t[:, :]) for t in rl],
                            [m2b, oneb] + rc + rl)
                self.mm_acc(pd, pd[:, 384:512], [(oneb[:, :], t[:, :]) for t in rc], [oneb] + rc)
                for k, dt_ in enumerate((D, DT, BDT, EgcB)):
                    cx.op("act", lambda e, k=k, dt_=dt_: e.activation(out=dt_[:, :], in_=pd[:, k * 128:(k + 1) * 128],
                                                                    func=AF.Exp), [pd], [dt_])
                qd = b16t["qd"]
                cx.op("dve", lambda e, qT=qT: e.scalar_tensor_tensor(out=qd[:, :], in0=qT, scalar=SCALE, in1=EgcB[:, :],
                                                                     op0=ALU.mult, op1=ALU.mult), [qkvT_T, EgcB], [qd])
                pg = HB["pdg"]
                self.mm_acc(pg, pg[:, 0:128], [(kT, kT)], [qkvT_T])
                self.mm_acc(pg, pg[:, 128:256], [(kT, qT)], [qkvT_T])
                cx.op("dve", lambda e, h=h: e.scalar_tensor_tensor(out=A_row[:, :], in0=pg[:, 0:128],
                                                                   scalar=beta[:, h:h + 1], in1=D[:, :], op0=ALU.mult,
                                                                   op1=ALU.mult), [pg, beta, D], [A_row])
                cx.op("dve", lambda e: e.tensor_tensor(out=A_col[:, :], in0=pg[:, 0:128], in1=BDT[:, :], op=ALU.mult),
                      [pg, BDT], [A_col])
                cx.op("dve", lambda e: e.scalar_tensor_tensor(out=tmpq[:, :], in0=pg[:, 128:256], scalar=SCALE,
                                                              in1=DT[:, :], op0=ALU.mult, op1=ALU.mult), [pg, DT], [tmpq])
                QKTm = b16t["QKTm"]
                cx.op("dve", lambda e: e.tensor_tensor(out=QKTm[:, :], in0=tmpq[:, :], in1=M1, op=ALU.mult),
                      [tmpq, cst], [QKTm])
                X, R = b16t["X0"], b16t["R0"]
                Xn, Rn = b16t["X1"], b16t["R1"]
                cx.op("dve", lambda e: e.tensor_tensor(out=tmpq[:, :], in0=A_col[:, :], in1=cst[:, C_NMC:C_NMC + 128],
                                                       op=ALU.mult), [A_col, cst], [tmpq])
                cx.op("dve", lambda e, X=X: e.tensor_tensor(out=X[:, :], in0=tmpq[:, :], in1=ID, op=ALU.add),
                      [tmpq, cst], [X])
                cx.op("dve", lambda e: e.tensor_tensor(out=tmpq[:, :], in0=A_row[:, :], in1=cst[:, C_NMR:C_NMR + 128],
                                                       op=ALU.mult), [A_row, cst], [tmpq])
                cx.op("dve", lambda e, R=R: e.tensor_tensor(out=R[:, :], in0=tmpq[:, :], in1=ID, op=ALU.add),
                      [tmpq, cst], [R])
                W1, W2 = b16t["W1"], b16t["W2"]
                Offa, Ona = HB["Offa"], HB["Ona"]
                cx.op("dve", lambda e: e.tensor_tensor(
                    out=Offa[:, :, :], in0=A_row[:, :].unsqueeze(1).broadcast_to([128, 6, 128]),
                    in1=cst[:, C_NMR + 128:C_NMR + 7 * 128].rearrange("p (l s) -> p l s", l=6), op=ALU.mult),
                    [A_row, cst], [Offa])
                cx.op("dve", lambda e: e.tensor_tensor(
                    out=Ona[:, :, :], in0=A_col[:, :].unsqueeze(1).broadcast_to([128, 6, 128]),
                    in1=cst[:, C_NMC + 128:C_NMC + 7 * 128].rearrange("p (l s) -> p l s", l=6), op=ALU.mult),
                    [A_col, cst], [Ona])
                pw = HB["pw"]
                for l in range(1, 7):
                    lastl = (l == 6)
                    self.mm_acc(pw, pw[:, 0:128], [(Offa[:, l - 1, :], X[:, :])], [Offa, X])
                    cx.op("act", lambda e: e.activation(out=W1[:, :], in_=pw[:, 0:128], func=AF.Copy), [pw], [W1])
                    if not lastl:
                        self.mm_acc(pw, pw[:, 128:256], [(Ona[:, l - 1, :], R[:, :])], [Ona, R])
                        cx.op("act", lambda e: e.activation(out=W2[:, :], in_=pw[:, 128:256], func=AF.Copy), [pw], [W2])
                    self.mm_acc(pw, pw[:, 256:384], [(R[:, :], W1[:, :])], [R, W1])
                    cx.op("dve", lambda e, X=X, Xn=Xn: e.tensor_tensor(out=Xn[:, :], in0=X[:, :], in1=pw[:, 256:384],
                                                                       op=ALU.add), [X, pw], [Xn])
                    if not lastl:
                        self.mm_acc(pw, pw[:, 384:512], [(X[:, :], W2[:, :])], [X, W2])
                        cx.op("dve", lambda e, R=R, Rn=Rn: e.tensor_tensor(out=Rn[:, :], in0=R[:, :], in1=pw[:, 384:512],
                                                                           op=ALU.add), [R, pw], [Rn])
                    X, Xn = Xn, X
                    R, Rn = Rn, R
                kgb, kd, wT = b16t["kgb"], b16t["kd"], b16t["wT"]
                cx.op("act", lambda e, h=h: e.activation(out=bv[:, :], in_=vtok[:, h * 256:(h + 1) * 256], func=AF.Copy,
                                                         scale=beta[:, h:h + 1]), [vtok, beta], [bv])
                cx.op("act", lambda e, h=h: e.activation(out=kgb[:, :], in_=ktok[:, h * 128:(h + 1) * 128], func=AF.Copy,
                                                         scale=be[:, h:h + 1]), [ktok, be], [kgb])
                cx.op("act", lambda e, h=h: e.activation(out=kd[:, :], in_=ktok[:, h * 128:(h + 1) * 128], func=AF.Copy,
                                                         scale=E3[:, nh + h:nh + h + 1]), [ktok, E3], [kd])
                self.mm_acc(pw, pw[:, 0:256], [(X[:, :], bv[:, :])], [X, bv])
                cx.op("act", lambda e: e.activation(out=u_f[:, :], in_=pw[:, 0:256], func=AF.Copy), [pw], [u_f])
                self.mm_acc(pg, pg[:, 256:384], [(kgb[:, :], X[:, :])], [kgb, X])
                cx.op("act", lambda e: e.activation(out=wT[:, :], in_=pg[:, 256:384], func=AF.Copy), [pg], [wT])
                cx.mark()
                self.mm_acc(pB0, pB0[:, 0:256], [(wT[:, :], S_b[h][:, :])], [wT, S_b[h]])
                cx.op("dve", lambda e: e.tensor_tensor(out=vnew[:, :], in0=u_f[:, :], in1=pB0[:, 0:256], op=ALU.subtract),
                      [u_f, pB0], [vnew])
                self.mm_acc(pB0, pB0[:, 256:512], [(qd[:, :], S_b[h][:, :]), (QKTm[:, :], vnew[:, :])],
                            [qd, S_b[h], QKTm, vnew])
                self.mm_acc(pB1, pB1[:, 0:256], [(kd[:, :], vnew[:, :])], [kd, vnew])
                cx.op("dve", lambda e, h=h: e.scalar_tensor_tensor(out=S_f[h][:, :], in0=S_f[h][:, :],
                                                                   scalar=E3[:, 2 * nh + h:2 * nh + h + 1], in1=pB1[:, 0:256],
                                                                   op0=ALU.mult, op1=ALU.add), [S_f[h], E3, pB1], [S_f[h]])
                cx.op("pool", lambda e, h=h: e.tensor_copy(out=S_b[h][:, :], in_=S_f[h][:, :]), [S_f[h]], [S_b[h]])
                self.mm_acc(pB1, pB1[:, 256:512], [(xTv[:, kc, :], win[:, kc, CG + h * 256:CG + (h + 1) * 256])
                                                    for kc in range(8)], [xT_T, win])
                self.silu(pB1, pB1[:, 256:512], gsh, gsh[:, :], HB["sgt"], HB["sgt"][:, :])
                self.rms_gate(pB0, pB0[:, 256:512], gsh[:, :], h, o_n, (o_f, junk, ssq, rstd))

        assert nh <= len(hsets)
        AB = {}

        def rec_heads(i):
            for h in range(nh):
                AB[(i, h)] = cx.split_at_mark(cx.rec(head, i, h))

        def segs(lst):
            out, cur = [], []
            for it in lst:
                if it[0] == "mark":
                    if cur:
                        out.append(cur)
                    cur = []
                else:
                    cur.append(it)
            if cur:
                out.append(cur)
            return out

        gq = []
        ngrp = (nt + GT - 1) // GT
        cx.play([cx.rec(G, 0)])
        if ngrp > 1:
            gq = segs(cx.rec(G, 1))
        nextg = 2

        def front_list(j):
            nonlocal gq, nextg
            out = []
            if j % GT == 0 and j > 0:
                for sg in gq:
                    out += sg
                gq = segs(cx.rec(G, nextg)) if nextg < ngrp else []
                nextg += 1
                out += cx.rec(front, j)
            else:
                out += cx.rec(front, j)
                left = GT - (j % GT)
                n = (len(gq) + left - 1) // max(1, left)
                for sg in gq[:n]:
                    out += sg
                gq = gq[n:]
            return out

        cx.play([front_list(0)])
        rec_heads(0)
        lists = [AB[(0, h)][0] for h in range(nh)]
        if nt > 1:
            lists.append(front_list(1))
        cx.play(lists)
        for i in range(nt):
            lists = [sum((AB[(i, h)][1] for h in range(nh)), [])]
            if i + 1 < nt:
                rec_heads(i + 1)
                lists += [AB[(i + 1, h)][0] for h in range(nh)]
            st_ = cx.rec(self.store_onT, fsets[i % 3]["o_n"], 2 * nh, i)
            if i + 2 < nt:
                lists.append(front_list(i + 2) + st_)
            else:
                lists.append(st_)
            lists.append(self.take_pref(nt - i))
            cx.play(lists)
            for h in range(nh):
                del AB[(i, h)]
        for h in range(nh):
            tks.append(cx.dma(dr[f"sout{li}"][:, h * 256:(h + 1) * 256], S_f[h][:, :], reads=[S_f[h]], key=f"Sst{h}"))
        return tks

    def phase_b(self, li, kind, src, srcn, dst, dstn, last):
        cx, dr, vec = self.cx, self.dr, self.vec
        pb = self.pb
        nt = self.nt
        nkc = 16 if kind == "gdn" else 8
        wout = self.wbuf[f"wout_{kind}"]
        NS = 3
        bsets = [dict(xt=cx.sb(f"xtb{k}", [128, 1024], F32), on=cx.sb(f"onb{k}", [128, 2048], BF16),
                      z=cx.sb(f"z{k}", [128, 1024], F32), st=cx.sb("bnst", [128, 2, 6], F32), mv=cx.sb("mv", [128, 2], F32),
                      rstd=cx.sb("rstdb", [128, 1], F32), p=[pb[2 * (k % NS)], pb[2 * (k % NS) + 1]])
                 for k in range(2 * NS)]
        tks = []
        R = self.ranks
        tpc = self.tpc[li]

        def loads(i):
            B = bsets[i % (2 * NS)]
            xt, on = B["xt"], B["on"]
            self.load_x_tile(src, srcn, i, xt)
            q, t = i // tpc, i % tpc
            if R == 1:
                cx.dma(on[:, 0:nkc * 128], self.drt[f"ont{li}_{q}"].ap()[t * 128:(t + 1) * 128, :],
                       reads=[self.dtile("ont", i)], writes=[on], key=on.name)
            else:
                srcap = self.drt[f"onta{li}_{q}"].ap().rearrange("(r t p) c -> t p r c", r=R, p=128)[t]
                cx.dma(on[:, 0:nkc * 128].rearrange("p (r c) -> p r c", r=R), srcap,
                       reads=[self.dtile("onta", q)], writes=[on], key=on.name)

        def tile(i):
            B = bsets[i % (2 * NS)]
            xt, on, z, st, mv, rstd = (B[k] for k in ("xt", "on", "z", "st", "mv", "rstd"))
            for hb in range(2):
                p = B["p"][hb]
                self.mm_acc(p, p[:, :], [(on[:, kc * 128:(kc + 1) * 128], wout[:, kc, hb * 512:(hb + 1) * 512])
                                         for kc in range(nkc)], [on, wout])
                cx.op("dve", lambda e, p=p, hb=hb: e.scalar_tensor_tensor(
                    out=z[:, hb * 512:(hb + 1) * 512], in0=xt[:, hb * 512:(hb + 1) * 512], scalar=DEEP_ALPHA,
                    in1=p[:, :], op0=ALU.mult, op1=ALU.add), [xt, p], [z])
            for hb in range(2):
                cx.op("dve", lambda e, hb=hb: e.bn_stats(out=st[:, hb, :], in_=z[:, hb * 512:(hb + 1) * 512]),
                      [z], [st])
            cx.op("dve", lambda e: e.bn_aggr(out=mv[:, :], in_=st[:, :, :]), [st], [mv])
            cx.op("act", lambda e: e.activation(out=rstd[:, 0:1], in_=mv[:, 1:2], func=AF.Ln,
                                                bias=self.cst[:, C_EPSL:C_EPSL + 1]), [mv, self.cst], [rstd])
            cx.op("act", lambda e: e.activation(out=rstd[:, 0:1], in_=rstd[:, 0:1], func=AF.Exp, scale=-0.5),
                  [rstd], [rstd])
            cx.op("dve", lambda e: e.tensor_scalar(out=z[:, :], in0=z[:, :], scalar1=mv[:, 0:1], scalar2=rstd[:, 0:1],
                                                   op0=ALU.subtract, op1=ALU.mult), [z, mv, rstd], [z])
            cx.op("pool", lambda e: e.tensor_tensor(out=z[:, :], in0=z[:, :], in1=vec[:, V_LNG:V_LNG + 1024],
                                                    op=ALU.mult), [z, vec], [z])
            cx.op("pool", lambda e: e.tensor_tensor(out=z[:, :], in0=z[:, :], in1=vec[:, V_LNB:V_LNB + 1024],
                                                    op=ALU.add), [z, vec], [z])
            cx.dma(dst[i * 128:(i + 1) * 128, :], z[:, :], reads=[z], writes=[self.dtile(dstn, i)],
                   key=z.name + "st")

        for i in range(0, min(nt, NS)):
            loads(i)
        for i0_ in range(0, nt, NS):
            lists = [cx.rec(tile, i) for i in range(i0_, min(nt, i0_ + NS))]
            nxt = list(range(i0_ + NS, min(nt, i0_ + 2 * NS)))
            if nxt:
                lists.append(cx.rec(lambda: [loads(i) for i in nxt]))
            lists.append(self.take_pref((nt - i0_ + NS - 1) // NS))
            cx.play(lists)
        return tks


def gdn_cols(r, R):
    nh = 8 // R
    h0 = r * nh
    return np.concatenate([np.arange(h0 * 128, (h0 + nh) * 128), 1024 + np.arange(h0 * 128, (h0 + nh) * 128),
                           2048 + np.arange(h0 * 256, (h0 + nh) * 256), 4096 + np.arange(h0 * 256, (h0 + nh) * 256),
                           6144 + np.arange(h0, h0 + nh), 6152 + np.arange(h0, h0 + nh)])


def gla_cols(r, R):
    nh = 4 // R
    h0 = r * nh
    return np.concatenate([np.arange(h0 * 128, (h0 + nh) * 128), 512 + np.arange(h0 * 128, (h0 + nh) * 128),
                           1024 + np.arange(h0 * 256, (h0 + nh) * 256), 2048 + np.arange(h0 * 256, (h0 + nh) * 256),
                           3072 + np.arange(16)])


def pack_vecs(kind, j, i, inp, r=0, R=1):
    v = np.zeros((128, VTOT), np.float32)
    v[:, V_LNG:V_LNG + 1024] = inp["ln_g"][i][None, :]
    v[:, V_LNB:V_LNB + 1024] = inp["ln_b"][i][None, :]
    if kind == "gdn":
        v[:, V_NW:V_NW + 256] = inp["gdn_norm_w"][j][None, :]
        nh = 8 // R
        v[:, V_A:V_A + nh] = inp["gdn_a_log"][j][None, r * nh:(r + 1) * nh]
        v[:, V_DT:V_DT + nh] = inp["gdn_dt_bias"][j][None, r * nh:(r + 1) * nh]
        cw = inp["gdn_conv_w"][j][:, gdn_cols(r, R)[:4 * nh * 128]]
        v[:, V_CW:V_CW + 4 * nh * 4] = cw.reshape(4, 4 * nh, 128).transpose(2, 1, 0).reshape(128, 4 * nh * 4)
    else:
        v[:, V_NW:V_NW + 256] = inp["gla_norm_w"][j][None, :]
        nh = 4 // R
        v[:, V_BGK:V_BGK + nh * 128] = inp["gla_b_gk"][j][None, r * nh * 128:(r + 1) * nh * 128]
    return v


LAYERS = ["gdn", "gla", "gdn", "gla"]


def make_in_map(xb, inp, nt, r=0, R=1):
    m = {"x": np.ascontiguousarray(xb, dtype=np.float32), "consts": make_consts(),
         "xhalo": np.zeros((len(LAYERS), 3, D_MODEL), np.float32)}
    vecs = []
    for i, kind in enumerate(LAYERS):
        j = i // 2
        vecs.append(pack_vecs(kind, j, i, inp, r, R))
        nh = (8 if kind == "gdn" else 4) // R
        cols = gdn_cols(r, R) if kind == "gdn" else gla_cols(r, R)
        m[f"win{i}"] = np.ascontiguousarray(inp[f"{kind}_w_in"][j][:, cols], dtype=np.float32)
        m[f"wout{i}"] = np.ascontiguousarray(inp[f"{kind}_w_out"][j], dtype=np.float32)
        m[f"sin{i}"] = np.zeros((128, nh * 256), np.float32)
        if kind == "gla":
            m[f"wgk{i}"] = np.ascontiguousarray(inp["gla_w_gk_up"][j][:, r * nh * 128:(r + 1) * nh * 128],
                                                dtype=np.float32)
    m["vecs"] = np.stack(vecs, 0)
    return m


_PROG = {}
RANKS = 4


def kernel(**inputs):
    inp = {k: np.asarray(v) for k, v in inputs.items()}
    x = inp["x"]
    B, T, _ = x.shape
    nt = T // 128
    R = RANKS
    if nt not in _PROG:
        _PROG[nt] = Prog(LAYERS, nt, ranks=R)
    P = _PROG[nt]
    in_maps = [make_in_map(x[c // R], inp, nt, c % R, R) for c in range(8)]
    res = run_bass_kernel_spmd(P.nc, in_maps, core_ids=list(range(8)))
    out = np.stack([np.asarray(res.results[b * R]["y"], dtype=np.float32).reshape(T, D_MODEL) for b in range(B)], 0)
    return out
```
